# Optimizing a Trainium2 kernel written in Bass

```python
import math
import jax, jax.numpy as jnp
from jax import lax
import numpy as np

D_MODEL = 1024
BATCH = 8
SEQ = 4096
DEPTH = 2

EPS = 1e-6
A_HEADS = 8
A_DHEAD = 64
A_QK = A_HEADS * 2 * A_DHEAD
A_V = A_HEADS * 2 * A_DHEAD
Q_BLOCK = 128
M_HEADS = 8
M_DQK = 64
M_DV = 128
M_QK = M_HEADS * M_DQK
M_V = M_HEADS * M_DV
CONV_K = 4
CHUNK = 64
F_BIAS_OFFSET = 3.0
D_FF = 2816
FFN_SCALE = 0.5
SPLIT_SIZES = [A_QK, A_QK, A_V, M_QK, M_QK, M_V, M_V, M_HEADS, M_HEADS, 2 * D_MODEL]
SPLIT_IDX = [int(v) for v in np.cumsum(SPLIT_SIZES)[:-1]]
C_IN = int(sum(SPLIT_SIZES))

kernel_name = "hybrid_diffattn_mlstm_macaron"


def rmsnorm(x, w):
    xf = x.astype(jnp.float32)
    y = xf * lax.rsqrt(jnp.mean(xf * xf, axis=-1, keepdims=True) + EPS)
    return (y * w.astype(jnp.float32)).astype(x.dtype)


def swiglu(h, w_gu, w_down):
    g, u = jnp.split(h @ w_gu, 2, axis=-1)
    return (jax.nn.silu(g) * u) @ w_down


def causal_depthwise_conv(x, w, b):
    c = x.shape[-1]
    y = lax.conv_general_dilated(x, w[:, None, :].astype(x.dtype), window_strides=(1,),
                                 padding=[(CONV_K - 1, 0)],
                                 dimension_numbers=('NWC', 'WIO', 'NWC'),
                                 feature_group_count=c)
    return y + b


def diff_attention(q, k, v, lam, lam_init, norm_w):
    b_, s_, _ = q.shape
    nb = s_ // Q_BLOCK
    scale = A_DHEAD ** -0.5
    qb = q.reshape(b_, nb, Q_BLOCK, A_HEADS, 2, A_DHEAD).transpose(1, 0, 3, 4, 2, 5)
    kh = k.reshape(b_, s_, A_HEADS, 2, A_DHEAD).transpose(0, 2, 3, 1, 4)
    vh = v.reshape(b_, s_, A_HEADS, 2 * A_DHEAD).transpose(0, 2, 1, 3)
    k_pos = jnp.arange(s_)
    lam32 = lam.astype(jnp.float32)

    def block(args):
        qblk, idx = args
        sc = jnp.einsum('bhmqd,bhmkd->bhmqk', qblk, kh).astype(jnp.float32) * scale
        q_pos = idx * Q_BLOCK + jnp.arange(Q_BLOCK)
        causal = k_pos[None, :] <= q_pos[:, None]
        sc = jnp.where(causal, sc, -jnp.inf)
        p = jax.nn.softmax(sc, axis=-1)
        a = p[:, :, 0] - lam32 * p[:, :, 1]
        return jnp.einsum('bhqk,bhkv->bhqv', a.astype(vh.dtype), vh)

    o = lax.map(block, (qb, jnp.arange(nb)))
    o = o.transpose(1, 0, 3, 2, 4).reshape(b_, s_, A_HEADS, 2 * A_DHEAD)
    o = rmsnorm(o, norm_w) * (1.0 - lam_init)
    return o.reshape(b_, s_, A_V)


def mlstm_chunkwise(q, k, v, ig, fg):
    b_, s_ = q.shape[:2]
    nc = s_ // CHUNK
    f32 = jnp.float32

    def chunk(t, d):
        return t.astype(f32).reshape(b_, nc, CHUNK, M_HEADS, d).transpose(0, 3, 1, 2, 4)

    qc = chunk(q, M_DQK) * (M_DQK ** -0.5)
    kc = chunk(k, M_DQK)
    vc = chunk(v, M_DV)
    igc = ig.astype(f32).reshape(b_, nc, CHUNK, M_HEADS).transpose(0, 3, 1, 2)
    logf = jax.nn.log_sigmoid(fg.astype(f32)).reshape(b_, nc, CHUNK, M_HEADS).transpose(0, 3, 1, 2)
    bcum = jnp.cumsum(logf, axis=-1)
    b_last = bcum[..., -1]

    a = b_last[..., None] - bcum + igc
    m_loc = jnp.max(a, axis=-1)
    w = jnp.exp(a - m_loc[..., None])
    c_loc = jnp.einsum('bhcl,bhcld,bhclv->bhcdv', w, kc, vc)
    n_loc = jnp.einsum('bhcl,bhcld->bhcd', w, kc)

    def step(carry, xs):
        c_st, n_st, m_st = carry
        bl, ml, cl, nl = xs
        m_new = jnp.maximum(bl + m_st, ml)
        sp = jnp.exp(bl + m_st - m_new)
        sl = jnp.exp(ml - m_new)
        c_new = sp[..., None, None] * c_st + sl[..., None, None] * cl
        n_new = sp[..., None] * n_st + sl[..., None] * nl
        return (c_new, n_new, m_new), (c_st, n_st, m_st)

    init = (jnp.zeros((b_, M_HEADS, M_DQK, M_DV), f32),
            jnp.zeros((b_, M_HEADS, M_DQK), f32),
            jnp.zeros((b_, M_HEADS), f32))
    xs = (jnp.moveaxis(b_last, 2, 0), jnp.moveaxis(m_loc, 2, 0),
          jnp.moveaxis(c_loc, 2, 0), jnp.moveaxis(n_loc, 2, 0))
    _, (c_prev, n_prev, m_prev) = lax.scan(step, init, xs)
    c_prev = jnp.moveaxis(c_prev, 0, 2)
    n_prev = jnp.moveaxis(n_prev, 0, 2)
    m_prev = jnp.moveaxis(m_prev, 0, 2)

    tril = jnp.tril(jnp.ones((CHUNK, CHUNK), dtype=bool))
    dmat = bcum[..., :, None] - bcum[..., None, :] + igc[..., None, :]
    dmat = jnp.where(tril, dmat, -jnp.inf)
    m_inter = bcum + m_prev[..., None]
    m_t = jnp.maximum(m_inter, jnp.max(dmat, axis=-1))
    sc_inter = jnp.exp(m_inter - m_t)
    p = jnp.exp(dmat - m_t[..., None]) * jnp.einsum('bhcld,bhcsd->bhcls', qc, kc)
    num = sc_inter[..., None] * jnp.einsum('bhcld,bhcdv->bhclv', qc, c_prev) + \
        jnp.einsum('bhcls,bhcsv->bhclv', p, vc)
    den = sc_inter * jnp.einsum('bhcld,bhcd->bhcl', qc, n_prev) + jnp.sum(p, axis=-1)
    h = num / jnp.maximum(jnp.abs(den), jnp.exp(-m_t))[..., None]
    h = h.transpose(0, 2, 3, 1, 4).reshape(b_, s_, M_HEADS, M_DV)
    return h.astype(q.dtype)


def setup_inputs(seed: int = 0) -> dict:
    key = jax.random.key(seed)
    ks = iter(jax.random.split(key, 32))
    L = DEPTH

    def nrm(shape, scale):
        return scale * jax.random.normal(next(ks), shape, jnp.float32)

    def gain(shape):
        return 1.0 + nrm(shape, 0.05)

    return {
        "x": nrm((BATCH, SEQ, D_MODEL), 1.0),
        "ffn1_norm_pre": gain((L, D_MODEL)),
        "ffn1_w_gu": nrm((L, D_MODEL, 2 * D_FF), D_MODEL ** -0.5),
        "ffn1_w_down": nrm((L, D_FF, D_MODEL), D_FF ** -0.5),
        "ffn1_norm_post": gain((L, D_MODEL)),
        "mix_norm_pre": gain((L, D_MODEL)),
        "w_in": nrm((L, D_MODEL, C_IN), D_MODEL ** -0.5),
        "attn_lam_q1": nrm((L, A_DHEAD), 0.1),
        "attn_lam_k1": nrm((L, A_DHEAD), 0.1),
        "attn_lam_q2": nrm((L, A_DHEAD), 0.1),
        "attn_lam_k2": nrm((L, A_DHEAD), 0.1),
        "attn_norm_w": gain((L, 2 * A_DHEAD)),
        "conv_w": nrm((L, CONV_K, 2 * M_QK), CONV_K ** -0.5),
        "conv_b": nrm((L, 2 * M_QK), 0.02),
        "igate_b": nrm((L, M_HEADS), 0.1),
        "fgate_b": F_BIAS_OFFSET + nrm((L, M_HEADS), 0.5),
        "mlstm_norm_w": gain((L, M_HEADS, M_DV)),
        "w_proj_a": nrm((L, A_V, D_MODEL), A_V ** -0.5),
        "w_proj_m": nrm((L, M_V, D_MODEL), M_V ** -0.5),
        "gate_b": nrm((L, 2 * D_MODEL), 0.01),
        "w_out": nrm((L, D_MODEL, D_MODEL), D_MODEL ** -0.5),
        "mix_norm_post": gain((L, D_MODEL)),
        "ffn2_norm_pre": gain((L, D_MODEL)),
        "ffn2_w_gu": nrm((L, D_MODEL, 2 * D_FF), D_MODEL ** -0.5),
        "ffn2_w_down": nrm((L, D_FF, D_MODEL), D_FF ** -0.5),
        "ffn2_norm_post": gain((L, D_MODEL)),
    }


def reference(x, ffn1_norm_pre, ffn1_w_gu, ffn1_w_down, ffn1_norm_post, mix_norm_pre, w_in,
              attn_lam_q1, attn_lam_k1, attn_lam_q2, attn_lam_k2, attn_norm_w, conv_w, conv_b,
              igate_b, fgate_b, mlstm_norm_w, w_proj_a, w_proj_m, gate_b, w_out, mix_norm_post,
              ffn2_norm_pre, ffn2_w_gu, ffn2_w_down, ffn2_norm_post):
    b_, s_, _ = x.shape
    for l in range(DEPTH):
        h = rmsnorm(x, ffn1_norm_pre[l])
        x = x + FFN_SCALE * rmsnorm(swiglu(h, ffn1_w_gu[l], ffn1_w_down[l]), ffn1_norm_post[l])

        h = rmsnorm(x, mix_norm_pre[l])
        z = h @ w_in[l]
        aq, ak, av, mq, mk, mv, mo, mi, mf, gl = jnp.split(z, SPLIT_IDX, axis=-1)

        lam_init = 0.8 - 0.6 * math.exp(-0.3 * l)
        lam = (jnp.exp(jnp.sum(attn_lam_q1[l] * attn_lam_k1[l]))
               - jnp.exp(jnp.sum(attn_lam_q2[l] * attn_lam_k2[l])) + lam_init)
        ya = diff_attention(aq, ak, av, lam, lam_init, attn_norm_w[l])

        mqk = jax.nn.silu(causal_depthwise_conv(jnp.concatenate([mq, mk], axis=-1), conv_w[l], conv_b[l]))
        mq_c, mk_c = jnp.split(mqk, 2, axis=-1)
        hm = mlstm_chunkwise(mq_c.reshape(b_, s_, M_HEADS, M_DQK),
                             mk_c.reshape(b_, s_, M_HEADS, M_DQK),
                             mv.reshape(b_, s_, M_HEADS, M_DV),
                             mi + igate_b[l], mf + fgate_b[l])
        ym = (jax.nn.sigmoid(mo) * rmsnorm(hm, mlstm_norm_w[l]).reshape(b_, s_, M_V))

        g_a, g_m = jnp.split(jax.nn.sigmoid(gl + gate_b[l]), 2, axis=-1)
        merged = g_a * (ya @ w_proj_a[l]) + g_m * (ym @ w_proj_m[l])
        x = x + rmsnorm(merged @ w_out[l], mix_norm_post[l])

        h = rmsnorm(x, ffn2_norm_pre[l])
        x = x + FFN_SCALE * rmsnorm(swiglu(h, ffn2_w_gu[l], ffn2_w_down[l]), ffn2_norm_post[l])
    return x
```

```python
import math
from contextlib import ExitStack

import numpy as np
import concourse.bass as bass
import concourse.mybir as mybir
from concourse.bass_utils import run_bass_kernel_spmd

F32 = mybir.dt.float32
BF16 = mybir.dt.bfloat16
AF = mybir.ActivationFunctionType
ALU = mybir.AluOpType
AX = mybir.AxisListType

D_MODEL = 1024
BATCH = 8
SEQ = 4096
DEPTH = 2
EPS = 1e-6
A_HEADS = 8
A_DHEAD = 64
M_HEADS = 8
M_DQK = 64
M_DV = 128
D_FF = 2816
C_IN = 8208
NT = SEQ // 128

O_AQ, O_AK, O_AV = 0, 1024, 2048
O_MQ, O_MK, O_MV, O_MO = 3072, 3584, 4096, 5120
O_MI, O_MF, O_GL = 6144, 6152, 6160


class Sem:
    def __init__(self, nc, name):
        self.h = nc.alloc_semaphore(name)
        self.v = 0
        self.name = name


class Buf:
    __slots__ = ("w", "r", "name")

    def __init__(self, name=""):
        self.w = None
        self.r = {}
        self.name = name


class Eng:
    def __init__(self, k, name, e, n_dma_sems=0):
        self.k = k
        self.name = name
        self.e = e
        self.sem = Sem(k.nc, "s_" + name)
        self.waited = {}
        self.dma_sems = [Sem(k.nc, f"d_{name}{i}") for i in range(n_dma_sems)]
        self.dma_rr = 0

    def wait(self, tok):
        sem, v = tok
        if self.waited.get(sem, 0) >= v:
            return
        self.e.wait_ge(sem.h, v)
        self.waited[sem] = v


class K:
    def __init__(self, nc):
        self.nc = nc
        self.pe = Eng(self, "pe", nc.tensor)
        self.act = Eng(self, "act", nc.scalar)
        self.dve = Eng(self, "dve", nc.vector)
        self.pool = Eng(self, "pool", nc.gpsimd, n_dma_sems=12)
        self.sp = Eng(self, "sp", nc.sync, n_dma_sems=20)
        self.engs = [self.pe, self.act, self.dve, self.pool, self.sp]
        self.const_t = nc.alloc_sbuf_tensor("const_cols", [128, 32], F32)
        self.const_b = Buf("consts")
        self.consts = {}

    def const(self, val):
        val = float(val)
        if val not in self.consts:
            i = len(self.consts)
            assert i < self.const_t.shape[1]
            ap = self.const_t[:, i:i + 1]
            self.op(self.pool, lambda: self.nc.gpsimd.memset(ap, val), writes=[self.const_b])
            self.consts[val] = ap
        return self.consts[val]

    def _deps(self, eng, reads, writes):
        for b in reads:
            if b.w is not None:
                eng.wait(b.w)
        for b in writes:
            if b.w is not None:
                eng.wait(b.w)
            for s, v in b.r.items():
                eng.wait((s, v))

    def _mark(self, tok, reads, writes):
        s, v = tok
        for b in reads:
            if b.r.get(s, 0) < v:
                b.r[s] = v
        for b in writes:
            b.w = tok
            b.r = {}

    def op(self, eng, fn, reads=(), writes=()):
        self._deps(eng, reads, writes)
        ins = fn()
        eng.sem.v += 1
        ins.then_inc(eng.sem.h, 1)
        self._mark((eng.sem, eng.sem.v), reads, writes)

    def mm(self, fns, reads=(), writes=()):
        eng = self.pe
        self._deps(eng, reads, writes)
        ins = None
        for fn in fns:
            ins = fn()
        eng.sem.v += 1
        ins.then_inc(eng.sem.h, 1)
        self._mark((eng.sem, eng.sem.v), reads, writes)

    def dma(self, q, out, in_, reads=(), writes=()):
        sem = q.dma_sems[q.dma_rr]
        q.dma_rr = (q.dma_rr + 1) % len(q.dma_sems)
        if sem.v:
            q.wait((sem, sem.v))
        self._deps(q, reads, writes)
        ins = q.e.dma_start(out=out, in_=in_)
        sem.v += 16
        ins.then_inc(sem.h, 16)
        self._mark((sem, sem.v), reads, writes)

    def barrier(self):
        sems = []
        for e in self.engs:
            sems.append(e.sem)
            sems.extend(e.dma_sems)
        for e in self.engs:
            for s in sems:
                if s.v:
                    e.wait((s, s.v))

    def finish(self):
        self.barrier()


_UID = [0]


def uname(name):
    _UID[0] += 1
    return f"{name}_{_UID[0]}"


class Ring:
    def __init__(self, st, nc, name, n, shape, dtype, psum=False):
        self.n = n
        self.i = 0
        self.t = []
        self.b = []
        for j in range(n):
            if psum:
                t = st.enter_context(nc.psum_tensor(uname(name), shape, dtype))
            else:
                t = st.enter_context(nc.sbuf_tensor(uname(name), shape, dtype))
            self.t.append(t)
            self.b.append(Buf(f"{name}{j}"))

    def next(self):
        j = self.i
        self.i = (self.i + 1) % self.n
        return self.t[j], self.b[j]


def sb(st, nc, name, shape, dtype):
    return st.enter_context(nc.sbuf_tensor(uname(name), shape, dtype))


def ps(st, nc, name, shape, dtype):
    return st.enter_context(nc.psum_tensor(uname(name), shape, dtype))


def load_bcast(k, st, name, vec_ap, n):
    nc = k.nc
    t = sb(st, nc, name, [128, n], F32)
    b = Buf(name)
    k.dma(k.sp, t[:], vec_ap.partition_broadcast(128), writes=[b])
    return t, b


def rms_rstd(k, ss, ss_b, rstd, rstd_b, n, eps=EPS, mul=1.0):
    nc = k.nc
    c_eps = k.const(eps)
    c_mul = k.const(math.log(mul))
    k.op(k.act, lambda: nc.scalar.activation(out=rstd, in_=ss, func=AF.Ln, scale=1.0 / n, bias=c_eps),
         reads=[ss_b, k.const_b], writes=[rstd_b])
    k.op(k.act, lambda: nc.scalar.activation(out=rstd, in_=rstd, func=AF.Exp, scale=-0.5, bias=c_mul),
         reads=[rstd_b, k.const_b], writes=[rstd_b])


def phase_ffn(k, ident, ident_b, x_in, x_out, w_gu, w_down, n_pre, n_post, xbuf=None):
    nc = k.nc
    TT = 512
    NTT = SEQ // TT
    NS = TT // 128
    NJ = D_FF // 128
    with ExitStack() as st:
        wgu = sb(st, nc, "wgu", [128, 8, 2 * D_FF], BF16)
        wdn = sb(st, nc, "wdn", [128, NJ, D_MODEL], BF16)
        CB = 1408
        wgu_b = {}
        for half in range(2):
            for gu in range(2):
                c0 = gu * D_FF + half * CB
                b = Buf(f"wgu{gu}{half}")
                wgu_b[(gu, half)] = b
                for c in range(8):
                    k.dma(k.pool, wgu[:, c, c0:c0 + CB], w_gu[c * 128:(c + 1) * 128, c0:c0 + CB], writes=[b])
        wdn_b = []
        for hh in range(2):
            b = Buf(f"wdn{hh}")
            wdn_b.append(b)
            k.dma(k.pool, wdn[:, hh * 11:(hh + 1) * 11, :],
                  w_down[hh * 1408:(hh + 1) * 1408, :].rearrange("(j p) n -> p j n", p=128), writes=[b])
        npre, npre_b = load_bcast(k, st, "npre", n_pre, D_MODEL)
        npost, npost_b = load_bcast(k, st, "npost", n_post, D_MODEL)

        xs = Ring(st, nc, "xs", 2, [128, D_MODEL], F32)
        hb = Ring(st, nc, "hb", 2, [128, D_MODEL], BF16)
        junk = Ring(st, nc, "junk", 1, [128, D_MODEL], BF16)
        stat = Ring(st, nc, "stat", 4, [128, 4], F32)
        hT = Ring(st, nc, "hT", 1, [128, 8, TT], BF16)
        actT = Ring(st, nc, "actT", 1, [128, NJ, TT], BF16)
        sg = Ring(st, nc, "sg", 2, [128, TT], F32)
        xr = Ring(st, nc, "xr", 2, [128, D_MODEL], F32)
        yo = Ring(st, nc, "yo", 1, [128, D_MODEL], F32)
        ptr = Ring(st, nc, "ptr", 1, [128, 8, 128], BF16, psum=True)
        pg = Ring(st, nc, "pg", 2, [128, TT], F32, psum=True)
        pu = Ring(st, nc, "pu", 2, [128, TT], F32, psum=True)
        py = Ring(st, nc, "py", 1, [128, 2, 512], F32, psum=True)

        for t in range(NTT):
            hT_t, hT_b = hT.next()
            for s in range(NS):
                tok0 = (t * NS + s) * 128
                ti = t * NS + s
                x_t, x_b = xs.next()
                k.dma(k.sp, x_t[:], x_in[tok0:tok0 + 128, :],
                      reads=[xbuf["in"][ti]] if xbuf else [], writes=[x_b])
                j_t, j_b = junk.next()
                s_t, s_b = stat.next()
                k.op(k.act, lambda: nc.scalar.activation(out=j_t[:], in_=x_t[:], func=AF.Square,
                                                         accum_out=s_t[:, 0:1]),
                     reads=[x_b], writes=[j_b, s_b])
                rms_rstd(k, s_t[:, 0:1], s_b, s_t[:, 1:2], s_b, D_MODEL)
                h_t, h_b = hb.next()
                k.op(k.dve, lambda: nc.vector.scalar_tensor_tensor(out=h_t[:], in0=x_t[:], scalar=s_t[:, 1:2],
                                                                   in1=npre[:], op0=ALU.mult, op1=ALU.mult),
                     reads=[x_b, s_b, npre_b], writes=[h_b])
                p_t, p_b = ptr.next()
                k.mm([(lambda c=c: nc.tensor.transpose(p_t[:, c, :], h_t[:, c * 128:(c + 1) * 128], ident[:]))
                      for c in range(8)], reads=[h_b, ident_b], writes=[p_b])
                k.op(k.act, lambda: nc.scalar.copy(out=hT_t[:, :, s * 128:(s + 1) * 128], in_=p_t[:]),
                     reads=[p_b], writes=[hT_b])
            a_t, a_b = actT.next()
            for j in range(NJ):
                half = j // 11
                g_t, g_b = pg.next()
                u_t, u_b = pu.next()
                k.mm([(lambda c=c: nc.tensor.matmul(g_t[:], wgu[:, c, j * 128:(j + 1) * 128], hT_t[:, c, :],
                                                    start=(c == 0), stop=(c == 7))) for c in range(8)],
                     reads=[wgu_b[(0, half)], hT_b], writes=[g_b])
                k.mm([(lambda c=c: nc.tensor.matmul(u_t[:], wgu[:, c, D_FF + j * 128:D_FF + (j + 1) * 128],
                                                    hT_t[:, c, :], start=(c == 0), stop=(c == 7)))
                      for c in range(8)],
                     reads=[wgu_b[(1, half)], hT_b], writes=[u_b])
                sg_t, sg_b = sg.next()
                k.op(k.act, lambda: nc.scalar.activation(out=sg_t[:], in_=g_t[:], func=AF.Silu),
                     reads=[g_b], writes=[sg_b])
                k.op(k.dve, lambda: nc.vector.tensor_tensor(out=a_t[:, j, :], in0=sg_t[:], in1=u_t[:],
                                                            op=ALU.mult),
                     reads=[sg_b, u_b], writes=[a_b])
            for s in range(NS):
                tok0 = (t * NS + s) * 128
                ti = t * NS + s
                y_t, y_b = py.next()
                fns = []
                for hf in range(2):
                    for j in range(NJ):
                        fns.append(lambda j=j, hf=hf: nc.tensor.matmul(
                            y_t[:, hf, :], a_t[:, j, s * 128:(s + 1) * 128], wdn[:, j, hf * 512:(hf + 1) * 512],
                            start=(j == 0), stop=(j == NJ - 1)))
                k.mm(fns, reads=[a_b] + wdn_b, writes=[y_b])
                xr_t, xr_b = xr.next()
                k.dma(k.sp, xr_t[:], x_in[tok0:tok0 + 128, :],
                      reads=[xbuf["in"][ti]] if xbuf else [], writes=[xr_b])
                j_t, j_b = junk.next()
                s_t, s_b = stat.next()
                for hf in range(2):
                    k.op(k.act, lambda: nc.scalar.activation(out=j_t[:, hf * 512:(hf + 1) * 512], in_=y_t[:, hf, :],
                                                             func=AF.Square, accum_out=s_t[:, 2 + hf:3 + hf]),
                         reads=[y_b], writes=[j_b, s_b])
                k.op(k.dve, lambda: nc.vector.tensor_tensor(out=s_t[:, 0:1], in0=s_t[:, 2:3], in1=s_t[:, 3:4],
                                                            op=ALU.add), reads=[s_b], writes=[s_b])
                rms_rstd(k, s_t[:, 0:1], s_b, s_t[:, 1:2], s_b, D_MODEL, mul=0.5)
                o_t, o_b = yo.next()
                for hf in range(2):
                    k.op(k.dve, lambda: nc.vector.tensor_tensor(
                        out=o_t[:, hf * 512:(hf + 1) * 512], in0=y_t[:, hf, :], in1=npost[:, hf * 512:(hf + 1) * 512],
                        op=ALU.mult), reads=[y_b, npost_b], writes=[o_b])
                k.op(k.dve, lambda: nc.vector.scalar_tensor_tensor(
                    out=xr_t[:], in0=o_t[:], scalar=s_t[:, 1:2], in1=xr_t[:], op0=ALU.mult, op1=ALU.add),
                    reads=[o_b, xr_b, s_b], writes=[xr_b])
                k.dma(k.sp, x_out[tok0:tok0 + 128, :], xr_t[:], reads=[xr_b],
                      writes=[xbuf["out"][ti]] if xbuf else [])
        k.barrier()


def norm_transpose(k, st_rings, ident, ident_b, x_src, tok0, npre, npre_b, hT_t, hT_b, s):
    nc = k.nc
    xs, hb, junk, stat, ptr = st_rings
    x_t, x_b = xs.next()
    k.dma(k.sp, x_t[:], x_src[tok0:tok0 + 128, :], writes=[x_b])
    j_t, j_b = junk.next()
    s_t, s_b = stat.next()
    k.op(k.act, lambda: nc.scalar.activation(out=j_t[:], in_=x_t[:], func=AF.Square, accum_out=s_t[:, 0:1]),
         reads=[x_b], writes=[j_b, s_b])
    rms_rstd(k, s_t[:, 0:1], s_b, s_t[:, 1:2], s_b, D_MODEL)
    h_t, h_b = hb.next()
    k.op(k.dve, lambda: nc.vector.scalar_tensor_tensor(out=h_t[:], in0=x_t[:], scalar=s_t[:, 1:2],
                                                       in1=npre[:], op0=ALU.mult, op1=ALU.mult),
         reads=[x_b, s_b, npre_b], writes=[h_b])
    p_t, p_b = ptr.next()
    k.mm([(lambda c=c: nc.tensor.transpose(p_t[:, c, :], h_t[:, c * 128:(c + 1) * 128], ident[:]))
          for c in range(8)], reads=[h_b, ident_b], writes=[p_b])
    k.op(k.act, lambda: nc.scalar.copy(out=hT_t[:, :, s * 128:(s + 1) * 128], in_=p_t[:]),
         reads=[p_b], writes=[hT_b])


def phase_proj(k, ident, ident_b, x_in, w_in, n_pre, gate_b_pc, S):
    nc = k.nc
    TT = 512
    NTT = SEQ // TT
    NS = TT // 128
    with ExitStack() as st:
        win = sb(st, nc, "win", [128, 8, C_IN], BF16)
        blocks = [(0, 2048), (2048, 4096), (4096, 6160), (6160, 8208)]
        wb = []
        for (c0, c1) in blocks:
            b = Buf()
            wb.append(b)
            for c in range(8):
                k.dma(k.pool, win[:, c, c0:c1], w_in[c * 128:(c + 1) * 128, c0:c1], writes=[b])
        npre, npre_b = load_bcast(k, st, "npre", n_pre, D_MODEL)
        gb = sb(st, nc, "gb", [128, 16], F32)
        gb_b = Buf()
        k.dma(k.sp, gb[:], gate_b_pc, writes=[gb_b])
        xs = Ring(st, nc, "xs", 2, [128, D_MODEL], F32)
        hb = Ring(st, nc, "hb", 2, [128, D_MODEL], BF16)
        junk = Ring(st, nc, "junk", 1, [128, D_MODEL], BF16)
        stat = Ring(st, nc, "stat", 4, [128, 4], F32)
        ptr = Ring(st, nc, "ptr", 1, [128, 8, 128], BF16, psum=True)
        rings = (xs, hb, junk, stat, ptr)
        hT = Ring(st, nc, "hT", 1, [128, 8, TT], BF16)
        of32 = Ring(st, nc, "of32", 4, [128, 512], F32)
        obf = Ring(st, nc, "obf", 4, [128, 512], BF16)
        pf = Ring(st, nc, "pf", 3, [128, 512], F32, psum=True)
        pt = Ring(st, nc, "pt", 3, [128, 512], F32, psum=True)
        ev = [0]

        def evac_copy(out_ap, in_ap, rd, wr):
            ev[0] ^= 1
            if ev[0]:
                k.op(k.dve, lambda: nc.vector.tensor_copy(out=out_ap, in_=in_ap), reads=rd, writes=wr)
            else:
                k.op(k.act, lambda: nc.scalar.copy(out=out_ap, in_=in_ap), reads=rd, writes=wr)

        for t in range(NTT):
            hT_t, hT_b = hT.next()
            T0 = t * TT
            for s in range(NS):
                norm_transpose(k, rings, ident, ident_b, x_in, T0 + s * 128, npre, npre_b, hT_t, hT_b, s)

            def fm_chunk(col0, wbuf):
                p_t, p_b = pf.next()
                k.mm([(lambda c=c: nc.tensor.matmul(p_t[:], win[:, c, col0:col0 + 128], hT_t[:, c, :],
                                                    start=(c == 0), stop=(c == 7))) for c in range(8)],
                     reads=[wbuf, hT_b], writes=[p_b])
                return p_t, p_b

            def tm_block(s, col0, n, wbuf):
                p_t, p_b = pt.next()
                k.mm([(lambda c=c: nc.tensor.matmul(p_t[:, 0:n], hT_t[:, c, s * 128:(s + 1) * 128],
                                                    win[:, c, col0:col0 + n], start=(c == 0), stop=(c == 7)))
                      for c in range(8)], reads=[wbuf, hT_b], writes=[p_b])
                return p_t, p_b

            for c in range(16):
                p_t, p_b = fm_chunk(c * 128, wb[0])
                o_t, o_b = obf.next()
                evac_copy(o_t[:], p_t[:], [p_b], [o_b])
                k.dma(k.sp, S["qkT"][c * 128:(c + 1) * 128, T0:T0 + TT], o_t[:], reads=[o_b])
            for c in range(8):
                p_t, p_b = fm_chunk(O_MQ + c * 128, wb[1])
                o_t, o_b = of32.next()
                evac_copy(o_t[:], p_t[:], [p_b], [o_b])
                k.dma(k.sp, S["mqkT"][c * 128:(c + 1) * 128, T0:T0 + TT], o_t[:], reads=[o_b])
            for s in range(NS):
                tok0 = T0 + s * 128
                for hf in range(2):
                    p_t, p_b = tm_block(s, O_AV + hf * 512, 512, wb[1])
                    o_t, o_b = obf.next()
                    evac_copy(o_t[:], p_t[:], [p_b], [o_b])
                    k.dma(k.sp, S["av"][tok0:tok0 + 128, hf * 512:(hf + 1) * 512], o_t[:], reads=[o_b])
                for hf in range(2):
                    p_t, p_b = tm_block(s, O_MV + hf * 512, 512, wb[2])
                    o_t, o_b = of32.next()
                    evac_copy(o_t[:], p_t[:], [p_b], [o_b])
                    k.dma(k.sp, S["mv"][tok0:tok0 + 128, hf * 512:(hf + 1) * 512], o_t[:], reads=[o_b])
                for hf in range(2):
                    p_t, p_b = tm_block(s, O_MO + hf * 512, 512, wb[2])
                    o_t, o_b = of32.next()
                    k.op(k.act, lambda: nc.scalar.activation(out=o_t[:], in_=p_t[:], func=AF.Sigmoid),
                         reads=[p_b], writes=[o_b])
                    k.dma(k.sp, S["mo"][tok0:tok0 + 128, hf * 512:(hf + 1) * 512], o_t[:], reads=[o_b])
                p_t, p_b = tm_block(s, O_MI, 16, wb[2])
                o_t, o_b = of32.next()
                evac_copy(o_t[:, 0:16], p_t[:, 0:16], [p_b], [o_b])
                k.dma(k.sp, S["mif"][tok0:tok0 + 128, :], o_t[:, 0:16], reads=[o_b])
            for c in range(16):
                p_t, p_b = fm_chunk(O_GL + c * 128, wb[3])
                o_t, o_b = of32.next()
                k.op(k.act, lambda: nc.scalar.activation(out=o_t[:], in_=p_t[:], func=AF.Sigmoid,
                                                         bias=gb[:, c:c + 1]),
                     reads=[p_b, gb_b], writes=[o_b])
                k.dma(k.sp, S["glT"][c * 128:(c + 1) * 128, T0:T0 + TT], o_t[:], reads=[o_b])
        k.barrier()


def phase_attn(k, cbf, cbf_b, cf32, cf32_b, S, lamp, anw_col, lam_init):
    nc = k.nc
    NG = SEQ // 512
    with ExitStack() as st:
        lt, lt_b = load_bcast(k, st, "lamp", lamp, 256)
        lw = sb(st, nc, "lamw", [128, 136], F32)
        lw_b = Buf()
        k.op(k.dve, lambda: nc.vector.tensor_tensor(out=lw[:, 0:128], in0=lt[:, 0:128], in1=lt[:, 128:256],
                                                    op=ALU.mult), reads=[lt_b], writes=[lw_b])
        k.op(k.dve, lambda: nc.vector.tensor_reduce(out=lw[:, 128:130],
                                                    in_=lw[:, 0:128].rearrange("p (a b) -> p a b", a=2),
                                                    axis=AX.X, op=ALU.add), reads=[lw_b], writes=[lw_b])
        k.op(k.act, lambda: nc.scalar.activation(out=lw[:, 130:132], in_=lw[:, 128:130], func=AF.Exp),
             reads=[lw_b], writes=[lw_b])
        k.op(k.dve, lambda: nc.vector.tensor_tensor(out=lw[:, 132:133], in0=lw[:, 130:131], in1=lw[:, 131:132],
                                                    op=ALU.subtract), reads=[lw_b], writes=[lw_b])
        k.op(k.dve, lambda: nc.vector.tensor_scalar(out=lw[:, 133:134], in0=lw[:, 132:133], scalar1=lam_init,
                                                    scalar2=-1.0, op0=ALU.add, op1=ALU.mult),
             reads=[lw_b], writes=[lw_b])
        neg_lam = lw[:, 133:134]
        nw = sb(st, nc, "anw", [128, 1], F32)
        nw_b = Buf()
        k.dma(k.sp, nw[:], anw_col, writes=[nw_b])
        k.op(k.dve, lambda: nc.vector.tensor_scalar(out=nw[:], in0=nw[:], scalar1=1.0 - lam_init, scalar2=None,
                                                    op0=ALU.mult), reads=[nw_b], writes=[nw_b])

        qT = Ring(st, nc, "qT", 2, [128, SEQ], BF16)
        kT = Ring(st, nc, "kT", 2, [128, SEQ], BF16)
        vh = Ring(st, nc, "vh", 2, [128, NT_(), 128], BF16)
        pT = Ring(st, nc, "pT", 6, [128, 512], BF16)
        f32r = Ring(st, nc, "af", 6, [128, 512], F32)
        yo = Ring(st, nc, "yao", 2, [128, 512], BF16)
        psr = Ring(st, nc, "psS", 4, [128, 512], F32, psum=True)
        po = [ps(st, nc, "po", [128, 512], F32) for _ in range(2)]
        pl = [ps(st, nc, "pl", [128, 512], F32) for _ in range(2)]
        po_b = [Buf(), Buf()]
        pl_b = [Buf(), Buf()]
        ones_bf = cbf[:, 1, :]
        tri = cbf[:, 2, :]

        for h in range(A_HEADS):
            q_t, q_b = qT.next()
            k_t, k_b = kT.next()
            v_t, v_b = vh.next()
            k.dma(k.sp, q_t[:], S["qkT"][h * 128:(h + 1) * 128, :], writes=[q_b])
            k.dma(k.sp, k_t[:], S["qkT"][1024 + h * 128:1024 + (h + 1) * 128, :], writes=[k_b])
            k.dma(k.sp, v_t[:], S["av"][:, h * 128:(h + 1) * 128].rearrange("(t p) d -> p t d", p=128),
                  writes=[v_b])
            for g in range(NG):
                nkt = 4 * g + 4
                pend = {}

                def stage_a(kt):
                    j = kt - 4 * g
                    c0 = 128 * j if j > 0 else 0
                    outs = []
                    for m in range(2):
                        s_t, s_b = psr.next()
                        k.mm([lambda: nc.tensor.matmul(
                            s_t[:, c0:512], k_t[m * 64:(m + 1) * 64, kt * 128:(kt + 1) * 128],
                            q_t[m * 64:(m + 1) * 64, g * 512 + c0:(g + 1) * 512], start=True, stop=True)],
                            reads=[q_b, k_b], writes=[s_b])
                        p_t, p_b = pT.next()
                        k.op(k.act, lambda: nc.scalar.activation(out=p_t[:, c0:512], in_=s_t[:, c0:512],
                                                                 func=AF.Exp, scale=A_DHEAD ** -0.5),
                             reads=[s_b], writes=[p_b])
                        if j >= 0:
                            k.op(k.pool, lambda: nc.gpsimd.tensor_tensor(out=p_t[:, c0:c0 + 128],
                                                                         in0=p_t[:, c0:c0 + 128], in1=tri,
                                                                         op=ALU.mult),
                                 reads=[p_b, cbf_b], writes=[p_b])
                        outs.append((p_t, p_b, c0))
                    pend[kt] = outs

                def stage_b(kt):
                    for m in range(2):
                        p_t, p_b, c0 = pend[kt][m]
                        k.mm([lambda: nc.tensor.matmul(po[m][:, c0:512], v_t[:, kt, :], p_t[:, c0:512],
                                                       start=(kt == 0), stop=(kt == nkt - 1)),
                              lambda: nc.tensor.matmul(pl[m][:, c0:512], ones_bf, p_t[:, c0:512],
                                                       start=(kt == 0), stop=(kt == nkt - 1))],
                             reads=[p_b, v_b, cbf_b], writes=[po_b[m], pl_b[m]])
                    del pend[kt]

                stage_a(0)
                for kt in range(1, nkt):
                    stage_a(kt)
                    stage_b(kt - 1)
                stage_b(nkt - 1)

                tt = []
                for m in range(2):
                    r_t, r_b = f32r.next()
                    k.op(k.dve, lambda: nc.vector.reciprocal(out=r_t[:], in_=pl[m][:]), reads=[pl_b[m]], writes=[r_b])
                    t_t, t_b = f32r.next()
                    k.op(k.dve, lambda: nc.vector.tensor_tensor(out=t_t[:], in0=po[m][:], in1=r_t[:], op=ALU.mult),
                         reads=[po_b[m], r_b], writes=[t_b])
                    tt.append((t_t, t_b))
                d_t, d_b = tt[0]
                k.op(k.dve, lambda: nc.vector.scalar_tensor_tensor(out=d_t[:], in0=tt[1][0][:], scalar=neg_lam,
                                                                   in1=d_t[:], op0=ALU.mult, op1=ALU.add),
                     reads=[tt[1][1], d_b, lw_b], writes=[d_b])
                sq_t, sq_b = f32r.next()
                k.op(k.act, lambda: nc.scalar.activation(out=sq_t[:], in_=d_t[:], func=AF.Square),
                     reads=[d_b], writes=[sq_b])
                s_t, s_b = psr.next()
                k.mm([lambda: nc.tensor.matmul(s_t[:], cf32[:, 0, :], sq_t[:], start=True, stop=True)],
                     reads=[sq_b, cf32_b], writes=[s_b])
                rs_t, rs_b = f32r.next()
                rms_rstd(k, s_t[:], s_b, rs_t[:], rs_b, 128)
                y_t, y_b = yo.next()
                k.op(k.dve, lambda: nc.vector.scalar_tensor_tensor(out=y_t[:], in0=d_t[:], scalar=nw[:, 0:1],
                                                                   in1=rs_t[:], op0=ALU.mult, op1=ALU.mult),
                     reads=[d_b, nw_b, rs_b], writes=[y_b])
                k.dma(k.sp, S["yaT"][h * 128:(h + 1) * 128, g * 512:(g + 1) * 512], y_t[:], reads=[y_b])
        k.barrier()


def NT_():
    return SEQ // 128


def make_scratch(nc, kind="Internal"):
    def d(name, shape, dt):
        return nc.dram_tensor(name, shape, dt, kind=kind).ap()
    return {
        "qkT": d("s_qkT", [2048, SEQ], BF16),
        "av": d("s_av", [SEQ, 1024], BF16),
        "mqkT": d("s_mqkT", [1024, SEQ], F32),
        "mv": d("s_mv", [SEQ, 1024], F32),
        "mo": d("s_mo", [SEQ, 1024], F32),
        "mif": d("s_mif", [SEQ, 16], F32),
        "glT": d("s_glT", [2048, SEQ], F32),
        "yaT": d("s_yaT", [1024, SEQ], BF16),
        "ymT": d("s_ymT", [1024, SEQ], BF16),
    }


def host_consts():
    import ml_dtypes
    p = np.arange(128)[:, None]
    f = np.arange(128)[None, :]
    tri = (p <= f).astype(np.float32)
    cbf = np.stack([np.eye(128, dtype=np.float32), np.ones((128, 128), np.float32), tri], axis=1)
    cf32 = np.stack([np.ones((128, 128), np.float32), tri, tri * (M_DQK ** -0.5)], axis=1)
    return {"cbf": np.ascontiguousarray(cbf).astype(ml_dtypes.bfloat16),
            "cf32": np.ascontiguousarray(cf32).astype(np.float32)}


def load_consts(k, st):
    nc = k.nc
    cbf_d = nc.dram_tensor("cbf", [128, 3, 128], BF16, kind="ExternalInput").ap()
    cf32_d = nc.dram_tensor("cf32", [128, 3, 128], F32, kind="ExternalInput").ap()
    cbf = sb(st, nc, "cbf_sb", [128, 3, 128], BF16)
    cf32 = sb(st, nc, "cf32_sb", [128, 3, 128], F32)
    cbf_b, cf32_b = Buf(), Buf()
    k.dma(k.sp, cbf[:], cbf_d, writes=[cbf_b])
    k.dma(k.sp, cf32[:], cf32_d, writes=[cf32_b])
    return cbf, cbf_b, cf32, cf32_b


def phase_mlstm(k, cbf, cbf_b, cf32, cf32_b, S, convw_pc, convb_pc, igb, fgb, mnw):
    nc = k.nc
    NTL = SEQ // 128
    ident = cbf[:, 0, :]
    ones32 = cf32[:, 0, :]
    triu32 = cf32[:, 1, :]
    mask8 = cf32[:, 2, :]
    with ExitStack() as st:
        qkc = sb(st, nc, "qkc", [128, 8, SEQ], BF16)
        qkc_b = [Buf() for _ in range(8)]
        cw = sb(st, nc, "cw", [128, 8, 4], F32)
        cb = sb(st, nc, "cb", [128, 8], F32)
        cw_b, cb_b = Buf(), Buf()
        k.dma(k.sp, cw[:], convw_pc, writes=[cw_b])
        k.dma(k.sp, cb[:], convb_pc, writes=[cb_b])
        with ExitStack() as st1:
            xp = Ring(st1, nc, "xp", 2, [128, SEQ + 3], F32)
            acc = Ring(st1, nc, "cacc", 1, [128, SEQ], F32)
            for i in range(2):
                k.op(k.pool, lambda: nc.gpsimd.memset(xp.t[i][:, 0:3], 0.0), writes=[xp.b[i]])
            for c in range(8):
                x_t, x_b = xp.next()
                k.dma(k.sp, x_t[:, 3:SEQ + 3], S["mqkT"][c * 128:(c + 1) * 128, :], writes=[x_b])
                a_t, a_b = acc.next()
                k.op(k.dve, lambda: nc.vector.tensor_scalar(out=a_t[:], in0=x_t[:, 0:SEQ], scalar1=cw[:, c, 0:1],
                                                            scalar2=None, op0=ALU.mult),
                     reads=[x_b, cw_b], writes=[a_b])
                for j in range(1, 4):
                    k.op(k.dve, lambda: nc.vector.scalar_tensor_tensor(
                        out=a_t[:], in0=x_t[:, j:j + SEQ], scalar=cw[:, c, j:j + 1], in1=a_t[:],
                        op0=ALU.mult, op1=ALU.add), reads=[x_b, cw_b, a_b], writes=[a_b])
                k.op(k.act, lambda: nc.scalar.activation(out=qkc[:, c, :], in_=a_t[:], func=AF.Silu,
                                                         bias=cb[:, c:c + 1]),
                     reads=[a_b, cb_b], writes=[qkc_b[c]])
        k.barrier()

        ktok = sb(st, nc, "ktok", [128, NTL, 512], BF16)
        ktok_b = Buf()
        NC8 = NTL * 8
        gt = sb(st, nc, "gt", [128, NTL, 16], F32)
        gt_b = Buf()
        k.dma(k.sp, gt[:], S["mif"].rearrange("(t p) c -> p t c", p=128), writes=[gt_b])
        igb_t, igb_b = load_bcast(k, st, "igb", igb, 8)
        fgb_t, fgb_b = load_bcast(k, st, "fgb", fgb, 8)
        mnw_t, mnw_b = load_bcast(k, st, "mnw", mnw, 1024)
        IG = sb(st, nc, "IG", [128, NTL, 8], F32)
        NLF = sb(st, nc, "NLF", [128, NTL, 8], F32)
        NGc = sb(st, nc, "NGc", [128, NTL, 8], F32)
        EQ = sb(st, nc, "EQ", [128, NTL, 8], F32)
        EK = sb(st, nc, "EK", [128, NTL, 8], F32)
        EE = sb(st, nc, "EE", [128, NTL, 8], F32)
        IG_b, NLF_b, NG_b, EQ_b, EK_b, EE_b = [Buf() for _ in range(6)]

        pa = Ring(st, nc, "pa", 2, [128, 4, 128], F32, psum=True)
        pr = Ring(st, nc, "pr", 2, [128, 4, 128], F32, psum=True)
        pu = Ring(st, nc, "pu", 2, [128, 4, 128], F32, psum=True)
        pm = ps(st, nc, "pm", [128, 512], F32)
        prd, prd_b = pm[:, 0:8], Buf()
        pun, pun_b = pm[:, 8:16], Buf()
        ptr = ps(st, nc, "ptrm", [128, 8, 128], BF16)
        ptr_b = Buf()

        for t in range(NTL):
            k.mm([(lambda c=c: nc.tensor.transpose(ptr[:, c, :], qkc[:, 4 + c, t * 128:(t + 1) * 128], ident))
                  for c in range(4)], reads=qkc_b[4:8] + [cbf_b], writes=[ptr_b])
            k.op(k.act, lambda: nc.scalar.copy(out=ktok[:, t, :], in_=ptr[:, 0:4, :].rearrange("p a b -> p (a b)")),
                 reads=[ptr_b], writes=[ktok_b])

        k.op(k.dve, lambda: nc.vector.tensor_tensor(out=IG[:], in0=gt[:, :, 0:8],
                                                    in1=igb_t[:].unsqueeze(1).to_broadcast([128, NTL, 8]),
                                                    op=ALU.add), reads=[gt_b, igb_b], writes=[IG_b])
        k.op(k.dve, lambda: nc.vector.tensor_tensor(out=NLF[:], in0=gt[:, :, 8:16],
                                                    in1=fgb_t[:].unsqueeze(1).to_broadcast([128, NTL, 8]),
                                                    op=ALU.add), reads=[gt_b, fgb_b], writes=[NLF_b])
        k.op(k.act, lambda: nc.scalar.activation(out=NLF[:], in_=NLF[:], func=AF.Exp, scale=-1.0),
             reads=[NLF_b], writes=[NLF_b])
        c_one = k.const(1.0)
        k.op(k.act, lambda: nc.scalar.activation(out=NLF[:], in_=NLF[:], func=AF.Ln, bias=c_one),
             reads=[NLF_b, k.const_b], writes=[NLF_b])
        nlf2 = NLF[:].rearrange("p a b -> p (a b)")
        pg_t, pg_b = pa.next()
        pg2 = pg_t[:].rearrange("p a b -> p (a b)")
        k.mm([lambda: nc.tensor.matmul(pg2[:, 0:NC8], triu32, nlf2, start=True, stop=True)],
             reads=[NLF_b, cf32_b], writes=[pg_b])
        k.op(k.dve, lambda: nc.vector.tensor_copy(out=NGc[:].rearrange("p a b -> p (a b)"), in_=pg2[:, 0:NC8]),
             reads=[pg_b], writes=[NG_b])
        pe_t, pe_b = pa.next()
        pe2 = pe_t[:].rearrange("p a b -> p (a b)")
        k.mm([lambda: nc.tensor.matmul(pe2[:, 0:NC8], ones32, nlf2, start=True, stop=True)],
             reads=[NLF_b, cf32_b], writes=[pe_b])
        k.op(k.act, lambda: nc.scalar.activation(out=EE[:].rearrange("p a b -> p (a b)"), in_=pe2[:, 0:NC8],
                                                 func=AF.Exp, scale=-1.0), reads=[pe_b], writes=[EE_b])
        k.op(k.act, lambda: nc.scalar.activation(out=EQ[:], in_=NGc[:], func=AF.Exp, scale=-1.0),
             reads=[NG_b], writes=[EQ_b])
        k.op(k.dve, lambda: nc.vector.tensor_tensor(out=EK[:], in0=IG[:], in1=NGc[:], op=ALU.add),
             reads=[IG_b, NG_b], writes=[EK_b])
        k.op(k.act, lambda: nc.scalar.activation(out=EK[:], in_=EK[:], func=AF.Exp), reads=[EK_b], writes=[EK_b])

        mvr = Ring(st, nc, "mvr", 2, [128, 8, 128], F32)
        mor = Ring(st, nc, "mor", 2, [128, 1024], F32)
        vtr = Ring(st, nc, "vtr", 2, [128, 8, 128], BF16)
        ekr = Ring(st, nc, "ekr", 2, [128, 8], BF16)
        ptT = Ring(st, nc, "ptT", 2, [128, 8, 128], BF16)
        Tst = sb(st, nc, "Tst", [128, 4, 129], F32)
        T_b = [Buf() for _ in range(8)]
        S8r = Ring(st, nc, "S8r", 2, [128, 4, 129], BF16)
        S8_bufs = {0: [Buf() for _ in range(8)], 1: [Buf() for _ in range(8)]}
        sm = Ring(st, nc, "msm", 2, [128, 48], F32)
        ho = Ring(st, nc, "mho", 1, [128, 8, 128], F32)
        sq = Ring(st, nc, "msq", 1, [128, 8, 128], F32)
        y1 = Ring(st, nc, "my1", 1, [128, 8, 128], F32)
        ymb = Ring(st, nc, "ymb", 2, [128, 1024], BF16)
        ymT = Ring(st, nc, "ymTs", 2, [128, 8, 128], BF16)
        s8_prev = None
        for t in range(NTL):
            tsl = slice(t * 128, (t + 1) * 128)
            mv_t, mv_b = mvr.next()
            k.dma(k.sp, mv_t[:], S["mv"][tsl, :].rearrange("p (a b) -> p a b", a=8), writes=[mv_b])
            mo_t, mo_b = mor.next()
            k.dma(k.sp, mo_t[:], S["mo"][tsl, :], writes=[mo_b])
            vt_t, vt_b = vtr.next()
            k.op(k.dve, lambda: nc.vector.tensor_tensor(out=vt_t[:], in0=mv_t[:],
                                                        in1=EK[:, t, :].unsqueeze(2).to_broadcast([128, 8, 128]),
                                                        op=ALU.mult), reads=[mv_b, EK_b], writes=[vt_b])
            ek_t, ek_b = ekr.next()
            k.op(k.dve, lambda: nc.vector.tensor_copy(out=ek_t[:], in_=EK[:, t, :]), reads=[EK_b], writes=[ek_b])
            pT_t, pT_b = ptT.next()
            a_te = [pa.next() for _ in range(2)]
            fns = []
            for i in range(4):
                for e in range(2):
                    P0 = e * 64
                    fns.append(lambda i=i, e=e, P0=P0: nc.tensor.matmul(
                        a_te[e][0][:, i, :], qkc[P0:P0 + 64, 4 + i, tsl], qkc[P0:P0 + 64, i, tsl],
                        start=True, stop=True))
            k.mm(fns, reads=qkc_b, writes=[a_te[0][1], a_te[1][1]])
            for e in range(2):
                k.op(k.dve, lambda: nc.vector.tensor_tensor(
                    out=pT_t[:, 4 * e:4 * e + 4, :], in0=a_te[e][0][:],
                    in1=mask8.unsqueeze(1).to_broadcast([128, 4, 128]), op=ALU.mult),
                    reads=[a_te[e][1], cf32_b], writes=[pT_b])
            r_ts = []
            for bk in range(2):
                r_t, r_b = pr.next()
                fns = []
                for hh in range(4):
                    h = 4 * bk + hh
                    P0 = (h % 2) * 64
                    fns.append(lambda h=h, hh=hh: nc.tensor.matmul(
                        r_t[:, hh, :], pT_t[:, (h % 2) * 4 + h // 2, :], vt_t[:, h, :], start=True, stop=(t == 0)))
                    if t > 0:
                        fns.append(lambda h=h, hh=hh, P0=P0: nc.tensor.matmul(
                            r_t[:, hh, :], qkc[P0:P0 + 64, h // 2, tsl], s8_prev[0][P0:P0 + 64, h // 2, 0:128],
                            start=False, stop=True))
                rd = [pT_b, vt_b] + qkc_b + (s8_prev[1] if t > 0 else [])
                k.mm(fns, reads=rd, writes=[r_b])
                r_ts.append((r_t, r_b))
            fns = []
            for h in range(8):
                P0 = (h % 2) * 64
                fns.append(lambda h=h: nc.tensor.matmul(prd[:, h:h + 1], pT_t[:, (h % 2) * 4 + h // 2, :],
                                                        ek_t[:, h:h + 1], start=True, stop=(t == 0)))
                if t > 0:
                    fns.append(lambda h=h, P0=P0: nc.tensor.matmul(
                        prd[:, h:h + 1], qkc[P0:P0 + 64, h // 2, tsl], s8_prev[0][P0:P0 + 64, h // 2, 128:129],
                        start=False, stop=True))
            k.mm(fns, reads=[pT_b, ek_b] + qkc_b + (s8_prev[1] if t > 0 else []), writes=[prd_b])
            u_ts = []
            for bk in range(2):
                u_t, u_b = pu.next()
                fns = []
                for hh in range(4):
                    h = 4 * bk + hh
                    fns.append(lambda h=h, hh=hh: nc.tensor.matmul(
                        u_t[:, hh, :], ktok[:, t, (h // 2) * 128:(h // 2 + 1) * 128], vt_t[:, h, :],
                        start=True, stop=True))
                k.mm(fns, reads=[ktok_b, vt_b], writes=[u_b])
                u_ts.append((u_t, u_b))
            k.mm([(lambda h=h: nc.tensor.matmul(pun[:, h:h + 1], ktok[:, t, (h // 2) * 128:(h // 2 + 1) * 128],
                                                ek_t[:, h:h + 1], start=True, stop=True)) for h in range(8)],
                 reads=[ktok_b, ek_b], writes=[pun_b])
            s8_t, _ = S8r.next()
            s8_bl = S8_bufs[t % 2]
            for h in range(8):
                P0 = (h % 2) * 64
                hp = h // 2
                u_t, u_b = u_ts[h // 4]
                if t == 0:
                    k.op(k.dve, lambda: nc.vector.tensor_copy(out=Tst[P0:P0 + 64, hp, 0:128],
                                                              in_=u_t[P0:P0 + 64, h % 4, :]),
                         reads=[u_b], writes=[T_b[h]])
                    k.op(k.dve, lambda: nc.vector.tensor_copy(out=Tst[P0:P0 + 64, hp, 128:129],
                                                              in_=pun[P0:P0 + 64, h:h + 1]),
                         reads=[pun_b], writes=[T_b[h]])
                else:
                    k.op(k.dve, lambda: nc.vector.scalar_tensor_tensor(
                        out=Tst[P0:P0 + 64, hp, 0:128], in0=Tst[P0:P0 + 64, hp, 0:128],
                        scalar=EE[P0:P0 + 64, t - 1, h:h + 1], in1=u_t[P0:P0 + 64, h % 4, :],
                        op0=ALU.mult, op1=ALU.add), reads=[T_b[h], EE_b, u_b], writes=[T_b[h]])
                    k.op(k.dve, lambda: nc.vector.scalar_tensor_tensor(
                        out=Tst[P0:P0 + 64, hp, 128:129], in0=Tst[P0:P0 + 64, hp, 128:129],
                        scalar=EE[P0:P0 + 64, t - 1, h:h + 1], in1=pun[P0:P0 + 64, h:h + 1],
                        op0=ALU.mult, op1=ALU.add), reads=[T_b[h], EE_b, pun_b], writes=[T_b[h]])
                if t < NTL - 1:
                    k.op(k.pool, lambda: nc.gpsimd.tensor_scalar(
                        out=s8_t[P0:P0 + 64, hp, :], in0=Tst[P0:P0 + 64, hp, :], scalar1=EE[P0:P0 + 64, t, h:h + 1],
                        scalar2=M_DQK ** -0.5, op0=ALU.mult, op1=ALU.mult),
                        reads=[T_b[h], EE_b], writes=[s8_bl[h]])
            s8_prev = (s8_t, s8_bl)
            m_t, m_b = sm.next()
            dn, dneg, rc, cc, ss, rstd = (m_t[:, 0:8], m_t[:, 8:16], m_t[:, 16:24], m_t[:, 24:32],
                                          m_t[:, 32:40], m_t[:, 40:48])
            k.op(k.dve, lambda: nc.vector.tensor_tensor(out=dn, in0=prd, in1=EQ[:, t, :], op=ALU.mult),
                 reads=[prd_b, EQ_b], writes=[m_b])
            k.op(k.dve, lambda: nc.vector.tensor_scalar(out=dneg, in0=dn, scalar1=-1.0, scalar2=None, op0=ALU.mult),
                 reads=[m_b], writes=[m_b])
            k.op(k.dve, lambda: nc.vector.tensor_tensor(out=dn, in0=dn, in1=dneg, op=ALU.max),
                 reads=[m_b], writes=[m_b])
            k.op(k.dve, lambda: nc.vector.tensor_scalar(out=dn, in0=dn, scalar1=1.0, scalar2=None, op0=ALU.max),
                 reads=[m_b], writes=[m_b])
            k.op(k.dve, lambda: nc.vector.reciprocal(out=rc, in_=dn), reads=[m_b], writes=[m_b])
            k.op(k.dve, lambda: nc.vector.tensor_tensor(out=cc, in0=rc, in1=EQ[:, t, :], op=ALU.mult),
                 reads=[m_b, EQ_b], writes=[m_b])
            ho_t, ho_b = ho.next()
            for bk in range(2):
                r_t, r_b = r_ts[bk]
                k.op(k.dve, lambda: nc.vector.tensor_tensor(
                    out=ho_t[:, 4 * bk:4 * bk + 4, :], in0=r_t[:],
                    in1=cc[:, 4 * bk:4 * bk + 4].unsqueeze(2).to_broadcast([128, 4, 128]), op=ALU.mult),
                    reads=[r_b, m_b], writes=[ho_b])
            sq_t, sq_b = sq.next()
            k.op(k.pool, lambda: nc.gpsimd.tensor_tensor(out=sq_t[:], in0=ho_t[:], in1=ho_t[:], op=ALU.mult),
                 reads=[ho_b], writes=[sq_b])
            k.op(k.dve, lambda: nc.vector.tensor_reduce(out=ss, in_=sq_t[:], axis=AX.X, op=ALU.add),
                 reads=[sq_b], writes=[m_b])
            rms_rstd(k, ss, m_b, rstd, m_b, M_DV)
            y1_t, y1_b = y1.next()
            k.op(k.dve, lambda: nc.vector.tensor_tensor(
                out=y1_t[:], in0=ho_t[:], in1=rstd.unsqueeze(2).to_broadcast([128, 8, 128]), op=ALU.mult),
                reads=[ho_b, m_b], writes=[y1_b])
            y1f = y1_t[:].rearrange("p a b -> p (a b)")
            k.op(k.pool, lambda: nc.gpsimd.tensor_tensor(out=y1f, in0=y1f, in1=mnw_t[:], op=ALU.mult),
                 reads=[y1_b, mnw_b], writes=[y1_b])
            yb_t, yb_b = ymb.next()
            k.op(k.dve, lambda: nc.vector.tensor_tensor(out=yb_t[:], in0=y1f, in1=mo_t[:], op=ALU.mult),
                 reads=[y1_b, mo_b], writes=[yb_b])
            k.mm([(lambda c=c: nc.tensor.transpose(ptr[:, c, :], yb_t[:, c * 128:(c + 1) * 128], ident))
                  for c in range(8)], reads=[yb_b, cbf_b], writes=[ptr_b])
            yT_t, yT_b = ymT.next()
            k.op(k.act, lambda: nc.scalar.copy(out=yT_t[:], in_=ptr[:]), reads=[ptr_b], writes=[yT_b])
            k.dma(k.sp, S["ymT"][:, tsl].rearrange("(c p) n -> p c n", p=128), yT_t[:], reads=[yT_b])
        k.barrier()


def phase_out(k, S, x_io, w_pa, w_pm, w_out, n_post):
    nc = k.nc
    TT = 512
    NTT = SEQ // TT
    NS = TT // 128
    with ExitStack() as st:
        ws = []
        for nm, w in (("wpa", w_pa), ("wpm", w_pm), ("wout", w_out)):
            t = sb(st, nc, nm, [128, 8, D_MODEL], BF16)
            b = Buf()
            for hh in range(2):
                k.dma(k.pool, t[:, hh * 4:(hh + 1) * 4, :],
                      w[hh * 512:(hh + 1) * 512, :].rearrange("(c p) n -> p c n", p=128), writes=[b])
            ws.append((t, b))
        (wpa, wpa_b), (wpm, wpm_b), (wout, wout_b) = ws
        npost, npost_b = load_bcast(k, st, "npost", n_post, D_MODEL)
        yaT = Ring(st, nc, "yaTs", 2, [128, 8, TT], BF16)
        ymT = Ring(st, nc, "ymTs", 2, [128, 8, TT], BF16)
        gT = Ring(st, nc, "gTs", 3, [128, 2, TT], F32)
        mg = Ring(st, nc, "mg", 1, [128, 8, TT], BF16)
        t1 = Ring(st, nc, "t1", 2, [128, TT], F32)
        xr = Ring(st, nc, "xr", 2, [128, D_MODEL], F32)
        yo = Ring(st, nc, "yo", 1, [128, D_MODEL], F32)
        junk = Ring(st, nc, "junk", 1, [128, D_MODEL], BF16)
        stat = Ring(st, nc, "stat", 4, [128, 4], F32)
        ppa = Ring(st, nc, "ppa", 2, [128, TT], F32, psum=True)
        ppm = Ring(st, nc, "ppm", 2, [128, TT], F32, psum=True)
        py = Ring(st, nc, "py", 1, [128, 2, 512], F32, psum=True)
        for t in range(NTT):
            T0 = t * TT
            a_t, a_b = yaT.next()
            m_t, m_b = ymT.next()
            k.dma(k.sp, a_t[:], S["yaT"][:, T0:T0 + TT].rearrange("(c p) n -> p c n", p=128), writes=[a_b])
            k.dma(k.sp, m_t[:], S["ymT"][:, T0:T0 + TT].rearrange("(c p) n -> p c n", p=128), writes=[m_b])
            g_t, g_b = mg.next()
            for c in range(8):
                gg_t, gg_b = gT.next()
                k.dma(k.sp, gg_t[:], S["glT"].rearrange("(a r) n -> r a n", a=2)[c * 128:(c + 1) * 128, :, T0:T0 + TT],
                      writes=[gg_b])
                pa_t, pa_b = ppa.next()
                pm_t, pm_b = ppm.next()
                k.mm([(lambda kc=kc: nc.tensor.matmul(pa_t[:], wpa[:, kc, c * 128:(c + 1) * 128], a_t[:, kc, :],
                                                      start=(kc == 0), stop=(kc == 7))) for kc in range(8)],
                     reads=[wpa_b, a_b], writes=[pa_b])
                k.mm([(lambda kc=kc: nc.tensor.matmul(pm_t[:], wpm[:, kc, c * 128:(c + 1) * 128], m_t[:, kc, :],
                                                      start=(kc == 0), stop=(kc == 7))) for kc in range(8)],
                     reads=[wpm_b, m_b], writes=[pm_b])
                u_t, u_b = t1.next()
                k.op(k.dve, lambda: nc.vector.tensor_tensor(out=u_t[:], in0=pa_t[:], in1=gg_t[:, 0, :], op=ALU.mult),
                     reads=[pa_b, gg_b], writes=[u_b])
                v_t, v_b = t1.next()
                k.op(k.dve, lambda: nc.vector.tensor_tensor(out=v_t[:], in0=pm_t[:], in1=gg_t[:, 1, :], op=ALU.mult),
                     reads=[pm_b, gg_b], writes=[v_b])
                k.op(k.pool, lambda: nc.gpsimd.tensor_tensor(out=g_t[:, c, :], in0=u_t[:], in1=v_t[:], op=ALU.add),
                     reads=[u_b, v_b], writes=[g_b])
            for s in range(NS):
                tok0 = T0 + s * 128
                y_t, y_b = py.next()
                fns = []
                for hf in range(2):
                    for kc in range(8):
                        fns.append(lambda kc=kc, hf=hf: nc.tensor.matmul(
                            y_t[:, hf, :], g_t[:, kc, s * 128:(s + 1) * 128], wout[:, kc, hf * 512:(hf + 1) * 512],
                            start=(kc == 0), stop=(kc == 7)))
                k.mm(fns, reads=[g_b, wout_b], writes=[y_b])
                xr_t, xr_b = xr.next()
                k.dma(k.sp, xr_t[:], x_io[tok0:tok0 + 128, :], writes=[xr_b])
                j_t, j_b = junk.next()
                s_t, s_b = stat.next()
                for hf in range(2):
                    k.op(k.act, lambda: nc.scalar.activation(out=j_t[:, hf * 512:(hf + 1) * 512], in_=y_t[:, hf, :],
                                                             func=AF.Square, accum_out=s_t[:, 2 + hf:3 + hf]),
                         reads=[y_b], writes=[j_b, s_b])
                k.op(k.dve, lambda: nc.vector.tensor_tensor(out=s_t[:, 0:1], in0=s_t[:, 2:3], in1=s_t[:, 3:4],
                                                            op=ALU.add), reads=[s_b], writes=[s_b])
                rms_rstd(k, s_t[:, 0:1], s_b, s_t[:, 1:2], s_b, D_MODEL)
                o_t, o_b = yo.next()
                for hf in range(2):
                    k.op(k.dve, lambda: nc.vector.tensor_tensor(
                        out=o_t[:, hf * 512:(hf + 1) * 512], in0=y_t[:, hf, :], in1=npost[:, hf * 512:(hf + 1) * 512],
                        op=ALU.mult), reads=[y_b, npost_b], writes=[o_b])
                k.op(k.dve, lambda: nc.vector.scalar_tensor_tensor(
                    out=xr_t[:], in0=o_t[:], scalar=s_t[:, 1:2], in1=xr_t[:], op0=ALU.mult, op1=ALU.add),
                    reads=[o_b, xr_b, s_b], writes=[xr_b])
                k.dma(k.sp, x_io[tok0:tok0 + 128, :], xr_t[:], reads=[xr_b])
        k.barrier()


PARAM_NAMES = ["ffn1_norm_pre", "ffn1_w_gu", "ffn1_w_down", "ffn1_norm_post", "mix_norm_pre", "w_in",
               "attn_lam_q1", "attn_lam_k1", "attn_lam_q2", "attn_lam_k2", "attn_norm_w", "conv_w", "conv_b",
               "igate_b", "fgate_b", "mlstm_norm_w", "w_proj_a", "w_proj_m", "gate_b", "w_out", "mix_norm_post",
               "ffn2_norm_pre", "ffn2_w_gu", "ffn2_w_down", "ffn2_norm_post"]


def build_program(depth=DEPTH):
    nc = bass.Bass("TRN2", target_bir_lowering=False)

    def din(name, shape, dt=F32):
        return nc.dram_tensor(name, shape, dt, kind="ExternalInput").ap()

    L = depth
    x = din("x", [SEQ, D_MODEL])
    out = nc.dram_tensor("out", [SEQ, D_MODEL], F32, kind="ExternalOutput").ap()
    P = {
        "ffn1_norm_pre": din("ffn1_norm_pre", [L, D_MODEL]),
        "ffn1_w_gu": din("ffn1_w_gu", [L, D_MODEL, 2 * D_FF]),
        "ffn1_w_down": din("ffn1_w_down", [L, D_FF, D_MODEL]),
        "ffn1_norm_post": din("ffn1_norm_post", [L, D_MODEL]),
        "mix_norm_pre": din("mix_norm_pre", [L, D_MODEL]),
        "w_in": din("w_in", [L, D_MODEL, C_IN]),
        "lamp": din("lamp", [L, 256]),
        "anw": din("anw", [L, 128, 1]),
        "convw_pc": din("convw_pc", [L, 128, 8, 4]),
        "convb_pc": din("convb_pc", [L, 128, 8]),
        "igate_b": din("igate_b", [L, 8]),
        "fgate_b": din("fgate_b", [L, 8]),
        "mlstm_norm_w": din("mlstm_norm_w", [L, 1024]),
        "w_proj_a": din("w_proj_a", [L, 1024, D_MODEL]),
        "w_proj_m": din("w_proj_m", [L, 1024, D_MODEL]),
        "gate_b_pc": din("gate_b_pc", [L, 128, 16]),
        "w_out": din("w_out", [L, D_MODEL, D_MODEL]),
        "mix_norm_post": din("mix_norm_post", [L, D_MODEL]),
        "ffn2_norm_pre": din("ffn2_norm_pre", [L, D_MODEL]),
        "ffn2_w_gu": din("ffn2_w_gu", [L, D_MODEL, 2 * D_FF]),
        "ffn2_w_down": din("ffn2_w_down", [L, D_FF, D_MODEL]),
        "ffn2_norm_post": din("ffn2_norm_post", [L, D_MODEL]),
    }
    S = make_scratch(nc)
    k = K(nc)
    with ExitStack() as st:
        cbf, cbf_b, cf32, cf32_b = load_consts(k, st)
        ident = cbf[:, 0, :]
        for l in range(L):
            lam_init = 0.8 - 0.6 * math.exp(-0.3 * l)
            phase_ffn(k, ident, cbf_b, x if l == 0 else out, out, P["ffn1_w_gu"][l], P["ffn1_w_down"][l],
                      P["ffn1_norm_pre"][l], P["ffn1_norm_post"][l])
            phase_proj(k, ident, cbf_b, out, P["w_in"][l], P["mix_norm_pre"][l], P["gate_b_pc"][l], S)
            phase_attn(k, cbf, cbf_b, cf32, cf32_b, S, P["lamp"][l], P["anw"][l], lam_init)
            phase_mlstm(k, cbf, cbf_b, cf32, cf32_b, S, P["convw_pc"][l], P["convb_pc"][l], P["igate_b"][l],
                        P["fgate_b"][l], P["mlstm_norm_w"][l])
            phase_out(k, S, out, P["w_proj_a"][l], P["w_proj_m"][l], P["w_out"][l], P["mix_norm_post"][l])
            phase_ffn(k, ident, cbf_b, out, out, P["ffn2_w_gu"][l], P["ffn2_w_down"][l],
                      P["ffn2_norm_pre"][l], P["ffn2_norm_post"][l])
        k.finish()
    return nc


def host_params(inp, depth=DEPTH):
    f = lambda a: np.ascontiguousarray(np.asarray(a, dtype=np.float32))
    L = depth
    d = {n: f(inp[n]) for n in ["ffn1_norm_pre", "ffn1_w_gu", "ffn1_w_down", "ffn1_norm_post", "mix_norm_pre",
                                "w_in", "igate_b", "fgate_b", "w_proj_a", "w_proj_m", "w_out", "mix_norm_post",
                                "ffn2_norm_pre", "ffn2_w_gu", "ffn2_w_down", "ffn2_norm_post"]}
    d["lamp"] = f(np.concatenate([inp["attn_lam_q1"], inp["attn_lam_q2"], inp["attn_lam_k1"], inp["attn_lam_k2"]],
                                 axis=1))
    d["anw"] = f(np.asarray(inp["attn_norm_w"]).reshape(L, 128, 1))
    cw = np.asarray(inp["conv_w"])
    d["convw_pc"] = f(cw.transpose(0, 2, 1).reshape(L, 8, 128, 4).transpose(0, 2, 1, 3))
    d["convb_pc"] = f(np.asarray(inp["conv_b"]).reshape(L, 8, 128).transpose(0, 2, 1))
    d["mlstm_norm_w"] = f(np.asarray(inp["mlstm_norm_w"]).reshape(L, 1024))
    d["gate_b_pc"] = f(np.asarray(inp["gate_b"]).reshape(L, 16, 128).transpose(0, 2, 1))
    d.update(host_consts())
    return d


_NC_CACHE = {}


def kernel(**inputs):
    if "nc" not in _NC_CACHE:
        _NC_CACHE["nc"] = build_program()
    nc = _NC_CACHE["nc"]
    params = host_params(inputs)
    x = np.asarray(inputs["x"], dtype=np.float32)
    in_maps = []
    for b in range(BATCH):
        m = dict(params)
        m["x"] = np.ascontiguousarray(x[b])
        in_maps.append(m)
    res = run_bass_kernel_spmd(nc, in_maps, core_ids=list(range(BATCH)))
    return np.stack([np.asarray(r["out"], dtype=np.float32) for r in res.results], axis=0)
```

```python
import math
from contextlib import ExitStack

import numpy as np
import concourse.bass as bass
import concourse.mybir as mybir
from concourse.bass_utils import run_bass_kernel_spmd

F32 = mybir.dt.float32
BF16 = mybir.dt.bfloat16
AF = mybir.ActivationFunctionType
ALU = mybir.AluOpType
AX = mybir.AxisListType

D_MODEL = 1024
BATCH = 8
SEQ = 4096
DEPTH = 2
EPS = 1e-6
A_HEADS = 8
A_DHEAD = 64
M_HEADS = 8
M_DQK = 64
M_DV = 128
D_FF = 2816
C_IN = 8208
NT = SEQ // 128

O_AQ, O_AK, O_AV = 0, 1024, 2048
O_MQ, O_MK, O_MV, O_MO = 3072, 3584, 4096, 5120
O_MI, O_MF, O_GL = 6144, 6152, 6160


class Sem:
    def __init__(self, nc, name):
        self.h = nc.alloc_semaphore(name)
        self.v = 0
        self.name = name


class Buf:
    __slots__ = ("w", "r", "name")

    def __init__(self, name=""):
        self.w = {}
        self.r = {}
        self.name = name


class Eng:
    def __init__(self, k, name, e, n_dma_sems=0):
        self.k = k
        self.name = name
        self.e = e
        self.sem = Sem(k.nc, "s_" + name)
        self.waited = {}
        self.dma_sems = [Sem(k.nc, f"d_{name}{i}") for i in range(n_dma_sems)]
        self.dma_rr = 0

    def wait(self, tok):
        sem, v = tok
        if self.waited.get(sem, 0) >= v:
            return
        self.e.wait_ge(sem.h, v)
        self.waited[sem] = v


class K:
    def __init__(self, nc):
        self.nc = nc
        self.pe = Eng(self, "pe", nc.tensor)
        self.act = Eng(self, "act", nc.scalar)
        self.dve = Eng(self, "dve", nc.vector)
        self.pool = Eng(self, "pool", nc.gpsimd, n_dma_sems=12)
        self.sp = Eng(self, "sp", nc.sync, n_dma_sems=20)
        self.engs = [self.pe, self.act, self.dve, self.pool, self.sp]
        self.const_t = nc.alloc_sbuf_tensor("const_cols", [128, 32], F32)
        self.const_b = Buf("consts")
        self.consts = {}

    def const(self, val):
        val = float(val)
        if val not in self.consts:
            i = len(self.consts)
            assert i < self.const_t.shape[1]
            ap = self.const_t[:, i:i + 1]
            self.op(self.pool, lambda: self.nc.gpsimd.memset(ap, val), writes=[self.const_b])
            self.consts[val] = ap
        return self.consts[val]

    def _deps(self, eng, reads, writes, join=False):
        for b in reads:
            for s, v in b.w.items():
                eng.wait((s, v))
        for b in writes:
            if not join:
                for s, v in b.w.items():
                    eng.wait((s, v))
            for s, v in b.r.items():
                eng.wait((s, v))

    def _mark(self, tok, reads, writes, join=False):
        s, v = tok
        for b in reads:
            if b.r.get(s, 0) < v:
                b.r[s] = v
        for b in writes:
            if join:
                b.w[s] = v
            else:
                b.w = {s: v}
            b.r = {}

    def op(self, eng, fn, reads=(), writes=()):
        self._deps(eng, reads, writes)
        ins = fn()
        eng.sem.v += 1
        ins.then_inc(eng.sem.h, 1)
        self._mark((eng.sem, eng.sem.v), reads, writes)

    def mm(self, fns, reads=(), writes=()):
        eng = self.pe
        self._deps(eng, reads, writes)
        ins = None
        for fn in fns:
            ins = fn()
        eng.sem.v += 1
        ins.then_inc(eng.sem.h, 1)
        self._mark((eng.sem, eng.sem.v), reads, writes)

    def dma(self, q, out, in_, reads=(), writes=(), join=False):
        sem = q.dma_sems[q.dma_rr]
        q.dma_rr = (q.dma_rr + 1) % len(q.dma_sems)
        if sem.v:
            q.wait((sem, sem.v))
        self._deps(q, reads, writes, join)
        ins = q.e.dma_start(out=out, in_=in_)
        sem.v += 16
        ins.then_inc(sem.h, 16)
        self._mark((sem, sem.v), reads, writes, join)

    def barrier(self):
        sems = []
        for e in self.engs:
            sems.append(e.sem)
            sems.extend(e.dma_sems)
        for e in self.engs:
            for s in sems:
                if s.v:
                    e.wait((s, s.v))

    def finish(self):
        self.barrier()


_UID = [0]


def uname(name):
    _UID[0] += 1
    return f"{name}_{_UID[0]}"


class Ring:
    def __init__(self, st, nc, name, n, shape, dtype, psum=False):
        self.n = n
        self.i = 0
        self.t = []
        self.b = []
        for j in range(n):
            if psum:
                t = st.enter_context(nc.psum_tensor(uname(name), shape, dtype))
            else:
                t = st.enter_context(nc.sbuf_tensor(uname(name), shape, dtype))
            self.t.append(t)
            self.b.append(Buf(f"{name}{j}"))

    def next(self):
        j = self.i
        self.i = (self.i + 1) % self.n
        return self.t[j], self.b[j]


def sb(st, nc, name, shape, dtype):
    return st.enter_context(nc.sbuf_tensor(uname(name), shape, dtype))


def ps(st, nc, name, shape, dtype):
    return st.enter_context(nc.psum_tensor(uname(name), shape, dtype))


def load_bcast(k, st, name, vec_ap, n):
    nc = k.nc
    t = sb(st, nc, name, [128, n], F32)
    b = Buf(name)
    k.dma(k.sp, t[:], vec_ap.partition_broadcast(128), writes=[b])
    return t, b


def rms_rstd(k, ss, ss_b, rstd, rstd_b, n, eps=EPS, mul=1.0):
    nc = k.nc
    c_eps = k.const(eps)
    c_mul = k.const(math.log(mul))
    k.op(k.act, lambda: nc.scalar.activation(out=rstd, in_=ss, func=AF.Ln, scale=1.0 / n, bias=c_eps),
         reads=[ss_b, k.const_b], writes=[rstd_b])
    k.op(k.act, lambda: nc.scalar.activation(out=rstd, in_=rstd, func=AF.Exp, scale=-0.5, bias=c_mul),
         reads=[rstd_b, k.const_b], writes=[rstd_b])


def norm_h(k, rings, x_src, tok0, npre, npre_b):
    nc = k.nc
    xs, hb, stat = rings
    x_t, x_b = xs.next()
    k.dma(k.sp, x_t[:], x_src[tok0:tok0 + 128, :], writes=[x_b])
    h_t, h_b = hb.next()
    s_t, s_b = stat.next()
    k.op(k.act, lambda: nc.scalar.activation(out=h_t[:], in_=x_t[:], func=AF.Square, accum_out=s_t[:, 0:1]),
         reads=[x_b], writes=[h_b, s_b])
    rms_rstd(k, s_t[:, 0:1], s_b, s_t[:, 1:2], s_b, D_MODEL)
    k.op(k.dve, lambda: nc.vector.scalar_tensor_tensor(out=h_t[:], in0=x_t[:], scalar=s_t[:, 1:2],
                                                       in1=npre[:], op0=ALU.mult, op1=ALU.mult),
         reads=[x_b, s_b, npre_b], writes=[h_b])
    return h_t, h_b


def transp_h(k, ptr, ident, ident_b, h_t, h_b, hT_t, hT_b, s):
    nc = k.nc
    p_t, p_b = ptr.next()
    k.mm([(lambda c=c: nc.tensor.transpose(p_t[:, c, :], h_t[:, c * 128:(c + 1) * 128], ident[:]))
          for c in range(8)], reads=[h_b, ident_b], writes=[p_b])
    k.op(k.act, lambda: nc.scalar.copy(out=hT_t[:, :, s * 128:(s + 1) * 128], in_=p_t[:]),
         reads=[p_b], writes=[hT_b])


def phase_ffn(k, ident, ident_b, x_in, x_out, w_gu, w_down, n_pre, n_post):
    nc = k.nc
    TT = 512
    NTT = SEQ // TT
    NS = TT // 128
    NJ = D_FF // 128
    with ExitStack() as st:
        wgu = sb(st, nc, "wgu", [128, 8, 2 * D_FF], BF16)
        wdn = sb(st, nc, "wdn", [128, NJ, D_MODEL], BF16)
        JB = [(0, 4), (4, 8), (8, 12), (12, 16), (16, 20), (20, 22)]
        wgu_b = {}
        for bi, (j0, j1) in enumerate(JB):
            for gu in range(2):
                c0 = gu * D_FF + j0 * 128
                c1 = gu * D_FF + j1 * 128
                b = Buf()
                for j in range(j0, j1):
                    wgu_b[(gu, j)] = b
                for c in range(8):
                    k.dma(k.pool, wgu[:, c, c0:c1], w_gu[c * 128:(c + 1) * 128, c0:c1], writes=[b], join=True)
        wdn_b = []
        for hh in range(2):
            b = Buf()
            wdn_b.append(b)
            k.dma(k.pool, wdn[:, hh * 11:(hh + 1) * 11, :],
                  w_down[hh * 1408:(hh + 1) * 1408, :].rearrange("(j p) n -> p j n", p=128), writes=[b])
        npre, npre_b = load_bcast(k, st, "npre", n_pre, D_MODEL)
        npost, npost_b = load_bcast(k, st, "npost", n_post, D_MODEL)

        xs = Ring(st, nc, "xs", 2, [128, D_MODEL], F32)
        hb = Ring(st, nc, "hb", 4, [128, D_MODEL], BF16)
        stat = Ring(st, nc, "stat", 8, [128, 4], F32)
        hT = Ring(st, nc, "hT", 1, [128, 8, TT], BF16)
        actT = Ring(st, nc, "actT", 1, [128, NJ, TT], BF16)
        sg = Ring(st, nc, "sg", 2, [128, TT], F32)
        xr = Ring(st, nc, "xr", 2, [128, D_MODEL], F32)
        yo = Ring(st, nc, "yo", 1, [128, D_MODEL], F32)
        ptr = Ring(st, nc, "ptr", 1, [128, 8, 128], BF16, psum=True)
        pg = Ring(st, nc, "pg", 2, [128, TT], F32, psum=True)
        pu = Ring(st, nc, "pu", 2, [128, TT], F32, psum=True)
        py = Ring(st, nc, "py", 1, [128, 2, 512], F32, psum=True)
        nrings = (xs, hb, stat)

        hT_t, hT_b = hT.next()
        hs = [norm_h(k, nrings, x_in, s * 128, npre, npre_b) for s in range(NS)]
        for s in range(NS):
            transp_h(k, ptr, ident, ident_b, hs[s][0], hs[s][1], hT_t, hT_b, s)
        for t in range(NTT):
            if t + 1 < NTT:
                hs = [norm_h(k, nrings, x_in, ((t + 1) * NS + s) * 128, npre, npre_b) for s in range(NS)]
            a_t, a_b = actT.next()
            for j in range(NJ):
                g_t, g_b = pg.next()
                u_t, u_b = pu.next()
                k.mm([(lambda c=c: nc.tensor.matmul(g_t[:], wgu[:, c, j * 128:(j + 1) * 128], hT_t[:, c, :],
                                                    start=(c == 0), stop=(c == 7))) for c in range(8)],
                     reads=[wgu_b[(0, j)], hT_b], writes=[g_b])
                k.mm([(lambda c=c: nc.tensor.matmul(u_t[:], wgu[:, c, D_FF + j * 128:D_FF + (j + 1) * 128],
                                                    hT_t[:, c, :], start=(c == 0), stop=(c == 7)))
                      for c in range(8)],
                     reads=[wgu_b[(1, j)], hT_b], writes=[u_b])
                sg_t, sg_b = sg.next()
                k.op(k.act, lambda: nc.scalar.activation(out=sg_t[:], in_=g_t[:], func=AF.Silu),
                     reads=[g_b], writes=[sg_b])
                k.op(k.dve, lambda: nc.vector.tensor_tensor(out=a_t[:, j, :], in0=sg_t[:], in1=u_t[:],
                                                            op=ALU.mult),
                     reads=[sg_b, u_b], writes=[a_b])
            if t + 1 < NTT:
                hT_t, hT_b = hT.next()
                for s in range(NS):
                    transp_h(k, ptr, ident, ident_b, hs[s][0], hs[s][1], hT_t, hT_b, s)
            for s in range(NS):
                tok0 = (t * NS + s) * 128
                y_t, y_b = py.next()
                fns = []
                for hf in range(2):
                    for j in range(NJ):
                        fns.append(lambda j=j, hf=hf: nc.tensor.matmul(
                            y_t[:, hf, :], a_t[:, j, s * 128:(s + 1) * 128], wdn[:, j, hf * 512:(hf + 1) * 512],
                            start=(j == 0), stop=(j == NJ - 1)))
                k.mm(fns, reads=[a_b] + wdn_b, writes=[y_b])
                xr_t, xr_b = xr.next()
                k.dma(k.sp, xr_t[:], x_in[tok0:tok0 + 128, :], writes=[xr_b])
                o_t, o_b = yo.next()
                s_t, s_b = stat.next()
                for hf in range(2):
                    k.op(k.act, lambda: nc.scalar.activation(out=o_t[:, hf * 512:(hf + 1) * 512], in_=y_t[:, hf, :],
                                                             func=AF.Square, accum_out=s_t[:, 2 + hf:3 + hf]),
                         reads=[y_b], writes=[o_b, s_b])
                k.op(k.dve, lambda: nc.vector.tensor_tensor(out=s_t[:, 0:1], in0=s_t[:, 2:3], in1=s_t[:, 3:4],
                                                            op=ALU.add), reads=[s_b], writes=[s_b])
                rms_rstd(k, s_t[:, 0:1], s_b, s_t[:, 1:2], s_b, D_MODEL, mul=0.5)
                for hf in range(2):
                    k.op(k.dve, lambda: nc.vector.tensor_tensor(
                        out=o_t[:, hf * 512:(hf + 1) * 512], in0=y_t[:, hf, :], in1=npost[:, hf * 512:(hf + 1) * 512],
                        op=ALU.mult), reads=[y_b, npost_b], writes=[o_b])
                k.op(k.dve, lambda: nc.vector.scalar_tensor_tensor(
                    out=xr_t[:], in0=o_t[:], scalar=s_t[:, 1:2], in1=xr_t[:], op0=ALU.mult, op1=ALU.add),
                    reads=[o_b, xr_b, s_b], writes=[xr_b])
                k.dma(k.sp, x_out[tok0:tok0 + 128, :], xr_t[:], reads=[xr_b])
        k.barrier()


def phase_proj(k, ident, ident_b, x_in, w_in, n_pre, gate_b_pc, S):
    nc = k.nc
    TT = 512
    NTT = SEQ // TT
    NS = TT // 128
    with ExitStack() as st:
        win = sb(st, nc, "win", [128, 8, C_IN], BF16)
        blocks = [(0, 512, 0), (512, 1024, 0), (1024, 2048, 0), (2048, 3072, 1), (3072, 4096, 1),
                  (4096, 5120, 2), (5120, 6160, 2), (6160, 7184, 3), (7184, 8208, 3)]
        wb = [Buf() for _ in range(4)]
        for (c0, c1, bi) in blocks:
            for c in range(8):
                k.dma(k.pool, win[:, c, c0:c1], w_in[c * 128:(c + 1) * 128, c0:c1], writes=[wb[bi]], join=True)
        npre, npre_b = load_bcast(k, st, "npre", n_pre, D_MODEL)
        gb = sb(st, nc, "gb", [128, 16], F32)
        gb_b = Buf()
        k.dma(k.sp, gb[:], gate_b_pc, writes=[gb_b])
        xs = Ring(st, nc, "xs", 2, [128, D_MODEL], F32)
        hb = Ring(st, nc, "hb", 4, [128, D_MODEL], BF16)
        stat = Ring(st, nc, "stat", 8, [128, 4], F32)
        ptr = Ring(st, nc, "ptr", 1, [128, 8, 128], BF16, psum=True)
        nrings = (xs, hb, stat)
        hT = Ring(st, nc, "hT", 1, [128, 8, TT], BF16)
        of32 = Ring(st, nc, "of32", 4, [128, 512], F32)
        obf = Ring(st, nc, "obf", 4, [128, 512], BF16)
        pf = Ring(st, nc, "pf", 3, [128, 512], F32, psum=True)
        pt = Ring(st, nc, "pt", 3, [128, 512], F32, psum=True)
        ev = [0]

        def evac_copy(out_ap, in_ap, rd, wr):
            ev[0] ^= 1
            if ev[0]:
                k.op(k.dve, lambda: nc.vector.tensor_copy(out=out_ap, in_=in_ap), reads=rd, writes=wr)
            else:
                k.op(k.act, lambda: nc.scalar.copy(out=out_ap, in_=in_ap), reads=rd, writes=wr)

        hT_t, hT_b = hT.next()
        hs = [norm_h(k, nrings, x_in, s * 128, npre, npre_b) for s in range(NS)]
        for s in range(NS):
            transp_h(k, ptr, ident, ident_b, hs[s][0], hs[s][1], hT_t, hT_b, s)
        for t in range(NTT):
            T0 = t * TT
            if t + 1 < NTT:
                hs = [norm_h(k, nrings, x_in, (t + 1) * TT + s * 128, npre, npre_b) for s in range(NS)]

            def fm_chunk(col0, wbuf):
                p_t, p_b = pf.next()
                k.mm([(lambda c=c: nc.tensor.matmul(p_t[:], win[:, c, col0:col0 + 128], hT_t[:, c, :],
                                                    start=(c == 0), stop=(c == 7))) for c in range(8)],
                     reads=[wbuf, hT_b], writes=[p_b])
                return p_t, p_b

            def tm_block(s, col0, n, wbuf):
                p_t, p_b = pt.next()
                k.mm([(lambda c=c: nc.tensor.matmul(p_t[:, 0:n], hT_t[:, c, s * 128:(s + 1) * 128],
                                                    win[:, c, col0:col0 + n], start=(c == 0), stop=(c == 7)))
                      for c in range(8)], reads=[wbuf, hT_b], writes=[p_b])
                return p_t, p_b

            for c in range(16):
                p_t, p_b = fm_chunk(c * 128, wb[0])
                o_t, o_b = obf.next()
                evac_copy(o_t[:], p_t[:], [p_b], [o_b])
                k.dma(k.sp, S["qkT"][c * 128:(c + 1) * 128, T0:T0 + TT], o_t[:], reads=[o_b])
            for c in range(8):
                p_t, p_b = fm_chunk(O_MQ + c * 128, wb[1])
                o_t, o_b = of32.next()
                evac_copy(o_t[:], p_t[:], [p_b], [o_b])
                k.dma(k.sp, S["mqkT"][c * 128:(c + 1) * 128, T0:T0 + TT], o_t[:], reads=[o_b])
            for s in range(NS):
                tok0 = T0 + s * 128
                for hf in range(2):
                    p_t, p_b = tm_block(s, O_AV + hf * 512, 512, wb[1])
                    o_t, o_b = obf.next()
                    evac_copy(o_t[:], p_t[:], [p_b], [o_b])
                    k.dma(k.sp, S["av"][tok0:tok0 + 128, hf * 512:(hf + 1) * 512], o_t[:], reads=[o_b])
                for hf in range(2):
                    p_t, p_b = tm_block(s, O_MV + hf * 512, 512, wb[2])
                    o_t, o_b = of32.next()
                    evac_copy(o_t[:], p_t[:], [p_b], [o_b])
                    k.dma(k.sp, S["mv"][tok0:tok0 + 128, hf * 512:(hf + 1) * 512], o_t[:], reads=[o_b])
                for hf in range(2):
                    p_t, p_b = tm_block(s, O_MO + hf * 512, 512, wb[2])
                    o_t, o_b = of32.next()
                    k.op(k.act, lambda: nc.scalar.activation(out=o_t[:], in_=p_t[:], func=AF.Sigmoid),
                         reads=[p_b], writes=[o_b])
                    k.dma(k.sp, S["mo"][tok0:tok0 + 128, hf * 512:(hf + 1) * 512], o_t[:], reads=[o_b])
                p_t, p_b = tm_block(s, O_MI, 16, wb[2])
                o_t, o_b = of32.next()
                evac_copy(o_t[:, 0:16], p_t[:, 0:16], [p_b], [o_b])
                k.dma(k.sp, S["mif"][tok0:tok0 + 128, :], o_t[:, 0:16], reads=[o_b])
            for c in range(16):
                p_t, p_b = fm_chunk(O_GL + c * 128, wb[3])
                o_t, o_b = of32.next()
                k.op(k.act, lambda: nc.scalar.activation(out=o_t[:], in_=p_t[:], func=AF.Sigmoid,
                                                         bias=gb[:, c:c + 1]),
                     reads=[p_b, gb_b], writes=[o_b])
                k.dma(k.sp, S["glT"][c * 128:(c + 1) * 128, T0:T0 + TT], o_t[:], reads=[o_b])
            if t + 1 < NTT:
                hT_t, hT_b = hT.next()
                for s in range(NS):
                    transp_h(k, ptr, ident, ident_b, hs[s][0], hs[s][1], hT_t, hT_b, s)
        k.barrier()


def phase_attn(k, cbf, cbf_b, cf32, cf32_b, S, lamp, anw_col, lam_init):
    nc = k.nc
    NG = SEQ // 512
    with ExitStack() as st:
        lt, lt_b = load_bcast(k, st, "lamp", lamp, 256)
        lw = sb(st, nc, "lamw", [128, 136], F32)
        lw_b = Buf()
        k.op(k.dve, lambda: nc.vector.tensor_tensor(out=lw[:, 0:128], in0=lt[:, 0:128], in1=lt[:, 128:256],
                                                    op=ALU.mult), reads=[lt_b], writes=[lw_b])
        k.op(k.dve, lambda: nc.vector.tensor_reduce(out=lw[:, 128:130],
                                                    in_=lw[:, 0:128].rearrange("p (a b) -> p a b", a=2),
                                                    axis=AX.X, op=ALU.add), reads=[lw_b], writes=[lw_b])
        k.op(k.act, lambda: nc.scalar.activation(out=lw[:, 130:132], in_=lw[:, 128:130], func=AF.Exp),
             reads=[lw_b], writes=[lw_b])
        k.op(k.dve, lambda: nc.vector.tensor_tensor(out=lw[:, 132:133], in0=lw[:, 130:131], in1=lw[:, 131:132],
                                                    op=ALU.subtract), reads=[lw_b], writes=[lw_b])
        k.op(k.dve, lambda: nc.vector.tensor_scalar(out=lw[:, 133:134], in0=lw[:, 132:133], scalar1=lam_init,
                                                    scalar2=-1.0, op0=ALU.add, op1=ALU.mult),
             reads=[lw_b], writes=[lw_b])
        neg_lam = lw[:, 133:134]
        nw = sb(st, nc, "anw", [128, 1], F32)
        nw_b = Buf()
        k.dma(k.sp, nw[:], anw_col, writes=[nw_b])
        k.op(k.dve, lambda: nc.vector.tensor_scalar(out=nw[:], in0=nw[:], scalar1=1.0 - lam_init, scalar2=None,
                                                    op0=ALU.mult), reads=[nw_b], writes=[nw_b])

        qT = Ring(st, nc, "qT", 2, [128, SEQ], BF16)
        kT = Ring(st, nc, "kT", 2, [128, SEQ], BF16)
        vh = Ring(st, nc, "vh", 2, [128, NT_(), 128], BF16)
        pT = Ring(st, nc, "pT", 6, [128, 512], BF16)
        f32r = Ring(st, nc, "af", 10, [128, 512], F32)
        yo = Ring(st, nc, "yao", 2, [128, 512], BF16)
        psr = Ring(st, nc, "psS", 4, [128, 512], F32, psum=True)
        po = [ps(st, nc, "po", [128, 512], F32) for _ in range(2)]
        pl = [ps(st, nc, "pl", [128, 512], F32) for _ in range(2)]
        po_b = [Buf(), Buf()]
        pl_b = [Buf(), Buf()]
        ones_bf = cbf[:, 1, :]
        tri = cbf[:, 2, :]

        for h in range(A_HEADS):
            q_t, q_b = qT.next()
            k_t, k_b = kT.next()
            v_t, v_b = vh.next()
            k.dma(k.sp, q_t[:], S["qkT"][h * 128:(h + 1) * 128, :], writes=[q_b])
            k.dma(k.sp, k_t[:], S["qkT"][1024 + h * 128:1024 + (h + 1) * 128, :], writes=[k_b])
            k.dma(k.sp, v_t[:], S["av"][:, h * 128:(h + 1) * 128].rearrange("(t p) d -> p t d", p=128),
                  writes=[v_b])
            for g in range(NG):
                nkt = 4 * g + 4
                pend = {}

                def stage_a(kt):
                    j = kt - 4 * g
                    c0 = 128 * j if j > 0 else 0
                    outs = []
                    sts = [psr.next() for _ in range(2)]
                    k.mm([(lambda m=m: nc.tensor.matmul(
                        sts[m][0][:, c0:512], k_t[m * 64:(m + 1) * 64, kt * 128:(kt + 1) * 128],
                        q_t[m * 64:(m + 1) * 64, g * 512 + c0:(g + 1) * 512], start=True, stop=True))
                        for m in range(2)], reads=[q_b, k_b], writes=[sts[0][1], sts[1][1]])
                    for m in range(2):
                        s_t, s_b = sts[m]
                        p_t, p_b = pT.next()
                        k.op(k.act, lambda: nc.scalar.activation(out=p_t[:, c0:512], in_=s_t[:, c0:512],
                                                                 func=AF.Exp, scale=A_DHEAD ** -0.5),
                             reads=[s_b], writes=[p_b])
                        if j >= 0:
                            k.op(k.pool, lambda: nc.gpsimd.tensor_tensor(out=p_t[:, c0:c0 + 128],
                                                                         in0=p_t[:, c0:c0 + 128], in1=tri,
                                                                         op=ALU.mult),
                                 reads=[p_b, cbf_b], writes=[p_b])
                        outs.append((p_t, p_b, c0))
                    pend[kt] = outs

                def stage_b(kt):
                    for m in range(2):
                        p_t, p_b, c0 = pend[kt][m]
                        k.mm([lambda: nc.tensor.matmul(po[m][:, c0:512], v_t[:, kt, :], p_t[:, c0:512],
                                                       start=(kt == 0), stop=(kt == nkt - 1)),
                              lambda: nc.tensor.matmul(pl[m][:, c0:512], ones_bf, p_t[:, c0:512],
                                                       start=(kt == 0), stop=(kt == nkt - 1))],
                             reads=[p_b, v_b, cbf_b], writes=[po_b[m], pl_b[m]])
                    del pend[kt]

                stage_a(0)
                for kt in range(1, nkt):
                    stage_a(kt)
                    stage_b(kt - 1)
                stage_b(nkt - 1)

                oc, lc = [], []
                for m in range(2):
                    o_t, o_b = f32r.next()
                    k.op(k.dve, lambda: nc.vector.tensor_copy(out=o_t[:], in_=po[m][:]), reads=[po_b[m]], writes=[o_b])
                    oc.append((o_t, o_b))
                    l_t, l_b = f32r.next()
                    k.op(k.act, lambda: nc.scalar.copy(out=l_t[:], in_=pl[m][:]), reads=[pl_b[m]], writes=[l_b])
                    lc.append((l_t, l_b))
                for m in range(2):
                    l_t, l_b = lc[m]
                    o_t, o_b = oc[m]
                    k.op(k.dve, lambda: nc.vector.reciprocal(out=l_t[:], in_=l_t[:]), reads=[l_b], writes=[l_b])
                    k.op(k.dve, lambda: nc.vector.tensor_tensor(out=o_t[:], in0=o_t[:], in1=l_t[:], op=ALU.mult),
                         reads=[o_b, l_b], writes=[o_b])
                d_t, d_b = oc[0]
                k.op(k.dve, lambda: nc.vector.scalar_tensor_tensor(out=d_t[:], in0=oc[1][0][:], scalar=neg_lam,
                                                                   in1=d_t[:], op0=ALU.mult, op1=ALU.add),
                     reads=[oc[1][1], d_b, lw_b], writes=[d_b])
                sq_t, sq_b = f32r.next()
                k.op(k.act, lambda: nc.scalar.activation(out=sq_t[:], in_=d_t[:], func=AF.Square),
                     reads=[d_b], writes=[sq_b])
                s_t, s_b = psr.next()
                k.mm([lambda: nc.tensor.matmul(s_t[:], cf32[:, 0, :], sq_t[:], start=True, stop=True)],
                     reads=[sq_b, cf32_b], writes=[s_b])
                rs_t, rs_b = f32r.next()
                rms_rstd(k, s_t[:], s_b, rs_t[:], rs_b, 128)
                y_t, y_b = yo.next()
                k.op(k.dve, lambda: nc.vector.scalar_tensor_tensor(out=y_t[:], in0=d_t[:], scalar=nw[:, 0:1],
                                                                   in1=rs_t[:], op0=ALU.mult, op1=ALU.mult),
                     reads=[d_b, nw_b, rs_b], writes=[y_b])
                k.dma(k.sp, S["yaT"][h * 128:(h + 1) * 128, g * 512:(g + 1) * 512], y_t[:], reads=[y_b])
        k.barrier()


def NT_():
    return SEQ // 128


def make_scratch(nc, kind="Internal"):
    def d(name, shape, dt):
        return nc.dram_tensor(name, shape, dt, kind=kind).ap()
    return {
        "qkT": d("s_qkT", [2048, SEQ], BF16),
        "av": d("s_av", [SEQ, 1024], BF16),
        "mqkT": d("s_mqkT", [1024, SEQ], F32),
        "mv": d("s_mv", [SEQ, 1024], F32),
        "mo": d("s_mo", [SEQ, 1024], F32),
        "mif": d("s_mif", [SEQ, 16], F32),
        "glT": d("s_glT", [2048, SEQ], F32),
        "yaT": d("s_yaT", [1024, SEQ], BF16),
        "ymT": d("s_ymT", [1024, SEQ], BF16),
    }


def host_consts():
    import ml_dtypes
    p = np.arange(128)[:, None]
    f = np.arange(128)[None, :]
    tri = (p <= f).astype(np.float32)
    cbf = np.stack([np.eye(128, dtype=np.float32), np.ones((128, 128), np.float32), tri], axis=1)
    cf32 = np.stack([np.ones((128, 128), np.float32), tri, tri * (M_DQK ** -0.5)], axis=1)
    return {"cbf": np.ascontiguousarray(cbf).astype(ml_dtypes.bfloat16),
            "cf32": np.ascontiguousarray(cf32).astype(np.float32)}


def load_consts(k, st):
    nc = k.nc
    cbf_d = nc.dram_tensor("cbf", [128, 3, 128], BF16, kind="ExternalInput").ap()
    cf32_d = nc.dram_tensor("cf32", [128, 3, 128], F32, kind="ExternalInput").ap()
    cbf = sb(st, nc, "cbf_sb", [128, 3, 128], BF16)
    cf32 = sb(st, nc, "cf32_sb", [128, 3, 128], F32)
    cbf_b, cf32_b = Buf(), Buf()
    k.dma(k.sp, cbf[:], cbf_d, writes=[cbf_b])
    k.dma(k.sp, cf32[:], cf32_d, writes=[cf32_b])
    return cbf, cbf_b, cf32, cf32_b


def phase_mlstm(k, cbf, cbf_b, cf32, cf32_b, S, convw_pc, convb_pc, igb, fgb, mnw):
    nc = k.nc
    NTL = SEQ // 128
    ident = cbf[:, 0, :]
    ones32 = cf32[:, 0, :]
    triu32 = cf32[:, 1, :]
    mask8 = cf32[:, 2, :]
    with ExitStack() as st:
        qkc = sb(st, nc, "qkc", [128, 8, SEQ], BF16)
        qkc_b = [Buf() for _ in range(8)]
        cw = sb(st, nc, "cw", [128, 8, 4], F32)
        cb = sb(st, nc, "cb", [128, 8], F32)
        cw_b, cb_b = Buf(), Buf()
        k.dma(k.sp, cw[:], convw_pc, writes=[cw_b])
        k.dma(k.sp, cb[:], convb_pc, writes=[cb_b])
        with ExitStack() as st1:
            xp = Ring(st1, nc, "xp", 2, [128, SEQ + 3], F32)
            acc = Ring(st1, nc, "cacc", 1, [128, SEQ], F32)
            for i in range(2):
                k.op(k.pool, lambda: nc.gpsimd.memset(xp.t[i][:, 0:3], 0.0), writes=[xp.b[i]])
            for c in range(8):
                x_t, x_b = xp.next()
                k.dma(k.sp, x_t[:, 3:SEQ + 3], S["mqkT"][c * 128:(c + 1) * 128, :], writes=[x_b])
                a_t, a_b = acc.next()
                k.op(k.dve, lambda: nc.vector.tensor_scalar(out=a_t[:], in0=x_t[:, 0:SEQ], scalar1=cw[:, c, 0:1],
                                                            scalar2=None, op0=ALU.mult),
                     reads=[x_b, cw_b], writes=[a_b])
                for j in range(1, 4):
                    k.op(k.dve, lambda: nc.vector.scalar_tensor_tensor(
                        out=a_t[:], in0=x_t[:, j:j + SEQ], scalar=cw[:, c, j:j + 1], in1=a_t[:],
                        op0=ALU.mult, op1=ALU.add), reads=[x_b, cw_b, a_b], writes=[a_b])
                k.op(k.act, lambda: nc.scalar.activation(out=qkc[:, c, :], in_=a_t[:], func=AF.Silu,
                                                         bias=cb[:, c:c + 1]),
                     reads=[a_b, cb_b], writes=[qkc_b[c]])
        k.barrier()

        ktok = sb(st, nc, "ktok", [128, NTL, 512], BF16)
        ktok_b = Buf()
        NC8 = NTL * 8
        gt = sb(st, nc, "gt", [128, NTL, 16], F32)
        gt_b = Buf()
        k.dma(k.sp, gt[:], S["mif"].rearrange("(t p) c -> p t c", p=128), writes=[gt_b])
        igb_t, igb_b = load_bcast(k, st, "igb", igb, 8)
        fgb_t, fgb_b = load_bcast(k, st, "fgb", fgb, 8)
        mnw_t, mnw_b = load_bcast(k, st, "mnw", mnw, 1024)
        IG = sb(st, nc, "IG", [128, NTL, 8], F32)
        NLF = sb(st, nc, "NLF", [128, NTL, 8], F32)
        NGc = sb(st, nc, "NGc", [128, NTL, 8], F32)
        EQ = sb(st, nc, "EQ", [128, NTL, 8], F32)
        EK = sb(st, nc, "EK", [128, NTL, 8], F32)
        EE = sb(st, nc, "EE", [128, NTL, 8], F32)
        IG_b, NLF_b, NG_b, EQ_b, EK_b, EE_b = [Buf() for _ in range(6)]

        pa = Ring(st, nc, "pa", 2, [128, 4, 128], F32, psum=True)
        pr = Ring(st, nc, "pr", 2, [128, 4, 128], F32, psum=True)
        pu = Ring(st, nc, "pu", 2, [128, 4, 128], F32, psum=True)
        pm = ps(st, nc, "pm", [128, 512], F32)
        prd, prd_b = pm[:, 0:8], Buf()
        pun, pun_b = pm[:, 8:16], Buf()
        ptr = ps(st, nc, "ptrm", [128, 8, 128], BF16)
        ptr_b = Buf()

        for t in range(NTL):
            k.mm([(lambda c=c: nc.tensor.transpose(ptr[:, c, :], qkc[:, 4 + c, t * 128:(t + 1) * 128], ident))
                  for c in range(4)], reads=qkc_b[4:8] + [cbf_b], writes=[ptr_b])
            k.op(k.act, lambda: nc.scalar.copy(out=ktok[:, t, :], in_=ptr[:, 0:4, :].rearrange("p a b -> p (a b)")),
                 reads=[ptr_b], writes=[ktok_b])

        k.op(k.dve, lambda: nc.vector.tensor_tensor(out=IG[:], in0=gt[:, :, 0:8],
                                                    in1=igb_t[:].unsqueeze(1).to_broadcast([128, NTL, 8]),
                                                    op=ALU.add), reads=[gt_b, igb_b], writes=[IG_b])
        k.op(k.dve, lambda: nc.vector.tensor_tensor(out=NLF[:], in0=gt[:, :, 8:16],
                                                    in1=fgb_t[:].unsqueeze(1).to_broadcast([128, NTL, 8]),
                                                    op=ALU.add), reads=[gt_b, fgb_b], writes=[NLF_b])
        k.op(k.act, lambda: nc.scalar.activation(out=NLF[:], in_=NLF[:], func=AF.Exp, scale=-1.0),
             reads=[NLF_b], writes=[NLF_b])
        c_one = k.const(1.0)
        k.op(k.act, lambda: nc.scalar.activation(out=NLF[:], in_=NLF[:], func=AF.Ln, bias=c_one),
             reads=[NLF_b, k.const_b], writes=[NLF_b])
        nlf2 = NLF[:].rearrange("p a b -> p (a b)")
        pg_t, pg_b = pa.next()
        pg2 = pg_t[:].rearrange("p a b -> p (a b)")
        k.mm([lambda: nc.tensor.matmul(pg2[:, 0:NC8], triu32, nlf2, start=True, stop=True)],
             reads=[NLF_b, cf32_b], writes=[pg_b])
        k.op(k.dve, lambda: nc.vector.tensor_copy(out=NGc[:].rearrange("p a b -> p (a b)"), in_=pg2[:, 0:NC8]),
             reads=[pg_b], writes=[NG_b])
        pe_t, pe_b = pa.next()
        pe2 = pe_t[:].rearrange("p a b -> p (a b)")
        k.mm([lambda: nc.tensor.matmul(pe2[:, 0:NC8], ones32, nlf2, start=True, stop=True)],
             reads=[NLF_b, cf32_b], writes=[pe_b])
        k.op(k.act, lambda: nc.scalar.activation(out=EE[:].rearrange("p a b -> p (a b)"), in_=pe2[:, 0:NC8],
                                                 func=AF.Exp, scale=-1.0), reads=[pe_b], writes=[EE_b])
        k.op(k.act, lambda: nc.scalar.activation(out=EQ[:], in_=NGc[:], func=AF.Exp, scale=-1.0),
             reads=[NG_b], writes=[EQ_b])
        k.op(k.dve, lambda: nc.vector.tensor_tensor(out=EK[:], in0=IG[:], in1=NGc[:], op=ALU.add),
             reads=[IG_b, NG_b], writes=[EK_b])
        k.op(k.act, lambda: nc.scalar.activation(out=EK[:], in_=EK[:], func=AF.Exp), reads=[EK_b], writes=[EK_b])

        mvr = Ring(st, nc, "mvr", 2, [128, 8, 128], F32)
        mor = Ring(st, nc, "mor", 2, [128, 1024], F32)
        vtr = Ring(st, nc, "vtr", 2, [128, 8, 128], BF16)
        ekr = Ring(st, nc, "ekr", 2, [128, 8], BF16)
        ptT = Ring(st, nc, "ptT", 2, [128, 8, 128], BF16)
        Tst = sb(st, nc, "Tst", [128, 4, 129], F32)
        T_b = [Buf() for _ in range(8)]
        S8r = Ring(st, nc, "S8r", 2, [128, 4, 129], BF16)
        S8_bufs = {0: [Buf() for _ in range(8)], 1: [Buf() for _ in range(8)]}
        sm = Ring(st, nc, "msm", 2, [128, 48], F32)
        ho = Ring(st, nc, "mho", 1, [128, 8, 128], F32)
        sq = Ring(st, nc, "msq", 1, [128, 8, 128], F32)
        y1 = Ring(st, nc, "my1", 1, [128, 8, 128], F32)
        ymb = Ring(st, nc, "ymb", 2, [128, 1024], BF16)
        ymT = Ring(st, nc, "ymTs", 2, [128, 8, 128], BF16)
        s8_prev = None
        for t in range(NTL):
            tsl = slice(t * 128, (t + 1) * 128)
            mv_t, mv_b = mvr.next()
            k.dma(k.sp, mv_t[:], S["mv"][tsl, :].rearrange("p (a b) -> p a b", a=8), writes=[mv_b])
            mo_t, mo_b = mor.next()
            k.dma(k.sp, mo_t[:], S["mo"][tsl, :], writes=[mo_b])
            vt_t, vt_b = vtr.next()
            k.op(k.dve, lambda: nc.vector.tensor_tensor(out=vt_t[:], in0=mv_t[:],
                                                        in1=EK[:, t, :].unsqueeze(2).to_broadcast([128, 8, 128]),
                                                        op=ALU.mult), reads=[mv_b, EK_b], writes=[vt_b])
            ek_t, ek_b = ekr.next()
            k.op(k.dve, lambda: nc.vector.tensor_copy(out=ek_t[:], in_=EK[:, t, :]), reads=[EK_b], writes=[ek_b])
            pT_t, pT_b = ptT.next()
            a_te = [pa.next() for _ in range(2)]
            fns = []
            for i in range(4):
                for e in range(2):
                    P0 = e * 64
                    fns.append(lambda i=i, e=e, P0=P0: nc.tensor.matmul(
                        a_te[e][0][:, i, :], qkc[P0:P0 + 64, 4 + i, tsl], qkc[P0:P0 + 64, i, tsl],
                        start=True, stop=True))
            k.mm(fns, reads=qkc_b, writes=[a_te[0][1], a_te[1][1]])
            for e in range(2):
                k.op(k.dve, lambda: nc.vector.tensor_tensor(
                    out=pT_t[:, 4 * e:4 * e + 4, :], in0=a_te[e][0][:],
                    in1=mask8.unsqueeze(1).to_broadcast([128, 4, 128]), op=ALU.mult),
                    reads=[a_te[e][1], cf32_b], writes=[pT_b])
            r_ts = []
            for bk in range(2):
                r_t, r_b = pr.next()
                fns = []
                for hh in range(4):
                    h = 4 * bk + hh
                    P0 = (h % 2) * 64
                    fns.append(lambda h=h, hh=hh: nc.tensor.matmul(
                        r_t[:, hh, :], pT_t[:, (h % 2) * 4 + h // 2, :], vt_t[:, h, :], start=True, stop=(t == 0)))
                    if t > 0:
                        fns.append(lambda h=h, hh=hh, P0=P0: nc.tensor.matmul(
                            r_t[:, hh, :], qkc[P0:P0 + 64, h // 2, tsl], s8_prev[0][P0:P0 + 64, h // 2, 0:128],
                            start=False, stop=True))
                rd = [pT_b, vt_b] + qkc_b + (s8_prev[1] if t > 0 else [])
                k.mm(fns, reads=rd, writes=[r_b])
                r_ts.append((r_t, r_b))
            fns = []
            for h in range(8):
                P0 = (h % 2) * 64
                fns.append(lambda h=h: nc.tensor.matmul(prd[:, h:h + 1], pT_t[:, (h % 2) * 4 + h // 2, :],
                                                        ek_t[:, h:h + 1], start=True, stop=(t == 0)))
                if t > 0:
                    fns.append(lambda h=h, P0=P0: nc.tensor.matmul(
                        prd[:, h:h + 1], qkc[P0:P0 + 64, h // 2, tsl], s8_prev[0][P0:P0 + 64, h // 2, 128:129],
                        start=False, stop=True))
            k.mm(fns, reads=[pT_b, ek_b] + qkc_b + (s8_prev[1] if t > 0 else []), writes=[prd_b])
            u_ts = []
            for bk in range(2):
                u_t, u_b = pu.next()
                fns = []
                for hh in range(4):
                    h = 4 * bk + hh
                    fns.append(lambda h=h, hh=hh: nc.tensor.matmul(
                        u_t[:, hh, :], ktok[:, t, (h // 2) * 128:(h // 2 + 1) * 128], vt_t[:, h, :],
                        start=True, stop=True))
                k.mm(fns, reads=[ktok_b, vt_b], writes=[u_b])
                u_ts.append((u_t, u_b))
            k.mm([(lambda h=h: nc.tensor.matmul(pun[:, h:h + 1], ktok[:, t, (h // 2) * 128:(h // 2 + 1) * 128],
                                                ek_t[:, h:h + 1], start=True, stop=True)) for h in range(8)],
                 reads=[ktok_b, ek_b], writes=[pun_b])
            s8_t, _ = S8r.next()
            s8_bl = S8_bufs[t % 2]
            for h in range(8):
                P0 = (h % 2) * 64
                hp = h // 2
                u_t, u_b = u_ts[h // 4]
                if t == 0:
                    k.op(k.dve, lambda: nc.vector.tensor_copy(out=Tst[P0:P0 + 64, hp, 0:128],
                                                              in_=u_t[P0:P0 + 64, h % 4, :]),
                         reads=[u_b], writes=[T_b[h]])
                    k.op(k.dve, lambda: nc.vector.tensor_copy(out=Tst[P0:P0 + 64, hp, 128:129],
                                                              in_=pun[P0:P0 + 64, h:h + 1]),
                         reads=[pun_b], writes=[T_b[h]])
                else:
                    k.op(k.dve, lambda: nc.vector.scalar_tensor_tensor(
                        out=Tst[P0:P0 + 64, hp, 0:128], in0=Tst[P0:P0 + 64, hp, 0:128],
                        scalar=EE[P0:P0 + 64, t - 1, h:h + 1], in1=u_t[P0:P0 + 64, h % 4, :],
                        op0=ALU.mult, op1=ALU.add), reads=[T_b[h], EE_b, u_b], writes=[T_b[h]])
                    k.op(k.dve, lambda: nc.vector.scalar_tensor_tensor(
                        out=Tst[P0:P0 + 64, hp, 128:129], in0=Tst[P0:P0 + 64, hp, 128:129],
                        scalar=EE[P0:P0 + 64, t - 1, h:h + 1], in1=pun[P0:P0 + 64, h:h + 1],
                        op0=ALU.mult, op1=ALU.add), reads=[T_b[h], EE_b, pun_b], writes=[T_b[h]])
                if t < NTL - 1:
                    k.op(k.pool, lambda: nc.gpsimd.tensor_scalar(
                        out=s8_t[P0:P0 + 64, hp, :], in0=Tst[P0:P0 + 64, hp, :], scalar1=EE[P0:P0 + 64, t, h:h + 1],
                        scalar2=M_DQK ** -0.5, op0=ALU.mult, op1=ALU.mult),
                        reads=[T_b[h], EE_b], writes=[s8_bl[h]])
            s8_prev = (s8_t, s8_bl)
            m_t, m_b = sm.next()
            dn, dneg, rc, cc, ss, rstd = (m_t[:, 0:8], m_t[:, 8:16], m_t[:, 16:24], m_t[:, 24:32],
                                          m_t[:, 32:40], m_t[:, 40:48])
            k.op(k.dve, lambda: nc.vector.tensor_tensor(out=dn, in0=prd, in1=EQ[:, t, :], op=ALU.mult),
                 reads=[prd_b, EQ_b], writes=[m_b])
            k.op(k.dve, lambda: nc.vector.tensor_scalar(out=dneg, in0=dn, scalar1=-1.0, scalar2=None, op0=ALU.mult),
                 reads=[m_b], writes=[m_b])
            k.op(k.dve, lambda: nc.vector.tensor_tensor(out=dn, in0=dn, in1=dneg, op=ALU.max),
                 reads=[m_b], writes=[m_b])
            k.op(k.dve, lambda: nc.vector.tensor_scalar(out=dn, in0=dn, scalar1=1.0, scalar2=None, op0=ALU.max),
                 reads=[m_b], writes=[m_b])
            k.op(k.dve, lambda: nc.vector.reciprocal(out=rc, in_=dn), reads=[m_b], writes=[m_b])
            k.op(k.dve, lambda: nc.vector.tensor_tensor(out=cc, in0=rc, in1=EQ[:, t, :], op=ALU.mult),
                 reads=[m_b, EQ_b], writes=[m_b])
            ho_t, ho_b = ho.next()
            for bk in range(2):
                r_t, r_b = r_ts[bk]
                k.op(k.dve, lambda: nc.vector.tensor_tensor(
                    out=ho_t[:, 4 * bk:4 * bk + 4, :], in0=r_t[:],
                    in1=cc[:, 4 * bk:4 * bk + 4].unsqueeze(2).to_broadcast([128, 4, 128]), op=ALU.mult),
                    reads=[r_b, m_b], writes=[ho_b])
            sq_t, sq_b = sq.next()
            k.op(k.pool, lambda: nc.gpsimd.tensor_tensor(out=sq_t[:], in0=ho_t[:], in1=ho_t[:], op=ALU.mult),
                 reads=[ho_b], writes=[sq_b])
            k.op(k.dve, lambda: nc.vector.tensor_reduce(out=ss, in_=sq_t[:], axis=AX.X, op=ALU.add),
                 reads=[sq_b], writes=[m_b])
            rms_rstd(k, ss, m_b, rstd, m_b, M_DV)
            y1_t, y1_b = y1.next()
            k.op(k.dve, lambda: nc.vector.tensor_tensor(
                out=y1_t[:], in0=ho_t[:], in1=rstd.unsqueeze(2).to_broadcast([128, 8, 128]), op=ALU.mult),
                reads=[ho_b, m_b], writes=[y1_b])
            y1f = y1_t[:].rearrange("p a b -> p (a b)")
            k.op(k.pool, lambda: nc.gpsimd.tensor_tensor(out=y1f, in0=y1f, in1=mnw_t[:], op=ALU.mult),
                 reads=[y1_b, mnw_b], writes=[y1_b])
            yb_t, yb_b = ymb.next()
            k.op(k.dve, lambda: nc.vector.tensor_tensor(out=yb_t[:], in0=y1f, in1=mo_t[:], op=ALU.mult),
                 reads=[y1_b, mo_b], writes=[yb_b])
            k.mm([(lambda c=c: nc.tensor.transpose(ptr[:, c, :], yb_t[:, c * 128:(c + 1) * 128], ident))
                  for c in range(8)], reads=[yb_b, cbf_b], writes=[ptr_b])
            yT_t, yT_b = ymT.next()
            k.op(k.act, lambda: nc.scalar.copy(out=yT_t[:], in_=ptr[:]), reads=[ptr_b], writes=[yT_b])
            k.dma(k.sp, S["ymT"][:, tsl].rearrange("(c p) n -> p c n", p=128), yT_t[:], reads=[yT_b])
        k.barrier()


def phase_out(k, S, x_io, w_pa, w_pm, w_out, n_post):
    nc = k.nc
    TT = 512
    NTT = SEQ // TT
    NS = TT // 128
    with ExitStack() as st:
        ws = []
        for nm, w in (("wpa", w_pa), ("wpm", w_pm), ("wout", w_out)):
            t = sb(st, nc, nm, [128, 8, D_MODEL], BF16)
            b = Buf()
            for hh in range(2):
                k.dma(k.pool, t[:, hh * 4:(hh + 1) * 4, :],
                      w[hh * 512:(hh + 1) * 512, :].rearrange("(c p) n -> p c n", p=128), writes=[b], join=True)
            ws.append((t, b))
        (wpa, wpa_b), (wpm, wpm_b), (wout, wout_b) = ws
        npost, npost_b = load_bcast(k, st, "npost", n_post, D_MODEL)
        yaT = Ring(st, nc, "yaTs", 2, [128, 8, TT], BF16)
        ymT = Ring(st, nc, "ymTs", 2, [128, 8, TT], BF16)
        gT = Ring(st, nc, "gTs", 3, [128, 2, TT], F32)
        mg = Ring(st, nc, "mg", 1, [128, 8, TT], BF16)
        t1 = Ring(st, nc, "t1", 2, [128, TT], F32)
        xr = Ring(st, nc, "xr", 2, [128, D_MODEL], F32)
        yo = Ring(st, nc, "yo", 1, [128, D_MODEL], F32)
        junk = Ring(st, nc, "junk", 1, [128, D_MODEL], BF16)
        stat = Ring(st, nc, "stat", 4, [128, 4], F32)
        ppa = Ring(st, nc, "ppa", 2, [128, TT], F32, psum=True)
        ppm = Ring(st, nc, "ppm", 2, [128, TT], F32, psum=True)
        py = Ring(st, nc, "py", 1, [128, 2, 512], F32, psum=True)
        for t in range(NTT):
            T0 = t * TT
            a_t, a_b = yaT.next()
            m_t, m_b = ymT.next()
            k.dma(k.sp, a_t[:], S["yaT"][:, T0:T0 + TT].rearrange("(c p) n -> p c n", p=128), writes=[a_b])
            k.dma(k.sp, m_t[:], S["ymT"][:, T0:T0 + TT].rearrange("(c p) n -> p c n", p=128), writes=[m_b])
            g_t, g_b = mg.next()
            for c in range(8):
                gg_t, gg_b = gT.next()
                k.dma(k.sp, gg_t[:], S["glT"].rearrange("(a r) n -> r a n", a=2)[c * 128:(c + 1) * 128, :, T0:T0 + TT],
                      writes=[gg_b])
                pa_t, pa_b = ppa.next()
                pm_t, pm_b = ppm.next()
                k.mm([(lambda kc=kc: nc.tensor.matmul(pa_t[:], wpa[:, kc, c * 128:(c + 1) * 128], a_t[:, kc, :],
                                                      start=(kc == 0), stop=(kc == 7))) for kc in range(8)],
                     reads=[wpa_b, a_b], writes=[pa_b])
                k.mm([(lambda kc=kc: nc.tensor.matmul(pm_t[:], wpm[:, kc, c * 128:(c + 1) * 128], m_t[:, kc, :],
                                                      start=(kc == 0), stop=(kc == 7))) for kc in range(8)],
                     reads=[wpm_b, m_b], writes=[pm_b])
                u_t, u_b = t1.next()
                k.op(k.dve, lambda: nc.vector.tensor_tensor(out=u_t[:], in0=pa_t[:], in1=gg_t[:, 0, :], op=ALU.mult),
                     reads=[pa_b, gg_b], writes=[u_b])
                v_t, v_b = t1.next()
                k.op(k.dve, lambda: nc.vector.tensor_tensor(out=v_t[:], in0=pm_t[:], in1=gg_t[:, 1, :], op=ALU.mult),
                     reads=[pm_b, gg_b], writes=[v_b])
                k.op(k.pool, lambda: nc.gpsimd.tensor_tensor(out=g_t[:, c, :], in0=u_t[:], in1=v_t[:], op=ALU.add),
                     reads=[u_b, v_b], writes=[g_b])
            for s in range(NS):
                tok0 = T0 + s * 128
                y_t, y_b = py.next()
                fns = []
                for hf in range(2):
                    for kc in range(8):
                        fns.append(lambda kc=kc, hf=hf: nc.tensor.matmul(
                            y_t[:, hf, :], g_t[:, kc, s * 128:(s + 1) * 128], wout[:, kc, hf * 512:(hf + 1) * 512],
                            start=(kc == 0), stop=(kc == 7)))
                k.mm(fns, reads=[g_b, wout_b], writes=[y_b])
                xr_t, xr_b = xr.next()
                k.dma(k.sp, xr_t[:], x_io[tok0:tok0 + 128, :], writes=[xr_b])
                j_t, j_b = junk.next()
                s_t, s_b = stat.next()
                for hf in range(2):
                    k.op(k.act, lambda: nc.scalar.activation(out=j_t[:, hf * 512:(hf + 1) * 512], in_=y_t[:, hf, :],
                                                             func=AF.Square, accum_out=s_t[:, 2 + hf:3 + hf]),
                         reads=[y_b], writes=[j_b, s_b])
                k.op(k.dve, lambda: nc.vector.tensor_tensor(out=s_t[:, 0:1], in0=s_t[:, 2:3], in1=s_t[:, 3:4],
                                                            op=ALU.add), reads=[s_b], writes=[s_b])
                rms_rstd(k, s_t[:, 0:1], s_b, s_t[:, 1:2], s_b, D_MODEL)
                o_t, o_b = yo.next()
                for hf in range(2):
                    k.op(k.dve, lambda: nc.vector.tensor_tensor(
                        out=o_t[:, hf * 512:(hf + 1) * 512], in0=y_t[:, hf, :], in1=npost[:, hf * 512:(hf + 1) * 512],
                        op=ALU.mult), reads=[y_b, npost_b], writes=[o_b])
                k.op(k.dve, lambda: nc.vector.scalar_tensor_tensor(
                    out=xr_t[:], in0=o_t[:], scalar=s_t[:, 1:2], in1=xr_t[:], op0=ALU.mult, op1=ALU.add),
                    reads=[o_b, xr_b, s_b], writes=[xr_b])
                k.dma(k.sp, x_io[tok0:tok0 + 128, :], xr_t[:], reads=[xr_b])
        k.barrier()


PARAM_NAMES = ["ffn1_norm_pre", "ffn1_w_gu", "ffn1_w_down", "ffn1_norm_post", "mix_norm_pre", "w_in",
               "attn_lam_q1", "attn_lam_k1", "attn_lam_q2", "attn_lam_k2", "attn_norm_w", "conv_w", "conv_b",
               "igate_b", "fgate_b", "mlstm_norm_w", "w_proj_a", "w_proj_m", "gate_b", "w_out", "mix_norm_post",
               "ffn2_norm_pre", "ffn2_w_gu", "ffn2_w_down", "ffn2_norm_post"]


def build_program(depth=DEPTH):
    nc = bass.Bass("TRN2", target_bir_lowering=False)

    def din(name, shape, dt=F32):
        return nc.dram_tensor(name, shape, dt, kind="ExternalInput").ap()

    L = depth
    x = din("x", [SEQ, D_MODEL])
    out = nc.dram_tensor("out", [SEQ, D_MODEL], F32, kind="ExternalOutput").ap()
    P = {
        "ffn1_norm_pre": din("ffn1_norm_pre", [L, D_MODEL]),
        "ffn1_w_gu": din("ffn1_w_gu", [L, D_MODEL, 2 * D_FF]),
        "ffn1_w_down": din("ffn1_w_down", [L, D_FF, D_MODEL]),
        "ffn1_norm_post": din("ffn1_norm_post", [L, D_MODEL]),
        "mix_norm_pre": din("mix_norm_pre", [L, D_MODEL]),
        "w_in": din("w_in", [L, D_MODEL, C_IN]),
        "lamp": din("lamp", [L, 256]),
        "anw": din("anw", [L, 128, 1]),
        "convw_pc": din("convw_pc", [L, 128, 8, 4]),
        "convb_pc": din("convb_pc", [L, 128, 8]),
        "igate_b": din("igate_b", [L, 8]),
        "fgate_b": din("fgate_b", [L, 8]),
        "mlstm_norm_w": din("mlstm_norm_w", [L, 1024]),
        "w_proj_a": din("w_proj_a", [L, 1024, D_MODEL]),
        "w_proj_m": din("w_proj_m", [L, 1024, D_MODEL]),
        "gate_b_pc": din("gate_b_pc", [L, 128, 16]),
        "w_out": din("w_out", [L, D_MODEL, D_MODEL]),
        "mix_norm_post": din("mix_norm_post", [L, D_MODEL]),
        "ffn2_norm_pre": din("ffn2_norm_pre", [L, D_MODEL]),
        "ffn2_w_gu": din("ffn2_w_gu", [L, D_MODEL, 2 * D_FF]),
        "ffn2_w_down": din("ffn2_w_down", [L, D_FF, D_MODEL]),
        "ffn2_norm_post": din("ffn2_norm_post", [L, D_MODEL]),
    }
    S = make_scratch(nc)
    k = K(nc)
    with ExitStack() as st:
        cbf, cbf_b, cf32, cf32_b = load_consts(k, st)
        ident = cbf[:, 0, :]
        for l in range(L):
            lam_init = 0.8 - 0.6 * math.exp(-0.3 * l)
            phase_ffn(k, ident, cbf_b, x if l == 0 else out, out, P["ffn1_w_gu"][l], P["ffn1_w_down"][l],
                      P["ffn1_norm_pre"][l], P["ffn1_norm_post"][l])
            phase_proj(k, ident, cbf_b, out, P["w_in"][l], P["mix_norm_pre"][l], P["gate_b_pc"][l], S)
            phase_attn(k, cbf, cbf_b, cf32, cf32_b, S, P["lamp"][l], P["anw"][l], lam_init)
            phase_mlstm(k, cbf, cbf_b, cf32, cf32_b, S, P["convw_pc"][l], P["convb_pc"][l], P["igate_b"][l],
                        P["fgate_b"][l], P["mlstm_norm_w"][l])
            phase_out(k, S, out, P["w_proj_a"][l], P["w_proj_m"][l], P["w_out"][l], P["mix_norm_post"][l])
            phase_ffn(k, ident, cbf_b, out, out, P["ffn2_w_gu"][l], P["ffn2_w_down"][l],
                      P["ffn2_norm_pre"][l], P["ffn2_norm_post"][l])
        k.finish()
    return nc


def host_params(inp, depth=DEPTH):
    f = lambda a: np.ascontiguousarray(np.asarray(a, dtype=np.float32))
    L = depth
    d = {n: f(inp[n]) for n in ["ffn1_norm_pre", "ffn1_w_gu", "ffn1_w_down", "ffn1_norm_post", "mix_norm_pre",
                                "w_in", "igate_b", "fgate_b", "w_proj_a", "w_proj_m", "w_out", "mix_norm_post",
                                "ffn2_norm_pre", "ffn2_w_gu", "ffn2_w_down", "ffn2_norm_post"]}
    d["lamp"] = f(np.concatenate([inp["attn_lam_q1"], inp["attn_lam_q2"], inp["attn_lam_k1"], inp["attn_lam_k2"]],
                                 axis=1))
    d["anw"] = f(np.asarray(inp["attn_norm_w"]).reshape(L, 128, 1))
    cw = np.asarray(inp["conv_w"])
    d["convw_pc"] = f(cw.transpose(0, 2, 1).reshape(L, 8, 128, 4).transpose(0, 2, 1, 3))
    d["convb_pc"] = f(np.asarray(inp["conv_b"]).reshape(L, 8, 128).transpose(0, 2, 1))
    d["mlstm_norm_w"] = f(np.asarray(inp["mlstm_norm_w"]).reshape(L, 1024))
    d["gate_b_pc"] = f(np.asarray(inp["gate_b"]).reshape(L, 16, 128).transpose(0, 2, 1))
    d.update(host_consts())
    return d


_NC_CACHE = {}


def kernel(**inputs):
    if "nc" not in _NC_CACHE:
        _NC_CACHE["nc"] = build_program()
    nc = _NC_CACHE["nc"]
    params = host_params(inputs)
    x = np.asarray(inputs["x"], dtype=np.float32)
    in_maps = []
    for b in range(BATCH):
        m = dict(params)
        m["x"] = np.ascontiguousarray(x[b])
        in_maps.append(m)
    res = run_bass_kernel_spmd(nc, in_maps, core_ids=list(range(BATCH)))
    return np.stack([np.asarray(r["out"], dtype=np.float32) for r in res.results], axis=0)
```

```python
import math
from contextlib import ExitStack

import numpy as np
import concourse.bass as bass
import concourse.mybir as mybir
from concourse.bass_utils import run_bass_kernel_spmd

F32 = mybir.dt.float32
BF16 = mybir.dt.bfloat16
AF = mybir.ActivationFunctionType
ALU = mybir.AluOpType
AX = mybir.AxisListType

D_MODEL = 1024
BATCH = 8
SEQ = 4096
DEPTH = 2
EPS = 1e-6
A_HEADS = 8
A_DHEAD = 64
M_HEADS = 8
M_DQK = 64
M_DV = 128
D_FF = 2816
C_IN = 8208
NT = SEQ // 128

O_AQ, O_AK, O_AV = 0, 1024, 2048
O_MQ, O_MK, O_MV, O_MO = 3072, 3584, 4096, 5120
O_MI, O_MF, O_GL = 6144, 6152, 6160


class Sem:
    def __init__(self, nc, name):
        self.h = nc.alloc_semaphore(name)
        self.v = 0
        self.name = name


class Buf:
    __slots__ = ("w", "r", "name")

    def __init__(self, name=""):
        self.w = {}
        self.r = {}
        self.name = name


class Eng:
    def __init__(self, k, name, e, n_dma_sems=0):
        self.k = k
        self.name = name
        self.e = e
        self.sem = Sem(k.nc, "s_" + name)
        self.waited = {}
        self.dma_sems = [Sem(k.nc, f"d_{name}{i}") for i in range(n_dma_sems)]
        self.dma_rr = 0

    def wait(self, tok):
        sem, v = tok
        if self.waited.get(sem, 0) >= v:
            return
        self.e.wait_ge(sem.h, v)
        self.waited[sem] = v


class K:
    def __init__(self, nc):
        self.nc = nc
        self.pe = Eng(self, "pe", nc.tensor)
        self.act = Eng(self, "act", nc.scalar)
        self.dve = Eng(self, "dve", nc.vector)
        self.pool = Eng(self, "pool", nc.gpsimd, n_dma_sems=4)
        self.sp = Eng(self, "sp", nc.sync, n_dma_sems=20)
        self.engs = [self.pe, self.act, self.dve, self.pool, self.sp]
        self.const_t = nc.alloc_sbuf_tensor("const_cols", [128, 32], F32)
        self.const_b = Buf("consts")
        self.consts = {}

    def const(self, val):
        val = float(val)
        if val not in self.consts:
            i = len(self.consts)
            assert i < self.const_t.shape[1]
            ap = self.const_t[:, i:i + 1]
            self.op(self.pool, lambda: self.nc.gpsimd.memset(ap, val), writes=[self.const_b])
            self.consts[val] = ap
        return self.consts[val]

    def _deps(self, eng, reads, writes, join=False):
        for b in reads:
            for s, v in b.w.items():
                eng.wait((s, v))
        for b in writes:
            if not join:
                for s, v in b.w.items():
                    eng.wait((s, v))
            for s, v in b.r.items():
                eng.wait((s, v))

    def _mark(self, tok, reads, writes, join=False):
        s, v = tok
        for b in reads:
            if b.r.get(s, 0) < v:
                b.r[s] = v
        for b in writes:
            if join:
                b.w[s] = v
            else:
                b.w = {s: v}
            b.r = {}

    def op(self, eng, fn, reads=(), writes=()):
        self._deps(eng, reads, writes)
        ins = fn()
        eng.sem.v += 1
        ins.then_inc(eng.sem.h, 1)
        self._mark((eng.sem, eng.sem.v), reads, writes)

    def mm(self, fns, reads=(), writes=()):
        eng = self.pe
        self._deps(eng, reads, writes)
        ins = None
        for fn in fns:
            ins = fn()
        eng.sem.v += 1
        ins.then_inc(eng.sem.h, 1)
        self._mark((eng.sem, eng.sem.v), reads, writes)

    def dma(self, q, out, in_, reads=(), writes=(), join=False):
        sem = q.dma_sems[q.dma_rr]
        q.dma_rr = (q.dma_rr + 1) % len(q.dma_sems)
        if sem.v:
            q.wait((sem, sem.v))
        self._deps(q, reads, writes, join)
        ins = q.e.dma_start(out=out, in_=in_)
        sem.v += 16
        ins.then_inc(sem.h, 16)
        self._mark((sem, sem.v), reads, writes, join)

    def barrier(self):
        sems = []
        for e in self.engs:
            sems.append(e.sem)
            sems.extend(e.dma_sems)
        for e in self.engs:
            for s in sems:
                if s.v:
                    e.wait((s, s.v))

    def finish(self):
        self.barrier()


_UID = [0]


def uname(name):
    _UID[0] += 1
    return f"{name}_{_UID[0]}"


class Ring:
    def __init__(self, st, nc, name, n, shape, dtype, psum=False):
        self.n = n
        self.i = 0
        self.t = []
        self.b = []
        for j in range(n):
            if psum:
                t = st.enter_context(nc.psum_tensor(uname(name), shape, dtype))
            else:
                t = st.enter_context(nc.sbuf_tensor(uname(name), shape, dtype))
            self.t.append(t)
            self.b.append(Buf(f"{name}{j}"))

    def next(self):
        j = self.i
        self.i = (self.i + 1) % self.n
        return self.t[j], self.b[j]


def sb(st, nc, name, shape, dtype):
    return st.enter_context(nc.sbuf_tensor(uname(name), shape, dtype))


def ps(st, nc, name, shape, dtype):
    return st.enter_context(nc.psum_tensor(uname(name), shape, dtype))


def load_bcast(k, st, name, vec_ap, n):
    nc = k.nc
    t = sb(st, nc, name, [128, n], F32)
    b = Buf(name)
    k.dma(k.sp, t[:], vec_ap.partition_broadcast(128), writes=[b])
    return t, b


def rms_rstd(k, ss, ss_b, rstd, rstd_b, n, eps=EPS, mul=1.0):
    nc = k.nc
    c_eps = k.const(eps)
    c_mul = k.const(math.log(mul))
    k.op(k.act, lambda: nc.scalar.activation(out=rstd, in_=ss, func=AF.Ln, scale=1.0 / n, bias=c_eps),
         reads=[ss_b, k.const_b], writes=[rstd_b])
    k.op(k.act, lambda: nc.scalar.activation(out=rstd, in_=rstd, func=AF.Exp, scale=-0.5, bias=c_mul),
         reads=[rstd_b, k.const_b], writes=[rstd_b])


def norm_h(k, rings, x_src, tok0, npre, npre_b):
    nc = k.nc
    xs, hb, stat = rings
    x_t, x_b = xs.next()
    k.dma(k.sp, x_t[:], x_src[tok0:tok0 + 128, :], writes=[x_b])
    h_t, h_b = hb.next()
    s_t, s_b = stat.next()
    k.op(k.act, lambda: nc.scalar.activation(out=h_t[:], in_=x_t[:], func=AF.Square, accum_out=s_t[:, 0:1]),
         reads=[x_b], writes=[h_b, s_b])
    rms_rstd(k, s_t[:, 0:1], s_b, s_t[:, 1:2], s_b, D_MODEL)
    k.op(k.dve, lambda: nc.vector.scalar_tensor_tensor(out=h_t[:], in0=x_t[:], scalar=s_t[:, 1:2],
                                                       in1=npre[:], op0=ALU.mult, op1=ALU.mult),
         reads=[x_b, s_b, npre_b], writes=[h_b])
    return h_t, h_b


def transp_h(k, ptr, ident, ident_b, h_t, h_b, hT_t, hT_b, s):
    nc = k.nc
    p_t, p_b = ptr.next()
    k.mm([(lambda c=c: nc.tensor.transpose(p_t[:, c, :], h_t[:, c * 128:(c + 1) * 128], ident[:]))
          for c in range(8)], reads=[h_b, ident_b], writes=[p_b])
    k.op(k.act, lambda: nc.scalar.copy(out=hT_t[:, :, s * 128:(s + 1) * 128], in_=p_t[:]),
         reads=[p_b], writes=[hT_b])


def phase_ffn(k, ident, ident_b, x_in, x_out, w_gu, w_down, n_pre, n_post):
    nc = k.nc
    TT = 512
    NTT = SEQ // TT
    NS = TT // 128
    NJ = D_FF // 128
    with ExitStack() as st:
        wgu = sb(st, nc, "wgu", [128, 8, 2 * D_FF], BF16)
        wdn = sb(st, nc, "wdn", [128, NJ, D_MODEL], BF16)
        JB = [(0, 4), (4, 8), (8, 12), (12, 16), (16, 20), (20, 22)]
        wgu_b = {}
        for bi, (j0, j1) in enumerate(JB):
            for gu in range(2):
                c0 = gu * D_FF + j0 * 128
                c1 = gu * D_FF + j1 * 128
                b = Buf()
                for j in range(j0, j1):
                    wgu_b[(gu, j)] = b
                for c in range(8):
                    k.dma(k.pool, wgu[:, c, c0:c1], w_gu[c * 128:(c + 1) * 128, c0:c1], writes=[b], join=True)
        wdn_b = []
        for hh in range(2):
            b = Buf()
            wdn_b.append(b)
            k.dma(k.pool, wdn[:, hh * 11:(hh + 1) * 11, :],
                  w_down[hh * 1408:(hh + 1) * 1408, :].rearrange("(j p) n -> p j n", p=128), writes=[b])
        npre, npre_b = load_bcast(k, st, "npre", n_pre, D_MODEL)
        npost, npost_b = load_bcast(k, st, "npost", n_post, D_MODEL)

        xs = Ring(st, nc, "xs", 2, [128, D_MODEL], F32)
        hb = Ring(st, nc, "hb", 4, [128, D_MODEL], BF16)
        stat = Ring(st, nc, "stat", 8, [128, 4], F32)
        hT = Ring(st, nc, "hT", 1, [128, 8, TT], BF16)
        actT = Ring(st, nc, "actT", 1, [128, NJ, TT], BF16)
        sg = Ring(st, nc, "sg", 2, [128, TT], F32)
        xr = Ring(st, nc, "xr", 2, [128, D_MODEL], F32)
        yo = Ring(st, nc, "yo", 1, [128, D_MODEL], F32)
        ptr = Ring(st, nc, "ptr", 1, [128, 8, 128], BF16, psum=True)
        pg = Ring(st, nc, "pg", 2, [128, TT], F32, psum=True)
        pu = Ring(st, nc, "pu", 2, [128, TT], F32, psum=True)
        py = Ring(st, nc, "py", 1, [128, 2, 512], F32, psum=True)
        nrings = (xs, hb, stat)

        hT_t, hT_b = hT.next()
        hs = [norm_h(k, nrings, x_in, s * 128, npre, npre_b) for s in range(NS)]
        for s in range(NS):
            transp_h(k, ptr, ident, ident_b, hs[s][0], hs[s][1], hT_t, hT_b, s)
        for t in range(NTT):
            if t + 1 < NTT:
                hs = [norm_h(k, nrings, x_in, ((t + 1) * NS + s) * 128, npre, npre_b) for s in range(NS)]
            a_t, a_b = actT.next()
            for j in range(NJ):
                g_t, g_b = pg.next()
                u_t, u_b = pu.next()
                k.mm([(lambda c=c: nc.tensor.matmul(g_t[:], wgu[:, c, j * 128:(j + 1) * 128], hT_t[:, c, :],
                                                    start=(c == 0), stop=(c == 7))) for c in range(8)],
                     reads=[wgu_b[(0, j)], hT_b], writes=[g_b])
                k.mm([(lambda c=c: nc.tensor.matmul(u_t[:], wgu[:, c, D_FF + j * 128:D_FF + (j + 1) * 128],
                                                    hT_t[:, c, :], start=(c == 0), stop=(c == 7)))
                      for c in range(8)],
                     reads=[wgu_b[(1, j)], hT_b], writes=[u_b])
                sg_t, sg_b = sg.next()
                k.op(k.act, lambda: nc.scalar.activation(out=sg_t[:], in_=g_t[:], func=AF.Silu),
                     reads=[g_b], writes=[sg_b])
                k.op(k.dve, lambda: nc.vector.tensor_tensor(out=a_t[:, j, :], in0=sg_t[:], in1=u_t[:],
                                                            op=ALU.mult),
                     reads=[sg_b, u_b], writes=[a_b])
            if t + 1 < NTT:
                hT_t, hT_b = hT.next()
                for s in range(NS):
                    transp_h(k, ptr, ident, ident_b, hs[s][0], hs[s][1], hT_t, hT_b, s)
            for s in range(NS):
                tok0 = (t * NS + s) * 128
                y_t, y_b = py.next()
                fns = []
                for hf in range(2):
                    for j in range(NJ):
                        fns.append(lambda j=j, hf=hf: nc.tensor.matmul(
                            y_t[:, hf, :], a_t[:, j, s * 128:(s + 1) * 128], wdn[:, j, hf * 512:(hf + 1) * 512],
                            start=(j == 0), stop=(j == NJ - 1)))
                k.mm(fns, reads=[a_b] + wdn_b, writes=[y_b])
                xr_t, xr_b = xr.next()
                k.dma(k.sp, xr_t[:], x_in[tok0:tok0 + 128, :], writes=[xr_b])
                o_t, o_b = yo.next()
                s_t, s_b = stat.next()
                for hf in range(2):
                    k.op(k.act, lambda: nc.scalar.activation(out=o_t[:, hf * 512:(hf + 1) * 512], in_=y_t[:, hf, :],
                                                             func=AF.Square, accum_out=s_t[:, 2 + hf:3 + hf]),
                         reads=[y_b], writes=[o_b, s_b])
                k.op(k.dve, lambda: nc.vector.tensor_tensor(out=s_t[:, 0:1], in0=s_t[:, 2:3], in1=s_t[:, 3:4],
                                                            op=ALU.add), reads=[s_b], writes=[s_b])
                rms_rstd(k, s_t[:, 0:1], s_b, s_t[:, 1:2], s_b, D_MODEL, mul=0.5)
                for hf in range(2):
                    k.op(k.dve, lambda: nc.vector.tensor_tensor(
                        out=o_t[:, hf * 512:(hf + 1) * 512], in0=y_t[:, hf, :], in1=npost[:, hf * 512:(hf + 1) * 512],
                        op=ALU.mult), reads=[y_b, npost_b], writes=[o_b])
                k.op(k.dve, lambda: nc.vector.scalar_tensor_tensor(
                    out=xr_t[:], in0=o_t[:], scalar=s_t[:, 1:2], in1=xr_t[:], op0=ALU.mult, op1=ALU.add),
                    reads=[o_b, xr_b, s_b], writes=[xr_b])
                k.dma(k.sp, x_out[tok0:tok0 + 128, :], xr_t[:], reads=[xr_b])
        k.barrier()


def phase_proj(k, ident, ident_b, x_in, w_in, n_pre, gate_b_pc, S):
    nc = k.nc
    TT = 512
    NTT = SEQ // TT
    NS = TT // 128
    with ExitStack() as st:
        win = sb(st, nc, "win", [128, 8, C_IN], BF16)
        blocks = [(0, 512, 0), (512, 1024, 0), (1024, 2048, 0), (2048, 3072, 1), (3072, 4096, 1),
                  (4096, 5120, 2), (5120, 6160, 2), (6160, 7184, 3), (7184, 8208, 3)]
        wb = [Buf() for _ in range(4)]
        for (c0, c1, bi) in blocks:
            for c in range(8):
                k.dma(k.pool, win[:, c, c0:c1], w_in[c * 128:(c + 1) * 128, c0:c1], writes=[wb[bi]], join=True)
        npre, npre_b = load_bcast(k, st, "npre", n_pre, D_MODEL)
        gb = sb(st, nc, "gb", [128, 16], F32)
        gb_b = Buf()
        k.dma(k.sp, gb[:], gate_b_pc, writes=[gb_b])
        xs = Ring(st, nc, "xs", 2, [128, D_MODEL], F32)
        hb = Ring(st, nc, "hb", 4, [128, D_MODEL], BF16)
        stat = Ring(st, nc, "stat", 8, [128, 4], F32)
        ptr = Ring(st, nc, "ptr", 1, [128, 8, 128], BF16, psum=True)
        nrings = (xs, hb, stat)
        hT = Ring(st, nc, "hT", 1, [128, 8, TT], BF16)
        of32 = Ring(st, nc, "of32", 4, [128, 512], F32)
        obf = Ring(st, nc, "obf", 4, [128, 512], BF16)
        pf = Ring(st, nc, "pf", 3, [128, 512], F32, psum=True)
        pt = Ring(st, nc, "pt", 3, [128, 512], F32, psum=True)
        ev = [0]

        def evac_copy(out_ap, in_ap, rd, wr):
            ev[0] ^= 1
            if ev[0]:
                k.op(k.dve, lambda: nc.vector.tensor_copy(out=out_ap, in_=in_ap), reads=rd, writes=wr)
            else:
                k.op(k.act, lambda: nc.scalar.copy(out=out_ap, in_=in_ap), reads=rd, writes=wr)

        hT_t, hT_b = hT.next()
        hs = [norm_h(k, nrings, x_in, s * 128, npre, npre_b) for s in range(NS)]
        for s in range(NS):
            transp_h(k, ptr, ident, ident_b, hs[s][0], hs[s][1], hT_t, hT_b, s)
        for t in range(NTT):
            T0 = t * TT
            if t + 1 < NTT:
                hs = [norm_h(k, nrings, x_in, (t + 1) * TT + s * 128, npre, npre_b) for s in range(NS)]

            def fm_chunk(col0, wbuf):
                p_t, p_b = pf.next()
                k.mm([(lambda c=c: nc.tensor.matmul(p_t[:], win[:, c, col0:col0 + 128], hT_t[:, c, :],
                                                    start=(c == 0), stop=(c == 7))) for c in range(8)],
                     reads=[wbuf, hT_b], writes=[p_b])
                return p_t, p_b

            def tm_block(s, col0, n, wbuf):
                p_t, p_b = pt.next()
                k.mm([(lambda c=c: nc.tensor.matmul(p_t[:, 0:n], hT_t[:, c, s * 128:(s + 1) * 128],
                                                    win[:, c, col0:col0 + n], start=(c == 0), stop=(c == 7)))
                      for c in range(8)], reads=[wbuf, hT_b], writes=[p_b])
                return p_t, p_b

            for c in range(16):
                p_t, p_b = fm_chunk(c * 128, wb[0])
                o_t, o_b = obf.next()
                evac_copy(o_t[:], p_t[:], [p_b], [o_b])
                k.dma(k.sp, S["qkT"][c * 128:(c + 1) * 128, T0:T0 + TT], o_t[:], reads=[o_b])
            for c in range(8):
                p_t, p_b = fm_chunk(O_MQ + c * 128, wb[1])
                o_t, o_b = of32.next()
                evac_copy(o_t[:], p_t[:], [p_b], [o_b])
                k.dma(k.sp, S["mqkT"][c * 128:(c + 1) * 128, T0:T0 + TT], o_t[:], reads=[o_b])
            for s in range(NS):
                tok0 = T0 + s * 128
                for hf in range(2):
                    p_t, p_b = tm_block(s, O_AV + hf * 512, 512, wb[1])
                    o_t, o_b = obf.next()
                    evac_copy(o_t[:], p_t[:], [p_b], [o_b])
                    k.dma(k.sp, S["av"][tok0:tok0 + 128, hf * 512:(hf + 1) * 512], o_t[:], reads=[o_b])
                for hf in range(2):
                    p_t, p_b = tm_block(s, O_MV + hf * 512, 512, wb[2])
                    o_t, o_b = of32.next()
                    evac_copy(o_t[:], p_t[:], [p_b], [o_b])
                    k.dma(k.sp, S["mv"][tok0:tok0 + 128, hf * 512:(hf + 1) * 512], o_t[:], reads=[o_b])
                for hf in range(2):
                    p_t, p_b = tm_block(s, O_MO + hf * 512, 512, wb[2])
                    o_t, o_b = of32.next()
                    k.op(k.act, lambda: nc.scalar.activation(out=o_t[:], in_=p_t[:], func=AF.Sigmoid),
                         reads=[p_b], writes=[o_b])
                    k.dma(k.sp, S["mo"][tok0:tok0 + 128, hf * 512:(hf + 1) * 512], o_t[:], reads=[o_b])
                p_t, p_b = tm_block(s, O_MI, 16, wb[2])
                o_t, o_b = of32.next()
                evac_copy(o_t[:, 0:16], p_t[:, 0:16], [p_b], [o_b])
                k.dma(k.sp, S["mif"][tok0:tok0 + 128, :], o_t[:, 0:16], reads=[o_b])
            for c in range(16):
                p_t, p_b = fm_chunk(O_GL + c * 128, wb[3])
                o_t, o_b = of32.next()
                k.op(k.act, lambda: nc.scalar.activation(out=o_t[:], in_=p_t[:], func=AF.Sigmoid,
                                                         bias=gb[:, c:c + 1]),
                     reads=[p_b, gb_b], writes=[o_b])
                k.dma(k.sp, S["glT"][c * 128:(c + 1) * 128, T0:T0 + TT], o_t[:], reads=[o_b])
            if t + 1 < NTT:
                hT_t, hT_b = hT.next()
                for s in range(NS):
                    transp_h(k, ptr, ident, ident_b, hs[s][0], hs[s][1], hT_t, hT_b, s)
        k.barrier()


def phase_attn(k, cbf, cbf_b, cf32, cf32_b, S, lamp, anw_col, lam_init):
    nc = k.nc
    NG = SEQ // 512
    with ExitStack() as st:
        lt, lt_b = load_bcast(k, st, "lamp", lamp, 256)
        lw = sb(st, nc, "lamw", [128, 136], F32)
        lw_b = Buf()
        k.op(k.dve, lambda: nc.vector.tensor_tensor(out=lw[:, 0:128], in0=lt[:, 0:128], in1=lt[:, 128:256],
                                                    op=ALU.mult), reads=[lt_b], writes=[lw_b])
        k.op(k.dve, lambda: nc.vector.tensor_reduce(out=lw[:, 128:130],
                                                    in_=lw[:, 0:128].rearrange("p (a b) -> p a b", a=2),
                                                    axis=AX.X, op=ALU.add), reads=[lw_b], writes=[lw_b])
        k.op(k.act, lambda: nc.scalar.activation(out=lw[:, 130:132], in_=lw[:, 128:130], func=AF.Exp),
             reads=[lw_b], writes=[lw_b])
        k.op(k.dve, lambda: nc.vector.tensor_tensor(out=lw[:, 132:133], in0=lw[:, 130:131], in1=lw[:, 131:132],
                                                    op=ALU.subtract), reads=[lw_b], writes=[lw_b])
        k.op(k.dve, lambda: nc.vector.tensor_scalar(out=lw[:, 133:134], in0=lw[:, 132:133], scalar1=lam_init,
                                                    scalar2=-1.0, op0=ALU.add, op1=ALU.mult),
             reads=[lw_b], writes=[lw_b])
        neg_lam = lw[:, 133:134]
        nw = sb(st, nc, "anw", [128, 1], F32)
        nw_b = Buf()
        k.dma(k.sp, nw[:], anw_col, writes=[nw_b])
        k.op(k.dve, lambda: nc.vector.tensor_scalar(out=nw[:], in0=nw[:], scalar1=1.0 - lam_init, scalar2=None,
                                                    op0=ALU.mult), reads=[nw_b], writes=[nw_b])

        qT = Ring(st, nc, "qT", 2, [128, SEQ], BF16)
        kT = Ring(st, nc, "kT", 2, [128, SEQ], BF16)
        vh = Ring(st, nc, "vh", 2, [128, NT_(), 128], BF16)
        pT = Ring(st, nc, "pT", 6, [128, 512], BF16)
        f32r = Ring(st, nc, "af", 10, [128, 512], F32)
        yo = Ring(st, nc, "yao", 2, [128, 512], BF16)
        psr = Ring(st, nc, "psS", 4, [128, 512], F32, psum=True)
        po = [ps(st, nc, "po", [128, 512], F32) for _ in range(2)]
        pl = [ps(st, nc, "pl", [128, 512], F32) for _ in range(2)]
        po_b = [Buf(), Buf()]
        pl_b = [Buf(), Buf()]
        ones_bf = cbf[:, 1, :]
        tri = cbf[:, 2, :]

        for h in range(A_HEADS):
            q_t, q_b = qT.next()
            k_t, k_b = kT.next()
            v_t, v_b = vh.next()
            k.dma(k.sp, q_t[:], S["qkT"][h * 128:(h + 1) * 128, :], writes=[q_b])
            k.dma(k.sp, k_t[:], S["qkT"][1024 + h * 128:1024 + (h + 1) * 128, :], writes=[k_b])
            k.dma(k.sp, v_t[:], S["av"][:, h * 128:(h + 1) * 128].rearrange("(t p) d -> p t d", p=128),
                  writes=[v_b])
            for g in range(NG):
                nkt = 4 * g + 4
                pend = {}

                def stage_a(kt):
                    j = kt - 4 * g
                    c0 = 128 * j if j > 0 else 0
                    outs = []
                    sts = [psr.next() for _ in range(2)]
                    k.mm([(lambda m=m: nc.tensor.matmul(
                        sts[m][0][:, c0:512], k_t[m * 64:(m + 1) * 64, kt * 128:(kt + 1) * 128],
                        q_t[m * 64:(m + 1) * 64, g * 512 + c0:(g + 1) * 512], start=True, stop=True))
                        for m in range(2)], reads=[q_b, k_b], writes=[sts[0][1], sts[1][1]])
                    for m in range(2):
                        s_t, s_b = sts[m]
                        p_t, p_b = pT.next()
                        k.op(k.act, lambda: nc.scalar.activation(out=p_t[:, c0:512], in_=s_t[:, c0:512],
                                                                 func=AF.Exp, scale=A_DHEAD ** -0.5),
                             reads=[s_b], writes=[p_b])
                        if j >= 0:
                            k.op(k.pool, lambda: nc.gpsimd.tensor_tensor(out=p_t[:, c0:c0 + 128],
                                                                         in0=p_t[:, c0:c0 + 128], in1=tri,
                                                                         op=ALU.mult),
                                 reads=[p_b, cbf_b], writes=[p_b])
                        outs.append((p_t, p_b, c0))
                    pend[kt] = outs

                def stage_b(kt):
                    for m in range(2):
                        p_t, p_b, c0 = pend[kt][m]
                        k.mm([lambda: nc.tensor.matmul(po[m][:, c0:512], v_t[:, kt, :], p_t[:, c0:512],
                                                       start=(kt == 0), stop=(kt == nkt - 1)),
                              lambda: nc.tensor.matmul(pl[m][:, c0:512], ones_bf, p_t[:, c0:512],
                                                       start=(kt == 0), stop=(kt == nkt - 1))],
                             reads=[p_b, v_b, cbf_b], writes=[po_b[m], pl_b[m]])
                    del pend[kt]

                stage_a(0)
                for kt in range(1, nkt):
                    stage_a(kt)
                    stage_b(kt - 1)
                stage_b(nkt - 1)

                oc, lc = [], []
                for m in range(2):
                    o_t, o_b = f32r.next()
                    k.op(k.dve, lambda: nc.vector.tensor_copy(out=o_t[:], in_=po[m][:]), reads=[po_b[m]], writes=[o_b])
                    oc.append((o_t, o_b))
                    l_t, l_b = f32r.next()
                    k.op(k.act, lambda: nc.scalar.copy(out=l_t[:], in_=pl[m][:]), reads=[pl_b[m]], writes=[l_b])
                    lc.append((l_t, l_b))
                for m in range(2):
                    l_t, l_b = lc[m]
                    o_t, o_b = oc[m]
                    k.op(k.dve, lambda: nc.vector.reciprocal(out=l_t[:], in_=l_t[:]), reads=[l_b], writes=[l_b])
                    k.op(k.dve, lambda: nc.vector.tensor_tensor(out=o_t[:], in0=o_t[:], in1=l_t[:], op=ALU.mult),
                         reads=[o_b, l_b], writes=[o_b])
                d_t, d_b = oc[0]
                k.op(k.dve, lambda: nc.vector.scalar_tensor_tensor(out=d_t[:], in0=oc[1][0][:], scalar=neg_lam,
                                                                   in1=d_t[:], op0=ALU.mult, op1=ALU.add),
                     reads=[oc[1][1], d_b, lw_b], writes=[d_b])
                sq_t, sq_b = f32r.next()
                k.op(k.act, lambda: nc.scalar.activation(out=sq_t[:], in_=d_t[:], func=AF.Square),
                     reads=[d_b], writes=[sq_b])
                s_t, s_b = psr.next()
                k.mm([lambda: nc.tensor.matmul(s_t[:], cf32[:, 0, :], sq_t[:], start=True, stop=True)],
                     reads=[sq_b, cf32_b], writes=[s_b])
                rs_t, rs_b = f32r.next()
                rms_rstd(k, s_t[:], s_b, rs_t[:], rs_b, 128)
                y_t, y_b = yo.next()
                k.op(k.dve, lambda: nc.vector.scalar_tensor_tensor(out=y_t[:], in0=d_t[:], scalar=nw[:, 0:1],
                                                                   in1=rs_t[:], op0=ALU.mult, op1=ALU.mult),
                     reads=[d_b, nw_b, rs_b], writes=[y_b])
                k.dma(k.sp, S["yaT"][h * 128:(h + 1) * 128, g * 512:(g + 1) * 512], y_t[:], reads=[y_b])
        k.barrier()


def NT_():
    return SEQ // 128


def make_scratch(nc, kind="Internal"):
    def d(name, shape, dt):
        return nc.dram_tensor(name, shape, dt, kind=kind).ap()
    return {
        "qkT": d("s_qkT", [2048, SEQ], BF16),
        "av": d("s_av", [SEQ, 1024], BF16),
        "mqkT": d("s_mqkT", [1024, SEQ], F32),
        "mv": d("s_mv", [SEQ, 1024], F32),
        "mo": d("s_mo", [SEQ, 1024], F32),
        "mif": d("s_mif", [SEQ, 16], F32),
        "glT": d("s_glT", [2048, SEQ], F32),
        "yaT": d("s_yaT", [1024, SEQ], BF16),
        "ymT": d("s_ymT", [1024, SEQ], BF16),
    }


def host_consts():
    import ml_dtypes
    p = np.arange(128)[:, None]
    f = np.arange(128)[None, :]
    tri = (p <= f).astype(np.float32)
    cbf = np.stack([np.eye(128, dtype=np.float32), np.ones((128, 128), np.float32), tri], axis=1)
    cf32 = np.stack([np.ones((128, 128), np.float32), tri, tri * (M_DQK ** -0.5)], axis=1)
    return {"cbf": np.ascontiguousarray(cbf).astype(ml_dtypes.bfloat16),
            "cf32": np.ascontiguousarray(cf32).astype(np.float32)}


def load_consts(k, st):
    nc = k.nc
    cbf_d = nc.dram_tensor("cbf", [128, 3, 128], BF16, kind="ExternalInput").ap()
    cf32_d = nc.dram_tensor("cf32", [128, 3, 128], F32, kind="ExternalInput").ap()
    cbf = sb(st, nc, "cbf_sb", [128, 3, 128], BF16)
    cf32 = sb(st, nc, "cf32_sb", [128, 3, 128], F32)
    cbf_b, cf32_b = Buf(), Buf()
    k.dma(k.sp, cbf[:], cbf_d, writes=[cbf_b])
    k.dma(k.sp, cf32[:], cf32_d, writes=[cf32_b])
    return cbf, cbf_b, cf32, cf32_b


def phase_mlstm(k, cbf, cbf_b, cf32, cf32_b, S, convw_pc, convb_pc, igb, fgb, mnw):
    nc = k.nc
    NTL = SEQ // 128
    ident = cbf[:, 0, :]
    ones32 = cf32[:, 0, :]
    triu32 = cf32[:, 1, :]
    mask8 = cf32[:, 2, :]
    with ExitStack() as st:
        qkc = sb(st, nc, "qkc", [128, 8, SEQ], BF16)
        qkc_b = [Buf() for _ in range(8)]
        cw = sb(st, nc, "cw", [128, 8, 4], F32)
        cb = sb(st, nc, "cb", [128, 8], F32)
        cw_b, cb_b = Buf(), Buf()
        k.dma(k.sp, cw[:], convw_pc, writes=[cw_b])
        k.dma(k.sp, cb[:], convb_pc, writes=[cb_b])
        with ExitStack() as st1:
            xp = Ring(st1, nc, "xp", 2, [128, SEQ + 3], F32)
            acc = Ring(st1, nc, "cacc", 1, [128, SEQ], F32)
            for i in range(2):
                k.op(k.pool, lambda: nc.gpsimd.memset(xp.t[i][:, 0:3], 0.0), writes=[xp.b[i]])
            for c in range(8):
                x_t, x_b = xp.next()
                k.dma(k.sp, x_t[:, 3:SEQ + 3], S["mqkT"][c * 128:(c + 1) * 128, :], writes=[x_b])
                a_t, a_b = acc.next()
                k.op(k.dve, lambda: nc.vector.tensor_scalar(out=a_t[:], in0=x_t[:, 0:SEQ], scalar1=cw[:, c, 0:1],
                                                            scalar2=None, op0=ALU.mult),
                     reads=[x_b, cw_b], writes=[a_b])
                for j in range(1, 4):
                    k.op(k.dve, lambda: nc.vector.scalar_tensor_tensor(
                        out=a_t[:], in0=x_t[:, j:j + SEQ], scalar=cw[:, c, j:j + 1], in1=a_t[:],
                        op0=ALU.mult, op1=ALU.add), reads=[x_b, cw_b, a_b], writes=[a_b])
                k.op(k.act, lambda: nc.scalar.activation(out=qkc[:, c, :], in_=a_t[:], func=AF.Silu,
                                                         bias=cb[:, c:c + 1]),
                     reads=[a_b, cb_b], writes=[qkc_b[c]])
        k.barrier()

        ktok = sb(st, nc, "ktok", [128, NTL, 512], BF16)
        ktok_b = Buf()
        NC8 = NTL * 8
        gt = sb(st, nc, "gt", [128, NTL, 16], F32)
        gt_b = Buf()
        k.dma(k.sp, gt[:], S["mif"].rearrange("(t p) c -> p t c", p=128), writes=[gt_b])
        igb_t, igb_b = load_bcast(k, st, "igb", igb, 8)
        fgb_t, fgb_b = load_bcast(k, st, "fgb", fgb, 8)
        mnw_t, mnw_b = load_bcast(k, st, "mnw", mnw, 1024)
        IG = sb(st, nc, "IG", [128, NTL, 8], F32)
        NLF = sb(st, nc, "NLF", [128, NTL, 8], F32)
        NGc = sb(st, nc, "NGc", [128, NTL, 8], F32)
        EQ = sb(st, nc, "EQ", [128, NTL, 8], F32)
        EK = sb(st, nc, "EK", [128, NTL, 8], F32)
        EE = sb(st, nc, "EE", [128, NTL, 8], F32)
        IG_b, NLF_b, NG_b, EQ_b, EK_b, EE_b = [Buf() for _ in range(6)]

        pa = Ring(st, nc, "pa", 2, [128, 4, 128], F32, psum=True)
        pr = Ring(st, nc, "pr", 2, [128, 4, 128], F32, psum=True)
        pu = Ring(st, nc, "pu", 2, [128, 4, 128], F32, psum=True)
        pm = ps(st, nc, "pm", [128, 512], F32)
        prd, prd_b = pm[:, 0:8], Buf()
        pun, pun_b = pm[:, 8:16], Buf()
        ptr = ps(st, nc, "ptrm", [128, 8, 128], BF16)
        ptr_b = Buf()

        for t in range(NTL):
            k.mm([(lambda c=c: nc.tensor.transpose(ptr[:, c, :], qkc[:, 4 + c, t * 128:(t + 1) * 128], ident))
                  for c in range(4)], reads=qkc_b[4:8] + [cbf_b], writes=[ptr_b])
            k.op(k.act, lambda: nc.scalar.copy(out=ktok[:, t, :], in_=ptr[:, 0:4, :].rearrange("p a b -> p (a b)")),
                 reads=[ptr_b], writes=[ktok_b])

        k.op(k.dve, lambda: nc.vector.tensor_tensor(out=IG[:], in0=gt[:, :, 0:8],
                                                    in1=igb_t[:].unsqueeze(1).to_broadcast([128, NTL, 8]),
                                                    op=ALU.add), reads=[gt_b, igb_b], writes=[IG_b])
        k.op(k.dve, lambda: nc.vector.tensor_tensor(out=NLF[:], in0=gt[:, :, 8:16],
                                                    in1=fgb_t[:].unsqueeze(1).to_broadcast([128, NTL, 8]),
                                                    op=ALU.add), reads=[gt_b, fgb_b], writes=[NLF_b])
        k.op(k.act, lambda: nc.scalar.activation(out=NLF[:], in_=NLF[:], func=AF.Exp, scale=-1.0),
             reads=[NLF_b], writes=[NLF_b])
        c_one = k.const(1.0)
        k.op(k.act, lambda: nc.scalar.activation(out=NLF[:], in_=NLF[:], func=AF.Ln, bias=c_one),
             reads=[NLF_b, k.const_b], writes=[NLF_b])
        nlf2 = NLF[:].rearrange("p a b -> p (a b)")
        pg_t, pg_b = pa.next()
        pg2 = pg_t[:].rearrange("p a b -> p (a b)")
        k.mm([lambda: nc.tensor.matmul(pg2[:, 0:NC8], triu32, nlf2, start=True, stop=True)],
             reads=[NLF_b, cf32_b], writes=[pg_b])
        k.op(k.dve, lambda: nc.vector.tensor_copy(out=NGc[:].rearrange("p a b -> p (a b)"), in_=pg2[:, 0:NC8]),
             reads=[pg_b], writes=[NG_b])
        pe_t, pe_b = pa.next()
        pe2 = pe_t[:].rearrange("p a b -> p (a b)")
        k.mm([lambda: nc.tensor.matmul(pe2[:, 0:NC8], ones32, nlf2, start=True, stop=True)],
             reads=[NLF_b, cf32_b], writes=[pe_b])
        k.op(k.act, lambda: nc.scalar.activation(out=EE[:].rearrange("p a b -> p (a b)"), in_=pe2[:, 0:NC8],
                                                 func=AF.Exp, scale=-1.0), reads=[pe_b], writes=[EE_b])
        k.op(k.act, lambda: nc.scalar.activation(out=EQ[:], in_=NGc[:], func=AF.Exp, scale=-1.0),
             reads=[NG_b], writes=[EQ_b])
        k.op(k.dve, lambda: nc.vector.tensor_tensor(out=EK[:], in0=IG[:], in1=NGc[:], op=ALU.add),
             reads=[IG_b, NG_b], writes=[EK_b])
        k.op(k.act, lambda: nc.scalar.activation(out=EK[:], in_=EK[:], func=AF.Exp), reads=[EK_b], writes=[EK_b])

        mvr = Ring(st, nc, "mvr", 2, [128, 8, 128], F32)
        mor = Ring(st, nc, "mor", 2, [128, 1024], F32)
        vtr = Ring(st, nc, "vtr", 2, [128, 8, 128], BF16)
        ekr = Ring(st, nc, "ekr", 2, [128, 8], BF16)
        ptT = Ring(st, nc, "ptT", 2, [128, 8, 128], BF16)
        Tst = sb(st, nc, "Tst", [128, 4, 129], F32)
        T_b = [Buf() for _ in range(8)]
        S8r = Ring(st, nc, "S8r", 2, [128, 4, 129], BF16)
        S8_bufs = {0: [Buf() for _ in range(8)], 1: [Buf() for _ in range(8)]}
        sm = Ring(st, nc, "msm", 2, [128, 48], F32)
        ho = Ring(st, nc, "mho", 1, [128, 8, 128], F32)
        sq = Ring(st, nc, "msq", 1, [128, 8, 128], F32)
        y1 = Ring(st, nc, "my1", 1, [128, 8, 128], F32)
        ymb = Ring(st, nc, "ymb", 2, [128, 1024], BF16)
        ymT = Ring(st, nc, "ymTs", 2, [128, 8, 128], BF16)
        s8_prev = None
        for t in range(NTL):
            tsl = slice(t * 128, (t + 1) * 128)
            mv_t, mv_b = mvr.next()
            k.dma(k.sp, mv_t[:], S["mv"][tsl, :].rearrange("p (a b) -> p a b", a=8), writes=[mv_b])
            mo_t, mo_b = mor.next()
            k.dma(k.sp, mo_t[:], S["mo"][tsl, :], writes=[mo_b])
            vt_t, vt_b = vtr.next()
            k.op(k.dve, lambda: nc.vector.tensor_tensor(out=vt_t[:], in0=mv_t[:],
                                                        in1=EK[:, t, :].unsqueeze(2).to_broadcast([128, 8, 128]),
                                                        op=ALU.mult), reads=[mv_b, EK_b], writes=[vt_b])
            ek_t, ek_b = ekr.next()
            k.op(k.dve, lambda: nc.vector.tensor_copy(out=ek_t[:], in_=EK[:, t, :]), reads=[EK_b], writes=[ek_b])
            pT_t, pT_b = ptT.next()
            a_te = [pa.next() for _ in range(2)]
            fns = []
            for i in range(4):
                for e in range(2):
                    P0 = e * 64
                    fns.append(lambda i=i, e=e, P0=P0: nc.tensor.matmul(
                        a_te[e][0][:, i, :], qkc[P0:P0 + 64, 4 + i, tsl], qkc[P0:P0 + 64, i, tsl],
                        start=True, stop=True))
            k.mm(fns, reads=qkc_b, writes=[a_te[0][1], a_te[1][1]])
            for e in range(2):
                k.op(k.dve, lambda: nc.vector.tensor_tensor(
                    out=pT_t[:, 4 * e:4 * e + 4, :], in0=a_te[e][0][:],
                    in1=mask8.unsqueeze(1).to_broadcast([128, 4, 128]), op=ALU.mult),
                    reads=[a_te[e][1], cf32_b], writes=[pT_b])
            r_ts = []
            for bk in range(2):
                r_t, r_b = pr.next()
                fns = []
                for hh in range(4):
                    h = 4 * bk + hh
                    P0 = (h % 2) * 64
                    fns.append(lambda h=h, hh=hh: nc.tensor.matmul(
                        r_t[:, hh, :], pT_t[:, (h % 2) * 4 + h // 2, :], vt_t[:, h, :], start=True, stop=(t == 0)))
                    if t > 0:
                        fns.append(lambda h=h, hh=hh, P0=P0: nc.tensor.matmul(
                            r_t[:, hh, :], qkc[P0:P0 + 64, h // 2, tsl], s8_prev[0][P0:P0 + 64, h // 2, 0:128],
                            start=False, stop=True))
                rd = [pT_b, vt_b] + qkc_b + (s8_prev[1] if t > 0 else [])
                k.mm(fns, reads=rd, writes=[r_b])
                r_ts.append((r_t, r_b))
            fns = []
            for h in range(8):
                P0 = (h % 2) * 64
                fns.append(lambda h=h: nc.tensor.matmul(prd[:, h:h + 1], pT_t[:, (h % 2) * 4 + h // 2, :],
                                                        ek_t[:, h:h + 1], start=True, stop=(t == 0)))
                if t > 0:
                    fns.append(lambda h=h, P0=P0: nc.tensor.matmul(
                        prd[:, h:h + 1], qkc[P0:P0 + 64, h // 2, tsl], s8_prev[0][P0:P0 + 64, h // 2, 128:129],
                        start=False, stop=True))
            k.mm(fns, reads=[pT_b, ek_b] + qkc_b + (s8_prev[1] if t > 0 else []), writes=[prd_b])
            u_ts = []
            for bk in range(2):
                u_t, u_b = pu.next()
                fns = []
                for hh in range(4):
                    h = 4 * bk + hh
                    fns.append(lambda h=h, hh=hh: nc.tensor.matmul(
                        u_t[:, hh, :], ktok[:, t, (h // 2) * 128:(h // 2 + 1) * 128], vt_t[:, h, :],
                        start=True, stop=True))
                k.mm(fns, reads=[ktok_b, vt_b], writes=[u_b])
                u_ts.append((u_t, u_b))
            k.mm([(lambda h=h: nc.tensor.matmul(pun[:, h:h + 1], ktok[:, t, (h // 2) * 128:(h // 2 + 1) * 128],
                                                ek_t[:, h:h + 1], start=True, stop=True)) for h in range(8)],
                 reads=[ktok_b, ek_b], writes=[pun_b])
            s8_t, _ = S8r.next()
            s8_bl = S8_bufs[t % 2]
            for h in range(8):
                P0 = (h % 2) * 64
                hp = h // 2
                u_t, u_b = u_ts[h // 4]
                if t == 0:
                    k.op(k.dve, lambda: nc.vector.tensor_copy(out=Tst[P0:P0 + 64, hp, 0:128],
                                                              in_=u_t[P0:P0 + 64, h % 4, :]),
                         reads=[u_b], writes=[T_b[h]])
                    k.op(k.dve, lambda: nc.vector.tensor_copy(out=Tst[P0:P0 + 64, hp, 128:129],
                                                              in_=pun[P0:P0 + 64, h:h + 1]),
                         reads=[pun_b], writes=[T_b[h]])
                else:
                    k.op(k.dve, lambda: nc.vector.scalar_tensor_tensor(
                        out=Tst[P0:P0 + 64, hp, 0:128], in0=Tst[P0:P0 + 64, hp, 0:128],
                        scalar=EE[P0:P0 + 64, t - 1, h:h + 1], in1=u_t[P0:P0 + 64, h % 4, :],
                        op0=ALU.mult, op1=ALU.add), reads=[T_b[h], EE_b, u_b], writes=[T_b[h]])
                    k.op(k.dve, lambda: nc.vector.scalar_tensor_tensor(
                        out=Tst[P0:P0 + 64, hp, 128:129], in0=Tst[P0:P0 + 64, hp, 128:129],
                        scalar=EE[P0:P0 + 64, t - 1, h:h + 1], in1=pun[P0:P0 + 64, h:h + 1],
                        op0=ALU.mult, op1=ALU.add), reads=[T_b[h], EE_b, pun_b], writes=[T_b[h]])
                if t < NTL - 1:
                    k.op(k.pool, lambda: nc.gpsimd.tensor_scalar(
                        out=s8_t[P0:P0 + 64, hp, :], in0=Tst[P0:P0 + 64, hp, :], scalar1=EE[P0:P0 + 64, t, h:h + 1],
                        scalar2=M_DQK ** -0.5, op0=ALU.mult, op1=ALU.mult),
                        reads=[T_b[h], EE_b], writes=[s8_bl[h]])
            s8_prev = (s8_t, s8_bl)
            m_t, m_b = sm.next()
            dn, dneg, rc, cc, ss, rstd = (m_t[:, 0:8], m_t[:, 8:16], m_t[:, 16:24], m_t[:, 24:32],
                                          m_t[:, 32:40], m_t[:, 40:48])
            k.op(k.dve, lambda: nc.vector.tensor_tensor(out=dn, in0=prd, in1=EQ[:, t, :], op=ALU.mult),
                 reads=[prd_b, EQ_b], writes=[m_b])
            k.op(k.dve, lambda: nc.vector.tensor_scalar(out=dneg, in0=dn, scalar1=-1.0, scalar2=None, op0=ALU.mult),
                 reads=[m_b], writes=[m_b])
            k.op(k.dve, lambda: nc.vector.tensor_tensor(out=dn, in0=dn, in1=dneg, op=ALU.max),
                 reads=[m_b], writes=[m_b])
            k.op(k.dve, lambda: nc.vector.tensor_scalar(out=dn, in0=dn, scalar1=1.0, scalar2=None, op0=ALU.max),
                 reads=[m_b], writes=[m_b])
            k.op(k.dve, lambda: nc.vector.reciprocal(out=rc, in_=dn), reads=[m_b], writes=[m_b])
            k.op(k.dve, lambda: nc.vector.tensor_tensor(out=cc, in0=rc, in1=EQ[:, t, :], op=ALU.mult),
                 reads=[m_b, EQ_b], writes=[m_b])
            ho_t, ho_b = ho.next()
            for bk in range(2):
                r_t, r_b = r_ts[bk]
                k.op(k.dve, lambda: nc.vector.tensor_tensor(
                    out=ho_t[:, 4 * bk:4 * bk + 4, :], in0=r_t[:],
                    in1=cc[:, 4 * bk:4 * bk + 4].unsqueeze(2).to_broadcast([128, 4, 128]), op=ALU.mult),
                    reads=[r_b, m_b], writes=[ho_b])
            sq_t, sq_b = sq.next()
            k.op(k.pool, lambda: nc.gpsimd.tensor_tensor(out=sq_t[:], in0=ho_t[:], in1=ho_t[:], op=ALU.mult),
                 reads=[ho_b], writes=[sq_b])
            k.op(k.dve, lambda: nc.vector.tensor_reduce(out=ss, in_=sq_t[:], axis=AX.X, op=ALU.add),
                 reads=[sq_b], writes=[m_b])
            rms_rstd(k, ss, m_b, rstd, m_b, M_DV)
            y1_t, y1_b = y1.next()
            k.op(k.dve, lambda: nc.vector.tensor_tensor(
                out=y1_t[:], in0=ho_t[:], in1=rstd.unsqueeze(2).to_broadcast([128, 8, 128]), op=ALU.mult),
                reads=[ho_b, m_b], writes=[y1_b])
            y1f = y1_t[:].rearrange("p a b -> p (a b)")
            k.op(k.pool, lambda: nc.gpsimd.tensor_tensor(out=y1f, in0=y1f, in1=mnw_t[:], op=ALU.mult),
                 reads=[y1_b, mnw_b], writes=[y1_b])
            yb_t, yb_b = ymb.next()
            k.op(k.dve, lambda: nc.vector.tensor_tensor(out=yb_t[:], in0=y1f, in1=mo_t[:], op=ALU.mult),
                 reads=[y1_b, mo_b], writes=[yb_b])
            k.mm([(lambda c=c: nc.tensor.transpose(ptr[:, c, :], yb_t[:, c * 128:(c + 1) * 128], ident))
                  for c in range(8)], reads=[yb_b, cbf_b], writes=[ptr_b])
            yT_t, yT_b = ymT.next()
            k.op(k.act, lambda: nc.scalar.copy(out=yT_t[:], in_=ptr[:]), reads=[ptr_b], writes=[yT_b])
            k.dma(k.sp, S["ymT"][:, tsl].rearrange("(c p) n -> p c n", p=128), yT_t[:], reads=[yT_b])
        k.barrier()


def phase_out(k, S, x_io, w_pa, w_pm, w_out, n_post):
    nc = k.nc
    TT = 512
    NTT = SEQ // TT
    NS = TT // 128
    with ExitStack() as st:
        ws = []
        for nm, w in (("wpa", w_pa), ("wpm", w_pm), ("wout", w_out)):
            t = sb(st, nc, nm, [128, 8, D_MODEL], BF16)
            b = Buf()
            for hh in range(2):
                k.dma(k.pool, t[:, hh * 4:(hh + 1) * 4, :],
                      w[hh * 512:(hh + 1) * 512, :].rearrange("(c p) n -> p c n", p=128), writes=[b], join=True)
            ws.append((t, b))
        (wpa, wpa_b), (wpm, wpm_b), (wout, wout_b) = ws
        npost, npost_b = load_bcast(k, st, "npost", n_post, D_MODEL)
        yaT = Ring(st, nc, "yaTs", 2, [128, 8, TT], BF16)
        ymT = Ring(st, nc, "ymTs", 2, [128, 8, TT], BF16)
        gT = Ring(st, nc, "gTs", 3, [128, 2, TT], F32)
        mg = Ring(st, nc, "mg", 1, [128, 8, TT], BF16)
        t1 = Ring(st, nc, "t1", 2, [128, TT], F32)
        xr = Ring(st, nc, "xr", 2, [128, D_MODEL], F32)
        yo = Ring(st, nc, "yo", 1, [128, D_MODEL], F32)
        junk = Ring(st, nc, "junk", 1, [128, D_MODEL], BF16)
        stat = Ring(st, nc, "stat", 4, [128, 4], F32)
        ppa = Ring(st, nc, "ppa", 2, [128, TT], F32, psum=True)
        ppm = Ring(st, nc, "ppm", 2, [128, TT], F32, psum=True)
        py = Ring(st, nc, "py", 1, [128, 2, 512], F32, psum=True)
        for t in range(NTT):
            T0 = t * TT
            a_t, a_b = yaT.next()
            m_t, m_b = ymT.next()
            k.dma(k.sp, a_t[:], S["yaT"][:, T0:T0 + TT].rearrange("(c p) n -> p c n", p=128), writes=[a_b])
            k.dma(k.sp, m_t[:], S["ymT"][:, T0:T0 + TT].rearrange("(c p) n -> p c n", p=128), writes=[m_b])
            g_t, g_b = mg.next()
            for c in range(8):
                gg_t, gg_b = gT.next()
                k.dma(k.sp, gg_t[:], S["glT"].rearrange("(a r) n -> r a n", a=2)[c * 128:(c + 1) * 128, :, T0:T0 + TT],
                      writes=[gg_b])
                pa_t, pa_b = ppa.next()
                pm_t, pm_b = ppm.next()
                k.mm([(lambda kc=kc: nc.tensor.matmul(pa_t[:], wpa[:, kc, c * 128:(c + 1) * 128], a_t[:, kc, :],
                                                      start=(kc == 0), stop=(kc == 7))) for kc in range(8)],
                     reads=[wpa_b, a_b], writes=[pa_b])
                k.mm([(lambda kc=kc: nc.tensor.matmul(pm_t[:], wpm[:, kc, c * 128:(c + 1) * 128], m_t[:, kc, :],
                                                      start=(kc == 0), stop=(kc == 7))) for kc in range(8)],
                     reads=[wpm_b, m_b], writes=[pm_b])
                u_t, u_b = t1.next()
                k.op(k.dve, lambda: nc.vector.tensor_tensor(out=u_t[:], in0=pa_t[:], in1=gg_t[:, 0, :], op=ALU.mult),
                     reads=[pa_b, gg_b], writes=[u_b])
                v_t, v_b = t1.next()
                k.op(k.dve, lambda: nc.vector.tensor_tensor(out=v_t[:], in0=pm_t[:], in1=gg_t[:, 1, :], op=ALU.mult),
                     reads=[pm_b, gg_b], writes=[v_b])
                k.op(k.pool, lambda: nc.gpsimd.tensor_tensor(out=g_t[:, c, :], in0=u_t[:], in1=v_t[:], op=ALU.add),
                     reads=[u_b, v_b], writes=[g_b])
            for s in range(NS):
                tok0 = T0 + s * 128
                y_t, y_b = py.next()
                fns = []
                for hf in range(2):
                    for kc in range(8):
                        fns.append(lambda kc=kc, hf=hf: nc.tensor.matmul(
                            y_t[:, hf, :], g_t[:, kc, s * 128:(s + 1) * 128], wout[:, kc, hf * 512:(hf + 1) * 512],
                            start=(kc == 0), stop=(kc == 7)))
                k.mm(fns, reads=[g_b, wout_b], writes=[y_b])
                xr_t, xr_b = xr.next()
                k.dma(k.sp, xr_t[:], x_io[tok0:tok0 + 128, :], writes=[xr_b])
                j_t, j_b = junk.next()
                s_t, s_b = stat.next()
                for hf in range(2):
                    k.op(k.act, lambda: nc.scalar.activation(out=j_t[:, hf * 512:(hf + 1) * 512], in_=y_t[:, hf, :],
                                                             func=AF.Square, accum_out=s_t[:, 2 + hf:3 + hf]),
                         reads=[y_b], writes=[j_b, s_b])
                k.op(k.dve, lambda: nc.vector.tensor_tensor(out=s_t[:, 0:1], in0=s_t[:, 2:3], in1=s_t[:, 3:4],
                                                            op=ALU.add), reads=[s_b], writes=[s_b])
                rms_rstd(k, s_t[:, 0:1], s_b, s_t[:, 1:2], s_b, D_MODEL)
                o_t, o_b = yo.next()
                for hf in range(2):
                    k.op(k.dve, lambda: nc.vector.tensor_tensor(
                        out=o_t[:, hf * 512:(hf + 1) * 512], in0=y_t[:, hf, :], in1=npost[:, hf * 512:(hf + 1) * 512],
                        op=ALU.mult), reads=[y_b, npost_b], writes=[o_b])
                k.op(k.dve, lambda: nc.vector.scalar_tensor_tensor(
                    out=xr_t[:], in0=o_t[:], scalar=s_t[:, 1:2], in1=xr_t[:], op0=ALU.mult, op1=ALU.add),
                    reads=[o_b, xr_b, s_b], writes=[xr_b])
                k.dma(k.sp, x_io[tok0:tok0 + 128, :], xr_t[:], reads=[xr_b])
        k.barrier()


PARAM_NAMES = ["ffn1_norm_pre", "ffn1_w_gu", "ffn1_w_down", "ffn1_norm_post", "mix_norm_pre", "w_in",
               "attn_lam_q1", "attn_lam_k1", "attn_lam_q2", "attn_lam_k2", "attn_norm_w", "conv_w", "conv_b",
               "igate_b", "fgate_b", "mlstm_norm_w", "w_proj_a", "w_proj_m", "gate_b", "w_out", "mix_norm_post",
               "ffn2_norm_pre", "ffn2_w_gu", "ffn2_w_down", "ffn2_norm_post"]


def build_program(depth=DEPTH):
    nc = bass.Bass("TRN2", target_bir_lowering=False)

    def din(name, shape, dt=F32):
        return nc.dram_tensor(name, shape, dt, kind="ExternalInput").ap()

    L = depth
    x = din("x", [SEQ, D_MODEL])
    out = nc.dram_tensor("out", [SEQ, D_MODEL], F32, kind="ExternalOutput").ap()
    P = {
        "ffn1_norm_pre": din("ffn1_norm_pre", [L, D_MODEL]),
        "ffn1_w_gu": din("ffn1_w_gu", [L, D_MODEL, 2 * D_FF]),
        "ffn1_w_down": din("ffn1_w_down", [L, D_FF, D_MODEL]),
        "ffn1_norm_post": din("ffn1_norm_post", [L, D_MODEL]),
        "mix_norm_pre": din("mix_norm_pre", [L, D_MODEL]),
        "w_in": din("w_in", [L, D_MODEL, C_IN]),
        "lamp": din("lamp", [L, 256]),
        "anw": din("anw", [L, 128, 1]),
        "convw_pc": din("convw_pc", [L, 128, 8, 4]),
        "convb_pc": din("convb_pc", [L, 128, 8]),
        "igate_b": din("igate_b", [L, 8]),
        "fgate_b": din("fgate_b", [L, 8]),
        "mlstm_norm_w": din("mlstm_norm_w", [L, 1024]),
        "w_proj_a": din("w_proj_a", [L, 1024, D_MODEL]),
        "w_proj_m": din("w_proj_m", [L, 1024, D_MODEL]),
        "gate_b_pc": din("gate_b_pc", [L, 128, 16]),
        "w_out": din("w_out", [L, D_MODEL, D_MODEL]),
        "mix_norm_post": din("mix_norm_post", [L, D_MODEL]),
        "ffn2_norm_pre": din("ffn2_norm_pre", [L, D_MODEL]),
        "ffn2_w_gu": din("ffn2_w_gu", [L, D_MODEL, 2 * D_FF]),
        "ffn2_w_down": din("ffn2_w_down", [L, D_FF, D_MODEL]),
        "ffn2_norm_post": din("ffn2_norm_post", [L, D_MODEL]),
    }
    S = make_scratch(nc)
    k = K(nc)
    with ExitStack() as st:
        cbf, cbf_b, cf32, cf32_b = load_consts(k, st)
        ident = cbf[:, 0, :]
        for l in range(L):
            lam_init = 0.8 - 0.6 * math.exp(-0.3 * l)
            phase_ffn(k, ident, cbf_b, x if l == 0 else out, out, P["ffn1_w_gu"][l], P["ffn1_w_down"][l],
                      P["ffn1_norm_pre"][l], P["ffn1_norm_post"][l])
            phase_proj(k, ident, cbf_b, out, P["w_in"][l], P["mix_norm_pre"][l], P["gate_b_pc"][l], S)
            phase_attn(k, cbf, cbf_b, cf32, cf32_b, S, P["lamp"][l], P["anw"][l], lam_init)
            phase_mlstm(k, cbf, cbf_b, cf32, cf32_b, S, P["convw_pc"][l], P["convb_pc"][l], P["igate_b"][l],
                        P["fgate_b"][l], P["mlstm_norm_w"][l])
            phase_out(k, S, out, P["w_proj_a"][l], P["w_proj_m"][l], P["w_out"][l], P["mix_norm_post"][l])
            phase_ffn(k, ident, cbf_b, out, out, P["ffn2_w_gu"][l], P["ffn2_w_down"][l],
                      P["ffn2_norm_pre"][l], P["ffn2_norm_post"][l])
        k.finish()
    return nc


def host_params(inp, depth=DEPTH):
    f = lambda a: np.ascontiguousarray(np.asarray(a, dtype=np.float32))
    L = depth
    d = {n: f(inp[n]) for n in ["ffn1_norm_pre", "ffn1_w_gu", "ffn1_w_down", "ffn1_norm_post", "mix_norm_pre",
                                "w_in", "igate_b", "fgate_b", "w_proj_a", "w_proj_m", "w_out", "mix_norm_post",
                                "ffn2_norm_pre", "ffn2_w_gu", "ffn2_w_down", "ffn2_norm_post"]}
    d["lamp"] = f(np.concatenate([inp["attn_lam_q1"], inp["attn_lam_q2"], inp["attn_lam_k1"], inp["attn_lam_k2"]],
                                 axis=1))
    d["anw"] = f(np.asarray(inp["attn_norm_w"]).reshape(L, 128, 1))
    cw = np.asarray(inp["conv_w"])
    d["convw_pc"] = f(cw.transpose(0, 2, 1).reshape(L, 8, 128, 4).transpose(0, 2, 1, 3))
    d["convb_pc"] = f(np.asarray(inp["conv_b"]).reshape(L, 8, 128).transpose(0, 2, 1))
    d["mlstm_norm_w"] = f(np.asarray(inp["mlstm_norm_w"]).reshape(L, 1024))
    d["gate_b_pc"] = f(np.asarray(inp["gate_b"]).reshape(L, 16, 128).transpose(0, 2, 1))
    d.update(host_consts())
    return d


_NC_CACHE = {}


def kernel(**inputs):
    if "nc" not in _NC_CACHE:
        _NC_CACHE["nc"] = build_program()
    nc = _NC_CACHE["nc"]
    params = host_params(inputs)
    x = np.asarray(inputs["x"], dtype=np.float32)
    in_maps = []
    for b in range(BATCH):
        m = dict(params)
        m["x"] = np.ascontiguousarray(x[b])
        in_maps.append(m)
    res = run_bass_kernel_spmd(nc, in_maps, core_ids=list(range(BATCH)))
    return np.stack([np.asarray(r["out"], dtype=np.float32) for r in res.results], axis=0)
```

```python
import math
from contextlib import ExitStack

import numpy as np
import concourse.bass as bass
import concourse.mybir as mybir
from concourse.bass_utils import run_bass_kernel_spmd

F32 = mybir.dt.float32
BF16 = mybir.dt.bfloat16
AF = mybir.ActivationFunctionType
ALU = mybir.AluOpType
AX = mybir.AxisListType

D_MODEL = 1024
BATCH = 8
SEQ = 4096
DEPTH = 2
EPS = 1e-6
A_HEADS = 8
A_DHEAD = 64
M_HEADS = 8
M_DQK = 64
M_DV = 128
D_FF = 2816
C_IN = 8208
NT = SEQ // 128

O_AQ, O_AK, O_AV = 0, 1024, 2048
O_MQ, O_MK, O_MV, O_MO = 3072, 3584, 4096, 5120
O_MI, O_MF, O_GL = 6144, 6152, 6160


class Sem:
    def __init__(self, nc, name):
        self.h = nc.alloc_semaphore(name)
        self.v = 0
        self.name = name


class Buf:
    __slots__ = ("w", "r", "name")

    def __init__(self, name=""):
        self.w = {}
        self.r = {}
        self.name = name


class Eng:
    def __init__(self, k, name, e, n_dma_sems=0):
        self.k = k
        self.name = name
        self.e = e
        self.sem = Sem(k.nc, "s_" + name)
        self.waited = {}
        self.dma_sems = [Sem(k.nc, f"d_{name}{i}") for i in range(n_dma_sems)]
        self.dma_rr = 0

    def wait(self, tok):
        sem, v = tok
        if self.waited.get(sem, 0) >= v:
            return
        self.e.wait_ge(sem.h, v)
        self.waited[sem] = v


class K:
    def __init__(self, nc):
        self.nc = nc
        self.pe = Eng(self, "pe", nc.tensor)
        self.act = Eng(self, "act", nc.scalar)
        self.dve = Eng(self, "dve", nc.vector)
        self.pool = Eng(self, "pool", nc.gpsimd, n_dma_sems=4)
        self.sp = Eng(self, "sp", nc.sync, n_dma_sems=20)
        self.engs = [self.pe, self.act, self.dve, self.pool, self.sp]
        self.const_t = nc.alloc_sbuf_tensor("const_cols", [128, 32], F32)
        self.const_b = Buf("consts")
        self.consts = {}

    def const(self, val):
        val = float(val)
        if val not in self.consts:
            i = len(self.consts)
            assert i < self.const_t.shape[1]
            ap = self.const_t[:, i:i + 1]
            self.op(self.pool, lambda: self.nc.gpsimd.memset(ap, val), writes=[self.const_b])
            self.consts[val] = ap
        return self.consts[val]

    def _deps(self, eng, reads, writes, join=False):
        for b in reads:
            for s, v in b.w.items():
                eng.wait((s, v))
        for b in writes:
            if not join:
                for s, v in b.w.items():
                    eng.wait((s, v))
            for s, v in b.r.items():
                eng.wait((s, v))

    def _mark(self, tok, reads, writes, join=False):
        s, v = tok
        for b in reads:
            if b.r.get(s, 0) < v:
                b.r[s] = v
        for b in writes:
            if join:
                b.w[s] = v
            else:
                b.w = {s: v}
            b.r = {}

    def op(self, eng, fn, reads=(), writes=()):
        self._deps(eng, reads, writes)
        ins = fn()
        eng.sem.v += 1
        ins.then_inc(eng.sem.h, 1)
        self._mark((eng.sem, eng.sem.v), reads, writes)

    def mm(self, fns, reads=(), writes=()):
        eng = self.pe
        self._deps(eng, reads, writes)
        ins = None
        for fn in fns:
            ins = fn()
        eng.sem.v += 1
        ins.then_inc(eng.sem.h, 1)
        self._mark((eng.sem, eng.sem.v), reads, writes)

    def dma(self, q, out, in_, reads=(), writes=(), join=False):
        sem = q.dma_sems[q.dma_rr]
        q.dma_rr = (q.dma_rr + 1) % len(q.dma_sems)
        if sem.v:
            q.wait((sem, sem.v))
        self._deps(q, reads, writes, join)
        ins = q.e.dma_start(out=out, in_=in_)
        sem.v += 16
        ins.then_inc(sem.h, 16)
        self._mark((sem, sem.v), reads, writes, join)

    def barrier(self):
        sems = []
        for e in self.engs:
            sems.append(e.sem)
            sems.extend(e.dma_sems)
        for e in self.engs:
            for s in sems:
                if s.v:
                    e.wait((s, s.v))

    def finish(self):
        self.barrier()


_UID = [0]


def uname(name):
    _UID[0] += 1
    return f"{name}_{_UID[0]}"


class Ring:
    def __init__(self, st, nc, name, n, shape, dtype, psum=False):
        self.n = n
        self.i = 0
        self.t = []
        self.b = []
        for j in range(n):
            if psum:
                t = st.enter_context(nc.psum_tensor(uname(name), shape, dtype))
            else:
                t = st.enter_context(nc.sbuf_tensor(uname(name), shape, dtype))
            self.t.append(t)
            self.b.append(Buf(f"{name}{j}"))

    def next(self):
        j = self.i
        self.i = (self.i + 1) % self.n
        return self.t[j], self.b[j]


def sb(st, nc, name, shape, dtype):
    return st.enter_context(nc.sbuf_tensor(uname(name), shape, dtype))


def ps(st, nc, name, shape, dtype):
    return st.enter_context(nc.psum_tensor(uname(name), shape, dtype))


def load_bcast(k, st, name, vec_ap, n):
    nc = k.nc
    t = sb(st, nc, name, [128, n], F32)
    b = Buf(name)
    k.dma(k.sp, t[:], vec_ap.partition_broadcast(128), writes=[b])
    return t, b


def rms_rstd(k, ss, ss_b, rstd, rstd_b, n, eps=EPS, mul=1.0):
    nc = k.nc
    c_eps = k.const(eps)
    c_mul = k.const(math.log(mul))
    k.op(k.act, lambda: nc.scalar.activation(out=rstd, in_=ss, func=AF.Ln, scale=1.0 / n, bias=c_eps),
         reads=[ss_b, k.const_b], writes=[rstd_b])
    k.op(k.act, lambda: nc.scalar.activation(out=rstd, in_=rstd, func=AF.Exp, scale=-0.5, bias=c_mul),
         reads=[rstd_b, k.const_b], writes=[rstd_b])


def norm_h(k, rings, x_src, tok0, npre, npre_b):
    nc = k.nc
    xs, hb, stat = rings
    x_t, x_b = xs.next()
    k.dma(k.sp, x_t[:], x_src[tok0:tok0 + 128, :], writes=[x_b])
    h_t, h_b = hb.next()
    s_t, s_b = stat.next()
    k.op(k.act, lambda: nc.scalar.activation(out=h_t[:], in_=x_t[:], func=AF.Square, accum_out=s_t[:, 0:1]),
         reads=[x_b], writes=[h_b, s_b])
    rms_rstd(k, s_t[:, 0:1], s_b, s_t[:, 1:2], s_b, D_MODEL)
    k.op(k.dve, lambda: nc.vector.scalar_tensor_tensor(out=h_t[:], in0=x_t[:], scalar=s_t[:, 1:2],
                                                       in1=npre[:], op0=ALU.mult, op1=ALU.mult),
         reads=[x_b, s_b, npre_b], writes=[h_b])
    return h_t, h_b


def transp_h(k, ptr, ident, ident_b, h_t, h_b, hT_t, hT_b, s):
    nc = k.nc
    p_t, p_b = ptr.next()
    k.mm([(lambda c=c: nc.tensor.transpose(p_t[:, c, :], h_t[:, c * 128:(c + 1) * 128], ident[:]))
          for c in range(8)], reads=[h_b, ident_b], writes=[p_b])
    k.op(k.act, lambda: nc.scalar.copy(out=hT_t[:, :, s * 128:(s + 1) * 128], in_=p_t[:]),
         reads=[p_b], writes=[hT_b])


def phase_ffn(k, ident, ident_b, x_in, x_out, w_gu, w_down, n_pre, n_post):
    nc = k.nc
    TT = 512
    NTT = SEQ // TT
    NS = TT // 128
    NJ = D_FF // 128
    with ExitStack() as st:
        wgu = sb(st, nc, "wgu", [128, 8, 2 * D_FF], BF16)
        wdn = sb(st, nc, "wdn", [128, NJ, D_MODEL], BF16)
        JB = [(0, 4), (4, 8), (8, 12), (12, 16), (16, 20), (20, 22)]
        wgu_b = {}
        for bi, (j0, j1) in enumerate(JB):
            for gu in range(2):
                c0 = gu * D_FF + j0 * 128
                c1 = gu * D_FF + j1 * 128
                b = Buf()
                for j in range(j0, j1):
                    wgu_b[(gu, j)] = b
                for c in range(8):
                    k.dma(k.pool, wgu[:, c, c0:c1], w_gu[c * 128:(c + 1) * 128, c0:c1], writes=[b], join=True)
        wdn_b = []
        for hh in range(2):
            b = Buf()
            wdn_b.append(b)
            k.dma(k.pool, wdn[:, hh * 11:(hh + 1) * 11, :],
                  w_down[hh * 1408:(hh + 1) * 1408, :].rearrange("(j p) n -> p j n", p=128), writes=[b])
        npre, npre_b = load_bcast(k, st, "npre", n_pre, D_MODEL)
        npost, npost_b = load_bcast(k, st, "npost", n_post, D_MODEL)

        xs = Ring(st, nc, "xs", 2, [128, D_MODEL], F32)
        hb = Ring(st, nc, "hb", 4, [128, D_MODEL], BF16)
        stat = Ring(st, nc, "stat", 8, [128, 4], F32)
        hT = Ring(st, nc, "hT", 1, [128, 8, TT], BF16)
        actT = Ring(st, nc, "actT", 1, [128, NJ, TT], BF16)
        sg = Ring(st, nc, "sg", 2, [128, TT], F32)
        xr = Ring(st, nc, "xr", 2, [128, D_MODEL], F32)
        yo = Ring(st, nc, "yo", 1, [128, D_MODEL], F32)
        ptr = Ring(st, nc, "ptr", 1, [128, 8, 128], BF16, psum=True)
        pg = Ring(st, nc, "pg", 2, [128, TT], F32, psum=True)
        pu = Ring(st, nc, "pu", 2, [128, TT], F32, psum=True)
        py = Ring(st, nc, "py", 1, [128, 2, 512], F32, psum=True)
        nrings = (xs, hb, stat)

        hT_t, hT_b = hT.next()
        hs = [norm_h(k, nrings, x_in, s * 128, npre, npre_b) for s in range(NS)]
        for s in range(NS):
            transp_h(k, ptr, ident, ident_b, hs[s][0], hs[s][1], hT_t, hT_b, s)
        for t in range(NTT):
            if t + 1 < NTT:
                hs = [norm_h(k, nrings, x_in, ((t + 1) * NS + s) * 128, npre, npre_b) for s in range(NS)]
            a_t, a_b = actT.next()
            for j in range(NJ):
                g_t, g_b = pg.next()
                u_t, u_b = pu.next()
                k.mm([(lambda c=c: nc.tensor.matmul(g_t[:], wgu[:, c, j * 128:(j + 1) * 128], hT_t[:, c, :],
                                                    start=(c == 0), stop=(c == 7))) for c in range(8)],
                     reads=[wgu_b[(0, j)], hT_b], writes=[g_b])
                k.mm([(lambda c=c: nc.tensor.matmul(u_t[:], wgu[:, c, D_FF + j * 128:D_FF + (j + 1) * 128],
                                                    hT_t[:, c, :], start=(c == 0), stop=(c == 7)))
                      for c in range(8)],
                     reads=[wgu_b[(1, j)], hT_b], writes=[u_b])
                sg_t, sg_b = sg.next()
                k.op(k.act, lambda: nc.scalar.activation(out=sg_t[:], in_=g_t[:], func=AF.Silu),
                     reads=[g_b], writes=[sg_b])
                k.op(k.dve, lambda: nc.vector.tensor_tensor(out=a_t[:, j, :], in0=sg_t[:], in1=u_t[:],
                                                            op=ALU.mult),
                     reads=[sg_b, u_b], writes=[a_b])
            if t + 1 < NTT:
                hT_t, hT_b = hT.next()
                for s in range(NS):
                    transp_h(k, ptr, ident, ident_b, hs[s][0], hs[s][1], hT_t, hT_b, s)
            for s in range(NS):
                tok0 = (t * NS + s) * 128
                y_t, y_b = py.next()
                fns = []
                for hf in range(2):
                    for j in range(NJ):
                        fns.append(lambda j=j, hf=hf: nc.tensor.matmul(
                            y_t[:, hf, :], a_t[:, j, s * 128:(s + 1) * 128], wdn[:, j, hf * 512:(hf + 1) * 512],
                            start=(j == 0), stop=(j == NJ - 1)))
                k.mm(fns, reads=[a_b] + wdn_b, writes=[y_b])
                xr_t, xr_b = xr.next()
                k.dma(k.sp, xr_t[:], x_in[tok0:tok0 + 128, :], writes=[xr_b])
                o_t, o_b = yo.next()
                s_t, s_b = stat.next()
                for hf in range(2):
                    k.op(k.act, lambda: nc.scalar.activation(out=o_t[:, hf * 512:(hf + 1) * 512], in_=y_t[:, hf, :],
                                                             func=AF.Square, accum_out=s_t[:, 2 + hf:3 + hf]),
                         reads=[y_b], writes=[o_b, s_b])
                k.op(k.dve, lambda: nc.vector.tensor_tensor(out=s_t[:, 0:1], in0=s_t[:, 2:3], in1=s_t[:, 3:4],
                                                            op=ALU.add), reads=[s_b], writes=[s_b])
                rms_rstd(k, s_t[:, 0:1], s_b, s_t[:, 1:2], s_b, D_MODEL, mul=0.5)
                for hf in range(2):
                    k.op(k.dve, lambda: nc.vector.tensor_tensor(
                        out=o_t[:, hf * 512:(hf + 1) * 512], in0=y_t[:, hf, :], in1=npost[:, hf * 512:(hf + 1) * 512],
                        op=ALU.mult), reads=[y_b, npost_b], writes=[o_b])
                k.op(k.dve, lambda: nc.vector.scalar_tensor_tensor(
                    out=xr_t[:], in0=o_t[:], scalar=s_t[:, 1:2], in1=xr_t[:], op0=ALU.mult, op1=ALU.add),
                    reads=[o_b, xr_b, s_b], writes=[xr_b])
                k.dma(k.sp, x_out[tok0:tok0 + 128, :], xr_t[:], reads=[xr_b])
        k.barrier()


def phase_proj(k, ident, ident_b, x_in, w_in, n_pre, gate_b_pc, S):
    nc = k.nc
    TT = 512
    NTT = SEQ // TT
    NS = TT // 128
    with ExitStack() as st:
        win = sb(st, nc, "win", [128, 8, C_IN], BF16)
        blocks = [(0, 512, 0), (512, 1024, 0), (1024, 2048, 0), (2048, 3072, 1), (3072, 4096, 1),
                  (4096, 5120, 2), (5120, 6160, 2), (6160, 7184, 3), (7184, 8208, 3)]
        wb = [Buf() for _ in range(4)]
        for (c0, c1, bi) in blocks:
            for c in range(8):
                k.dma(k.pool, win[:, c, c0:c1], w_in[c * 128:(c + 1) * 128, c0:c1], writes=[wb[bi]], join=True)
        npre, npre_b = load_bcast(k, st, "npre", n_pre, D_MODEL)
        gb = sb(st, nc, "gb", [128, 16], F32)
        gb_b = Buf()
        k.dma(k.sp, gb[:], gate_b_pc, writes=[gb_b])
        xs = Ring(st, nc, "xs", 2, [128, D_MODEL], F32)
        hb = Ring(st, nc, "hb", 4, [128, D_MODEL], BF16)
        stat = Ring(st, nc, "stat", 8, [128, 4], F32)
        ptr = Ring(st, nc, "ptr", 1, [128, 8, 128], BF16, psum=True)
        nrings = (xs, hb, stat)
        hT = Ring(st, nc, "hT", 1, [128, 8, TT], BF16)
        of32 = Ring(st, nc, "of32", 4, [128, 512], F32)
        obf = Ring(st, nc, "obf", 4, [128, 512], BF16)
        pf = Ring(st, nc, "pf", 3, [128, 512], F32, psum=True)
        pt = Ring(st, nc, "pt", 3, [128, 512], F32, psum=True)
        ev = [0]

        def evac_copy(out_ap, in_ap, rd, wr):
            ev[0] ^= 1
            if ev[0]:
                k.op(k.dve, lambda: nc.vector.tensor_copy(out=out_ap, in_=in_ap), reads=rd, writes=wr)
            else:
                k.op(k.act, lambda: nc.scalar.copy(out=out_ap, in_=in_ap), reads=rd, writes=wr)

        hT_t, hT_b = hT.next()
        hs = [norm_h(k, nrings, x_in, s * 128, npre, npre_b) for s in range(NS)]
        for s in range(NS):
            transp_h(k, ptr, ident, ident_b, hs[s][0], hs[s][1], hT_t, hT_b, s)
        for t in range(NTT):
            T0 = t * TT
            if t + 1 < NTT:
                hs = [norm_h(k, nrings, x_in, (t + 1) * TT + s * 128, npre, npre_b) for s in range(NS)]

            def fm_chunk(col0, wbuf):
                p_t, p_b = pf.next()
                k.mm([(lambda c=c: nc.tensor.matmul(p_t[:], win[:, c, col0:col0 + 128], hT_t[:, c, :],
                                                    start=(c == 0), stop=(c == 7))) for c in range(8)],
                     reads=[wbuf, hT_b], writes=[p_b])
                return p_t, p_b

            def tm_block(s, col0, n, wbuf):
                p_t, p_b = pt.next()
                k.mm([(lambda c=c: nc.tensor.matmul(p_t[:, 0:n], hT_t[:, c, s * 128:(s + 1) * 128],
                                                    win[:, c, col0:col0 + n], start=(c == 0), stop=(c == 7)))
                      for c in range(8)], reads=[wbuf, hT_b], writes=[p_b])
                return p_t, p_b

            for c in range(16):
                p_t, p_b = fm_chunk(c * 128, wb[0])
                o_t, o_b = obf.next()
                evac_copy(o_t[:], p_t[:], [p_b], [o_b])
                k.dma(k.sp, S["qkT"][c * 128:(c + 1) * 128, T0:T0 + TT], o_t[:], reads=[o_b])
            for c in range(8):
                p_t, p_b = fm_chunk(O_MQ + c * 128, wb[1])
                o_t, o_b = of32.next()
                evac_copy(o_t[:], p_t[:], [p_b], [o_b])
                k.dma(k.sp, S["mqkT"][c * 128:(c + 1) * 128, T0:T0 + TT], o_t[:], reads=[o_b])
            for s in range(NS):
                tok0 = T0 + s * 128
                for hf in range(2):
                    p_t, p_b = tm_block(s, O_AV + hf * 512, 512, wb[1])
                    o_t, o_b = obf.next()
                    evac_copy(o_t[:], p_t[:], [p_b], [o_b])
                    k.dma(k.sp, S["av"][tok0:tok0 + 128, hf * 512:(hf + 1) * 512], o_t[:], reads=[o_b])
                for hf in range(2):
                    p_t, p_b = tm_block(s, O_MV + hf * 512, 512, wb[2])
                    o_t, o_b = of32.next()
                    evac_copy(o_t[:], p_t[:], [p_b], [o_b])
                    k.dma(k.sp, S["mv"][tok0:tok0 + 128, hf * 512:(hf + 1) * 512], o_t[:], reads=[o_b])
                for hf in range(2):
                    p_t, p_b = tm_block(s, O_MO + hf * 512, 512, wb[2])
                    o_t, o_b = of32.next()
                    k.op(k.act, lambda: nc.scalar.activation(out=o_t[:], in_=p_t[:], func=AF.Sigmoid),
                         reads=[p_b], writes=[o_b])
                    k.dma(k.sp, S["mo"][tok0:tok0 + 128, hf * 512:(hf + 1) * 512], o_t[:], reads=[o_b])
                p_t, p_b = tm_block(s, O_MI, 16, wb[2])
                o_t, o_b = of32.next()
                evac_copy(o_t[:, 0:16], p_t[:, 0:16], [p_b], [o_b])
                k.dma(k.sp, S["mif"][tok0:tok0 + 128, :], o_t[:, 0:16], reads=[o_b])
            for c in range(16):
                p_t, p_b = fm_chunk(O_GL + c * 128, wb[3])
                o_t, o_b = of32.next()
                k.op(k.act, lambda: nc.scalar.activation(out=o_t[:], in_=p_t[:], func=AF.Sigmoid,
                                                         bias=gb[:, c:c + 1]),
                     reads=[p_b, gb_b], writes=[o_b])
                k.dma(k.sp, S["glT"][c * 128:(c + 1) * 128, T0:T0 + TT], o_t[:], reads=[o_b])
            if t + 1 < NTT:
                hT_t, hT_b = hT.next()
                for s in range(NS):
                    transp_h(k, ptr, ident, ident_b, hs[s][0], hs[s][1], hT_t, hT_b, s)
        k.barrier()


def phase_attn(k, cbf, cbf_b, cf32, cf32_b, S, lamp, anw_col, lam_init):
    nc = k.nc
    NG = SEQ // 512
    with ExitStack() as st:
        lt, lt_b = load_bcast(k, st, "lamp", lamp, 256)
        lw = sb(st, nc, "lamw", [128, 136], F32)
        lw_b = Buf()
        k.op(k.dve, lambda: nc.vector.tensor_tensor(out=lw[:, 0:128], in0=lt[:, 0:128], in1=lt[:, 128:256],
                                                    op=ALU.mult), reads=[lt_b], writes=[lw_b])
        k.op(k.dve, lambda: nc.vector.tensor_reduce(out=lw[:, 128:130],
                                                    in_=lw[:, 0:128].rearrange("p (a b) -> p a b", a=2),
                                                    axis=AX.X, op=ALU.add), reads=[lw_b], writes=[lw_b])
        k.op(k.act, lambda: nc.scalar.activation(out=lw[:, 130:132], in_=lw[:, 128:130], func=AF.Exp),
             reads=[lw_b], writes=[lw_b])
        k.op(k.dve, lambda: nc.vector.tensor_tensor(out=lw[:, 132:133], in0=lw[:, 130:131], in1=lw[:, 131:132],
                                                    op=ALU.subtract), reads=[lw_b], writes=[lw_b])
        k.op(k.dve, lambda: nc.vector.tensor_scalar(out=lw[:, 133:134], in0=lw[:, 132:133], scalar1=lam_init,
                                                    scalar2=-1.0, op0=ALU.add, op1=ALU.mult),
             reads=[lw_b], writes=[lw_b])
        neg_lam = lw[:, 133:134]
        nw = sb(st, nc, "anw", [128, 1], F32)
        nw_b = Buf()
        k.dma(k.sp, nw[:], anw_col, writes=[nw_b])
        k.op(k.dve, lambda: nc.vector.tensor_scalar(out=nw[:], in0=nw[:], scalar1=1.0 - lam_init, scalar2=None,
                                                    op0=ALU.mult), reads=[nw_b], writes=[nw_b])

        qT = Ring(st, nc, "qT", 2, [128, SEQ], BF16)
        kT = Ring(st, nc, "kT", 2, [128, SEQ], BF16)
        vh = Ring(st, nc, "vh", 2, [128, NT_(), 128], BF16)
        pT = Ring(st, nc, "pT", 6, [128, 512], BF16)
        f32r = Ring(st, nc, "af", 12, [128, 512], F32)
        yo = Ring(st, nc, "yao", 2, [128, 512], BF16)
        psr = Ring(st, nc, "psS", 4, [128, 512], F32, psum=True)
        po = [ps(st, nc, "po", [128, 512], F32) for _ in range(2)]
        pl = [ps(st, nc, "pl", [128, 512], F32) for _ in range(2)]
        po_b = [Buf(), Buf()]
        pl_b = [Buf(), Buf()]
        ones_bf = cbf[:, 1, :]
        tri = cbf[:, 2, :]

        todo = []
        for h in range(A_HEADS):
            q_t, q_b = qT.next()
            k_t, k_b = kT.next()
            v_t, v_b = vh.next()
            k.dma(k.sp, q_t[:], S["qkT"][h * 128:(h + 1) * 128, :], writes=[q_b])
            k.dma(k.sp, k_t[:], S["qkT"][1024 + h * 128:1024 + (h + 1) * 128, :], writes=[k_b])
            k.dma(k.sp, v_t[:], S["av"][:, h * 128:(h + 1) * 128].rearrange("(t p) d -> p t d", p=128),
                  writes=[v_b])
            for g in range(NG):
                nkt = 4 * g + 4
                pend = {}

                def stage_a(kt):
                    j = kt - 4 * g
                    c0 = 128 * j if j > 0 else 0
                    outs = []
                    sts = [psr.next() for _ in range(2)]
                    k.mm([(lambda m=m: nc.tensor.matmul(
                        sts[m][0][:, c0:512], k_t[m * 64:(m + 1) * 64, kt * 128:(kt + 1) * 128],
                        q_t[m * 64:(m + 1) * 64, g * 512 + c0:(g + 1) * 512], start=True, stop=True))
                        for m in range(2)], reads=[q_b, k_b], writes=[sts[0][1], sts[1][1]])
                    for m in range(2):
                        s_t, s_b = sts[m]
                        p_t, p_b = pT.next()
                        k.op(k.act, lambda: nc.scalar.activation(out=p_t[:, c0:512], in_=s_t[:, c0:512],
                                                                 func=AF.Exp, scale=A_DHEAD ** -0.5),
                             reads=[s_b], writes=[p_b])
                        if j >= 0:
                            k.op(k.pool, lambda: nc.gpsimd.tensor_tensor(out=p_t[:, c0:c0 + 128],
                                                                         in0=p_t[:, c0:c0 + 128], in1=tri,
                                                                         op=ALU.mult),
                                 reads=[p_b, cbf_b], writes=[p_b])
                        outs.append((p_t, p_b, c0))
                    pend[kt] = outs

                def stage_b(kt):
                    for m in range(2):
                        p_t, p_b, c0 = pend[kt][m]
                        k.mm([lambda: nc.tensor.matmul(po[m][:, c0:512], v_t[:, kt, :], p_t[:, c0:512],
                                                       start=(kt == 0), stop=(kt == nkt - 1)),
                              lambda: nc.tensor.matmul(pl[m][:, c0:512], ones_bf, p_t[:, c0:512],
                                                       start=(kt == 0), stop=(kt == nkt - 1))],
                             reads=[p_b, v_b, cbf_b], writes=[po_b[m], pl_b[m]])
                    del pend[kt]

                stage_a(0)
                for kt in range(1, nkt):
                    stage_a(kt)
                    stage_b(kt - 1)
                    if todo and kt >= 3:
                        todo.pop(0)()
                stage_b(nkt - 1)
                while todo:
                    todo.pop(0)()

                oc, lc = [], []
                for m in range(2):
                    o_t, o_b = f32r.next()
                    k.op(k.dve, lambda: nc.vector.tensor_copy(out=o_t[:], in_=po[m][:]), reads=[po_b[m]], writes=[o_b])
                    oc.append((o_t, o_b))
                    l_t, l_b = f32r.next()
                    k.op(k.act, lambda: nc.scalar.copy(out=l_t[:], in_=pl[m][:]), reads=[pl_b[m]], writes=[l_b])
                    lc.append((l_t, l_b))
                for m in range(2):
                    l_t, l_b = lc[m]
                    o_t, o_b = oc[m]
                    k.op(k.dve, lambda: nc.vector.reciprocal(out=l_t[:], in_=l_t[:]), reads=[l_b], writes=[l_b])
                    k.op(k.dve, lambda: nc.vector.tensor_tensor(out=o_t[:], in0=o_t[:], in1=l_t[:], op=ALU.mult),
                         reads=[o_b, l_b], writes=[o_b])
                d_t, d_b = oc[0]
                k.op(k.dve, lambda: nc.vector.scalar_tensor_tensor(out=d_t[:], in0=oc[1][0][:], scalar=neg_lam,
                                                                   in1=d_t[:], op0=ALU.mult, op1=ALU.add),
                     reads=[oc[1][1], d_b, lw_b], writes=[d_b])

                def make_tail(h=h, g=g, d_t=d_t, d_b=d_b):
                    st8 = {}

                    def e3():
                        st8["sq"] = f32r.next()
                        sq_t, sq_b = st8["sq"]
                        k.op(k.act, lambda: nc.scalar.activation(out=sq_t[:], in_=d_t[:], func=AF.Square),
                             reads=[d_b], writes=[sq_b])

                    def e4():
                        sq_t, sq_b = st8["sq"]
                        st8["ss"] = psr.next()
                        s_t, s_b = st8["ss"]
                        k.mm([lambda: nc.tensor.matmul(s_t[:], cf32[:, 0, :], sq_t[:], start=True, stop=True)],
                             reads=[sq_b, cf32_b], writes=[s_b])

                    def e5():
                        s_t, s_b = st8["ss"]
                        st8["rs"] = f32r.next()
                        rs_t, rs_b = st8["rs"]
                        rms_rstd(k, s_t[:], s_b, rs_t[:], rs_b, 128)

                    def e6():
                        rs_t, rs_b = st8["rs"]
                        y_t, y_b = yo.next()
                        k.op(k.dve, lambda: nc.vector.scalar_tensor_tensor(out=y_t[:], in0=d_t[:], scalar=nw[:, 0:1],
                                                                           in1=rs_t[:], op0=ALU.mult, op1=ALU.mult),
                             reads=[d_b, nw_b, rs_b], writes=[y_b])
                        k.dma(k.sp, S["yaT"][h * 128:(h + 1) * 128, g * 512:(g + 1) * 512], y_t[:], reads=[y_b])
                    return [e3, e4, e5, e6]
                todo.extend(make_tail())
        while todo:
            todo.pop(0)()
        k.barrier()


def NT_():
    return SEQ // 128


def make_scratch(nc, kind="Internal"):
    def d(name, shape, dt):
        return nc.dram_tensor(name, shape, dt, kind=kind).ap()
    return {
        "qkT": d("s_qkT", [2048, SEQ], BF16),
        "av": d("s_av", [SEQ, 1024], BF16),
        "mqkT": d("s_mqkT", [1024, SEQ], F32),
        "mv": d("s_mv", [SEQ, 1024], F32),
        "mo": d("s_mo", [SEQ, 1024], F32),
        "mif": d("s_mif", [SEQ, 16], F32),
        "glT": d("s_glT", [2048, SEQ], F32),
        "yaT": d("s_yaT", [1024, SEQ], BF16),
        "ymT": d("s_ymT", [1024, SEQ], BF16),
    }


def host_consts():
    import ml_dtypes
    p = np.arange(128)[:, None]
    f = np.arange(128)[None, :]
    tri = (p <= f).astype(np.float32)
    cbf = np.stack([np.eye(128, dtype=np.float32), np.ones((128, 128), np.float32), tri], axis=1)
    cf32 = np.stack([np.ones((128, 128), np.float32), tri, tri * (M_DQK ** -0.5)], axis=1)
    return {"cbf": np.ascontiguousarray(cbf).astype(ml_dtypes.bfloat16),
            "cf32": np.ascontiguousarray(cf32).astype(np.float32)}


def load_consts(k, st):
    nc = k.nc
    cbf_d = nc.dram_tensor("cbf", [128, 3, 128], BF16, kind="ExternalInput").ap()
    cf32_d = nc.dram_tensor("cf32", [128, 3, 128], F32, kind="ExternalInput").ap()
    cbf = sb(st, nc, "cbf_sb", [128, 3, 128], BF16)
    cf32 = sb(st, nc, "cf32_sb", [128, 3, 128], F32)
    cbf_b, cf32_b = Buf(), Buf()
    k.dma(k.sp, cbf[:], cbf_d, writes=[cbf_b])
    k.dma(k.sp, cf32[:], cf32_d, writes=[cf32_b])
    return cbf, cbf_b, cf32, cf32_b


def phase_mlstm(k, cbf, cbf_b, cf32, cf32_b, S, convw_pc, convb_pc, igb, fgb, mnw):
    nc = k.nc
    NTL = SEQ // 128
    ident = cbf[:, 0, :]
    ones32 = cf32[:, 0, :]
    triu32 = cf32[:, 1, :]
    mask8 = cf32[:, 2, :]
    with ExitStack() as st:
        qkc = sb(st, nc, "qkc", [128, 8, SEQ], BF16)
        qkc_b = [Buf() for _ in range(8)]
        cw = sb(st, nc, "cw", [128, 8, 4], F32)
        cb = sb(st, nc, "cb", [128, 8], F32)
        cw_b, cb_b = Buf(), Buf()
        k.dma(k.sp, cw[:], convw_pc, writes=[cw_b])
        k.dma(k.sp, cb[:], convb_pc, writes=[cb_b])
        with ExitStack() as st1:
            xp = Ring(st1, nc, "xp", 2, [128, SEQ + 3], F32)
            acc = Ring(st1, nc, "cacc", 1, [128, SEQ], F32)
            for i in range(2):
                k.op(k.pool, lambda: nc.gpsimd.memset(xp.t[i][:, 0:3], 0.0), writes=[xp.b[i]])
            for c in range(8):
                x_t, x_b = xp.next()
                k.dma(k.sp, x_t[:, 3:SEQ + 3], S["mqkT"][c * 128:(c + 1) * 128, :], writes=[x_b])
                a_t, a_b = acc.next()
                k.op(k.dve, lambda: nc.vector.tensor_scalar(out=a_t[:], in0=x_t[:, 0:SEQ], scalar1=cw[:, c, 0:1],
                                                            scalar2=None, op0=ALU.mult),
                     reads=[x_b, cw_b], writes=[a_b])
                for j in range(1, 4):
                    k.op(k.dve, lambda: nc.vector.scalar_tensor_tensor(
                        out=a_t[:], in0=x_t[:, j:j + SEQ], scalar=cw[:, c, j:j + 1], in1=a_t[:],
                        op0=ALU.mult, op1=ALU.add), reads=[x_b, cw_b, a_b], writes=[a_b])
                k.op(k.act, lambda: nc.scalar.activation(out=qkc[:, c, :], in_=a_t[:], func=AF.Silu,
                                                         bias=cb[:, c:c + 1]),
                     reads=[a_b, cb_b], writes=[qkc_b[c]])
        k.barrier()

        ktok = sb(st, nc, "ktok", [128, NTL, 512], BF16)
        ktok_b = Buf()
        NC8 = NTL * 8
        gt = sb(st, nc, "gt", [128, NTL, 16], F32)
        gt_b = Buf()
        k.dma(k.sp, gt[:], S["mif"].rearrange("(t p) c -> p t c", p=128), writes=[gt_b])
        igb_t, igb_b = load_bcast(k, st, "igb", igb, 8)
        fgb_t, fgb_b = load_bcast(k, st, "fgb", fgb, 8)
        mnw_t, mnw_b = load_bcast(k, st, "mnw", mnw, 1024)
        IG = sb(st, nc, "IG", [128, NTL, 8], F32)
        NLF = sb(st, nc, "NLF", [128, NTL, 8], F32)
        NGc = sb(st, nc, "NGc", [128, NTL, 8], F32)
        EQ = sb(st, nc, "EQ", [128, NTL, 8], F32)
        EK = sb(st, nc, "EK", [128, NTL, 8], F32)
        EE = sb(st, nc, "EE", [128, NTL, 8], F32)
        IG_b, NLF_b, NG_b, EQ_b, EK_b, EE_b = [Buf() for _ in range(6)]

        pa = Ring(st, nc, "pa", 2, [128, 4, 128], F32, psum=True)
        pr = Ring(st, nc, "pr", 2, [128, 4, 128], F32, psum=True)
        pu = Ring(st, nc, "pu", 2, [128, 4, 128], F32, psum=True)
        pm = ps(st, nc, "pm", [128, 512], F32)
        prd, prd_b = pm[:, 0:8], Buf()
        pun, pun_b = pm[:, 8:16], Buf()
        ptr = ps(st, nc, "ptrm", [128, 8, 128], BF16)
        ptr_b = Buf()

        for t in range(NTL):
            k.mm([(lambda c=c: nc.tensor.transpose(ptr[:, c, :], qkc[:, 4 + c, t * 128:(t + 1) * 128], ident))
                  for c in range(4)], reads=qkc_b[4:8] + [cbf_b], writes=[ptr_b])
            k.op(k.act, lambda: nc.scalar.copy(out=ktok[:, t, :], in_=ptr[:, 0:4, :].rearrange("p a b -> p (a b)")),
                 reads=[ptr_b], writes=[ktok_b])

        k.op(k.dve, lambda: nc.vector.tensor_tensor(out=IG[:], in0=gt[:, :, 0:8],
                                                    in1=igb_t[:].unsqueeze(1).to_broadcast([128, NTL, 8]),
                                                    op=ALU.add), reads=[gt_b, igb_b], writes=[IG_b])
        k.op(k.dve, lambda: nc.vector.tensor_tensor(out=NLF[:], in0=gt[:, :, 8:16],
                                                    in1=fgb_t[:].unsqueeze(1).to_broadcast([128, NTL, 8]),
                                                    op=ALU.add), reads=[gt_b, fgb_b], writes=[NLF_b])
        k.op(k.act, lambda: nc.scalar.activation(out=NLF[:], in_=NLF[:], func=AF.Exp, scale=-1.0),
             reads=[NLF_b], writes=[NLF_b])
        c_one = k.const(1.0)
        k.op(k.act, lambda: nc.scalar.activation(out=NLF[:], in_=NLF[:], func=AF.Ln, bias=c_one),
             reads=[NLF_b, k.const_b], writes=[NLF_b])
        nlf2 = NLF[:].rearrange("p a b -> p (a b)")
        pg_t, pg_b = pa.next()
        pg2 = pg_t[:].rearrange("p a b -> p (a b)")
        k.mm([lambda: nc.tensor.matmul(pg2[:, 0:NC8], triu32, nlf2, start=True, stop=True)],
             reads=[NLF_b, cf32_b], writes=[pg_b])
        k.op(k.dve, lambda: nc.vector.tensor_copy(out=NGc[:].rearrange("p a b -> p (a b)"), in_=pg2[:, 0:NC8]),
             reads=[pg_b], writes=[NG_b])
        pe_t, pe_b = pa.next()
        pe2 = pe_t[:].rearrange("p a b -> p (a b)")
        k.mm([lambda: nc.tensor.matmul(pe2[:, 0:NC8], ones32, nlf2, start=True, stop=True)],
             reads=[NLF_b, cf32_b], writes=[pe_b])
        k.op(k.act, lambda: nc.scalar.activation(out=EE[:].rearrange("p a b -> p (a b)"), in_=pe2[:, 0:NC8],
                                                 func=AF.Exp, scale=-1.0), reads=[pe_b], writes=[EE_b])
        k.op(k.act, lambda: nc.scalar.activation(out=EQ[:], in_=NGc[:], func=AF.Exp, scale=-1.0),
             reads=[NG_b], writes=[EQ_b])
        k.op(k.dve, lambda: nc.vector.tensor_tensor(out=EK[:], in0=IG[:], in1=NGc[:], op=ALU.add),
             reads=[IG_b, NG_b], writes=[EK_b])
        k.op(k.act, lambda: nc.scalar.activation(out=EK[:], in_=EK[:], func=AF.Exp), reads=[EK_b], writes=[EK_b])

        mvr = Ring(st, nc, "mvr", 2, [128, 8, 128], F32)
        mor = Ring(st, nc, "mor", 2, [128, 1024], F32)
        vtr = Ring(st, nc, "vtr", 2, [128, 8, 128], BF16)
        ekr = Ring(st, nc, "ekr", 2, [128, 8], BF16)
        ptT = Ring(st, nc, "ptT", 2, [128, 8, 128], BF16)
        Tst = sb(st, nc, "Tst", [128, 4, 129], F32)
        T_b = [Buf() for _ in range(8)]
        S8r = Ring(st, nc, "S8r", 2, [128, 4, 129], BF16)
        S8_bufs = {0: [Buf() for _ in range(8)], 1: [Buf() for _ in range(8)]}
        sm = Ring(st, nc, "msm", 2, [128, 48], F32)
        ho = Ring(st, nc, "mho", 1, [128, 8, 128], F32)
        sq = Ring(st, nc, "msq", 1, [128, 8, 128], F32)
        y1 = Ring(st, nc, "my1", 1, [128, 8, 128], F32)
        ymb = Ring(st, nc, "ymb", 2, [128, 1024], BF16)
        ymT = Ring(st, nc, "ymTs", 2, [128, 8, 128], BF16)
        s8_prev = None
        for t in range(NTL):
            tsl = slice(t * 128, (t + 1) * 128)
            mv_t, mv_b = mvr.next()
            k.dma(k.sp, mv_t[:], S["mv"][tsl, :].rearrange("p (a b) -> p a b", a=8), writes=[mv_b])
            mo_t, mo_b = mor.next()
            k.dma(k.sp, mo_t[:], S["mo"][tsl, :], writes=[mo_b])
            vt_t, vt_b = vtr.next()
            k.op(k.dve, lambda: nc.vector.tensor_tensor(out=vt_t[:], in0=mv_t[:],
                                                        in1=EK[:, t, :].unsqueeze(2).to_broadcast([128, 8, 128]),
                                                        op=ALU.mult), reads=[mv_b, EK_b], writes=[vt_b])
            ek_t, ek_b = ekr.next()
            k.op(k.dve, lambda: nc.vector.tensor_copy(out=ek_t[:], in_=EK[:, t, :]), reads=[EK_b], writes=[ek_b])
            pT_t, pT_b = ptT.next()
            a_te = [pa.next() for _ in range(2)]
            fns = []
            for i in range(4):
                for e in range(2):
                    P0 = e * 64
                    fns.append(lambda i=i, e=e, P0=P0: nc.tensor.matmul(
                        a_te[e][0][:, i, :], qkc[P0:P0 + 64, 4 + i, tsl], qkc[P0:P0 + 64, i, tsl],
                        start=True, stop=True))
            k.mm(fns, reads=qkc_b, writes=[a_te[0][1], a_te[1][1]])
            for e in range(2):
                k.op(k.dve, lambda: nc.vector.tensor_tensor(
                    out=pT_t[:, 4 * e:4 * e + 4, :], in0=a_te[e][0][:],
                    in1=mask8.unsqueeze(1).to_broadcast([128, 4, 128]), op=ALU.mult),
                    reads=[a_te[e][1], cf32_b], writes=[pT_b])
            r_ts = []
            for bk in range(2):
                r_t, r_b = pr.next()
                fns = []
                for hh in range(4):
                    h = 4 * bk + hh
                    P0 = (h % 2) * 64
                    fns.append(lambda h=h, hh=hh: nc.tensor.matmul(
                        r_t[:, hh, :], pT_t[:, (h % 2) * 4 + h // 2, :], vt_t[:, h, :], start=True, stop=(t == 0)))
                    if t > 0:
                        fns.append(lambda h=h, hh=hh, P0=P0: nc.tensor.matmul(
                            r_t[:, hh, :], qkc[P0:P0 + 64, h // 2, tsl], s8_prev[0][P0:P0 + 64, h // 2, 0:128],
                            start=False, stop=True))
                rd = [pT_b, vt_b] + qkc_b + (s8_prev[1] if t > 0 else [])
                k.mm(fns, reads=rd, writes=[r_b])
                r_ts.append((r_t, r_b))
            fns = []
            for h in range(8):
                P0 = (h % 2) * 64
                fns.append(lambda h=h: nc.tensor.matmul(prd[:, h:h + 1], pT_t[:, (h % 2) * 4 + h // 2, :],
                                                        ek_t[:, h:h + 1], start=True, stop=(t == 0)))
                if t > 0:
                    fns.append(lambda h=h, P0=P0: nc.tensor.matmul(
                        prd[:, h:h + 1], qkc[P0:P0 + 64, h // 2, tsl], s8_prev[0][P0:P0 + 64, h // 2, 128:129],
                        start=False, stop=True))
            k.mm(fns, reads=[pT_b, ek_b] + qkc_b + (s8_prev[1] if t > 0 else []), writes=[prd_b])
            u_ts = []
            for bk in range(2):
                u_t, u_b = pu.next()
                fns = []
                for hh in range(4):
                    h = 4 * bk + hh
                    fns.append(lambda h=h, hh=hh: nc.tensor.matmul(
                        u_t[:, hh, :], ktok[:, t, (h // 2) * 128:(h // 2 + 1) * 128], vt_t[:, h, :],
                        start=True, stop=True))
                k.mm(fns, reads=[ktok_b, vt_b], writes=[u_b])
                u_ts.append((u_t, u_b))
            k.mm([(lambda h=h: nc.tensor.matmul(pun[:, h:h + 1], ktok[:, t, (h // 2) * 128:(h // 2 + 1) * 128],
                                                ek_t[:, h:h + 1], start=True, stop=True)) for h in range(8)],
                 reads=[ktok_b, ek_b], writes=[pun_b])
            s8_t, _ = S8r.next()
            s8_bl = S8_bufs[t % 2]
            for h in range(8):
                P0 = (h % 2) * 64
                hp = h // 2
                u_t, u_b = u_ts[h // 4]
                if t == 0:
                    k.op(k.dve, lambda: nc.vector.tensor_copy(out=Tst[P0:P0 + 64, hp, 0:128],
                                                              in_=u_t[P0:P0 + 64, h % 4, :]),
                         reads=[u_b], writes=[T_b[h]])
                    k.op(k.dve, lambda: nc.vector.tensor_copy(out=Tst[P0:P0 + 64, hp, 128:129],
                                                              in_=pun[P0:P0 + 64, h:h + 1]),
                         reads=[pun_b], writes=[T_b[h]])
                else:
                    k.op(k.dve, lambda: nc.vector.scalar_tensor_tensor(
                        out=Tst[P0:P0 + 64, hp, 0:128], in0=Tst[P0:P0 + 64, hp, 0:128],
                        scalar=EE[P0:P0 + 64, t - 1, h:h + 1], in1=u_t[P0:P0 + 64, h % 4, :],
                        op0=ALU.mult, op1=ALU.add), reads=[T_b[h], EE_b, u_b], writes=[T_b[h]])
                    k.op(k.dve, lambda: nc.vector.scalar_tensor_tensor(
                        out=Tst[P0:P0 + 64, hp, 128:129], in0=Tst[P0:P0 + 64, hp, 128:129],
                        scalar=EE[P0:P0 + 64, t - 1, h:h + 1], in1=pun[P0:P0 + 64, h:h + 1],
                        op0=ALU.mult, op1=ALU.add), reads=[T_b[h], EE_b, pun_b], writes=[T_b[h]])
                if t < NTL - 1:
                    k.op(k.pool, lambda: nc.gpsimd.tensor_scalar(
                        out=s8_t[P0:P0 + 64, hp, :], in0=Tst[P0:P0 + 64, hp, :], scalar1=EE[P0:P0 + 64, t, h:h + 1],
                        scalar2=M_DQK ** -0.5, op0=ALU.mult, op1=ALU.mult),
                        reads=[T_b[h], EE_b], writes=[s8_bl[h]])
            s8_prev = (s8_t, s8_bl)
            m_t, m_b = sm.next()
            dn, dneg, rc, cc, ss, rstd = (m_t[:, 0:8], m_t[:, 8:16], m_t[:, 16:24], m_t[:, 24:32],
                                          m_t[:, 32:40], m_t[:, 40:48])
            k.op(k.dve, lambda: nc.vector.tensor_tensor(out=dn, in0=prd, in1=EQ[:, t, :], op=ALU.mult),
                 reads=[prd_b, EQ_b], writes=[m_b])
            k.op(k.dve, lambda: nc.vector.tensor_scalar(out=dneg, in0=dn, scalar1=-1.0, scalar2=None, op0=ALU.mult),
                 reads=[m_b], writes=[m_b])
            k.op(k.dve, lambda: nc.vector.tensor_tensor(out=dn, in0=dn, in1=dneg, op=ALU.max),
                 reads=[m_b], writes=[m_b])
            k.op(k.dve, lambda: nc.vector.tensor_scalar(out=dn, in0=dn, scalar1=1.0, scalar2=None, op0=ALU.max),
                 reads=[m_b], writes=[m_b])
            k.op(k.dve, lambda: nc.vector.reciprocal(out=rc, in_=dn), reads=[m_b], writes=[m_b])
            k.op(k.dve, lambda: nc.vector.tensor_tensor(out=cc, in0=rc, in1=EQ[:, t, :], op=ALU.mult),
                 reads=[m_b, EQ_b], writes=[m_b])
            ho_t, ho_b = ho.next()
            for bk in range(2):
                r_t, r_b = r_ts[bk]
                k.op(k.dve, lambda: nc.vector.tensor_tensor(
                    out=ho_t[:, 4 * bk:4 * bk + 4, :], in0=r_t[:],
                    in1=cc[:, 4 * bk:4 * bk + 4].unsqueeze(2).to_broadcast([128, 4, 128]), op=ALU.mult),
                    reads=[r_b, m_b], writes=[ho_b])
            sq_t, sq_b = sq.next()
            k.op(k.pool, lambda: nc.gpsimd.tensor_tensor(out=sq_t[:], in0=ho_t[:], in1=ho_t[:], op=ALU.mult),
                 reads=[ho_b], writes=[sq_b])
            k.op(k.dve, lambda: nc.vector.tensor_reduce(out=ss, in_=sq_t[:], axis=AX.X, op=ALU.add),
                 reads=[sq_b], writes=[m_b])
            rms_rstd(k, ss, m_b, rstd, m_b, M_DV)
            y1_t, y1_b = y1.next()
            k.op(k.dve, lambda: nc.vector.tensor_tensor(
                out=y1_t[:], in0=ho_t[:], in1=rstd.unsqueeze(2).to_broadcast([128, 8, 128]), op=ALU.mult),
                reads=[ho_b, m_b], writes=[y1_b])
            y1f = y1_t[:].rearrange("p a b -> p (a b)")
            k.op(k.pool, lambda: nc.gpsimd.tensor_tensor(out=y1f, in0=y1f, in1=mnw_t[:], op=ALU.mult),
                 reads=[y1_b, mnw_b], writes=[y1_b])
            yb_t, yb_b = ymb.next()
            k.op(k.dve, lambda: nc.vector.tensor_tensor(out=yb_t[:], in0=y1f, in1=mo_t[:], op=ALU.mult),
                 reads=[y1_b, mo_b], writes=[yb_b])
            k.mm([(lambda c=c: nc.tensor.transpose(ptr[:, c, :], yb_t[:, c * 128:(c + 1) * 128], ident))
                  for c in range(8)], reads=[yb_b, cbf_b], writes=[ptr_b])
            yT_t, yT_b = ymT.next()
            k.op(k.act, lambda: nc.scalar.copy(out=yT_t[:], in_=ptr[:]), reads=[ptr_b], writes=[yT_b])
            k.dma(k.sp, S["ymT"][:, tsl].rearrange("(c p) n -> p c n", p=128), yT_t[:], reads=[yT_b])
        k.barrier()


def phase_out(k, S, x_io, w_pa, w_pm, w_out, n_post):
    nc = k.nc
    TT = 512
    NTT = SEQ // TT
    NS = TT // 128
    with ExitStack() as st:
        ws = []
        for nm, w in (("wpa", w_pa), ("wpm", w_pm), ("wout", w_out)):
            t = sb(st, nc, nm, [128, 8, D_MODEL], BF16)
            b = Buf()
            for hh in range(2):
                k.dma(k.pool, t[:, hh * 4:(hh + 1) * 4, :],
                      w[hh * 512:(hh + 1) * 512, :].rearrange("(c p) n -> p c n", p=128), writes=[b], join=True)
            ws.append((t, b))
        (wpa, wpa_b), (wpm, wpm_b), (wout, wout_b) = ws
        npost, npost_b = load_bcast(k, st, "npost", n_post, D_MODEL)
        yaT = Ring(st, nc, "yaTs", 2, [128, 8, TT], BF16)
        ymT = Ring(st, nc, "ymTs", 2, [128, 8, TT], BF16)
        gT = Ring(st, nc, "gTs", 3, [128, 2, TT], F32)
        mg = Ring(st, nc, "mg", 1, [128, 8, TT], BF16)
        t1 = Ring(st, nc, "t1", 2, [128, TT], F32)
        xr = Ring(st, nc, "xr", 2, [128, D_MODEL], F32)
        yo = Ring(st, nc, "yo", 1, [128, D_MODEL], F32)
        junk = Ring(st, nc, "junk", 1, [128, D_MODEL], BF16)
        stat = Ring(st, nc, "stat", 4, [128, 4], F32)
        ppa = Ring(st, nc, "ppa", 2, [128, TT], F32, psum=True)
        ppm = Ring(st, nc, "ppm", 2, [128, TT], F32, psum=True)
        py = Ring(st, nc, "py", 1, [128, 2, 512], F32, psum=True)
        for t in range(NTT):
            T0 = t * TT
            a_t, a_b = yaT.next()
            m_t, m_b = ymT.next()
            k.dma(k.sp, a_t[:], S["yaT"][:, T0:T0 + TT].rearrange("(c p) n -> p c n", p=128), writes=[a_b])
            k.dma(k.sp, m_t[:], S["ymT"][:, T0:T0 + TT].rearrange("(c p) n -> p c n", p=128), writes=[m_b])
            g_t, g_b = mg.next()
            for c in range(8):
                gg_t, gg_b = gT.next()
                k.dma(k.sp, gg_t[:], S["glT"].rearrange("(a r) n -> r a n", a=2)[c * 128:(c + 1) * 128, :, T0:T0 + TT],
                      writes=[gg_b])
                pa_t, pa_b = ppa.next()
                pm_t, pm_b = ppm.next()
                k.mm([(lambda kc=kc: nc.tensor.matmul(pa_t[:], wpa[:, kc, c * 128:(c + 1) * 128], a_t[:, kc, :],
                                                      start=(kc == 0), stop=(kc == 7))) for kc in range(8)],
                     reads=[wpa_b, a_b], writes=[pa_b])
                k.mm([(lambda kc=kc: nc.tensor.matmul(pm_t[:], wpm[:, kc, c * 128:(c + 1) * 128], m_t[:, kc, :],
                                                      start=(kc == 0), stop=(kc == 7))) for kc in range(8)],
                     reads=[wpm_b, m_b], writes=[pm_b])
                u_t, u_b = t1.next()
                k.op(k.dve, lambda: nc.vector.tensor_tensor(out=u_t[:], in0=pa_t[:], in1=gg_t[:, 0, :], op=ALU.mult),
                     reads=[pa_b, gg_b], writes=[u_b])
                v_t, v_b = t1.next()
                k.op(k.dve, lambda: nc.vector.tensor_tensor(out=v_t[:], in0=pm_t[:], in1=gg_t[:, 1, :], op=ALU.mult),
                     reads=[pm_b, gg_b], writes=[v_b])
                k.op(k.pool, lambda: nc.gpsimd.tensor_tensor(out=g_t[:, c, :], in0=u_t[:], in1=v_t[:], op=ALU.add),
                     reads=[u_b, v_b], writes=[g_b])
            for s in range(NS):
                tok0 = T0 + s * 128
                y_t, y_b = py.next()
                fns = []
                for hf in range(2):
                    for kc in range(8):
                        fns.append(lambda kc=kc, hf=hf: nc.tensor.matmul(
                            y_t[:, hf, :], g_t[:, kc, s * 128:(s + 1) * 128], wout[:, kc, hf * 512:(hf + 1) * 512],
                            start=(kc == 0), stop=(kc == 7)))
                k.mm(fns, reads=[g_b, wout_b], writes=[y_b])
                xr_t, xr_b = xr.next()
                k.dma(k.sp, xr_t[:], x_io[tok0:tok0 + 128, :], writes=[xr_b])
                j_t, j_b = junk.next()
                s_t, s_b = stat.next()
                for hf in range(2):
                    k.op(k.act, lambda: nc.scalar.activation(out=j_t[:, hf * 512:(hf + 1) * 512], in_=y_t[:, hf, :],
                                                             func=AF.Square, accum_out=s_t[:, 2 + hf:3 + hf]),
                         reads=[y_b], writes=[j_b, s_b])
                k.op(k.dve, lambda: nc.vector.tensor_tensor(out=s_t[:, 0:1], in0=s_t[:, 2:3], in1=s_t[:, 3:4],
                                                            op=ALU.add), reads=[s_b], writes=[s_b])
                rms_rstd(k, s_t[:, 0:1], s_b, s_t[:, 1:2], s_b, D_MODEL)
                o_t, o_b = yo.next()
                for hf in range(2):
                    k.op(k.dve, lambda: nc.vector.tensor_tensor(
                        out=o_t[:, hf * 512:(hf + 1) * 512], in0=y_t[:, hf, :], in1=npost[:, hf * 512:(hf + 1) * 512],
                        op=ALU.mult), reads=[y_b, npost_b], writes=[o_b])
                k.op(k.dve, lambda: nc.vector.scalar_tensor_tensor(
                    out=xr_t[:], in0=o_t[:], scalar=s_t[:, 1:2], in1=xr_t[:], op0=ALU.mult, op1=ALU.add),
                    reads=[o_b, xr_b, s_b], writes=[xr_b])
                k.dma(k.sp, x_io[tok0:tok0 + 128, :], xr_t[:], reads=[xr_b])
        k.barrier()


PARAM_NAMES = ["ffn1_norm_pre", "ffn1_w_gu", "ffn1_w_down", "ffn1_norm_post", "mix_norm_pre", "w_in",
               "attn_lam_q1", "attn_lam_k1", "attn_lam_q2", "attn_lam_k2", "attn_norm_w", "conv_w", "conv_b",
               "igate_b", "fgate_b", "mlstm_norm_w", "w_proj_a", "w_proj_m", "gate_b", "w_out", "mix_norm_post",
               "ffn2_norm_pre", "ffn2_w_gu", "ffn2_w_down", "ffn2_norm_post"]


def build_program(depth=DEPTH):
    nc = bass.Bass("TRN2", target_bir_lowering=False)

    def din(name, shape, dt=F32):
        return nc.dram_tensor(name, shape, dt, kind="ExternalInput").ap()

    L = depth
    x = din("x", [SEQ, D_MODEL])
    out = nc.dram_tensor("out", [SEQ, D_MODEL], F32, kind="ExternalOutput").ap()
    P = {
        "ffn1_norm_pre": din("ffn1_norm_pre", [L, D_MODEL]),
        "ffn1_w_gu": din("ffn1_w_gu", [L, D_MODEL, 2 * D_FF]),
        "ffn1_w_down": din("ffn1_w_down", [L, D_FF, D_MODEL]),
        "ffn1_norm_post": din("ffn1_norm_post", [L, D_MODEL]),
        "mix_norm_pre": din("mix_norm_pre", [L, D_MODEL]),
        "w_in": din("w_in", [L, D_MODEL, C_IN]),
        "lamp": din("lamp", [L, 256]),
        "anw": din("anw", [L, 128, 1]),
        "convw_pc": din("convw_pc", [L, 128, 8, 4]),
        "convb_pc": din("convb_pc", [L, 128, 8]),
        "igate_b": din("igate_b", [L, 8]),
        "fgate_b": din("fgate_b", [L, 8]),
        "mlstm_norm_w": din("mlstm_norm_w", [L, 1024]),
        "w_proj_a": din("w_proj_a", [L, 1024, D_MODEL]),
        "w_proj_m": din("w_proj_m", [L, 1024, D_MODEL]),
        "gate_b_pc": din("gate_b_pc", [L, 128, 16]),
        "w_out": din("w_out", [L, D_MODEL, D_MODEL]),
        "mix_norm_post": din("mix_norm_post", [L, D_MODEL]),
        "ffn2_norm_pre": din("ffn2_norm_pre", [L, D_MODEL]),
        "ffn2_w_gu": din("ffn2_w_gu", [L, D_MODEL, 2 * D_FF]),
        "ffn2_w_down": din("ffn2_w_down", [L, D_FF, D_MODEL]),
        "ffn2_norm_post": din("ffn2_norm_post", [L, D_MODEL]),
    }
    S = make_scratch(nc)
    k = K(nc)
    with ExitStack() as st:
        cbf, cbf_b, cf32, cf32_b = load_consts(k, st)
        ident = cbf[:, 0, :]
        for l in range(L):
            lam_init = 0.8 - 0.6 * math.exp(-0.3 * l)
            phase_ffn(k, ident, cbf_b, x if l == 0 else out, out, P["ffn1_w_gu"][l], P["ffn1_w_down"][l],
                      P["ffn1_norm_pre"][l], P["ffn1_norm_post"][l])
            phase_proj(k, ident, cbf_b, out, P["w_in"][l], P["mix_norm_pre"][l], P["gate_b_pc"][l], S)
            phase_attn(k, cbf, cbf_b, cf32, cf32_b, S, P["lamp"][l], P["anw"][l], lam_init)
            phase_mlstm(k, cbf, cbf_b, cf32, cf32_b, S, P["convw_pc"][l], P["convb_pc"][l], P["igate_b"][l],
                        P["fgate_b"][l], P["mlstm_norm_w"][l])
            phase_out(k, S, out, P["w_proj_a"][l], P["w_proj_m"][l], P["w_out"][l], P["mix_norm_post"][l])
            phase_ffn(k, ident, cbf_b, out, out, P["ffn2_w_gu"][l], P["ffn2_w_down"][l],
                      P["ffn2_norm_pre"][l], P["ffn2_norm_post"][l])
        k.finish()
    return nc


def host_params(inp, depth=DEPTH):
    f = lambda a: np.ascontiguousarray(np.asarray(a, dtype=np.float32))
    L = depth
    d = {n: f(inp[n]) for n in ["ffn1_norm_pre", "ffn1_w_gu", "ffn1_w_down", "ffn1_norm_post", "mix_norm_pre",
                                "w_in", "igate_b", "fgate_b", "w_proj_a", "w_proj_m", "w_out", "mix_norm_post",
                                "ffn2_norm_pre", "ffn2_w_gu", "ffn2_w_down", "ffn2_norm_post"]}
    d["lamp"] = f(np.concatenate([inp["attn_lam_q1"], inp["attn_lam_q2"], inp["attn_lam_k1"], inp["attn_lam_k2"]],
                                 axis=1))
    d["anw"] = f(np.asarray(inp["attn_norm_w"]).reshape(L, 128, 1))
    cw = np.asarray(inp["conv_w"])
    d["convw_pc"] = f(cw.transpose(0, 2, 1).reshape(L, 8, 128, 4).transpose(0, 2, 1, 3))
    d["convb_pc"] = f(np.asarray(inp["conv_b"]).reshape(L, 8, 128).transpose(0, 2, 1))
    d["mlstm_norm_w"] = f(np.asarray(inp["mlstm_norm_w"]).reshape(L, 1024))
    d["gate_b_pc"] = f(np.asarray(inp["gate_b"]).reshape(L, 16, 128).transpose(0, 2, 1))
    d.update(host_consts())
    return d


_NC_CACHE = {}


def kernel(**inputs):
    if "nc" not in _NC_CACHE:
        _NC_CACHE["nc"] = build_program()
    nc = _NC_CACHE["nc"]
    params = host_params(inputs)
    x = np.asarray(inputs["x"], dtype=np.float32)
    in_maps = []
    for b in range(BATCH):
        m = dict(params)
        m["x"] = np.ascontiguousarray(x[b])
        in_maps.append(m)
    res = run_bass_kernel_spmd(nc, in_maps, core_ids=list(range(BATCH)))
    return np.stack([np.asarray(r["out"], dtype=np.float32) for r in res.results], axis=0)
```

```python
import math
from contextlib import ExitStack

import numpy as np
import concourse.bass as bass
import concourse.mybir as mybir
from concourse.bass_utils import run_bass_kernel_spmd

F32 = mybir.dt.float32
BF16 = mybir.dt.bfloat16
AF = mybir.ActivationFunctionType
ALU = mybir.AluOpType
AX = mybir.AxisListType

D_MODEL = 1024
BATCH = 8
SEQ = 4096
DEPTH = 2
EPS = 1e-6
A_HEADS = 8
A_DHEAD = 64
M_HEADS = 8
M_DQK = 64
M_DV = 128
D_FF = 2816
C_IN = 8208
NT = SEQ // 128

O_AQ, O_AK, O_AV = 0, 1024, 2048
O_MQ, O_MK, O_MV, O_MO = 3072, 3584, 4096, 5120
O_MI, O_MF, O_GL = 6144, 6152, 6160


class Sem:
    def __init__(self, nc, name):
        self.h = nc.alloc_semaphore(name)
        self.v = 0
        self.name = name


class Buf:
    __slots__ = ("w", "r", "name")

    def __init__(self, name=""):
        self.w = {}
        self.r = {}
        self.name = name


class Eng:
    def __init__(self, k, name, e, n_dma_sems=0):
        self.k = k
        self.name = name
        self.e = e
        self.sem = Sem(k.nc, "s_" + name)
        self.waited = {}
        self.dma_sems = [Sem(k.nc, f"d_{name}{i}") for i in range(n_dma_sems)]
        self.dma_rr = 0

    def wait(self, tok):
        sem, v = tok
        if self.waited.get(sem, 0) >= v:
            return
        self.e.wait_ge(sem.h, v)
        self.waited[sem] = v


class K:
    def __init__(self, nc):
        self.nc = nc
        self.pe = Eng(self, "pe", nc.tensor)
        self.act = Eng(self, "act", nc.scalar)
        self.dve = Eng(self, "dve", nc.vector)
        self.pool = Eng(self, "pool", nc.gpsimd, n_dma_sems=4)
        self.sp = Eng(self, "sp", nc.sync, n_dma_sems=20)
        self.engs = [self.pe, self.act, self.dve, self.pool, self.sp]
        self.const_t = nc.alloc_sbuf_tensor("const_cols", [128, 32], F32)
        self.const_b = Buf("consts")
        self.consts = {}
        for v in (EPS, 0.0, math.log(0.5), 1.0):
            self.const(v)

    def const(self, val):
        val = float(val)
        if val not in self.consts:
            i = len(self.consts)
            assert i < self.const_t.shape[1]
            ap = self.const_t[:, i:i + 1]
            self.op(self.pool, lambda: self.nc.gpsimd.memset(ap, val), writes=[self.const_b])
            self.consts[val] = ap
        return self.consts[val]

    def _deps(self, eng, reads, writes, join=False):
        for b in reads:
            for s, v in b.w.items():
                eng.wait((s, v))
        for b in writes:
            if not join:
                for s, v in b.w.items():
                    eng.wait((s, v))
            for s, v in b.r.items():
                eng.wait((s, v))

    def _mark(self, tok, reads, writes, join=False):
        s, v = tok
        for b in reads:
            if b.r.get(s, 0) < v:
                b.r[s] = v
        for b in writes:
            if join:
                b.w[s] = v
            else:
                b.w = {s: v}
            b.r = {}

    def op(self, eng, fn, reads=(), writes=()):
        self._deps(eng, reads, writes)
        ins = fn()
        eng.sem.v += 1
        ins.then_inc(eng.sem.h, 1)
        self._mark((eng.sem, eng.sem.v), reads, writes)

    def mm(self, fns, reads=(), writes=()):
        eng = self.pe
        self._deps(eng, reads, writes)
        ins = None
        for fn in fns:
            ins = fn()
        eng.sem.v += 1
        ins.then_inc(eng.sem.h, 1)
        self._mark((eng.sem, eng.sem.v), reads, writes)

    def dma(self, q, out, in_, reads=(), writes=(), join=False):
        sem = q.dma_sems[q.dma_rr]
        q.dma_rr = (q.dma_rr + 1) % len(q.dma_sems)
        if sem.v:
            q.wait((sem, sem.v))
        self._deps(q, reads, writes, join)
        ins = q.e.dma_start(out=out, in_=in_)
        sem.v += 16
        ins.then_inc(sem.h, 16)
        self._mark((sem, sem.v), reads, writes, join)

    def barrier(self):
        sems = []
        for e in self.engs:
            sems.append(e.sem)
            sems.extend(e.dma_sems)
        for e in self.engs:
            for s in sems:
                if s.v:
                    e.wait((s, s.v))

    def finish(self):
        self.barrier()


_UID = [0]


def uname(name):
    _UID[0] += 1
    return f"{name}_{_UID[0]}"


class Ring:
    def __init__(self, st, nc, name, n, shape, dtype, psum=False):
        self.n = n
        self.i = 0
        self.t = []
        self.b = []
        for j in range(n):
            if psum:
                t = st.enter_context(nc.psum_tensor(uname(name), shape, dtype))
            else:
                t = st.enter_context(nc.sbuf_tensor(uname(name), shape, dtype))
            self.t.append(t)
            self.b.append(Buf(f"{name}{j}"))

    def next(self):
        j = self.i
        self.i = (self.i + 1) % self.n
        return self.t[j], self.b[j]


def sb(st, nc, name, shape, dtype):
    return st.enter_context(nc.sbuf_tensor(uname(name), shape, dtype))


def ps(st, nc, name, shape, dtype):
    return st.enter_context(nc.psum_tensor(uname(name), shape, dtype))


def load_bcast(k, st, name, vec_ap, n):
    nc = k.nc
    t = sb(st, nc, name, [128, n], F32)
    b = Buf(name)
    k.dma(k.sp, t[:], vec_ap.partition_broadcast(128), writes=[b])
    return t, b


def rms_rstd(k, ss, ss_b, rstd, rstd_b, n, eps=EPS, mul=1.0):
    nc = k.nc
    c_eps = k.const(eps)
    c_mul = k.const(math.log(mul))
    k.op(k.act, lambda: nc.scalar.activation(out=rstd, in_=ss, func=AF.Ln, scale=1.0 / n, bias=c_eps),
         reads=[ss_b, k.const_b], writes=[rstd_b])
    k.op(k.act, lambda: nc.scalar.activation(out=rstd, in_=rstd, func=AF.Exp, scale=-0.5, bias=c_mul),
         reads=[rstd_b, k.const_b], writes=[rstd_b])


def norm_h(k, rings, x_src, tok0, npre, npre_b):
    nc = k.nc
    xs, hb, stat = rings
    x_t, x_b = xs.next()
    k.dma(k.sp, x_t[:], x_src[tok0:tok0 + 128, :], writes=[x_b])
    h_t, h_b = hb.next()
    s_t, s_b = stat.next()
    k.op(k.act, lambda: nc.scalar.activation(out=h_t[:], in_=x_t[:], func=AF.Square, accum_out=s_t[:, 0:1]),
         reads=[x_b], writes=[h_b, s_b])
    rms_rstd(k, s_t[:, 0:1], s_b, s_t[:, 1:2], s_b, D_MODEL)
    k.op(k.dve, lambda: nc.vector.scalar_tensor_tensor(out=h_t[:], in0=x_t[:], scalar=s_t[:, 1:2],
                                                       in1=npre[:], op0=ALU.mult, op1=ALU.mult),
         reads=[x_b, s_b, npre_b], writes=[h_b])
    return h_t, h_b


def transp_h(k, ptr, ident, ident_b, h_t, h_b, hT_t, hT_b, s):
    nc = k.nc
    p_t, p_b = ptr.next()
    k.mm([(lambda c=c: nc.tensor.transpose(p_t[:, c, :], h_t[:, c * 128:(c + 1) * 128], ident[:]))
          for c in range(8)], reads=[h_b, ident_b], writes=[p_b])
    k.op(k.act, lambda: nc.scalar.copy(out=hT_t[:, :, s * 128:(s + 1) * 128], in_=p_t[:]),
         reads=[p_b], writes=[hT_b])


def phase_ffn(k, ident, ident_b, x_in, x_out, w_gu, w_down, n_pre, n_post):
    nc = k.nc
    TT = 512
    NTT = SEQ // TT
    NS = TT // 128
    NJ = D_FF // 128
    with ExitStack() as st:
        wgu = sb(st, nc, "wgu", [128, 8, 2 * D_FF], BF16)
        wdn = sb(st, nc, "wdn", [128, NJ, D_MODEL], BF16)
        JB = [(0, 4), (4, 8), (8, 12), (12, 16), (16, 20), (20, 22)]
        wgu_b = {}
        for bi, (j0, j1) in enumerate(JB):
            for gu in range(2):
                c0 = gu * D_FF + j0 * 128
                c1 = gu * D_FF + j1 * 128
                b = Buf()
                for j in range(j0, j1):
                    wgu_b[(gu, j)] = b
                for c in range(8):
                    k.dma(k.pool, wgu[:, c, c0:c1], w_gu[c * 128:(c + 1) * 128, c0:c1], writes=[b], join=True)
        wdn_b = []
        for hh in range(2):
            b = Buf()
            wdn_b.append(b)
            k.dma(k.pool, wdn[:, hh * 11:(hh + 1) * 11, :],
                  w_down[hh * 1408:(hh + 1) * 1408, :].rearrange("(j p) n -> p j n", p=128), writes=[b])
        npre, npre_b = load_bcast(k, st, "npre", n_pre, D_MODEL)
        npost, npost_b = load_bcast(k, st, "npost", n_post, D_MODEL)

        xs = Ring(st, nc, "xs", 2, [128, D_MODEL], F32)
        hb = Ring(st, nc, "hb", 4, [128, D_MODEL], BF16)
        stat = Ring(st, nc, "stat", 8, [128, 4], F32)
        hT = Ring(st, nc, "hT", 1, [128, 8, TT], BF16)
        actT = Ring(st, nc, "actT", 1, [128, NJ, TT], BF16)
        sg = Ring(st, nc, "sg", 2, [128, TT], F32)
        xr = Ring(st, nc, "xr", 2, [128, D_MODEL], F32)
        yo = Ring(st, nc, "yo", 1, [128, D_MODEL], F32)
        ptr = Ring(st, nc, "ptr", 1, [128, 8, 128], BF16, psum=True)
        pg = Ring(st, nc, "pg", 2, [128, TT], F32, psum=True)
        pu = Ring(st, nc, "pu", 2, [128, TT], F32, psum=True)
        py = Ring(st, nc, "py", 1, [128, 2, 512], F32, psum=True)
        nrings = (xs, hb, stat)

        hT_t, hT_b = hT.next()
        hs = [norm_h(k, nrings, x_in, s * 128, npre, npre_b) for s in range(NS)]
        for s in range(NS):
            transp_h(k, ptr, ident, ident_b, hs[s][0], hs[s][1], hT_t, hT_b, s)
        for t in range(NTT):
            if t + 1 < NTT:
                hs = [norm_h(k, nrings, x_in, ((t + 1) * NS + s) * 128, npre, npre_b) for s in range(NS)]
            a_t, a_b = actT.next()
            for j in range(NJ):
                g_t, g_b = pg.next()
                u_t, u_b = pu.next()
                k.mm([(lambda c=c: nc.tensor.matmul(g_t[:], wgu[:, c, j * 128:(j + 1) * 128], hT_t[:, c, :],
                                                    start=(c == 0), stop=(c == 7))) for c in range(8)],
                     reads=[wgu_b[(0, j)], hT_b], writes=[g_b])
                k.mm([(lambda c=c: nc.tensor.matmul(u_t[:], wgu[:, c, D_FF + j * 128:D_FF + (j + 1) * 128],
                                                    hT_t[:, c, :], start=(c == 0), stop=(c == 7)))
                      for c in range(8)],
                     reads=[wgu_b[(1, j)], hT_b], writes=[u_b])
                sg_t, sg_b = sg.next()
                k.op(k.act, lambda: nc.scalar.activation(out=sg_t[:], in_=g_t[:], func=AF.Silu),
                     reads=[g_b], writes=[sg_b])
                k.op(k.dve, lambda: nc.vector.tensor_tensor(out=a_t[:, j, :], in0=sg_t[:], in1=u_t[:],
                                                            op=ALU.mult),
                     reads=[sg_b, u_b], writes=[a_b])
            if t + 1 < NTT:
                hT_t, hT_b = hT.next()
                for s in range(NS):
                    transp_h(k, ptr, ident, ident_b, hs[s][0], hs[s][1], hT_t, hT_b, s)
            for s in range(NS):
                tok0 = (t * NS + s) * 128
                y_t, y_b = py.next()
                fns = []
                for hf in range(2):
                    for j in range(NJ):
                        fns.append(lambda j=j, hf=hf: nc.tensor.matmul(
                            y_t[:, hf, :], a_t[:, j, s * 128:(s + 1) * 128], wdn[:, j, hf * 512:(hf + 1) * 512],
                            start=(j == 0), stop=(j == NJ - 1)))
                k.mm(fns, reads=[a_b] + wdn_b, writes=[y_b])
                xr_t, xr_b = xr.next()
                k.dma(k.sp, xr_t[:], x_in[tok0:tok0 + 128, :], writes=[xr_b])
                o_t, o_b = yo.next()
                s_t, s_b = stat.next()
                for hf in range(2):
                    k.op(k.act, lambda: nc.scalar.activation(out=o_t[:, hf * 512:(hf + 1) * 512], in_=y_t[:, hf, :],
                                                             func=AF.Square, accum_out=s_t[:, 2 + hf:3 + hf]),
                         reads=[y_b], writes=[o_b, s_b])
                k.op(k.dve, lambda: nc.vector.tensor_tensor(out=s_t[:, 0:1], in0=s_t[:, 2:3], in1=s_t[:, 3:4],
                                                            op=ALU.add), reads=[s_b], writes=[s_b])
                rms_rstd(k, s_t[:, 0:1], s_b, s_t[:, 1:2], s_b, D_MODEL, mul=0.5)
                for hf in range(2):
                    k.op(k.dve, lambda: nc.vector.tensor_tensor(
                        out=o_t[:, hf * 512:(hf + 1) * 512], in0=y_t[:, hf, :], in1=npost[:, hf * 512:(hf + 1) * 512],
                        op=ALU.mult), reads=[y_b, npost_b], writes=[o_b])
                k.op(k.dve, lambda: nc.vector.scalar_tensor_tensor(
                    out=xr_t[:], in0=o_t[:], scalar=s_t[:, 1:2], in1=xr_t[:], op0=ALU.mult, op1=ALU.add),
                    reads=[o_b, xr_b, s_b], writes=[xr_b])
                k.dma(k.sp, x_out[tok0:tok0 + 128, :], xr_t[:], reads=[xr_b])
        k.barrier()


def phase_proj(k, ident, ident_b, x_in, w_in, n_pre, gate_b_pc, S):
    nc = k.nc
    TT = 512
    NTT = SEQ // TT
    NS = TT // 128
    with ExitStack() as st:
        win = sb(st, nc, "win", [128, 8, C_IN], BF16)
        blocks = [(0, 512, 0), (512, 1024, 0), (1024, 2048, 0), (2048, 3072, 1), (3072, 4096, 1),
                  (4096, 5120, 2), (5120, 6160, 2), (6160, 7184, 3), (7184, 8208, 3)]
        wb = [Buf() for _ in range(4)]
        for (c0, c1, bi) in blocks:
            for c in range(8):
                k.dma(k.pool, win[:, c, c0:c1], w_in[c * 128:(c + 1) * 128, c0:c1], writes=[wb[bi]], join=True)
        npre, npre_b = load_bcast(k, st, "npre", n_pre, D_MODEL)
        gb = sb(st, nc, "gb", [128, 16], F32)
        gb_b = Buf()
        k.dma(k.sp, gb[:], gate_b_pc, writes=[gb_b])
        xs = Ring(st, nc, "xs", 2, [128, D_MODEL], F32)
        hb = Ring(st, nc, "hb", 4, [128, D_MODEL], BF16)
        stat = Ring(st, nc, "stat", 8, [128, 4], F32)
        ptr = Ring(st, nc, "ptr", 1, [128, 8, 128], BF16, psum=True)
        nrings = (xs, hb, stat)
        hT = Ring(st, nc, "hT", 1, [128, 8, TT], BF16)
        of32 = Ring(st, nc, "of32", 4, [128, 512], F32)
        obf = Ring(st, nc, "obf", 4, [128, 512], BF16)
        pf = Ring(st, nc, "pf", 3, [128, 512], F32, psum=True)
        pt = Ring(st, nc, "pt", 3, [128, 512], F32, psum=True)
        ev = [0]

        def evac_copy(out_ap, in_ap, rd, wr):
            ev[0] ^= 1
            if ev[0]:
                k.op(k.dve, lambda: nc.vector.tensor_copy(out=out_ap, in_=in_ap), reads=rd, writes=wr)
            else:
                k.op(k.act, lambda: nc.scalar.copy(out=out_ap, in_=in_ap), reads=rd, writes=wr)

        hT_t, hT_b = hT.next()
        hs = [norm_h(k, nrings, x_in, s * 128, npre, npre_b) for s in range(NS)]
        for s in range(NS):
            transp_h(k, ptr, ident, ident_b, hs[s][0], hs[s][1], hT_t, hT_b, s)
        for t in range(NTT):
            T0 = t * TT
            if t + 1 < NTT:
                hs = [norm_h(k, nrings, x_in, (t + 1) * TT + s * 128, npre, npre_b) for s in range(NS)]

            def fm_chunk(col0, wbuf):
                p_t, p_b = pf.next()
                k.mm([(lambda c=c: nc.tensor.matmul(p_t[:], win[:, c, col0:col0 + 128], hT_t[:, c, :],
                                                    start=(c == 0), stop=(c == 7))) for c in range(8)],
                     reads=[wbuf, hT_b], writes=[p_b])
                return p_t, p_b

            def tm_block(s, col0, n, wbuf):
                p_t, p_b = pt.next()
                k.mm([(lambda c=c: nc.tensor.matmul(p_t[:, 0:n], hT_t[:, c, s * 128:(s + 1) * 128],
                                                    win[:, c, col0:col0 + n], start=(c == 0), stop=(c == 7)))
                      for c in range(8)], reads=[wbuf, hT_b], writes=[p_b])
                return p_t, p_b

            for c in range(16):
                p_t, p_b = fm_chunk(c * 128, wb[0])
                o_t, o_b = obf.next()
                evac_copy(o_t[:], p_t[:], [p_b], [o_b])
                k.dma(k.sp, S["qkT"][c * 128:(c + 1) * 128, T0:T0 + TT], o_t[:], reads=[o_b])
            for c in range(8):
                p_t, p_b = fm_chunk(O_MQ + c * 128, wb[1])
                o_t, o_b = of32.next()
                evac_copy(o_t[:], p_t[:], [p_b], [o_b])
                k.dma(k.sp, S["mqkT"][c * 128:(c + 1) * 128, T0:T0 + TT], o_t[:], reads=[o_b])
            for s in range(NS):
                tok0 = T0 + s * 128
                for hf in range(2):
                    p_t, p_b = tm_block(s, O_AV + hf * 512, 512, wb[1])
                    o_t, o_b = obf.next()
                    evac_copy(o_t[:], p_t[:], [p_b], [o_b])
                    k.dma(k.sp, S["av"][tok0:tok0 + 128, hf * 512:(hf + 1) * 512], o_t[:], reads=[o_b])
                for hf in range(2):
                    p_t, p_b = tm_block(s, O_MV + hf * 512, 512, wb[2])
                    o_t, o_b = of32.next()
                    evac_copy(o_t[:], p_t[:], [p_b], [o_b])
                    k.dma(k.sp, S["mv"][tok0:tok0 + 128, hf * 512:(hf + 1) * 512], o_t[:], reads=[o_b])
                for hf in range(2):
                    p_t, p_b = tm_block(s, O_MO + hf * 512, 512, wb[2])
                    o_t, o_b = of32.next()
                    k.op(k.act, lambda: nc.scalar.activation(out=o_t[:], in_=p_t[:], func=AF.Sigmoid),
                         reads=[p_b], writes=[o_b])
                    k.dma(k.sp, S["mo"][tok0:tok0 + 128, hf * 512:(hf + 1) * 512], o_t[:], reads=[o_b])
                p_t, p_b = tm_block(s, O_MI, 16, wb[2])
                o_t, o_b = of32.next()
                evac_copy(o_t[:, 0:16], p_t[:, 0:16], [p_b], [o_b])
                k.dma(k.sp, S["mif"][tok0:tok0 + 128, :], o_t[:, 0:16], reads=[o_b])
            for c in range(16):
                p_t, p_b = fm_chunk(O_GL + c * 128, wb[3])
                o_t, o_b = of32.next()
                k.op(k.act, lambda: nc.scalar.activation(out=o_t[:], in_=p_t[:], func=AF.Sigmoid,
                                                         bias=gb[:, c:c + 1]),
                     reads=[p_b, gb_b], writes=[o_b])
                k.dma(k.sp, S["glT"][c * 128:(c + 1) * 128, T0:T0 + TT], o_t[:], reads=[o_b])
            if t + 1 < NTT:
                hT_t, hT_b = hT.next()
                for s in range(NS):
                    transp_h(k, ptr, ident, ident_b, hs[s][0], hs[s][1], hT_t, hT_b, s)
        k.barrier()


def phase_attn(k, cbf, cbf_b, cf32, cf32_b, S, lamp, anw_col, lam_init):
    nc = k.nc
    NG = SEQ // 512
    with ExitStack() as st:
        lt, lt_b = load_bcast(k, st, "lamp", lamp, 256)
        lw = sb(st, nc, "lamw", [128, 136], F32)
        lw_b = Buf()
        k.op(k.dve, lambda: nc.vector.tensor_tensor(out=lw[:, 0:128], in0=lt[:, 0:128], in1=lt[:, 128:256],
                                                    op=ALU.mult), reads=[lt_b], writes=[lw_b])
        k.op(k.dve, lambda: nc.vector.tensor_reduce(out=lw[:, 128:130],
                                                    in_=lw[:, 0:128].rearrange("p (a b) -> p a b", a=2),
                                                    axis=AX.X, op=ALU.add), reads=[lw_b], writes=[lw_b])
        k.op(k.act, lambda: nc.scalar.activation(out=lw[:, 130:132], in_=lw[:, 128:130], func=AF.Exp),
             reads=[lw_b], writes=[lw_b])
        k.op(k.dve, lambda: nc.vector.tensor_tensor(out=lw[:, 132:133], in0=lw[:, 130:131], in1=lw[:, 131:132],
                                                    op=ALU.subtract), reads=[lw_b], writes=[lw_b])
        k.op(k.dve, lambda: nc.vector.tensor_scalar(out=lw[:, 133:134], in0=lw[:, 132:133], scalar1=lam_init,
                                                    scalar2=-1.0, op0=ALU.add, op1=ALU.mult),
             reads=[lw_b], writes=[lw_b])
        neg_lam = lw[:, 133:134]
        nw = sb(st, nc, "anw", [128, 1], F32)
        nw_b = Buf()
        k.dma(k.sp, nw[:], anw_col, writes=[nw_b])
        k.op(k.dve, lambda: nc.vector.tensor_scalar(out=nw[:], in0=nw[:], scalar1=1.0 - lam_init, scalar2=None,
                                                    op0=ALU.mult), reads=[nw_b], writes=[nw_b])

        qT = Ring(st, nc, "qT", 2, [128, SEQ], BF16)
        kT = Ring(st, nc, "kT", 2, [128, SEQ], BF16)
        vh = Ring(st, nc, "vh", 2, [128, NT_(), 128], BF16)
        pT = Ring(st, nc, "pT", 6, [128, 512], BF16)
        f32r = Ring(st, nc, "af", 12, [128, 512], F32)
        yo = Ring(st, nc, "yao", 2, [128, 512], BF16)
        psr = Ring(st, nc, "psS", 4, [128, 512], F32, psum=True)
        po = [ps(st, nc, "po", [128, 512], F32) for _ in range(2)]
        pl = [ps(st, nc, "pl", [128, 512], F32) for _ in range(2)]
        po_b = [Buf(), Buf()]
        pl_b = [Buf(), Buf()]
        ones_bf = cbf[:, 1, :]
        tri = cbf[:, 2, :]

        todo = []
        for h in range(A_HEADS):
            q_t, q_b = qT.next()
            k_t, k_b = kT.next()
            v_t, v_b = vh.next()
            k.dma(k.sp, q_t[:], S["qkT"][h * 128:(h + 1) * 128, :], writes=[q_b])
            k.dma(k.sp, k_t[:], S["qkT"][1024 + h * 128:1024 + (h + 1) * 128, :], writes=[k_b])
            k.dma(k.sp, v_t[:], S["av"][:, h * 128:(h + 1) * 128].rearrange("(t p) d -> p t d", p=128),
                  writes=[v_b])
            for g in range(NG):
                nkt = 4 * g + 4
                pend = {}

                def stage_a(kt):
                    j = kt - 4 * g
                    c0 = 128 * j if j > 0 else 0
                    outs = []
                    sts = [psr.next() for _ in range(2)]
                    k.mm([(lambda m=m: nc.tensor.matmul(
                        sts[m][0][:, c0:512], k_t[m * 64:(m + 1) * 64, kt * 128:(kt + 1) * 128],
                        q_t[m * 64:(m + 1) * 64, g * 512 + c0:(g + 1) * 512], start=True, stop=True))
                        for m in range(2)], reads=[q_b, k_b], writes=[sts[0][1], sts[1][1]])
                    for m in range(2):
                        s_t, s_b = sts[m]
                        p_t, p_b = pT.next()
                        k.op(k.act, lambda: nc.scalar.activation(out=p_t[:, c0:512], in_=s_t[:, c0:512],
                                                                 func=AF.Exp, scale=A_DHEAD ** -0.5),
                             reads=[s_b], writes=[p_b])
                        if j >= 0:
                            k.op(k.pool, lambda: nc.gpsimd.tensor_tensor(out=p_t[:, c0:c0 + 128],
                                                                         in0=p_t[:, c0:c0 + 128], in1=tri,
                                                                         op=ALU.mult),
                                 reads=[p_b, cbf_b], writes=[p_b])
                        outs.append((p_t, p_b, c0))
                    pend[kt] = outs

                def stage_b(kt):
                    for m in range(2):
                        p_t, p_b, c0 = pend[kt][m]
                        k.mm([lambda: nc.tensor.matmul(po[m][:, c0:512], v_t[:, kt, :], p_t[:, c0:512],
                                                       start=(kt == 0), stop=(kt == nkt - 1)),
                              lambda: nc.tensor.matmul(pl[m][:, c0:512], ones_bf, p_t[:, c0:512],
                                                       start=(kt == 0), stop=(kt == nkt - 1))],
                             reads=[p_b, v_b, cbf_b], writes=[po_b[m], pl_b[m]])
                    del pend[kt]

                stage_a(0)
                for kt in range(1, nkt):
                    stage_a(kt)
                    stage_b(kt - 1)
                    if todo and kt >= 3:
                        todo.pop(0)()
                stage_b(nkt - 1)
                while todo:
                    todo.pop(0)()

                oc, lc = [], []
                for m in range(2):
                    o_t, o_b = f32r.next()
                    k.op(k.dve, lambda: nc.vector.tensor_copy(out=o_t[:], in_=po[m][:]), reads=[po_b[m]], writes=[o_b])
                    oc.append((o_t, o_b))
                    l_t, l_b = f32r.next()
                    k.op(k.act, lambda: nc.scalar.copy(out=l_t[:], in_=pl[m][:]), reads=[pl_b[m]], writes=[l_b])
                    lc.append((l_t, l_b))
                for m in range(2):
                    l_t, l_b = lc[m]
                    o_t, o_b = oc[m]
                    k.op(k.dve, lambda: nc.vector.reciprocal(out=l_t[:], in_=l_t[:]), reads=[l_b], writes=[l_b])
                    k.op(k.dve, lambda: nc.vector.tensor_tensor(out=o_t[:], in0=o_t[:], in1=l_t[:], op=ALU.mult),
                         reads=[o_b, l_b], writes=[o_b])
                d_t, d_b = oc[0]
                k.op(k.dve, lambda: nc.vector.scalar_tensor_tensor(out=d_t[:], in0=oc[1][0][:], scalar=neg_lam,
                                                                   in1=d_t[:], op0=ALU.mult, op1=ALU.add),
                     reads=[oc[1][1], d_b, lw_b], writes=[d_b])

                def make_tail(h=h, g=g, d_t=d_t, d_b=d_b):
                    st8 = {}

                    def e3():
                        st8["sq"] = f32r.next()
                        sq_t, sq_b = st8["sq"]
                        k.op(k.act, lambda: nc.scalar.activation(out=sq_t[:], in_=d_t[:], func=AF.Square),
                             reads=[d_b], writes=[sq_b])

                    def e4():
                        sq_t, sq_b = st8["sq"]
                        st8["ss"] = psr.next()
                        s_t, s_b = st8["ss"]
                        k.mm([lambda: nc.tensor.matmul(s_t[:], cf32[:, 0, :], sq_t[:], start=True, stop=True)],
                             reads=[sq_b, cf32_b], writes=[s_b])

                    def e5():
                        s_t, s_b = st8["ss"]
                        st8["rs"] = f32r.next()
                        rs_t, rs_b = st8["rs"]
                        rms_rstd(k, s_t[:], s_b, rs_t[:], rs_b, 128)

                    def e6():
                        rs_t, rs_b = st8["rs"]
                        y_t, y_b = yo.next()
                        k.op(k.dve, lambda: nc.vector.scalar_tensor_tensor(out=y_t[:], in0=d_t[:], scalar=nw[:, 0:1],
                                                                           in1=rs_t[:], op0=ALU.mult, op1=ALU.mult),
                             reads=[d_b, nw_b, rs_b], writes=[y_b])
                        k.dma(k.sp, S["yaT"][h * 128:(h + 1) * 128, g * 512:(g + 1) * 512], y_t[:], reads=[y_b])
                    return [e3, e4, e5, e6]
                todo.extend(make_tail())
        while todo:
            todo.pop(0)()
        k.barrier()


def NT_():
    return SEQ // 128


def make_scratch(nc, kind="Internal"):
    def d(name, shape, dt):
        return nc.dram_tensor(name, shape, dt, kind=kind).ap()
    return {
        "qkT": d("s_qkT", [2048, SEQ], BF16),
        "av": d("s_av", [SEQ, 1024], BF16),
        "mqkT": d("s_mqkT", [1024, SEQ], F32),
        "mv": d("s_mv", [SEQ, 1024], F32),
        "mo": d("s_mo", [SEQ, 1024], F32),
        "mif": d("s_mif", [SEQ, 16], F32),
        "glT": d("s_glT", [2048, SEQ], F32),
        "yaT": d("s_yaT", [1024, SEQ], BF16),
        "ymT": d("s_ymT", [1024, SEQ], BF16),
    }


def host_consts():
    import ml_dtypes
    p = np.arange(128)[:, None]
    f = np.arange(128)[None, :]
    tri = (p <= f).astype(np.float32)
    cbf = np.stack([np.eye(128, dtype=np.float32), np.ones((128, 128), np.float32), tri], axis=1)
    cf32 = np.stack([np.ones((128, 128), np.float32), tri, tri * (M_DQK ** -0.5)], axis=1)
    return {"cbf": np.ascontiguousarray(cbf).astype(ml_dtypes.bfloat16),
            "cf32": np.ascontiguousarray(cf32).astype(np.float32)}


def load_consts(k, st):
    nc = k.nc
    cbf_d = nc.dram_tensor("cbf", [128, 3, 128], BF16, kind="ExternalInput").ap()
    cf32_d = nc.dram_tensor("cf32", [128, 3, 128], F32, kind="ExternalInput").ap()
    cbf = sb(st, nc, "cbf_sb", [128, 3, 128], BF16)
    cf32 = sb(st, nc, "cf32_sb", [128, 3, 128], F32)
    cbf_b, cf32_b = Buf(), Buf()
    k.dma(k.sp, cbf[:], cbf_d, writes=[cbf_b])
    k.dma(k.sp, cf32[:], cf32_d, writes=[cf32_b])
    return cbf, cbf_b, cf32, cf32_b


def phase_mlstm(k, cbf, cbf_b, cf32, cf32_b, S, convw_pc, convb_pc, igb, fgb, mnw):
    nc = k.nc
    NTL = SEQ // 128
    ident = cbf[:, 0, :]
    ones32 = cf32[:, 0, :]
    triu32 = cf32[:, 1, :]
    mask8 = cf32[:, 2, :]
    with ExitStack() as st:
        qkc = sb(st, nc, "qkc", [128, 8, SEQ], BF16)
        qkc_b = [Buf() for _ in range(8)]
        cw = sb(st, nc, "cw", [128, 8, 4], F32)
        cb = sb(st, nc, "cb", [128, 8], F32)
        cw_b, cb_b = Buf(), Buf()
        k.dma(k.sp, cw[:], convw_pc, writes=[cw_b])
        k.dma(k.sp, cb[:], convb_pc, writes=[cb_b])
        with ExitStack() as st1:
            xp = Ring(st1, nc, "xp", 2, [128, SEQ + 3], F32)
            acc = Ring(st1, nc, "cacc", 1, [128, SEQ], F32)
            for i in range(2):
                k.op(k.pool, lambda: nc.gpsimd.memset(xp.t[i][:, 0:3], 0.0), writes=[xp.b[i]])
            for c in range(8):
                x_t, x_b = xp.next()
                k.dma(k.sp, x_t[:, 3:SEQ + 3], S["mqkT"][c * 128:(c + 1) * 128, :], writes=[x_b])
                a_t, a_b = acc.next()
                k.op(k.dve, lambda: nc.vector.tensor_scalar(out=a_t[:], in0=x_t[:, 0:SEQ], scalar1=cw[:, c, 0:1],
                                                            scalar2=None, op0=ALU.mult),
                     reads=[x_b, cw_b], writes=[a_b])
                for j in range(1, 4):
                    k.op(k.dve, lambda: nc.vector.scalar_tensor_tensor(
                        out=a_t[:], in0=x_t[:, j:j + SEQ], scalar=cw[:, c, j:j + 1], in1=a_t[:],
                        op0=ALU.mult, op1=ALU.add), reads=[x_b, cw_b, a_b], writes=[a_b])
                k.op(k.act, lambda: nc.scalar.activation(out=qkc[:, c, :], in_=a_t[:], func=AF.Silu,
                                                         bias=cb[:, c:c + 1]),
                     reads=[a_b, cb_b], writes=[qkc_b[c]])
        k.barrier()

        ktok = sb(st, nc, "ktok", [128, NTL, 512], BF16)
        ktok_b = Buf()
        NC8 = NTL * 8
        gt = sb(st, nc, "gt", [128, NTL, 16], F32)
        gt_b = Buf()
        k.dma(k.sp, gt[:], S["mif"].rearrange("(t p) c -> p t c", p=128), writes=[gt_b])
        igb_t, igb_b = load_bcast(k, st, "igb", igb, 8)
        fgb_t, fgb_b = load_bcast(k, st, "fgb", fgb, 8)
        mnw_t, mnw_b = load_bcast(k, st, "mnw", mnw, 1024)
        IG = sb(st, nc, "IG", [128, NTL, 8], F32)
        NLF = sb(st, nc, "NLF", [128, NTL, 8], F32)
        NGc = sb(st, nc, "NGc", [128, NTL, 8], F32)
        EQ = sb(st, nc, "EQ", [128, NTL, 8], F32)
        EK = sb(st, nc, "EK", [128, NTL, 8], F32)
        EE = sb(st, nc, "EE", [128, NTL, 8], F32)
        IG_b, NLF_b, NG_b, EQ_b, EK_b, EE_b = [Buf() for _ in range(6)]

        pa = Ring(st, nc, "pa", 2, [128, 4, 128], F32, psum=True)
        pr = Ring(st, nc, "pr", 2, [128, 4, 128], F32, psum=True)
        pu = Ring(st, nc, "pu", 2, [128, 4, 128], F32, psum=True)
        pm = ps(st, nc, "pm", [128, 512], F32)
        prd, prd_b = pm[:, 0:8], Buf()
        pun, pun_b = pm[:, 8:16], Buf()
        ptr = ps(st, nc, "ptrm", [128, 8, 128], BF16)
        ptr_b = Buf()

        for t in range(NTL):
            k.mm([(lambda c=c: nc.tensor.transpose(ptr[:, c, :], qkc[:, 4 + c, t * 128:(t + 1) * 128], ident))
                  for c in range(4)], reads=qkc_b[4:8] + [cbf_b], writes=[ptr_b])
            k.op(k.act, lambda: nc.scalar.copy(out=ktok[:, t, :], in_=ptr[:, 0:4, :].rearrange("p a b -> p (a b)")),
                 reads=[ptr_b], writes=[ktok_b])

        k.op(k.dve, lambda: nc.vector.tensor_tensor(out=IG[:], in0=gt[:, :, 0:8],
                                                    in1=igb_t[:].unsqueeze(1).to_broadcast([128, NTL, 8]),
                                                    op=ALU.add), reads=[gt_b, igb_b], writes=[IG_b])
        k.op(k.dve, lambda: nc.vector.tensor_tensor(out=NLF[:], in0=gt[:, :, 8:16],
                                                    in1=fgb_t[:].unsqueeze(1).to_broadcast([128, NTL, 8]),
                                                    op=ALU.add), reads=[gt_b, fgb_b], writes=[NLF_b])
        k.op(k.act, lambda: nc.scalar.activation(out=NLF[:], in_=NLF[:], func=AF.Exp, scale=-1.0),
             reads=[NLF_b], writes=[NLF_b])
        c_one = k.const(1.0)
        k.op(k.act, lambda: nc.scalar.activation(out=NLF[:], in_=NLF[:], func=AF.Ln, bias=c_one),
             reads=[NLF_b, k.const_b], writes=[NLF_b])
        nlf2 = NLF[:].rearrange("p a b -> p (a b)")
        pg_t, pg_b = pa.next()
        pg2 = pg_t[:].rearrange("p a b -> p (a b)")
        k.mm([lambda: nc.tensor.matmul(pg2[:, 0:NC8], triu32, nlf2, start=True, stop=True)],
             reads=[NLF_b, cf32_b], writes=[pg_b])
        k.op(k.dve, lambda: nc.vector.tensor_copy(out=NGc[:].rearrange("p a b -> p (a b)"), in_=pg2[:, 0:NC8]),
             reads=[pg_b], writes=[NG_b])
        pe_t, pe_b = pa.next()
        pe2 = pe_t[:].rearrange("p a b -> p (a b)")
        k.mm([lambda: nc.tensor.matmul(pe2[:, 0:NC8], ones32, nlf2, start=True, stop=True)],
             reads=[NLF_b, cf32_b], writes=[pe_b])
        k.op(k.act, lambda: nc.scalar.activation(out=EE[:].rearrange("p a b -> p (a b)"), in_=pe2[:, 0:NC8],
                                                 func=AF.Exp, scale=-1.0), reads=[pe_b], writes=[EE_b])
        k.op(k.act, lambda: nc.scalar.activation(out=EQ[:], in_=NGc[:], func=AF.Exp, scale=-1.0),
             reads=[NG_b], writes=[EQ_b])
        k.op(k.dve, lambda: nc.vector.tensor_tensor(out=EK[:], in0=IG[:], in1=NGc[:], op=ALU.add),
             reads=[IG_b, NG_b], writes=[EK_b])
        k.op(k.act, lambda: nc.scalar.activation(out=EK[:], in_=EK[:], func=AF.Exp), reads=[EK_b], writes=[EK_b])

        mvr = Ring(st, nc, "mvr", 2, [128, 8, 128], F32)
        mor = Ring(st, nc, "mor", 2, [128, 1024], F32)
        vtr = Ring(st, nc, "vtr", 2, [128, 8, 128], BF16)
        ekr = Ring(st, nc, "ekr", 2, [128, 8], BF16)
        ptT = Ring(st, nc, "ptT", 2, [128, 8, 128], BF16)
        Tst = sb(st, nc, "Tst", [128, 4, 129], F32)
        T_b = [Buf() for _ in range(8)]
        S8r = Ring(st, nc, "S8r", 2, [128, 4, 129], BF16)
        S8_bufs = {0: [Buf() for _ in range(8)], 1: [Buf() for _ in range(8)]}
        sm = Ring(st, nc, "msm", 2, [128, 48], F32)
        ho = Ring(st, nc, "mho", 1, [128, 8, 128], F32)
        sq = Ring(st, nc, "msq", 1, [128, 8, 128], F32)
        y1 = Ring(st, nc, "my1", 1, [128, 8, 128], F32)
        ymb = Ring(st, nc, "ymb", 2, [128, 1024], BF16)
        ymT = Ring(st, nc, "ymTs", 2, [128, 8, 128], BF16)
        ctx = {}
        s8 = {}

        def stage1(t):
            c = ctx[t] = {}
            tsl = slice(t * 128, (t + 1) * 128)
            mv_t, mv_b = mvr.next()
            k.dma(k.sp, mv_t[:], S["mv"][tsl, :].rearrange("p (a b) -> p a b", a=8), writes=[mv_b])
            c["mo"] = mor.next()
            k.dma(k.sp, c["mo"][0][:], S["mo"][tsl, :], writes=[c["mo"][1]])
            vt_t, vt_b = c["vt"] = vtr.next()
            k.op(k.dve, lambda: nc.vector.tensor_tensor(out=vt_t[:], in0=mv_t[:],
                                                        in1=EK[:, t, :].unsqueeze(2).to_broadcast([128, 8, 128]),
                                                        op=ALU.mult), reads=[mv_b, EK_b], writes=[vt_b])
            ek_t, ek_b = c["ek"] = ekr.next()
            k.op(k.dve, lambda: nc.vector.tensor_copy(out=ek_t[:], in_=EK[:, t, :]), reads=[EK_b], writes=[ek_b])
            pT_t, pT_b = c["pT"] = ptT.next()
            a_te = [pa.next() for _ in range(2)]
            fns = []
            for i in range(4):
                for e in range(2):
                    P0 = e * 64
                    fns.append(lambda i=i, e=e, P0=P0: nc.tensor.matmul(
                        a_te[e][0][:, i, :], qkc[P0:P0 + 64, 4 + i, tsl], qkc[P0:P0 + 64, i, tsl],
                        start=True, stop=True))
            k.mm(fns, reads=qkc_b, writes=[a_te[0][1], a_te[1][1]])
            for e in range(2):
                k.op(k.dve, lambda: nc.vector.tensor_tensor(
                    out=pT_t[:, 4 * e:4 * e + 4, :], in0=a_te[e][0][:],
                    in1=mask8.unsqueeze(1).to_broadcast([128, 4, 128]), op=ALU.mult),
                    reads=[a_te[e][1], cf32_b], writes=[pT_b])

        def stage2(t):
            c = ctx[t]
            tsl = slice(t * 128, (t + 1) * 128)
            pT_t, pT_b = c["pT"]
            vt_t, vt_b = c["vt"]
            ek_t, ek_b = c["ek"]
            s8_prev = s8.get(t - 1)
            r_ts = c["r"] = []
            for bk in range(2):
                r_t, r_b = pr.next()
                fns = []
                for hh in range(4):
                    h = 4 * bk + hh
                    P0 = (h % 2) * 64
                    fns.append(lambda h=h, hh=hh: nc.tensor.matmul(
                        r_t[:, hh, :], pT_t[:, (h % 2) * 4 + h // 2, :], vt_t[:, h, :], start=True, stop=(t == 0)))
                    if t > 0:
                        fns.append(lambda h=h, hh=hh, P0=P0: nc.tensor.matmul(
                            r_t[:, hh, :], qkc[P0:P0 + 64, h // 2, tsl], s8_prev[0][P0:P0 + 64, h // 2, 0:128],
                            start=False, stop=True))
                rd = [pT_b, vt_b] + qkc_b + (s8_prev[1] if t > 0 else [])
                k.mm(fns, reads=rd, writes=[r_b])
                r_ts.append((r_t, r_b))
            fns = []
            for h in range(8):
                P0 = (h % 2) * 64
                fns.append(lambda h=h: nc.tensor.matmul(prd[:, h:h + 1], pT_t[:, (h % 2) * 4 + h // 2, :],
                                                        ek_t[:, h:h + 1], start=True, stop=(t == 0)))
                if t > 0:
                    fns.append(lambda h=h, P0=P0: nc.tensor.matmul(
                        prd[:, h:h + 1], qkc[P0:P0 + 64, h // 2, tsl], s8_prev[0][P0:P0 + 64, h // 2, 128:129],
                        start=False, stop=True))
            k.mm(fns, reads=[pT_b, ek_b] + qkc_b + (s8_prev[1] if t > 0 else []), writes=[prd_b])
            u_ts = []
            for bk in range(2):
                u_t, u_b = pu.next()
                fns = []
                for hh in range(4):
                    h = 4 * bk + hh
                    fns.append(lambda h=h, hh=hh: nc.tensor.matmul(
                        u_t[:, hh, :], ktok[:, t, (h // 2) * 128:(h // 2 + 1) * 128], vt_t[:, h, :],
                        start=True, stop=True))
                k.mm(fns, reads=[ktok_b, vt_b], writes=[u_b])
                u_ts.append((u_t, u_b))
            k.mm([(lambda h=h: nc.tensor.matmul(pun[:, h:h + 1], ktok[:, t, (h // 2) * 128:(h // 2 + 1) * 128],
                                                ek_t[:, h:h + 1], start=True, stop=True)) for h in range(8)],
                 reads=[ktok_b, ek_b], writes=[pun_b])
            s8_t, _ = S8r.next()
            s8_bl = S8_bufs[t % 2]
            for h in range(8):
                P0 = (h % 2) * 64
                hp = h // 2
                u_t, u_b = u_ts[h // 4]
                if t == 0:
                    k.op(k.dve, lambda: nc.vector.tensor_copy(out=Tst[P0:P0 + 64, hp, 0:128],
                                                              in_=u_t[P0:P0 + 64, h % 4, :]),
                         reads=[u_b], writes=[T_b[h]])
                    k.op(k.dve, lambda: nc.vector.tensor_copy(out=Tst[P0:P0 + 64, hp, 128:129],
                                                              in_=pun[P0:P0 + 64, h:h + 1]),
                         reads=[pun_b], writes=[T_b[h]])
                else:
                    k.op(k.dve, lambda: nc.vector.scalar_tensor_tensor(
                        out=Tst[P0:P0 + 64, hp, 0:128], in0=Tst[P0:P0 + 64, hp, 0:128],
                        scalar=EE[P0:P0 + 64, t - 1, h:h + 1], in1=u_t[P0:P0 + 64, h % 4, :],
                        op0=ALU.mult, op1=ALU.add), reads=[T_b[h], EE_b, u_b], writes=[T_b[h]])
                    k.op(k.dve, lambda: nc.vector.scalar_tensor_tensor(
                        out=Tst[P0:P0 + 64, hp, 128:129], in0=Tst[P0:P0 + 64, hp, 128:129],
                        scalar=EE[P0:P0 + 64, t - 1, h:h + 1], in1=pun[P0:P0 + 64, h:h + 1],
                        op0=ALU.mult, op1=ALU.add), reads=[T_b[h], EE_b, pun_b], writes=[T_b[h]])
                if t < NTL - 1:
                    k.op(k.pool, lambda: nc.gpsimd.tensor_scalar(
                        out=s8_t[P0:P0 + 64, hp, :], in0=Tst[P0:P0 + 64, hp, :], scalar1=EE[P0:P0 + 64, t, h:h + 1],
                        scalar2=M_DQK ** -0.5, op0=ALU.mult, op1=ALU.mult),
                        reads=[T_b[h], EE_b], writes=[s8_bl[h]])
            s8[t] = (s8_t, s8_bl)
            s8.pop(t - 2, None)

        def stage3a(t):
            c = ctx[t]
            r_ts = c["r"]
            mo_t, mo_b = c["mo"]
            m_t, m_b = sm.next()
            dn, dneg, rc, cc, ss, rstd = (m_t[:, 0:8], m_t[:, 8:16], m_t[:, 16:24], m_t[:, 24:32],
                                          m_t[:, 32:40], m_t[:, 40:48])
            k.op(k.dve, lambda: nc.vector.tensor_tensor(out=dn, in0=prd, in1=EQ[:, t, :], op=ALU.mult),
                 reads=[prd_b, EQ_b], writes=[m_b])
            k.op(k.dve, lambda: nc.vector.tensor_scalar(out=dneg, in0=dn, scalar1=-1.0, scalar2=None, op0=ALU.mult),
                 reads=[m_b], writes=[m_b])
            k.op(k.dve, lambda: nc.vector.tensor_tensor(out=dn, in0=dn, in1=dneg, op=ALU.max),
                 reads=[m_b], writes=[m_b])
            k.op(k.dve, lambda: nc.vector.tensor_scalar(out=dn, in0=dn, scalar1=1.0, scalar2=None, op0=ALU.max),
                 reads=[m_b], writes=[m_b])
            k.op(k.dve, lambda: nc.vector.reciprocal(out=rc, in_=dn), reads=[m_b], writes=[m_b])
            k.op(k.dve, lambda: nc.vector.tensor_tensor(out=cc, in0=rc, in1=EQ[:, t, :], op=ALU.mult),
                 reads=[m_b, EQ_b], writes=[m_b])
            ho_t, ho_b = ho.next()
            for bk in range(2):
                r_t, r_b = r_ts[bk]
                k.op(k.dve, lambda: nc.vector.tensor_tensor(
                    out=ho_t[:, 4 * bk:4 * bk + 4, :], in0=r_t[:],
                    in1=cc[:, 4 * bk:4 * bk + 4].unsqueeze(2).to_broadcast([128, 4, 128]), op=ALU.mult),
                    reads=[r_b, m_b], writes=[ho_b])
            sq_t, sq_b = sq.next()
            k.op(k.pool, lambda: nc.gpsimd.tensor_tensor(out=sq_t[:], in0=ho_t[:], in1=ho_t[:], op=ALU.mult),
                 reads=[ho_b], writes=[sq_b])
            k.op(k.dve, lambda: nc.vector.tensor_reduce(out=ss, in_=sq_t[:], axis=AX.X, op=ALU.add),
                 reads=[sq_b], writes=[m_b])
            rms_rstd(k, ss, m_b, rstd, m_b, M_DV)
            y1_t, y1_b = y1.next()
            k.op(k.dve, lambda: nc.vector.tensor_tensor(
                out=y1_t[:], in0=ho_t[:], in1=rstd.unsqueeze(2).to_broadcast([128, 8, 128]), op=ALU.mult),
                reads=[ho_b, m_b], writes=[y1_b])
            y1f = y1_t[:].rearrange("p a b -> p (a b)")
            k.op(k.pool, lambda: nc.gpsimd.tensor_tensor(out=y1f, in0=y1f, in1=mnw_t[:], op=ALU.mult),
                 reads=[y1_b, mnw_b], writes=[y1_b])
            yb_t, yb_b = c["yb"] = ymb.next()
            k.op(k.dve, lambda: nc.vector.tensor_tensor(out=yb_t[:], in0=y1f, in1=mo_t[:], op=ALU.mult),
                 reads=[y1_b, mo_b], writes=[yb_b])

        def stage3b(t):
            c = ctx.pop(t)
            tsl = slice(t * 128, (t + 1) * 128)
            yb_t, yb_b = c["yb"]
            k.mm([(lambda cc_=cc_: nc.tensor.transpose(ptr[:, cc_, :], yb_t[:, cc_ * 128:(cc_ + 1) * 128], ident))
                  for cc_ in range(8)], reads=[yb_b, cbf_b], writes=[ptr_b])
            yT_t, yT_b = ymT.next()
            k.op(k.act, lambda: nc.scalar.copy(out=yT_t[:], in_=ptr[:]), reads=[ptr_b], writes=[yT_b])
            k.dma(k.sp, S["ymT"][:, tsl].rearrange("(c p) n -> p c n", p=128), yT_t[:], reads=[yT_b])

        stage1(0)
        stage2(0)
        for t in range(NTL):
            if t + 1 < NTL:
                stage1(t + 1)
            stage3a(t)
            if t + 1 < NTL:
                stage2(t + 1)
            stage3b(t)
        k.barrier()


def phase_out(k, S, x_io, w_pa, w_pm, w_out, n_post):
    nc = k.nc
    TT = 512
    NTT = SEQ // TT
    NS = TT // 128
    with ExitStack() as st:
        ws = []
        for nm, w in (("wpa", w_pa), ("wpm", w_pm), ("wout", w_out)):
            t = sb(st, nc, nm, [128, 8, D_MODEL], BF16)
            b = Buf()
            for hh in range(2):
                k.dma(k.pool, t[:, hh * 4:(hh + 1) * 4, :],
                      w[hh * 512:(hh + 1) * 512, :].rearrange("(c p) n -> p c n", p=128), writes=[b], join=True)
            ws.append((t, b))
        (wpa, wpa_b), (wpm, wpm_b), (wout, wout_b) = ws
        npost, npost_b = load_bcast(k, st, "npost", n_post, D_MODEL)
        yaT = Ring(st, nc, "yaTs", 2, [128, 8, TT], BF16)
        ymT = Ring(st, nc, "ymTs", 2, [128, 8, TT], BF16)
        gT = Ring(st, nc, "gTs", 3, [128, 2, TT], F32)
        mg = Ring(st, nc, "mg", 1, [128, 8, TT], BF16)
        t1 = Ring(st, nc, "t1", 2, [128, TT], F32)
        xr = Ring(st, nc, "xr", 2, [128, D_MODEL], F32)
        yo = Ring(st, nc, "yo", 1, [128, D_MODEL], F32)
        junk = Ring(st, nc, "junk", 1, [128, D_MODEL], BF16)
        stat = Ring(st, nc, "stat", 4, [128, 4], F32)
        ppa = Ring(st, nc, "ppa", 2, [128, TT], F32, psum=True)
        ppm = Ring(st, nc, "ppm", 2, [128, TT], F32, psum=True)
        py = Ring(st, nc, "py", 1, [128, 2, 512], F32, psum=True)
        for t in range(NTT):
            T0 = t * TT
            a_t, a_b = yaT.next()
            m_t, m_b = ymT.next()
            k.dma(k.sp, a_t[:], S["yaT"][:, T0:T0 + TT].rearrange("(c p) n -> p c n", p=128), writes=[a_b])
            k.dma(k.sp, m_t[:], S["ymT"][:, T0:T0 + TT].rearrange("(c p) n -> p c n", p=128), writes=[m_b])
            g_t, g_b = mg.next()
            for c in range(8):
                gg_t, gg_b = gT.next()
                k.dma(k.sp, gg_t[:], S["glT"].rearrange("(a r) n -> r a n", a=2)[c * 128:(c + 1) * 128, :, T0:T0 + TT],
                      writes=[gg_b])
                pa_t, pa_b = ppa.next()
                pm_t, pm_b = ppm.next()
                k.mm([(lambda kc=kc: nc.tensor.matmul(pa_t[:], wpa[:, kc, c * 128:(c + 1) * 128], a_t[:, kc, :],
                                                      start=(kc == 0), stop=(kc == 7))) for kc in range(8)],
                     reads=[wpa_b, a_b], writes=[pa_b])
                k.mm([(lambda kc=kc: nc.tensor.matmul(pm_t[:], wpm[:, kc, c * 128:(c + 1) * 128], m_t[:, kc, :],
                                                      start=(kc == 0), stop=(kc == 7))) for kc in range(8)],
                     reads=[wpm_b, m_b], writes=[pm_b])
                u_t, u_b = t1.next()
                k.op(k.dve, lambda: nc.vector.tensor_tensor(out=u_t[:], in0=pa_t[:], in1=gg_t[:, 0, :], op=ALU.mult),
                     reads=[pa_b, gg_b], writes=[u_b])
                v_t, v_b = t1.next()
                k.op(k.dve, lambda: nc.vector.tensor_tensor(out=v_t[:], in0=pm_t[:], in1=gg_t[:, 1, :], op=ALU.mult),
                     reads=[pm_b, gg_b], writes=[v_b])
                k.op(k.pool, lambda: nc.gpsimd.tensor_tensor(out=g_t[:, c, :], in0=u_t[:], in1=v_t[:], op=ALU.add),
                     reads=[u_b, v_b], writes=[g_b])
            for s in range(NS):
                tok0 = T0 + s * 128
                y_t, y_b = py.next()
                fns = []
                for hf in range(2):
                    for kc in range(8):
                        fns.append(lambda kc=kc, hf=hf: nc.tensor.matmul(
                            y_t[:, hf, :], g_t[:, kc, s * 128:(s + 1) * 128], wout[:, kc, hf * 512:(hf + 1) * 512],
                            start=(kc == 0), stop=(kc == 7)))
                k.mm(fns, reads=[g_b, wout_b], writes=[y_b])
                xr_t, xr_b = xr.next()
                k.dma(k.sp, xr_t[:], x_io[tok0:tok0 + 128, :], writes=[xr_b])
                j_t, j_b = junk.next()
                s_t, s_b = stat.next()
                for hf in range(2):
                    k.op(k.act, lambda: nc.scalar.activation(out=j_t[:, hf * 512:(hf + 1) * 512], in_=y_t[:, hf, :],
                                                             func=AF.Square, accum_out=s_t[:, 2 + hf:3 + hf]),
                         reads=[y_b], writes=[j_b, s_b])
                k.op(k.dve, lambda: nc.vector.tensor_tensor(out=s_t[:, 0:1], in0=s_t[:, 2:3], in1=s_t[:, 3:4],
                                                            op=ALU.add), reads=[s_b], writes=[s_b])
                rms_rstd(k, s_t[:, 0:1], s_b, s_t[:, 1:2], s_b, D_MODEL)
                o_t, o_b = yo.next()
                for hf in range(2):
                    k.op(k.dve, lambda: nc.vector.tensor_tensor(
                        out=o_t[:, hf * 512:(hf + 1) * 512], in0=y_t[:, hf, :], in1=npost[:, hf * 512:(hf + 1) * 512],
                        op=ALU.mult), reads=[y_b, npost_b], writes=[o_b])
                k.op(k.dve, lambda: nc.vector.scalar_tensor_tensor(
                    out=xr_t[:], in0=o_t[:], scalar=s_t[:, 1:2], in1=xr_t[:], op0=ALU.mult, op1=ALU.add),
                    reads=[o_b, xr_b, s_b], writes=[xr_b])
                k.dma(k.sp, x_io[tok0:tok0 + 128, :], xr_t[:], reads=[xr_b])
        k.barrier()


PARAM_NAMES = ["ffn1_norm_pre", "ffn1_w_gu", "ffn1_w_down", "ffn1_norm_post", "mix_norm_pre", "w_in",
               "attn_lam_q1", "attn_lam_k1", "attn_lam_q2", "attn_lam_k2", "attn_norm_w", "conv_w", "conv_b",
               "igate_b", "fgate_b", "mlstm_norm_w", "w_proj_a", "w_proj_m", "gate_b", "w_out", "mix_norm_post",
               "ffn2_norm_pre", "ffn2_w_gu", "ffn2_w_down", "ffn2_norm_post"]


def build_program(depth=DEPTH):
    nc = bass.Bass("TRN2", target_bir_lowering=False)

    def din(name, shape, dt=F32):
        return nc.dram_tensor(name, shape, dt, kind="ExternalInput").ap()

    L = depth
    x = din("x", [SEQ, D_MODEL])
    out = nc.dram_tensor("out", [SEQ, D_MODEL], F32, kind="ExternalOutput").ap()
    P = {
        "ffn1_norm_pre": din("ffn1_norm_pre", [L, D_MODEL]),
        "ffn1_w_gu": din("ffn1_w_gu", [L, D_MODEL, 2 * D_FF]),
        "ffn1_w_down": din("ffn1_w_down", [L, D_FF, D_MODEL]),
        "ffn1_norm_post": din("ffn1_norm_post", [L, D_MODEL]),
        "mix_norm_pre": din("mix_norm_pre", [L, D_MODEL]),
        "w_in": din("w_in", [L, D_MODEL, C_IN]),
        "lamp": din("lamp", [L, 256]),
        "anw": din("anw", [L, 128, 1]),
        "convw_pc": din("convw_pc", [L, 128, 8, 4]),
        "convb_pc": din("convb_pc", [L, 128, 8]),
        "igate_b": din("igate_b", [L, 8]),
        "fgate_b": din("fgate_b", [L, 8]),
        "mlstm_norm_w": din("mlstm_norm_w", [L, 1024]),
        "w_proj_a": din("w_proj_a", [L, 1024, D_MODEL]),
        "w_proj_m": din("w_proj_m", [L, 1024, D_MODEL]),
        "gate_b_pc": din("gate_b_pc", [L, 128, 16]),
        "w_out": din("w_out", [L, D_MODEL, D_MODEL]),
        "mix_norm_post": din("mix_norm_post", [L, D_MODEL]),
        "ffn2_norm_pre": din("ffn2_norm_pre", [L, D_MODEL]),
        "ffn2_w_gu": din("ffn2_w_gu", [L, D_MODEL, 2 * D_FF]),
        "ffn2_w_down": din("ffn2_w_down", [L, D_FF, D_MODEL]),
        "ffn2_norm_post": din("ffn2_norm_post", [L, D_MODEL]),
    }
    S = make_scratch(nc)
    k = K(nc)
    with ExitStack() as st:
        cbf, cbf_b, cf32, cf32_b = load_consts(k, st)
        ident = cbf[:, 0, :]
        for l in range(L):
            lam_init = 0.8 - 0.6 * math.exp(-0.3 * l)
            phase_ffn(k, ident, cbf_b, x if l == 0 else out, out, P["ffn1_w_gu"][l], P["ffn1_w_down"][l],
                      P["ffn1_norm_pre"][l], P["ffn1_norm_post"][l])
            phase_proj(k, ident, cbf_b, out, P["w_in"][l], P["mix_norm_pre"][l], P["gate_b_pc"][l], S)
            phase_attn(k, cbf, cbf_b, cf32, cf32_b, S, P["lamp"][l], P["anw"][l], lam_init)
            phase_mlstm(k, cbf, cbf_b, cf32, cf32_b, S, P["convw_pc"][l], P["convb_pc"][l], P["igate_b"][l],
                        P["fgate_b"][l], P["mlstm_norm_w"][l])
            phase_out(k, S, out, P["w_proj_a"][l], P["w_proj_m"][l], P["w_out"][l], P["mix_norm_post"][l])
            phase_ffn(k, ident, cbf_b, out, out, P["ffn2_w_gu"][l], P["ffn2_w_down"][l],
                      P["ffn2_norm_pre"][l], P["ffn2_norm_post"][l])
        k.finish()
    return nc


def host_params(inp, depth=DEPTH):
    f = lambda a: np.ascontiguousarray(np.asarray(a, dtype=np.float32))
    L = depth
    d = {n: f(inp[n]) for n in ["ffn1_norm_pre", "ffn1_w_gu", "ffn1_w_down", "ffn1_norm_post", "mix_norm_pre",
                                "w_in", "igate_b", "fgate_b", "w_proj_a", "w_proj_m", "w_out", "mix_norm_post",
                                "ffn2_norm_pre", "ffn2_w_gu", "ffn2_w_down", "ffn2_norm_post"]}
    d["lamp"] = f(np.concatenate([inp["attn_lam_q1"], inp["attn_lam_q2"], inp["attn_lam_k1"], inp["attn_lam_k2"]],
                                 axis=1))
    d["anw"] = f(np.asarray(inp["attn_norm_w"]).reshape(L, 128, 1))
    cw = np.asarray(inp["conv_w"])
    d["convw_pc"] = f(cw.transpose(0, 2, 1).reshape(L, 8, 128, 4).transpose(0, 2, 1, 3))
    d["convb_pc"] = f(np.asarray(inp["conv_b"]).reshape(L, 8, 128).transpose(0, 2, 1))
    d["mlstm_norm_w"] = f(np.asarray(inp["mlstm_norm_w"]).reshape(L, 1024))
    d["gate_b_pc"] = f(np.asarray(inp["gate_b"]).reshape(L, 16, 128).transpose(0, 2, 1))
    d.update(host_consts())
    return d


_NC_CACHE = {}


def kernel(**inputs):
    if "nc" not in _NC_CACHE:
        _NC_CACHE["nc"] = build_program()
    nc = _NC_CACHE["nc"]
    params = host_params(inputs)
    x = np.asarray(inputs["x"], dtype=np.float32)
    in_maps = []
    for b in range(BATCH):
        m = dict(params)
        m["x"] = np.ascontiguousarray(x[b])
        in_maps.append(m)
    res = run_bass_kernel_spmd(nc, in_maps, core_ids=list(range(BATCH)))
    return np.stack([np.asarray(r["out"], dtype=np.float32) for r in res.results], axis=0)
```

```python
import math
from contextlib import ExitStack

import numpy as np
import concourse.bass as bass
import concourse.mybir as mybir
from concourse.bass_utils import run_bass_kernel_spmd

F32 = mybir.dt.float32
BF16 = mybir.dt.bfloat16
AF = mybir.ActivationFunctionType
ALU = mybir.AluOpType
AX = mybir.AxisListType

D_MODEL = 1024
BATCH = 8
SEQ = 4096
DEPTH = 2
EPS = 1e-6
A_HEADS = 8
A_DHEAD = 64
M_HEADS = 8
M_DQK = 64
M_DV = 128
D_FF = 2816
C_IN = 8208
NT = SEQ // 128

O_AQ, O_AK, O_AV = 0, 1024, 2048
O_MQ, O_MK, O_MV, O_MO = 3072, 3584, 4096, 5120
O_MI, O_MF, O_GL = 6144, 6152, 6160


class Sem:
    def __init__(self, nc, name):
        self.h = nc.alloc_semaphore(name)
        self.v = 0
        self.name = name


class Buf:
    __slots__ = ("w", "r", "name")

    def __init__(self, name=""):
        self.w = {}
        self.r = {}
        self.name = name


class Eng:
    def __init__(self, k, name, e, n_dma_sems=0):
        self.k = k
        self.name = name
        self.e = e
        self.sem = Sem(k.nc, "s_" + name)
        self.waited = {}
        self.dma_sems = [Sem(k.nc, f"d_{name}{i}") for i in range(n_dma_sems)]
        self.dma_rr = 0

    def wait(self, tok):
        sem, v = tok
        if self.waited.get(sem, 0) >= v:
            return
        self.e.wait_ge(sem.h, v)
        self.waited[sem] = v


class K:
    def __init__(self, nc):
        self.nc = nc
        self.pe = Eng(self, "pe", nc.tensor)
        self.act = Eng(self, "act", nc.scalar)
        self.dve = Eng(self, "dve", nc.vector)
        self.pool = Eng(self, "pool", nc.gpsimd, n_dma_sems=4)
        self.sp = Eng(self, "sp", nc.sync, n_dma_sems=12)
        self.engs = [self.pe, self.act, self.dve, self.pool, self.sp]
        self.const_t = nc.alloc_sbuf_tensor("const_cols", [128, 32], F32)
        self.const_b = Buf("consts")
        self.consts = {}
        for v in (EPS, 0.0, math.log(0.5), 1.0):
            self.const(v)

    def const(self, val):
        val = float(val)
        if val not in self.consts:
            i = len(self.consts)
            assert i < self.const_t.shape[1]
            ap = self.const_t[:, i:i + 1]
            self.op(self.pool, lambda: self.nc.gpsimd.memset(ap, val), writes=[self.const_b])
            self.consts[val] = ap
        return self.consts[val]

    def _deps(self, eng, reads, writes, join=False):
        for b in reads:
            for s, v in b.w.items():
                eng.wait((s, v))
        for b in writes:
            if not join:
                for s, v in b.w.items():
                    eng.wait((s, v))
            for s, v in b.r.items():
                eng.wait((s, v))

    def _mark(self, tok, reads, writes, join=False):
        s, v = tok
        for b in reads:
            if b.r.get(s, 0) < v:
                b.r[s] = v
        for b in writes:
            if join:
                b.w[s] = v
            else:
                b.w = {s: v}
            b.r = {}

    def op(self, eng, fn, reads=(), writes=()):
        self._deps(eng, reads, writes)
        ins = fn()
        eng.sem.v += 1
        ins.then_inc(eng.sem.h, 1)
        self._mark((eng.sem, eng.sem.v), reads, writes)

    def mm(self, fns, reads=(), writes=()):
        eng = self.pe
        self._deps(eng, reads, writes)
        ins = None
        for fn in fns:
            ins = fn()
        eng.sem.v += 1
        ins.then_inc(eng.sem.h, 1)
        self._mark((eng.sem, eng.sem.v), reads, writes)

    def dma(self, q, out, in_, reads=(), writes=(), join=False):
        sem = q.dma_sems[q.dma_rr]
        q.dma_rr = (q.dma_rr + 1) % len(q.dma_sems)
        if sem.v:
            q.wait((sem, sem.v))
        self._deps(q, reads, writes, join)
        ins = q.e.dma_start(out=out, in_=in_)
        sem.v += 16
        ins.then_inc(sem.h, 16)
        self._mark((sem, sem.v), reads, writes, join)

    def barrier(self):
        sems = []
        for e in self.engs:
            sems.append(e.sem)
            sems.extend(e.dma_sems)
        for e in self.engs:
            for s in sems:
                if s.v:
                    e.wait((s, s.v))

    def finish(self):
        self.barrier()


_UID = [0]


def uname(name):
    _UID[0] += 1
    return f"{name}_{_UID[0]}"


class Ring:
    def __init__(self, st, nc, name, n, shape, dtype, psum=False):
        self.n = n
        self.i = 0
        self.t = []
        self.b = []
        for j in range(n):
            if psum:
                t = st.enter_context(nc.psum_tensor(uname(name), shape, dtype))
            else:
                t = st.enter_context(nc.sbuf_tensor(uname(name), shape, dtype))
            self.t.append(t)
            self.b.append(Buf(f"{name}{j}"))

    def next(self):
        j = self.i
        self.i = (self.i + 1) % self.n
        return self.t[j], self.b[j]


def sb(st, nc, name, shape, dtype):
    return st.enter_context(nc.sbuf_tensor(uname(name), shape, dtype))


def ps(st, nc, name, shape, dtype):
    return st.enter_context(nc.psum_tensor(uname(name), shape, dtype))


def load_bcast(k, st, name, vec_ap, n):
    nc = k.nc
    t = sb(st, nc, name, [128, n], F32)
    b = Buf(name)
    k.dma(k.sp, t[:], vec_ap.partition_broadcast(128), writes=[b])
    return t, b


def rms_rstd(k, ss, ss_b, rstd, rstd_b, n, eps=EPS, mul=1.0):
    nc = k.nc
    c_eps = k.const(eps)
    c_mul = k.const(math.log(mul))
    k.op(k.act, lambda: nc.scalar.activation(out=rstd, in_=ss, func=AF.Ln, scale=1.0 / n, bias=c_eps),
         reads=[ss_b, k.const_b], writes=[rstd_b])
    k.op(k.act, lambda: nc.scalar.activation(out=rstd, in_=rstd, func=AF.Exp, scale=-0.5, bias=c_mul),
         reads=[rstd_b, k.const_b], writes=[rstd_b])


def norm_h(k, rings, x_src, tok0, npre, npre_b):
    nc = k.nc
    xs, hb, stat = rings
    x_t, x_b = xs.next()
    k.dma(k.sp, x_t[:], x_src[tok0:tok0 + 128, :], writes=[x_b])
    h_t, h_b = hb.next()
    s_t, s_b = stat.next()
    k.op(k.act, lambda: nc.scalar.activation(out=h_t[:], in_=x_t[:], func=AF.Square, accum_out=s_t[:, 0:1]),
         reads=[x_b], writes=[h_b, s_b])
    rms_rstd(k, s_t[:, 0:1], s_b, s_t[:, 1:2], s_b, D_MODEL)
    k.op(k.dve, lambda: nc.vector.scalar_tensor_tensor(out=h_t[:], in0=x_t[:], scalar=s_t[:, 1:2],
                                                       in1=npre[:], op0=ALU.mult, op1=ALU.mult),
         reads=[x_b, s_b, npre_b], writes=[h_b])
    return h_t, h_b


def transp_h(k, ptr, ident, ident_b, h_t, h_b, hT_t, hT_b, s):
    nc = k.nc
    p_t, p_b = ptr.next()
    k.mm([(lambda c=c: nc.tensor.transpose(p_t[:, c, :], h_t[:, c * 128:(c + 1) * 128], ident[:]))
          for c in range(8)], reads=[h_b, ident_b], writes=[p_b])
    k.op(k.act, lambda: nc.scalar.copy(out=hT_t[:, :, s * 128:(s + 1) * 128], in_=p_t[:]),
         reads=[p_b], writes=[hT_b])


def phase_ffn(k, ident, ident_b, x_in, x_out, w_gu, w_down, n_pre, n_post):
    nc = k.nc
    TT = 512
    NTT = SEQ // TT
    NS = TT // 128
    NJ = D_FF // 128
    with ExitStack() as st:
        wgu = sb(st, nc, "wgu", [128, 8, 2 * D_FF], BF16)
        wdn = sb(st, nc, "wdn", [128, NJ, D_MODEL], BF16)
        JB = [(0, 4), (4, 8), (8, 12), (12, 16), (16, 20), (20, 22)]
        wgu_b = {}
        for bi, (j0, j1) in enumerate(JB):
            for gu in range(2):
                c0 = gu * D_FF + j0 * 128
                c1 = gu * D_FF + j1 * 128
                b = Buf()
                for j in range(j0, j1):
                    wgu_b[(gu, j)] = b
                for c in range(8):
                    k.dma(k.pool, wgu[:, c, c0:c1], w_gu[c * 128:(c + 1) * 128, c0:c1], writes=[b], join=True)
        wdn_b = []
        for hh in range(2):
            b = Buf()
            wdn_b.append(b)
            k.dma(k.pool, wdn[:, hh * 11:(hh + 1) * 11, :],
                  w_down[hh * 1408:(hh + 1) * 1408, :].rearrange("(j p) n -> p j n", p=128), writes=[b])
        npre, npre_b = load_bcast(k, st, "npre", n_pre, D_MODEL)
        npost, npost_b = load_bcast(k, st, "npost", n_post, D_MODEL)

        xs = Ring(st, nc, "xs", 2, [128, D_MODEL], F32)
        hb = Ring(st, nc, "hb", 4, [128, D_MODEL], BF16)
        stat = Ring(st, nc, "stat", 8, [128, 4], F32)
        hT = Ring(st, nc, "hT", 1, [128, 8, TT], BF16)
        actT = Ring(st, nc, "actT", 1, [128, NJ, TT], BF16)
        sg = Ring(st, nc, "sg", 2, [128, TT], F32)
        xr = Ring(st, nc, "xr", 2, [128, D_MODEL], F32)
        yo = Ring(st, nc, "yo", 1, [128, D_MODEL], F32)
        ptr = Ring(st, nc, "ptr", 1, [128, 8, 128], BF16, psum=True)
        pg = Ring(st, nc, "pg", 2, [128, TT], F32, psum=True)
        pu = Ring(st, nc, "pu", 2, [128, TT], F32, psum=True)
        py = Ring(st, nc, "py", 1, [128, 2, 512], F32, psum=True)
        nrings = (xs, hb, stat)

        hT_t, hT_b = hT.next()
        hs = [norm_h(k, nrings, x_in, s * 128, npre, npre_b) for s in range(NS)]
        for s in range(NS):
            transp_h(k, ptr, ident, ident_b, hs[s][0], hs[s][1], hT_t, hT_b, s)
        for t in range(NTT):
            if t + 1 < NTT:
                hs = [norm_h(k, nrings, x_in, ((t + 1) * NS + s) * 128, npre, npre_b) for s in range(NS)]
            a_t, a_b = actT.next()
            for j in range(NJ):
                g_t, g_b = pg.next()
                u_t, u_b = pu.next()
                k.mm([(lambda c=c: nc.tensor.matmul(g_t[:], wgu[:, c, j * 128:(j + 1) * 128], hT_t[:, c, :],
                                                    start=(c == 0), stop=(c == 7))) for c in range(8)],
                     reads=[wgu_b[(0, j)], hT_b], writes=[g_b])
                k.mm([(lambda c=c: nc.tensor.matmul(u_t[:], wgu[:, c, D_FF + j * 128:D_FF + (j + 1) * 128],
                                                    hT_t[:, c, :], start=(c == 0), stop=(c == 7)))
                      for c in range(8)],
                     reads=[wgu_b[(1, j)], hT_b], writes=[u_b])
                sg_t, sg_b = sg.next()
                k.op(k.act, lambda: nc.scalar.activation(out=sg_t[:], in_=g_t[:], func=AF.Silu),
                     reads=[g_b], writes=[sg_b])
                k.op(k.dve, lambda: nc.vector.tensor_tensor(out=a_t[:, j, :], in0=sg_t[:], in1=u_t[:],
                                                            op=ALU.mult),
                     reads=[sg_b, u_b], writes=[a_b])
            if t + 1 < NTT:
                hT_t, hT_b = hT.next()
                for s in range(NS):
                    transp_h(k, ptr, ident, ident_b, hs[s][0], hs[s][1], hT_t, hT_b, s)
            for s in range(NS):
                tok0 = (t * NS + s) * 128
                y_t, y_b = py.next()
                fns = []
                for hf in range(2):
                    for j in range(NJ):
                        fns.append(lambda j=j, hf=hf: nc.tensor.matmul(
                            y_t[:, hf, :], a_t[:, j, s * 128:(s + 1) * 128], wdn[:, j, hf * 512:(hf + 1) * 512],
                            start=(j == 0), stop=(j == NJ - 1)))
                k.mm(fns, reads=[a_b] + wdn_b, writes=[y_b])
                xr_t, xr_b = xr.next()
                k.dma(k.sp, xr_t[:], x_in[tok0:tok0 + 128, :], writes=[xr_b])
                o_t, o_b = yo.next()
                s_t, s_b = stat.next()
                for hf in range(2):
                    k.op(k.act, lambda: nc.scalar.activation(out=o_t[:, hf * 512:(hf + 1) * 512], in_=y_t[:, hf, :],
                                                             func=AF.Square, accum_out=s_t[:, 2 + hf:3 + hf]),
                         reads=[y_b], writes=[o_b, s_b])
                k.op(k.dve, lambda: nc.vector.tensor_tensor(out=s_t[:, 0:1], in0=s_t[:, 2:3], in1=s_t[:, 3:4],
                                                            op=ALU.add), reads=[s_b], writes=[s_b])
                rms_rstd(k, s_t[:, 0:1], s_b, s_t[:, 1:2], s_b, D_MODEL, mul=0.5)
                for hf in range(2):
                    k.op(k.dve, lambda: nc.vector.tensor_tensor(
                        out=o_t[:, hf * 512:(hf + 1) * 512], in0=y_t[:, hf, :], in1=npost[:, hf * 512:(hf + 1) * 512],
                        op=ALU.mult), reads=[y_b, npost_b], writes=[o_b])
                k.op(k.dve, lambda: nc.vector.scalar_tensor_tensor(
                    out=xr_t[:], in0=o_t[:], scalar=s_t[:, 1:2], in1=xr_t[:], op0=ALU.mult, op1=ALU.add),
                    reads=[o_b, xr_b, s_b], writes=[xr_b])
                k.dma(k.sp, x_out[tok0:tok0 + 128, :], xr_t[:], reads=[xr_b])
        k.barrier()


def phase_proj(k, ident, ident_b, x_in, w_in, n_pre, gate_b_pc, S):
    nc = k.nc
    TT = 512
    NTT = SEQ // TT
    NS = TT // 128
    with ExitStack() as st:
        win = sb(st, nc, "win", [128, 8, C_IN], BF16)
        blocks = [(0, 512, 0), (512, 1024, 0), (1024, 2048, 0), (2048, 3072, 1), (3072, 4096, 1),
                  (4096, 5120, 2), (5120, 6160, 2), (6160, 7184, 3), (7184, 8208, 3)]
        wb = [Buf() for _ in range(4)]
        for (c0, c1, bi) in blocks:
            for c in range(8):
                k.dma(k.pool, win[:, c, c0:c1], w_in[c * 128:(c + 1) * 128, c0:c1], writes=[wb[bi]], join=True)
        npre, npre_b = load_bcast(k, st, "npre", n_pre, D_MODEL)
        gb = sb(st, nc, "gb", [128, 16], F32)
        gb_b = Buf()
        k.dma(k.sp, gb[:], gate_b_pc, writes=[gb_b])
        xs = Ring(st, nc, "xs", 2, [128, D_MODEL], F32)
        hb = Ring(st, nc, "hb", 4, [128, D_MODEL], BF16)
        stat = Ring(st, nc, "stat", 8, [128, 4], F32)
        ptr = Ring(st, nc, "ptr", 1, [128, 8, 128], BF16, psum=True)
        nrings = (xs, hb, stat)
        hT = Ring(st, nc, "hT", 1, [128, 8, TT], BF16)
        of32 = Ring(st, nc, "of32", 4, [128, 512], F32)
        obf = Ring(st, nc, "obf", 4, [128, 512], BF16)
        pf = Ring(st, nc, "pf", 3, [128, 512], F32, psum=True)
        pt = Ring(st, nc, "pt", 3, [128, 512], F32, psum=True)
        ev = [0]

        def evac_copy(out_ap, in_ap, rd, wr):
            ev[0] ^= 1
            if ev[0]:
                k.op(k.dve, lambda: nc.vector.tensor_copy(out=out_ap, in_=in_ap), reads=rd, writes=wr)
            else:
                k.op(k.act, lambda: nc.scalar.copy(out=out_ap, in_=in_ap), reads=rd, writes=wr)

        hT_t, hT_b = hT.next()
        hs = [norm_h(k, nrings, x_in, s * 128, npre, npre_b) for s in range(NS)]
        for s in range(NS):
            transp_h(k, ptr, ident, ident_b, hs[s][0], hs[s][1], hT_t, hT_b, s)
        for t in range(NTT):
            T0 = t * TT
            if t + 1 < NTT:
                hs = [norm_h(k, nrings, x_in, (t + 1) * TT + s * 128, npre, npre_b) for s in range(NS)]

            def fm_chunk(col0, wbuf):
                p_t, p_b = pf.next()
                k.mm([(lambda c=c: nc.tensor.matmul(p_t[:], win[:, c, col0:col0 + 128], hT_t[:, c, :],
                                                    start=(c == 0), stop=(c == 7))) for c in range(8)],
                     reads=[wbuf, hT_b], writes=[p_b])
                return p_t, p_b

            def tm_block(s, col0, n, wbuf):
                p_t, p_b = pt.next()
                k.mm([(lambda c=c: nc.tensor.matmul(p_t[:, 0:n], hT_t[:, c, s * 128:(s + 1) * 128],
                                                    win[:, c, col0:col0 + n], start=(c == 0), stop=(c == 7)))
                      for c in range(8)], reads=[wbuf, hT_b], writes=[p_b])
                return p_t, p_b

            for c in range(16):
                p_t, p_b = fm_chunk(c * 128, wb[0])
                o_t, o_b = obf.next()
                evac_copy(o_t[:], p_t[:], [p_b], [o_b])
                k.dma(k.sp, S["qkT"][c * 128:(c + 1) * 128, T0:T0 + TT], o_t[:], reads=[o_b])
            for c in range(8):
                p_t, p_b = fm_chunk(O_MQ + c * 128, wb[1])
                o_t, o_b = of32.next()
                evac_copy(o_t[:], p_t[:], [p_b], [o_b])
                k.dma(k.sp, S["mqkT"][c * 128:(c + 1) * 128, T0:T0 + TT], o_t[:], reads=[o_b])
            for s in range(NS):
                tok0 = T0 + s * 128
                for hf in range(2):
                    p_t, p_b = tm_block(s, O_AV + hf * 512, 512, wb[1])
                    o_t, o_b = obf.next()
                    evac_copy(o_t[:], p_t[:], [p_b], [o_b])
                    k.dma(k.sp, S["av"][tok0:tok0 + 128, hf * 512:(hf + 1) * 512], o_t[:], reads=[o_b])
                for hf in range(2):
                    p_t, p_b = tm_block(s, O_MV + hf * 512, 512, wb[2])
                    o_t, o_b = of32.next()
                    evac_copy(o_t[:], p_t[:], [p_b], [o_b])
                    k.dma(k.sp, S["mv"][tok0:tok0 + 128, hf * 512:(hf + 1) * 512], o_t[:], reads=[o_b])
                for hf in range(2):
                    p_t, p_b = tm_block(s, O_MO + hf * 512, 512, wb[2])
                    o_t, o_b = of32.next()
                    k.op(k.act, lambda: nc.scalar.activation(out=o_t[:], in_=p_t[:], func=AF.Sigmoid),
                         reads=[p_b], writes=[o_b])
                    k.dma(k.sp, S["mo"][tok0:tok0 + 128, hf * 512:(hf + 1) * 512], o_t[:], reads=[o_b])
                p_t, p_b = tm_block(s, O_MI, 16, wb[2])
                o_t, o_b = of32.next()
                evac_copy(o_t[:, 0:16], p_t[:, 0:16], [p_b], [o_b])
                k.dma(k.sp, S["mif"][tok0:tok0 + 128, :], o_t[:, 0:16], reads=[o_b])
            for c in range(16):
                p_t, p_b = fm_chunk(O_GL + c * 128, wb[3])
                o_t, o_b = of32.next()
                k.op(k.act, lambda: nc.scalar.activation(out=o_t[:], in_=p_t[:], func=AF.Sigmoid,
                                                         bias=gb[:, c:c + 1]),
                     reads=[p_b, gb_b], writes=[o_b])
                k.dma(k.sp, S["glT"][c * 128:(c + 1) * 128, T0:T0 + TT], o_t[:], reads=[o_b])
            if t + 1 < NTT:
                hT_t, hT_b = hT.next()
                for s in range(NS):
                    transp_h(k, ptr, ident, ident_b, hs[s][0], hs[s][1], hT_t, hT_b, s)
        k.barrier()


def phase_attn(k, cbf, cbf_b, cf32, cf32_b, S, lamp, anw_col, lam_init):
    nc = k.nc
    NG = SEQ // 512
    with ExitStack() as st:
        lt, lt_b = load_bcast(k, st, "lamp", lamp, 256)
        lw = sb(st, nc, "lamw", [128, 136], F32)
        lw_b = Buf()
        k.op(k.dve, lambda: nc.vector.tensor_tensor(out=lw[:, 0:128], in0=lt[:, 0:128], in1=lt[:, 128:256],
                                                    op=ALU.mult), reads=[lt_b], writes=[lw_b])
        k.op(k.dve, lambda: nc.vector.tensor_reduce(out=lw[:, 128:130],
                                                    in_=lw[:, 0:128].rearrange("p (a b) -> p a b", a=2),
                                                    axis=AX.X, op=ALU.add), reads=[lw_b], writes=[lw_b])
        k.op(k.act, lambda: nc.scalar.activation(out=lw[:, 130:132], in_=lw[:, 128:130], func=AF.Exp),
             reads=[lw_b], writes=[lw_b])
        k.op(k.dve, lambda: nc.vector.tensor_tensor(out=lw[:, 132:133], in0=lw[:, 130:131], in1=lw[:, 131:132],
                                                    op=ALU.subtract), reads=[lw_b], writes=[lw_b])
        k.op(k.dve, lambda: nc.vector.tensor_scalar(out=lw[:, 133:134], in0=lw[:, 132:133], scalar1=lam_init,
                                                    scalar2=-1.0, op0=ALU.add, op1=ALU.mult),
             reads=[lw_b], writes=[lw_b])
        neg_lam = lw[:, 133:134]
        nw = sb(st, nc, "anw", [128, 1], F32)
        nw_b = Buf()
        k.dma(k.sp, nw[:], anw_col, writes=[nw_b])
        k.op(k.dve, lambda: nc.vector.tensor_scalar(out=nw[:], in0=nw[:], scalar1=1.0 - lam_init, scalar2=None,
                                                    op0=ALU.mult), reads=[nw_b], writes=[nw_b])

        qT = Ring(st, nc, "qT", 2, [128, SEQ], BF16)
        kT = Ring(st, nc, "kT", 2, [128, SEQ], BF16)
        vh = Ring(st, nc, "vh", 2, [128, NT_(), 128], BF16)
        pT = Ring(st, nc, "pT", 6, [128, 512], BF16)
        f32r = Ring(st, nc, "af", 12, [128, 512], F32)
        yo = Ring(st, nc, "yao", 2, [128, 512], BF16)
        psr = Ring(st, nc, "psS", 4, [128, 512], F32, psum=True)
        po = [ps(st, nc, "po", [128, 512], F32) for _ in range(2)]
        pl = [ps(st, nc, "pl", [128, 512], F32) for _ in range(2)]
        po_b = [Buf(), Buf()]
        pl_b = [Buf(), Buf()]
        ones_bf = cbf[:, 1, :]
        tri = cbf[:, 2, :]

        todo = []
        for h in range(A_HEADS):
            q_t, q_b = qT.next()
            k_t, k_b = kT.next()
            v_t, v_b = vh.next()
            k.dma(k.sp, q_t[:], S["qkT"][h * 128:(h + 1) * 128, :], writes=[q_b])
            k.dma(k.sp, k_t[:], S["qkT"][1024 + h * 128:1024 + (h + 1) * 128, :], writes=[k_b])
            vsrc = S["av"][:, h * 128:(h + 1) * 128].rearrange("(t p) d -> p t d", p=128)
            nq = max(1, NT_() // 8)
            for qd in range(nq):
                t0_, t1_ = qd * NT_() // nq, (qd + 1) * NT_() // nq
                k.dma(k.sp, v_t[:, t0_:t1_, :], vsrc[:, t0_:t1_, :], writes=[v_b], join=True)
            for g in range(NG):
                nkt = 4 * g + 4
                pend = {}

                def stage_a(kt):
                    j = kt - 4 * g
                    c0 = 128 * j if j > 0 else 0
                    outs = []
                    sts = [psr.next() for _ in range(2)]
                    k.mm([(lambda m=m: nc.tensor.matmul(
                        sts[m][0][:, c0:512], k_t[m * 64:(m + 1) * 64, kt * 128:(kt + 1) * 128],
                        q_t[m * 64:(m + 1) * 64, g * 512 + c0:(g + 1) * 512], start=True, stop=True))
                        for m in range(2)], reads=[q_b, k_b], writes=[sts[0][1], sts[1][1]])
                    for m in range(2):
                        s_t, s_b = sts[m]
                        p_t, p_b = pT.next()
                        k.op(k.act, lambda: nc.scalar.activation(out=p_t[:, c0:512], in_=s_t[:, c0:512],
                                                                 func=AF.Exp, scale=A_DHEAD ** -0.5),
                             reads=[s_b], writes=[p_b])
                        if j >= 0:
                            k.op(k.pool, lambda: nc.gpsimd.tensor_tensor(out=p_t[:, c0:c0 + 128],
                                                                         in0=p_t[:, c0:c0 + 128], in1=tri,
                                                                         op=ALU.mult),
                                 reads=[p_b, cbf_b], writes=[p_b])
                        outs.append((p_t, p_b, c0))
                    pend[kt] = outs

                def stage_b(kt):
                    for m in range(2):
                        p_t, p_b, c0 = pend[kt][m]
                        k.mm([lambda: nc.tensor.matmul(po[m][:, c0:512], v_t[:, kt, :], p_t[:, c0:512],
                                                       start=(kt == 0), stop=(kt == nkt - 1)),
                              lambda: nc.tensor.matmul(pl[m][:, c0:512], ones_bf, p_t[:, c0:512],
                                                       start=(kt == 0), stop=(kt == nkt - 1))],
                             reads=[p_b, v_b, cbf_b], writes=[po_b[m], pl_b[m]])
                    del pend[kt]

                stage_a(0)
                for kt in range(1, nkt):
                    stage_a(kt)
                    stage_b(kt - 1)
                    if todo and kt >= 3:
                        todo.pop(0)()
                stage_b(nkt - 1)
                while todo:
                    todo.pop(0)()

                oc, lc = [], []
                for m in range(2):
                    o_t, o_b = f32r.next()
                    k.op(k.dve, lambda: nc.vector.tensor_copy(out=o_t[:], in_=po[m][:]), reads=[po_b[m]], writes=[o_b])
                    oc.append((o_t, o_b))
                    l_t, l_b = f32r.next()
                    k.op(k.act, lambda: nc.scalar.copy(out=l_t[:], in_=pl[m][:]), reads=[pl_b[m]], writes=[l_b])
                    lc.append((l_t, l_b))
                for m in range(2):
                    l_t, l_b = lc[m]
                    o_t, o_b = oc[m]
                    k.op(k.dve, lambda: nc.vector.reciprocal(out=l_t[:], in_=l_t[:]), reads=[l_b], writes=[l_b])
                    k.op(k.dve, lambda: nc.vector.tensor_tensor(out=o_t[:], in0=o_t[:], in1=l_t[:], op=ALU.mult),
                         reads=[o_b, l_b], writes=[o_b])
                d_t, d_b = oc[0]
                k.op(k.dve, lambda: nc.vector.scalar_tensor_tensor(out=d_t[:], in0=oc[1][0][:], scalar=neg_lam,
                                                                   in1=d_t[:], op0=ALU.mult, op1=ALU.add),
                     reads=[oc[1][1], d_b, lw_b], writes=[d_b])

                def make_tail(h=h, g=g, d_t=d_t, d_b=d_b):
                    st8 = {}

                    def e3():
                        st8["sq"] = f32r.next()
                        sq_t, sq_b = st8["sq"]
                        k.op(k.act, lambda: nc.scalar.activation(out=sq_t[:], in_=d_t[:], func=AF.Square),
                             reads=[d_b], writes=[sq_b])

                    def e4():
                        sq_t, sq_b = st8["sq"]
                        st8["ss"] = psr.next()
                        s_t, s_b = st8["ss"]
                        k.mm([lambda: nc.tensor.matmul(s_t[:], cf32[:, 0, :], sq_t[:], start=True, stop=True)],
                             reads=[sq_b, cf32_b], writes=[s_b])

                    def e5():
                        s_t, s_b = st8["ss"]
                        st8["rs"] = f32r.next()
                        rs_t, rs_b = st8["rs"]
                        rms_rstd(k, s_t[:], s_b, rs_t[:], rs_b, 128)

                    def e6():
                        rs_t, rs_b = st8["rs"]
                        y_t, y_b = yo.next()
                        k.op(k.dve, lambda: nc.vector.scalar_tensor_tensor(out=y_t[:], in0=d_t[:], scalar=nw[:, 0:1],
                                                                           in1=rs_t[:], op0=ALU.mult, op1=ALU.mult),
                             reads=[d_b, nw_b, rs_b], writes=[y_b])
                        k.dma(k.sp, S["yaT"][h * 128:(h + 1) * 128, g * 512:(g + 1) * 512], y_t[:], reads=[y_b])
                    return [e3, e4, e5, e6]
                todo.extend(make_tail())
        while todo:
            todo.pop(0)()
        k.barrier()


def NT_():
    return SEQ // 128


def make_scratch(nc, kind="Internal"):
    def d(name, shape, dt):
        return nc.dram_tensor(name, shape, dt, kind=kind).ap()
    return {
        "qkT": d("s_qkT", [2048, SEQ], BF16),
        "av": d("s_av", [SEQ, 1024], BF16),
        "mqkT": d("s_mqkT", [1024, SEQ], F32),
        "mv": d("s_mv", [SEQ, 1024], F32),
        "mo": d("s_mo", [SEQ, 1024], F32),
        "mif": d("s_mif", [SEQ, 16], F32),
        "glT": d("s_glT", [2048, SEQ], F32),
        "yaT": d("s_yaT", [1024, SEQ], BF16),
        "ymT": d("s_ymT", [1024, SEQ], BF16),
    }


def host_consts():
    import ml_dtypes
    p = np.arange(128)[:, None]
    f = np.arange(128)[None, :]
    tri = (p <= f).astype(np.float32)
    cbf = np.stack([np.eye(128, dtype=np.float32), np.ones((128, 128), np.float32), tri], axis=1)
    cf32 = np.stack([np.ones((128, 128), np.float32), tri, tri * (M_DQK ** -0.5)], axis=1)
    return {"cbf": np.ascontiguousarray(cbf).astype(ml_dtypes.bfloat16),
            "cf32": np.ascontiguousarray(cf32).astype(np.float32)}


def load_consts(k, st):
    nc = k.nc
    cbf_d = nc.dram_tensor("cbf", [128, 3, 128], BF16, kind="ExternalInput").ap()
    cf32_d = nc.dram_tensor("cf32", [128, 3, 128], F32, kind="ExternalInput").ap()
    cbf = sb(st, nc, "cbf_sb", [128, 3, 128], BF16)
    cf32 = sb(st, nc, "cf32_sb", [128, 3, 128], F32)
    cbf_b, cf32_b = Buf(), Buf()
    k.dma(k.sp, cbf[:], cbf_d, writes=[cbf_b])
    k.dma(k.sp, cf32[:], cf32_d, writes=[cf32_b])
    return cbf, cbf_b, cf32, cf32_b


def phase_mlstm(k, cbf, cbf_b, cf32, cf32_b, S, convw_pc, convb_pc, igb, fgb, mnw):
    nc = k.nc
    NTL = SEQ // 128
    ident = cbf[:, 0, :]
    ones32 = cf32[:, 0, :]
    triu32 = cf32[:, 1, :]
    mask8 = cf32[:, 2, :]
    with ExitStack() as st:
        qkc = sb(st, nc, "qkc", [128, 8, SEQ], BF16)
        qkc_b = [Buf() for _ in range(8)]
        cw = sb(st, nc, "cw", [128, 8, 4], F32)
        cb = sb(st, nc, "cb", [128, 8], F32)
        cw_b, cb_b = Buf(), Buf()
        k.dma(k.sp, cw[:], convw_pc, writes=[cw_b])
        k.dma(k.sp, cb[:], convb_pc, writes=[cb_b])
        with ExitStack() as st1:
            xp = Ring(st1, nc, "xp", 2, [128, SEQ + 3], F32)
            acc = Ring(st1, nc, "cacc", 1, [128, SEQ], F32)
            for i in range(2):
                k.op(k.pool, lambda: nc.gpsimd.memset(xp.t[i][:, 0:3], 0.0), writes=[xp.b[i]])
            for c in range(8):
                x_t, x_b = xp.next()
                k.dma(k.sp, x_t[:, 3:SEQ + 3], S["mqkT"][c * 128:(c + 1) * 128, :], writes=[x_b])
                a_t, a_b = acc.next()
                k.op(k.dve, lambda: nc.vector.tensor_scalar(out=a_t[:], in0=x_t[:, 0:SEQ], scalar1=cw[:, c, 0:1],
                                                            scalar2=None, op0=ALU.mult),
                     reads=[x_b, cw_b], writes=[a_b])
                for j in range(1, 4):
                    k.op(k.dve, lambda: nc.vector.scalar_tensor_tensor(
                        out=a_t[:], in0=x_t[:, j:j + SEQ], scalar=cw[:, c, j:j + 1], in1=a_t[:],
                        op0=ALU.mult, op1=ALU.add), reads=[x_b, cw_b, a_b], writes=[a_b])
                k.op(k.act, lambda: nc.scalar.activation(out=qkc[:, c, :], in_=a_t[:], func=AF.Silu,
                                                         bias=cb[:, c:c + 1]),
                     reads=[a_b, cb_b], writes=[qkc_b[c]])
        k.barrier()

        ktok = sb(st, nc, "ktok", [128, NTL, 512], BF16)
        ktok_b = Buf()
        NC8 = NTL * 8
        gt = sb(st, nc, "gt", [128, NTL, 16], F32)
        gt_b = Buf()
        gsrc = S["mif"].rearrange("(t p) c -> p t c", p=128)
        nq = max(1, NTL // 8)
        for qd in range(nq):
            t0_, t1_ = qd * NTL // nq, (qd + 1) * NTL // nq
            k.dma(k.sp, gt[:, t0_:t1_, :], gsrc[:, t0_:t1_, :], writes=[gt_b], join=True)
        igb_t, igb_b = load_bcast(k, st, "igb", igb, 8)
        fgb_t, fgb_b = load_bcast(k, st, "fgb", fgb, 8)
        mnw_t, mnw_b = load_bcast(k, st, "mnw", mnw, 1024)
        IG = sb(st, nc, "IG", [128, NTL, 8], F32)
        NLF = sb(st, nc, "NLF", [128, NTL, 8], F32)
        NGc = sb(st, nc, "NGc", [128, NTL, 8], F32)
        EQ = sb(st, nc, "EQ", [128, NTL, 8], F32)
        EK = sb(st, nc, "EK", [128, NTL, 8], F32)
        EE = sb(st, nc, "EE", [128, NTL, 8], F32)
        IG_b, NLF_b, NG_b, EQ_b, EK_b, EE_b = [Buf() for _ in range(6)]

        pa = Ring(st, nc, "pa", 2, [128, 4, 128], F32, psum=True)
        pr = Ring(st, nc, "pr", 2, [128, 4, 128], F32, psum=True)
        pu = Ring(st, nc, "pu", 2, [128, 4, 128], F32, psum=True)
        pm = ps(st, nc, "pm", [128, 512], F32)
        prd, prd_b = pm[:, 0:8], Buf()
        pun, pun_b = pm[:, 8:16], Buf()
        ptr = ps(st, nc, "ptrm", [128, 8, 128], BF16)
        ptr_b = Buf()

        for t in range(NTL):
            k.mm([(lambda c=c: nc.tensor.transpose(ptr[:, c, :], qkc[:, 4 + c, t * 128:(t + 1) * 128], ident))
                  for c in range(4)], reads=qkc_b[4:8] + [cbf_b], writes=[ptr_b])
            k.op(k.act, lambda: nc.scalar.copy(out=ktok[:, t, :], in_=ptr[:, 0:4, :].rearrange("p a b -> p (a b)")),
                 reads=[ptr_b], writes=[ktok_b])

        k.op(k.dve, lambda: nc.vector.tensor_tensor(out=IG[:], in0=gt[:, :, 0:8],
                                                    in1=igb_t[:].unsqueeze(1).to_broadcast([128, NTL, 8]),
                                                    op=ALU.add), reads=[gt_b, igb_b], writes=[IG_b])
        k.op(k.dve, lambda: nc.vector.tensor_tensor(out=NLF[:], in0=gt[:, :, 8:16],
                                                    in1=fgb_t[:].unsqueeze(1).to_broadcast([128, NTL, 8]),
                                                    op=ALU.add), reads=[gt_b, fgb_b], writes=[NLF_b])
        k.op(k.act, lambda: nc.scalar.activation(out=NLF[:], in_=NLF[:], func=AF.Exp, scale=-1.0),
             reads=[NLF_b], writes=[NLF_b])
        c_one = k.const(1.0)
        k.op(k.act, lambda: nc.scalar.activation(out=NLF[:], in_=NLF[:], func=AF.Ln, bias=c_one),
             reads=[NLF_b, k.const_b], writes=[NLF_b])
        nlf2 = NLF[:].rearrange("p a b -> p (a b)")
        pg_t, pg_b = pa.next()
        pg2 = pg_t[:].rearrange("p a b -> p (a b)")
        k.mm([lambda: nc.tensor.matmul(pg2[:, 0:NC8], triu32, nlf2, start=True, stop=True)],
             reads=[NLF_b, cf32_b], writes=[pg_b])
        k.op(k.dve, lambda: nc.vector.tensor_copy(out=NGc[:].rearrange("p a b -> p (a b)"), in_=pg2[:, 0:NC8]),
             reads=[pg_b], writes=[NG_b])
        pe_t, pe_b = pa.next()
        pe2 = pe_t[:].rearrange("p a b -> p (a b)")
        k.mm([lambda: nc.tensor.matmul(pe2[:, 0:NC8], ones32, nlf2, start=True, stop=True)],
             reads=[NLF_b, cf32_b], writes=[pe_b])
        k.op(k.act, lambda: nc.scalar.activation(out=EE[:].rearrange("p a b -> p (a b)"), in_=pe2[:, 0:NC8],
                                                 func=AF.Exp, scale=-1.0), reads=[pe_b], writes=[EE_b])
        k.op(k.act, lambda: nc.scalar.activation(out=EQ[:], in_=NGc[:], func=AF.Exp, scale=-1.0),
             reads=[NG_b], writes=[EQ_b])
        k.op(k.dve, lambda: nc.vector.tensor_tensor(out=EK[:], in0=IG[:], in1=NGc[:], op=ALU.add),
             reads=[IG_b, NG_b], writes=[EK_b])
        k.op(k.act, lambda: nc.scalar.activation(out=EK[:], in_=EK[:], func=AF.Exp), reads=[EK_b], writes=[EK_b])

        mvr = Ring(st, nc, "mvr", 2, [128, 8, 128], F32)
        mor = Ring(st, nc, "mor", 2, [128, 1024], F32)
        vtr = Ring(st, nc, "vtr", 2, [128, 8, 128], BF16)
        ekr = Ring(st, nc, "ekr", 2, [128, 8], BF16)
        ptT = Ring(st, nc, "ptT", 2, [128, 8, 128], BF16)
        Tst = sb(st, nc, "Tst", [128, 4, 129], F32)
        T_b = [Buf() for _ in range(8)]
        S8r = Ring(st, nc, "S8r", 2, [128, 4, 129], BF16)
        S8_bufs = {0: [Buf() for _ in range(8)], 1: [Buf() for _ in range(8)]}
        sm = Ring(st, nc, "msm", 2, [128, 48], F32)
        ho = Ring(st, nc, "mho", 1, [128, 8, 128], F32)
        sq = Ring(st, nc, "msq", 1, [128, 8, 128], F32)
        y1 = Ring(st, nc, "my1", 1, [128, 8, 128], F32)
        ymb = Ring(st, nc, "ymb", 2, [128, 1024], BF16)
        ymT = Ring(st, nc, "ymTs", 2, [128, 8, 128], BF16)
        ctx = {}
        s8 = {}

        def stage1(t):
            c = ctx[t] = {}
            tsl = slice(t * 128, (t + 1) * 128)
            mv_t, mv_b = mvr.next()
            k.dma(k.sp, mv_t[:], S["mv"][tsl, :].rearrange("p (a b) -> p a b", a=8), writes=[mv_b])
            c["mo"] = mor.next()
            k.dma(k.sp, c["mo"][0][:], S["mo"][tsl, :], writes=[c["mo"][1]])
            vt_t, vt_b = c["vt"] = vtr.next()
            k.op(k.dve, lambda: nc.vector.tensor_tensor(out=vt_t[:], in0=mv_t[:],
                                                        in1=EK[:, t, :].unsqueeze(2).to_broadcast([128, 8, 128]),
                                                        op=ALU.mult), reads=[mv_b, EK_b], writes=[vt_b])
            ek_t, ek_b = c["ek"] = ekr.next()
            k.op(k.dve, lambda: nc.vector.tensor_copy(out=ek_t[:], in_=EK[:, t, :]), reads=[EK_b], writes=[ek_b])
            pT_t, pT_b = c["pT"] = ptT.next()
            a_te = [pa.next() for _ in range(2)]
            fns = []
            for i in range(4):
                for e in range(2):
                    P0 = e * 64
                    fns.append(lambda i=i, e=e, P0=P0: nc.tensor.matmul(
                        a_te[e][0][:, i, :], qkc[P0:P0 + 64, 4 + i, tsl], qkc[P0:P0 + 64, i, tsl],
                        start=True, stop=True))
            k.mm(fns, reads=qkc_b, writes=[a_te[0][1], a_te[1][1]])
            for e in range(2):
                k.op(k.dve, lambda: nc.vector.tensor_tensor(
                    out=pT_t[:, 4 * e:4 * e + 4, :], in0=a_te[e][0][:],
                    in1=mask8.unsqueeze(1).to_broadcast([128, 4, 128]), op=ALU.mult),
                    reads=[a_te[e][1], cf32_b], writes=[pT_b])

        def stage2(t):
            c = ctx[t]
            tsl = slice(t * 128, (t + 1) * 128)
            pT_t, pT_b = c["pT"]
            vt_t, vt_b = c["vt"]
            ek_t, ek_b = c["ek"]
            s8_prev = s8.get(t - 1)
            r_ts = c["r"] = []
            for bk in range(2):
                r_t, r_b = pr.next()
                fns = []
                for hh in range(4):
                    h = 4 * bk + hh
                    P0 = (h % 2) * 64
                    fns.append(lambda h=h, hh=hh: nc.tensor.matmul(
                        r_t[:, hh, :], pT_t[:, (h % 2) * 4 + h // 2, :], vt_t[:, h, :], start=True, stop=(t == 0)))
                    if t > 0:
                        fns.append(lambda h=h, hh=hh, P0=P0: nc.tensor.matmul(
                            r_t[:, hh, :], qkc[P0:P0 + 64, h // 2, tsl], s8_prev[0][P0:P0 + 64, h // 2, 0:128],
                            start=False, stop=True))
                rd = [pT_b, vt_b] + qkc_b + (s8_prev[1] if t > 0 else [])
                k.mm(fns, reads=rd, writes=[r_b])
                r_ts.append((r_t, r_b))
            fns = []
            for h in range(8):
                P0 = (h % 2) * 64
                fns.append(lambda h=h: nc.tensor.matmul(prd[:, h:h + 1], pT_t[:, (h % 2) * 4 + h // 2, :],
                                                        ek_t[:, h:h + 1], start=True, stop=(t == 0)))
                if t > 0:
                    fns.append(lambda h=h, P0=P0: nc.tensor.matmul(
                        prd[:, h:h + 1], qkc[P0:P0 + 64, h // 2, tsl], s8_prev[0][P0:P0 + 64, h // 2, 128:129],
                        start=False, stop=True))
            k.mm(fns, reads=[pT_b, ek_b] + qkc_b + (s8_prev[1] if t > 0 else []), writes=[prd_b])
            u_ts = []
            for bk in range(2):
                u_t, u_b = pu.next()
                fns = []
                for hh in range(4):
                    h = 4 * bk + hh
                    fns.append(lambda h=h, hh=hh: nc.tensor.matmul(
                        u_t[:, hh, :], ktok[:, t, (h // 2) * 128:(h // 2 + 1) * 128], vt_t[:, h, :],
                        start=True, stop=True))
                k.mm(fns, reads=[ktok_b, vt_b], writes=[u_b])
                u_ts.append((u_t, u_b))
            k.mm([(lambda h=h: nc.tensor.matmul(pun[:, h:h + 1], ktok[:, t, (h // 2) * 128:(h // 2 + 1) * 128],
                                                ek_t[:, h:h + 1], start=True, stop=True)) for h in range(8)],
                 reads=[ktok_b, ek_b], writes=[pun_b])
            s8_t, _ = S8r.next()
            s8_bl = S8_bufs[t % 2]
            for h in range(8):
                P0 = (h % 2) * 64
                hp = h // 2
                u_t, u_b = u_ts[h // 4]
                if t == 0:
                    k.op(k.dve, lambda: nc.vector.tensor_copy(out=Tst[P0:P0 + 64, hp, 0:128],
                                                              in_=u_t[P0:P0 + 64, h % 4, :]),
                         reads=[u_b], writes=[T_b[h]])
                    k.op(k.dve, lambda: nc.vector.tensor_copy(out=Tst[P0:P0 + 64, hp, 128:129],
                                                              in_=pun[P0:P0 + 64, h:h + 1]),
                         reads=[pun_b], writes=[T_b[h]])
                else:
                    k.op(k.dve, lambda: nc.vector.scalar_tensor_tensor(
                        out=Tst[P0:P0 + 64, hp, 0:128], in0=Tst[P0:P0 + 64, hp, 0:128],
                        scalar=EE[P0:P0 + 64, t - 1, h:h + 1], in1=u_t[P0:P0 + 64, h % 4, :],
                        op0=ALU.mult, op1=ALU.add), reads=[T_b[h], EE_b, u_b], writes=[T_b[h]])
                    k.op(k.dve, lambda: nc.vector.scalar_tensor_tensor(
                        out=Tst[P0:P0 + 64, hp, 128:129], in0=Tst[P0:P0 + 64, hp, 128:129],
                        scalar=EE[P0:P0 + 64, t - 1, h:h + 1], in1=pun[P0:P0 + 64, h:h + 1],
                        op0=ALU.mult, op1=ALU.add), reads=[T_b[h], EE_b, pun_b], writes=[T_b[h]])
                if t < NTL - 1:
                    k.op(k.pool, lambda: nc.gpsimd.tensor_scalar(
                        out=s8_t[P0:P0 + 64, hp, :], in0=Tst[P0:P0 + 64, hp, :], scalar1=EE[P0:P0 + 64, t, h:h + 1],
                        scalar2=M_DQK ** -0.5, op0=ALU.mult, op1=ALU.mult),
                        reads=[T_b[h], EE_b], writes=[s8_bl[h]])
            s8[t] = (s8_t, s8_bl)
            s8.pop(t - 2, None)

        def stage3a(t):
            c = ctx[t]
            r_ts = c["r"]
            mo_t, mo_b = c["mo"]
            m_t, m_b = sm.next()
            dn, dneg, rc, cc, ss, rstd = (m_t[:, 0:8], m_t[:, 8:16], m_t[:, 16:24], m_t[:, 24:32],
                                          m_t[:, 32:40], m_t[:, 40:48])
            k.op(k.dve, lambda: nc.vector.tensor_tensor(out=dn, in0=prd, in1=EQ[:, t, :], op=ALU.mult),
                 reads=[prd_b, EQ_b], writes=[m_b])
            k.op(k.dve, lambda: nc.vector.tensor_scalar(out=dneg, in0=dn, scalar1=-1.0, scalar2=None, op0=ALU.mult),
                 reads=[m_b], writes=[m_b])
            k.op(k.dve, lambda: nc.vector.tensor_tensor(out=dn, in0=dn, in1=dneg, op=ALU.max),
                 reads=[m_b], writes=[m_b])
            k.op(k.dve, lambda: nc.vector.tensor_scalar(out=dn, in0=dn, scalar1=1.0, scalar2=None, op0=ALU.max),
                 reads=[m_b], writes=[m_b])
            k.op(k.dve, lambda: nc.vector.reciprocal(out=rc, in_=dn), reads=[m_b], writes=[m_b])
            k.op(k.dve, lambda: nc.vector.tensor_tensor(out=cc, in0=rc, in1=EQ[:, t, :], op=ALU.mult),
                 reads=[m_b, EQ_b], writes=[m_b])
            ho_t, ho_b = ho.next()
            for bk in range(2):
                r_t, r_b = r_ts[bk]
                k.op(k.dve, lambda: nc.vector.tensor_tensor(
                    out=ho_t[:, 4 * bk:4 * bk + 4, :], in0=r_t[:],
                    in1=cc[:, 4 * bk:4 * bk + 4].unsqueeze(2).to_broadcast([128, 4, 128]), op=ALU.mult),
                    reads=[r_b, m_b], writes=[ho_b])
            sq_t, sq_b = sq.next()
            k.op(k.pool, lambda: nc.gpsimd.tensor_tensor(out=sq_t[:], in0=ho_t[:], in1=ho_t[:], op=ALU.mult),
                 reads=[ho_b], writes=[sq_b])
            k.op(k.dve, lambda: nc.vector.tensor_reduce(out=ss, in_=sq_t[:], axis=AX.X, op=ALU.add),
                 reads=[sq_b], writes=[m_b])
            rms_rstd(k, ss, m_b, rstd, m_b, M_DV)
            y1_t, y1_b = y1.next()
            k.op(k.dve, lambda: nc.vector.tensor_tensor(
                out=y1_t[:], in0=ho_t[:], in1=rstd.unsqueeze(2).to_broadcast([128, 8, 128]), op=ALU.mult),
                reads=[ho_b, m_b], writes=[y1_b])
            y1f = y1_t[:].rearrange("p a b -> p (a b)")
            k.op(k.pool, lambda: nc.gpsimd.tensor_tensor(out=y1f, in0=y1f, in1=mnw_t[:], op=ALU.mult),
                 reads=[y1_b, mnw_b], writes=[y1_b])
            yb_t, yb_b = c["yb"] = ymb.next()
            k.op(k.dve, lambda: nc.vector.tensor_tensor(out=yb_t[:], in0=y1f, in1=mo_t[:], op=ALU.mult),
                 reads=[y1_b, mo_b], writes=[yb_b])

        def stage3b(t):
            c = ctx.pop(t)
            tsl = slice(t * 128, (t + 1) * 128)
            yb_t, yb_b = c["yb"]
            k.mm([(lambda cc_=cc_: nc.tensor.transpose(ptr[:, cc_, :], yb_t[:, cc_ * 128:(cc_ + 1) * 128], ident))
                  for cc_ in range(8)], reads=[yb_b, cbf_b], writes=[ptr_b])
            yT_t, yT_b = ymT.next()
            k.op(k.act, lambda: nc.scalar.copy(out=yT_t[:], in_=ptr[:]), reads=[ptr_b], writes=[yT_b])
            k.dma(k.sp, S["ymT"][:, tsl].rearrange("(c p) n -> p c n", p=128), yT_t[:], reads=[yT_b])

        stage1(0)
        stage2(0)
        for t in range(NTL):
            if t + 1 < NTL:
                stage1(t + 1)
            stage3a(t)
            if t + 1 < NTL:
                stage2(t + 1)
            stage3b(t)
        k.barrier()


def phase_out(k, S, x_io, w_pa, w_pm, w_out, n_post):
    nc = k.nc
    TT = 512
    NTT = SEQ // TT
    NS = TT // 128
    with ExitStack() as st:
        ws = []
        for nm, w in (("wpa", w_pa), ("wpm", w_pm), ("wout", w_out)):
            t = sb(st, nc, nm, [128, 8, D_MODEL], BF16)
            b = Buf()
            for hh in range(2):
                k.dma(k.pool, t[:, hh * 4:(hh + 1) * 4, :],
                      w[hh * 512:(hh + 1) * 512, :].rearrange("(c p) n -> p c n", p=128), writes=[b], join=True)
            ws.append((t, b))
        (wpa, wpa_b), (wpm, wpm_b), (wout, wout_b) = ws
        npost, npost_b = load_bcast(k, st, "npost", n_post, D_MODEL)
        yaT = Ring(st, nc, "yaTs", 2, [128, 8, TT], BF16)
        ymT = Ring(st, nc, "ymTs", 2, [128, 8, TT], BF16)
        gT = Ring(st, nc, "gTs", 3, [128, 2, TT], F32)
        mg = Ring(st, nc, "mg", 1, [128, 8, TT], BF16)
        t1 = Ring(st, nc, "t1", 2, [128, TT], F32)
        xr = Ring(st, nc, "xr", 2, [128, D_MODEL], F32)
        yo = Ring(st, nc, "yo", 1, [128, D_MODEL], F32)
        junk = Ring(st, nc, "junk", 1, [128, D_MODEL], BF16)
        stat = Ring(st, nc, "stat", 4, [128, 4], F32)
        ppa = Ring(st, nc, "ppa", 2, [128, TT], F32, psum=True)
        ppm = Ring(st, nc, "ppm", 2, [128, TT], F32, psum=True)
        py = Ring(st, nc, "py", 1, [128, 2, 512], F32, psum=True)
        for t in range(NTT):
            T0 = t * TT
            a_t, a_b = yaT.next()
            m_t, m_b = ymT.next()
            k.dma(k.sp, a_t[:], S["yaT"][:, T0:T0 + TT].rearrange("(c p) n -> p c n", p=128), writes=[a_b])
            k.dma(k.sp, m_t[:], S["ymT"][:, T0:T0 + TT].rearrange("(c p) n -> p c n", p=128), writes=[m_b])
            g_t, g_b = mg.next()
            for c in range(8):
                gg_t, gg_b = gT.next()
                k.dma(k.sp, gg_t[:], S["glT"].rearrange("(a r) n -> r a n", a=2)[c * 128:(c + 1) * 128, :, T0:T0 + TT],
                      writes=[gg_b])
                pa_t, pa_b = ppa.next()
                pm_t, pm_b = ppm.next()
                k.mm([(lambda kc=kc: nc.tensor.matmul(pa_t[:], wpa[:, kc, c * 128:(c + 1) * 128], a_t[:, kc, :],
                                                      start=(kc == 0), stop=(kc == 7))) for kc in range(8)],
                     reads=[wpa_b, a_b], writes=[pa_b])
                k.mm([(lambda kc=kc: nc.tensor.matmul(pm_t[:], wpm[:, kc, c * 128:(c + 1) * 128], m_t[:, kc, :],
                                                      start=(kc == 0), stop=(kc == 7))) for kc in range(8)],
                     reads=[wpm_b, m_b], writes=[pm_b])
                u_t, u_b = t1.next()
                k.op(k.dve, lambda: nc.vector.tensor_tensor(out=u_t[:], in0=pa_t[:], in1=gg_t[:, 0, :], op=ALU.mult),
                     reads=[pa_b, gg_b], writes=[u_b])
                v_t, v_b = t1.next()
                k.op(k.dve, lambda: nc.vector.tensor_tensor(out=v_t[:], in0=pm_t[:], in1=gg_t[:, 1, :], op=ALU.mult),
                     reads=[pm_b, gg_b], writes=[v_b])
                k.op(k.pool, lambda: nc.gpsimd.tensor_tensor(out=g_t[:, c, :], in0=u_t[:], in1=v_t[:], op=ALU.add),
                     reads=[u_b, v_b], writes=[g_b])
            for s in range(NS):
                tok0 = T0 + s * 128
                y_t, y_b = py.next()
                fns = []
                for hf in range(2):
                    for kc in range(8):
                        fns.append(lambda kc=kc, hf=hf: nc.tensor.matmul(
                            y_t[:, hf, :], g_t[:, kc, s * 128:(s + 1) * 128], wout[:, kc, hf * 512:(hf + 1) * 512],
                            start=(kc == 0), stop=(kc == 7)))
                k.mm(fns, reads=[g_b, wout_b], writes=[y_b])
                xr_t, xr_b = xr.next()
                k.dma(k.sp, xr_t[:], x_io[tok0:tok0 + 128, :], writes=[xr_b])
                j_t, j_b = junk.next()
                s_t, s_b = stat.next()
                for hf in range(2):
                    k.op(k.act, lambda: nc.scalar.activation(out=j_t[:, hf * 512:(hf + 1) * 512], in_=y_t[:, hf, :],
                                                             func=AF.Square, accum_out=s_t[:, 2 + hf:3 + hf]),
                         reads=[y_b], writes=[j_b, s_b])
                k.op(k.dve, lambda: nc.vector.tensor_tensor(out=s_t[:, 0:1], in0=s_t[:, 2:3], in1=s_t[:, 3:4],
                                                            op=ALU.add), reads=[s_b], writes=[s_b])
                rms_rstd(k, s_t[:, 0:1], s_b, s_t[:, 1:2], s_b, D_MODEL)
                o_t, o_b = yo.next()
                for hf in range(2):
                    k.op(k.dve, lambda: nc.vector.tensor_tensor(
                        out=o_t[:, hf * 512:(hf + 1) * 512], in0=y_t[:, hf, :], in1=npost[:, hf * 512:(hf + 1) * 512],
                        op=ALU.mult), reads=[y_b, npost_b], writes=[o_b])
                k.op(k.dve, lambda: nc.vector.scalar_tensor_tensor(
                    out=xr_t[:], in0=o_t[:], scalar=s_t[:, 1:2], in1=xr_t[:], op0=ALU.mult, op1=ALU.add),
                    reads=[o_b, xr_b, s_b], writes=[xr_b])
                k.dma(k.sp, x_io[tok0:tok0 + 128, :], xr_t[:], reads=[xr_b])
        k.barrier()


PARAM_NAMES = ["ffn1_norm_pre", "ffn1_w_gu", "ffn1_w_down", "ffn1_norm_post", "mix_norm_pre", "w_in",
               "attn_lam_q1", "attn_lam_k1", "attn_lam_q2", "attn_lam_k2", "attn_norm_w", "conv_w", "conv_b",
               "igate_b", "fgate_b", "mlstm_norm_w", "w_proj_a", "w_proj_m", "gate_b", "w_out", "mix_norm_post",
               "ffn2_norm_pre", "ffn2_w_gu", "ffn2_w_down", "ffn2_norm_post"]


def build_program(depth=DEPTH):
    nc = bass.Bass("TRN2", target_bir_lowering=False)

    def din(name, shape, dt=F32):
        return nc.dram_tensor(name, shape, dt, kind="ExternalInput").ap()

    L = depth
    x = din("x", [SEQ, D_MODEL])
    out = nc.dram_tensor("out", [SEQ, D_MODEL], F32, kind="ExternalOutput").ap()
    P = {
        "ffn1_norm_pre": din("ffn1_norm_pre", [L, D_MODEL]),
        "ffn1_w_gu": din("ffn1_w_gu", [L, D_MODEL, 2 * D_FF]),
        "ffn1_w_down": din("ffn1_w_down", [L, D_FF, D_MODEL]),
        "ffn1_norm_post": din("ffn1_norm_post", [L, D_MODEL]),
        "mix_norm_pre": din("mix_norm_pre", [L, D_MODEL]),
        "w_in": din("w_in", [L, D_MODEL, C_IN]),
        "lamp": din("lamp", [L, 256]),
        "anw": din("anw", [L, 128, 1]),
        "convw_pc": din("convw_pc", [L, 128, 8, 4]),
        "convb_pc": din("convb_pc", [L, 128, 8]),
        "igate_b": din("igate_b", [L, 8]),
        "fgate_b": din("fgate_b", [L, 8]),
        "mlstm_norm_w": din("mlstm_norm_w", [L, 1024]),
        "w_proj_a": din("w_proj_a", [L, 1024, D_MODEL]),
        "w_proj_m": din("w_proj_m", [L, 1024, D_MODEL]),
        "gate_b_pc": din("gate_b_pc", [L, 128, 16]),
        "w_out": din("w_out", [L, D_MODEL, D_MODEL]),
        "mix_norm_post": din("mix_norm_post", [L, D_MODEL]),
        "ffn2_norm_pre": din("ffn2_norm_pre", [L, D_MODEL]),
        "ffn2_w_gu": din("ffn2_w_gu", [L, D_MODEL, 2 * D_FF]),
        "ffn2_w_down": din("ffn2_w_down", [L, D_FF, D_MODEL]),
        "ffn2_norm_post": din("ffn2_norm_post", [L, D_MODEL]),
    }
    S = make_scratch(nc)
    k = K(nc)
    with ExitStack() as st:
        cbf, cbf_b, cf32, cf32_b = load_consts(k, st)
        ident = cbf[:, 0, :]
        for l in range(L):
            lam_init = 0.8 - 0.6 * math.exp(-0.3 * l)
            phase_ffn(k, ident, cbf_b, x if l == 0 else out, out, P["ffn1_w_gu"][l], P["ffn1_w_down"][l],
                      P["ffn1_norm_pre"][l], P["ffn1_norm_post"][l])
            phase_proj(k, ident, cbf_b, out, P["w_in"][l], P["mix_norm_pre"][l], P["gate_b_pc"][l], S)
            phase_attn(k, cbf, cbf_b, cf32, cf32_b, S, P["lamp"][l], P["anw"][l], lam_init)
            phase_mlstm(k, cbf, cbf_b, cf32, cf32_b, S, P["convw_pc"][l], P["convb_pc"][l], P["igate_b"][l],
                        P["fgate_b"][l], P["mlstm_norm_w"][l])
            phase_out(k, S, out, P["w_proj_a"][l], P["w_proj_m"][l], P["w_out"][l], P["mix_norm_post"][l])
            phase_ffn(k, ident, cbf_b, out, out, P["ffn2_w_gu"][l], P["ffn2_w_down"][l],
                      P["ffn2_norm_pre"][l], P["ffn2_norm_post"][l])
        k.finish()
    return nc


def host_params(inp, depth=DEPTH):
    f = lambda a: np.ascontiguousarray(np.asarray(a, dtype=np.float32))
    L = depth
    d = {n: f(inp[n]) for n in ["ffn1_norm_pre", "ffn1_w_gu", "ffn1_w_down", "ffn1_norm_post", "mix_norm_pre",
                                "w_in", "igate_b", "fgate_b", "w_proj_a", "w_proj_m", "w_out", "mix_norm_post",
                                "ffn2_norm_pre", "ffn2_w_gu", "ffn2_w_down", "ffn2_norm_post"]}
    d["lamp"] = f(np.concatenate([inp["attn_lam_q1"], inp["attn_lam_q2"], inp["attn_lam_k1"], inp["attn_lam_k2"]],
                                 axis=1))
    d["anw"] = f(np.asarray(inp["attn_norm_w"]).reshape(L, 128, 1))
    cw = np.asarray(inp["conv_w"])
    d["convw_pc"] = f(cw.transpose(0, 2, 1).reshape(L, 8, 128, 4).transpose(0, 2, 1, 3))
    d["convb_pc"] = f(np.asarray(inp["conv_b"]).reshape(L, 8, 128).transpose(0, 2, 1))
    d["mlstm_norm_w"] = f(np.asarray(inp["mlstm_norm_w"]).reshape(L, 1024))
    d["gate_b_pc"] = f(np.asarray(inp["gate_b"]).reshape(L, 16, 128).transpose(0, 2, 1))
    d.update(host_consts())
    return d


_NC_CACHE = {}


def kernel(**inputs):
    if "nc" not in _NC_CACHE:
        _NC_CACHE["nc"] = build_program()
    nc = _NC_CACHE["nc"]
    params = host_params(inputs)
    x = np.asarray(inputs["x"], dtype=np.float32)
    in_maps = []
    for b in range(BATCH):
        m = dict(params)
        m["x"] = np.ascontiguousarray(x[b])
        in_maps.append(m)
    res = run_bass_kernel_spmd(nc, in_maps, core_ids=list(range(BATCH)))
    return np.stack([np.asarray(r["out"], dtype=np.float32) for r in res.results], axis=0)
```

```python
import math
from contextlib import ExitStack

import numpy as np
import concourse.bass as bass
import concourse.mybir as mybir
from concourse.bass_utils import run_bass_kernel_spmd

F32 = mybir.dt.float32
BF16 = mybir.dt.bfloat16
AF = mybir.ActivationFunctionType
ALU = mybir.AluOpType
AX = mybir.AxisListType

D_MODEL = 1024
BATCH = 8
SEQ = 4096
DEPTH = 2
EPS = 1e-6
A_HEADS = 8
A_DHEAD = 64
M_HEADS = 8
M_DQK = 64
M_DV = 128
D_FF = 2816
C_IN = 8208
NT = SEQ // 128

O_AQ, O_AK, O_AV = 0, 1024, 2048
O_MQ, O_MK, O_MV, O_MO = 3072, 3584, 4096, 5120
O_MI, O_MF, O_GL = 6144, 6152, 6160


class Sem:
    def __init__(self, nc, name):
        self.h = nc.alloc_semaphore(name)
        self.v = 0
        self.name = name


class Buf:
    __slots__ = ("w", "r", "name")

    def __init__(self, name=""):
        self.w = {}
        self.r = {}
        self.name = name


class Eng:
    def __init__(self, k, name, e, n_dma_sems=0):
        self.k = k
        self.name = name
        self.e = e
        self.sem = Sem(k.nc, "s_" + name)
        self.waited = {}
        self.dma_sems = [Sem(k.nc, f"d_{name}{i}") for i in range(n_dma_sems)]
        self.dma_rr = 0

    def wait(self, tok):
        sem, v = tok
        if self.waited.get(sem, 0) >= v:
            return
        self.e.wait_ge(sem.h, v)
        self.waited[sem] = v


class K:
    def __init__(self, nc):
        self.nc = nc
        self.pe = Eng(self, "pe", nc.tensor)
        self.act = Eng(self, "act", nc.scalar)
        self.dve = Eng(self, "dve", nc.vector)
        self.pool = Eng(self, "pool", nc.gpsimd, n_dma_sems=4)
        self.sp = Eng(self, "sp", nc.sync, n_dma_sems=12)
        self.engs = [self.pe, self.act, self.dve, self.pool, self.sp]
        self.const_t = nc.alloc_sbuf_tensor("const_cols", [128, 32], F32)
        self.const_b = Buf("consts")
        self.consts = {}
        for v in (EPS, 0.0, math.log(0.5), 1.0):
            self.const(v)

    def const(self, val):
        val = float(val)
        if val not in self.consts:
            i = len(self.consts)
            assert i < self.const_t.shape[1]
            ap = self.const_t[:, i:i + 1]
            self.op(self.pool, lambda: self.nc.gpsimd.memset(ap, val), writes=[self.const_b])
            self.consts[val] = ap
        return self.consts[val]

    def _deps(self, eng, reads, writes, join=False):
        for b in reads:
            for s, v in b.w.items():
                eng.wait((s, v))
        for b in writes:
            if not join:
                for s, v in b.w.items():
                    eng.wait((s, v))
            for s, v in b.r.items():
                eng.wait((s, v))

    def _mark(self, tok, reads, writes, join=False):
        s, v = tok
        for b in reads:
            if b.r.get(s, 0) < v:
                b.r[s] = v
        for b in writes:
            if join:
                b.w[s] = v
            else:
                b.w = {s: v}
            b.r = {}

    def op(self, eng, fn, reads=(), writes=()):
        self._deps(eng, reads, writes)
        ins = fn()
        eng.sem.v += 1
        ins.then_inc(eng.sem.h, 1)
        self._mark((eng.sem, eng.sem.v), reads, writes)

    def mm(self, fns, reads=(), writes=()):
        eng = self.pe
        self._deps(eng, reads, writes)
        ins = None
        for fn in fns:
            ins = fn()
        eng.sem.v += 1
        ins.then_inc(eng.sem.h, 1)
        self._mark((eng.sem, eng.sem.v), reads, writes)

    def dma(self, q, out, in_, reads=(), writes=(), join=False):
        sem = q.dma_sems[q.dma_rr]
        q.dma_rr = (q.dma_rr + 1) % len(q.dma_sems)
        if sem.v:
            q.wait((sem, sem.v))
        self._deps(q, reads, writes, join)
        ins = q.e.dma_start(out=out, in_=in_)
        sem.v += 16
        ins.then_inc(sem.h, 16)
        self._mark((sem, sem.v), reads, writes, join)

    def barrier(self):
        sems = []
        for e in self.engs:
            sems.append(e.sem)
            sems.extend(e.dma_sems)
        for e in self.engs:
            for s in sems:
                if s.v:
                    e.wait((s, s.v))

    def finish(self):
        self.barrier()


_UID = [0]


def uname(name):
    _UID[0] += 1
    return f"{name}_{_UID[0]}"


class Ring:
    def __init__(self, st, nc, name, n, shape, dtype, psum=False):
        self.n = n
        self.i = 0
        self.t = []
        self.b = []
        for j in range(n):
            if psum:
                t = st.enter_context(nc.psum_tensor(uname(name), shape, dtype))
            else:
                t = st.enter_context(nc.sbuf_tensor(uname(name), shape, dtype))
            self.t.append(t)
            self.b.append(Buf(f"{name}{j}"))

    def next(self):
        j = self.i
        self.i = (self.i + 1) % self.n
        return self.t[j], self.b[j]


def sb(st, nc, name, shape, dtype):
    return st.enter_context(nc.sbuf_tensor(uname(name), shape, dtype))


def ps(st, nc, name, shape, dtype):
    return st.enter_context(nc.psum_tensor(uname(name), shape, dtype))


def load_bcast(k, st, name, vec_ap, n):
    nc = k.nc
    t = sb(st, nc, name, [128, n], F32)
    b = Buf(name)
    k.dma(k.sp, t[:], vec_ap.partition_broadcast(128), writes=[b])
    return t, b


def rms_rstd(k, ss, ss_b, rstd, rstd_b, n, eps=EPS, mul=1.0):
    nc = k.nc
    c_eps = k.const(eps)
    c_mul = k.const(math.log(mul))
    k.op(k.act, lambda: nc.scalar.activation(out=rstd, in_=ss, func=AF.Ln, scale=1.0 / n, bias=c_eps),
         reads=[ss_b, k.const_b], writes=[rstd_b])
    k.op(k.act, lambda: nc.scalar.activation(out=rstd, in_=rstd, func=AF.Exp, scale=-0.5, bias=c_mul),
         reads=[rstd_b, k.const_b], writes=[rstd_b])


def norm_h(k, rings, x_src, tok0, npre, npre_b):
    nc = k.nc
    xs, hb, stat = rings
    x_t, x_b = xs.next()
    k.dma(k.sp, x_t[:], x_src[tok0:tok0 + 128, :], writes=[x_b])
    h_t, h_b = hb.next()
    s_t, s_b = stat.next()
    k.op(k.act, lambda: nc.scalar.activation(out=h_t[:], in_=x_t[:], func=AF.Square, accum_out=s_t[:, 0:1]),
         reads=[x_b], writes=[h_b, s_b])
    rms_rstd(k, s_t[:, 0:1], s_b, s_t[:, 1:2], s_b, D_MODEL)
    k.op(k.dve, lambda: nc.vector.scalar_tensor_tensor(out=h_t[:], in0=x_t[:], scalar=s_t[:, 1:2],
                                                       in1=npre[:], op0=ALU.mult, op1=ALU.mult),
         reads=[x_b, s_b, npre_b], writes=[h_b])
    return h_t, h_b


def transp_h(k, ptr, ident, ident_b, h_t, h_b, hT_t, hT_b, s):
    nc = k.nc
    p_t, p_b = ptr.next()
    k.mm([(lambda c=c: nc.tensor.transpose(p_t[:, c, :], h_t[:, c * 128:(c + 1) * 128], ident[:]))
          for c in range(8)], reads=[h_b, ident_b], writes=[p_b])
    k.op(k.act, lambda: nc.scalar.copy(out=hT_t[:, :, s * 128:(s + 1) * 128], in_=p_t[:]),
         reads=[p_b], writes=[hT_b])


def phase_ffn(k, ident, ident_b, x_in, x_out, w_gu, w_down, n_pre, n_post):
    nc = k.nc
    TT = 512
    NTT = SEQ // TT
    NS = TT // 128
    NJ = D_FF // 128
    with ExitStack() as st:
        wgu = sb(st, nc, "wgu", [128, 8, 2 * D_FF], BF16)
        wdn = sb(st, nc, "wdn", [128, NJ, D_MODEL], BF16)
        JB = [(0, 4), (4, 8), (8, 12), (12, 16), (16, 20), (20, 22)]
        wgu_b = {}
        for bi, (j0, j1) in enumerate(JB):
            for gu in range(2):
                c0 = gu * D_FF + j0 * 128
                c1 = gu * D_FF + j1 * 128
                b = Buf()
                for j in range(j0, j1):
                    wgu_b[(gu, j)] = b
                for c in range(8):
                    k.dma(k.pool, wgu[:, c, c0:c1], w_gu[c * 128:(c + 1) * 128, c0:c1], writes=[b], join=True)
        wdn_b = []
        for hh in range(2):
            b = Buf()
            wdn_b.append(b)
            k.dma(k.pool, wdn[:, hh * 11:(hh + 1) * 11, :],
                  w_down[hh * 1408:(hh + 1) * 1408, :].rearrange("(j p) n -> p j n", p=128), writes=[b])
        npre, npre_b = load_bcast(k, st, "npre", n_pre, D_MODEL)
        npost, npost_b = load_bcast(k, st, "npost", n_post, D_MODEL)

        xs = Ring(st, nc, "xs", 2, [128, D_MODEL], F32)
        hb = Ring(st, nc, "hb", 4, [128, D_MODEL], BF16)
        stat = Ring(st, nc, "stat", 8, [128, 4], F32)
        hT = Ring(st, nc, "hT", 1, [128, 8, TT], BF16)
        actT = Ring(st, nc, "actT", 1, [128, NJ, TT], BF16)
        sg = Ring(st, nc, "sg", 2, [128, TT], F32)
        xr = Ring(st, nc, "xr", 2, [128, D_MODEL], F32)
        yo = Ring(st, nc, "yo", 1, [128, D_MODEL], F32)
        ptr = Ring(st, nc, "ptr", 1, [128, 8, 128], BF16, psum=True)
        pg = Ring(st, nc, "pgu", 3, [128, TT], F32, psum=True)
        pu = pg
        py = Ring(st, nc, "py", 2, [128, 2, 512], F32, psum=True)
        nrings = (xs, hb, stat)

        hT_t, hT_b = hT.next()
        hs = [norm_h(k, nrings, x_in, s * 128, npre, npre_b) for s in range(NS)]
        for s in range(NS):
            transp_h(k, ptr, ident, ident_b, hs[s][0], hs[s][1], hT_t, hT_b, s)
        for t in range(NTT):
            if t + 1 < NTT:
                hs = [norm_h(k, nrings, x_in, ((t + 1) * NS + s) * 128, npre, npre_b) for s in range(NS)]
            a_t, a_b = actT.next()
            for j in range(NJ):
                g_t, g_b = pg.next()
                u_t, u_b = pu.next()
                k.mm([(lambda c=c: nc.tensor.matmul(g_t[:], wgu[:, c, j * 128:(j + 1) * 128], hT_t[:, c, :],
                                                    start=(c == 0), stop=(c == 7))) for c in range(8)],
                     reads=[wgu_b[(0, j)], hT_b], writes=[g_b])
                k.mm([(lambda c=c: nc.tensor.matmul(u_t[:], wgu[:, c, D_FF + j * 128:D_FF + (j + 1) * 128],
                                                    hT_t[:, c, :], start=(c == 0), stop=(c == 7)))
                      for c in range(8)],
                     reads=[wgu_b[(1, j)], hT_b], writes=[u_b])
                sg_t, sg_b = sg.next()
                k.op(k.act, lambda: nc.scalar.activation(out=sg_t[:], in_=g_t[:], func=AF.Silu),
                     reads=[g_b], writes=[sg_b])
                k.op(k.dve, lambda: nc.vector.tensor_tensor(out=a_t[:, j, :], in0=sg_t[:], in1=u_t[:],
                                                            op=ALU.mult),
                     reads=[sg_b, u_b], writes=[a_b])
            if t + 1 < NTT:
                hT_t, hT_b = hT.next()
                for s in range(NS):
                    transp_h(k, ptr, ident, ident_b, hs[s][0], hs[s][1], hT_t, hT_b, s)
            for s in range(NS):
                tok0 = (t * NS + s) * 128
                y_t, y_b = py.next()
                fns = []
                for hf in range(2):
                    for j in range(NJ):
                        fns.append(lambda j=j, hf=hf: nc.tensor.matmul(
                            y_t[:, hf, :], a_t[:, j, s * 128:(s + 1) * 128], wdn[:, j, hf * 512:(hf + 1) * 512],
                            start=(j == 0), stop=(j == NJ - 1)))
                k.mm(fns, reads=[a_b] + wdn_b, writes=[y_b])
                xr_t, xr_b = xr.next()
                k.dma(k.sp, xr_t[:], x_in[tok0:tok0 + 128, :], writes=[xr_b])
                o_t, o_b = yo.next()
                s_t, s_b = stat.next()
                for hf in range(2):
                    k.op(k.act, lambda: nc.scalar.activation(out=o_t[:, hf * 512:(hf + 1) * 512], in_=y_t[:, hf, :],
                                                             func=AF.Square, accum_out=s_t[:, 2 + hf:3 + hf]),
                         reads=[y_b], writes=[o_b, s_b])
                k.op(k.dve, lambda: nc.vector.tensor_tensor(out=s_t[:, 0:1], in0=s_t[:, 2:3], in1=s_t[:, 3:4],
                                                            op=ALU.add), reads=[s_b], writes=[s_b])
                rms_rstd(k, s_t[:, 0:1], s_b, s_t[:, 1:2], s_b, D_MODEL, mul=0.5)
                for hf in range(2):
                    k.op(k.dve, lambda: nc.vector.tensor_tensor(
                        out=o_t[:, hf * 512:(hf + 1) * 512], in0=y_t[:, hf, :], in1=npost[:, hf * 512:(hf + 1) * 512],
                        op=ALU.mult), reads=[y_b, npost_b], writes=[o_b])
                k.op(k.dve, lambda: nc.vector.scalar_tensor_tensor(
                    out=xr_t[:], in0=o_t[:], scalar=s_t[:, 1:2], in1=xr_t[:], op0=ALU.mult, op1=ALU.add),
                    reads=[o_b, xr_b, s_b], writes=[xr_b])
                k.dma(k.sp, x_out[tok0:tok0 + 128, :], xr_t[:], reads=[xr_b])
        k.barrier()


def phase_proj(k, ident, ident_b, x_in, w_in, n_pre, gate_b_pc, S):
    nc = k.nc
    TT = 512
    NTT = SEQ // TT
    NS = TT // 128
    with ExitStack() as st:
        win = sb(st, nc, "win", [128, 8, C_IN], BF16)
        blocks = [(0, 512, 0), (512, 1024, 0), (1024, 2048, 0), (2048, 3072, 1), (3072, 4096, 1),
                  (4096, 5120, 2), (5120, 6160, 2), (6160, 7184, 3), (7184, 8208, 3)]
        wb = [Buf() for _ in range(4)]
        for (c0, c1, bi) in blocks:
            for c in range(8):
                k.dma(k.pool, win[:, c, c0:c1], w_in[c * 128:(c + 1) * 128, c0:c1], writes=[wb[bi]], join=True)
        npre, npre_b = load_bcast(k, st, "npre", n_pre, D_MODEL)
        gb = sb(st, nc, "gb", [128, 16], F32)
        gb_b = Buf()
        k.dma(k.sp, gb[:], gate_b_pc, writes=[gb_b])
        xs = Ring(st, nc, "xs", 2, [128, D_MODEL], F32)
        hb = Ring(st, nc, "hb", 4, [128, D_MODEL], BF16)
        stat = Ring(st, nc, "stat", 8, [128, 4], F32)
        ptr = Ring(st, nc, "ptr", 1, [128, 8, 128], BF16, psum=True)
        nrings = (xs, hb, stat)
        hT = Ring(st, nc, "hT", 1, [128, 8, TT], BF16)
        of32 = Ring(st, nc, "of32", 4, [128, 512], F32)
        obf = Ring(st, nc, "obf", 4, [128, 512], BF16)
        pf = Ring(st, nc, "pf", 3, [128, 512], F32, psum=True)
        pt = Ring(st, nc, "pt", 3, [128, 512], F32, psum=True)
        ev = [0]

        def evac_copy(out_ap, in_ap, rd, wr):
            ev[0] ^= 1
            if ev[0]:
                k.op(k.dve, lambda: nc.vector.tensor_copy(out=out_ap, in_=in_ap), reads=rd, writes=wr)
            else:
                k.op(k.act, lambda: nc.scalar.copy(out=out_ap, in_=in_ap), reads=rd, writes=wr)

        hT_t, hT_b = hT.next()
        hs = [norm_h(k, nrings, x_in, s * 128, npre, npre_b) for s in range(NS)]
        for s in range(NS):
            transp_h(k, ptr, ident, ident_b, hs[s][0], hs[s][1], hT_t, hT_b, s)
        for t in range(NTT):
            T0 = t * TT
            if t + 1 < NTT:
                hs = [norm_h(k, nrings, x_in, (t + 1) * TT + s * 128, npre, npre_b) for s in range(NS)]

            def fm_chunk(col0, wbuf):
                p_t, p_b = pf.next()
                k.mm([(lambda c=c: nc.tensor.matmul(p_t[:], win[:, c, col0:col0 + 128], hT_t[:, c, :],
                                                    start=(c == 0), stop=(c == 7))) for c in range(8)],
                     reads=[wbuf, hT_b], writes=[p_b])
                return p_t, p_b

            def tm_block(s, col0, n, wbuf):
                p_t, p_b = pt.next()
                k.mm([(lambda c=c: nc.tensor.matmul(p_t[:, 0:n], hT_t[:, c, s * 128:(s + 1) * 128],
                                                    win[:, c, col0:col0 + n], start=(c == 0), stop=(c == 7)))
                      for c in range(8)], reads=[wbuf, hT_b], writes=[p_b])
                return p_t, p_b

            for c in range(16):
                p_t, p_b = fm_chunk(c * 128, wb[0])
                o_t, o_b = obf.next()
                evac_copy(o_t[:], p_t[:], [p_b], [o_b])
                k.dma(k.sp, S["qkT"][c * 128:(c + 1) * 128, T0:T0 + TT], o_t[:], reads=[o_b])
            for c in range(8):
                p_t, p_b = fm_chunk(O_MQ + c * 128, wb[1])
                o_t, o_b = of32.next()
                evac_copy(o_t[:], p_t[:], [p_b], [o_b])
                k.dma(k.sp, S["mqkT"][c * 128:(c + 1) * 128, T0:T0 + TT], o_t[:], reads=[o_b])
            for s in range(NS):
                tok0 = T0 + s * 128
                for hf in range(2):
                    p_t, p_b = tm_block(s, O_AV + hf * 512, 512, wb[1])
                    o_t, o_b = obf.next()
                    evac_copy(o_t[:], p_t[:], [p_b], [o_b])
                    k.dma(k.sp, S["av"][tok0:tok0 + 128, hf * 512:(hf + 1) * 512], o_t[:], reads=[o_b])
                for hf in range(2):
                    p_t, p_b = tm_block(s, O_MV + hf * 512, 512, wb[2])
                    o_t, o_b = of32.next()
                    evac_copy(o_t[:], p_t[:], [p_b], [o_b])
                    k.dma(k.sp, S["mv"][tok0:tok0 + 128, hf * 512:(hf + 1) * 512], o_t[:], reads=[o_b])
                for hf in range(2):
                    p_t, p_b = tm_block(s, O_MO + hf * 512, 512, wb[2])
                    o_t, o_b = of32.next()
                    k.op(k.act, lambda: nc.scalar.activation(out=o_t[:], in_=p_t[:], func=AF.Sigmoid),
                         reads=[p_b], writes=[o_b])
                    k.dma(k.sp, S["mo"][tok0:tok0 + 128, hf * 512:(hf + 1) * 512], o_t[:], reads=[o_b])
                p_t, p_b = tm_block(s, O_MI, 16, wb[2])
                o_t, o_b = of32.next()
                evac_copy(o_t[:, 0:16], p_t[:, 0:16], [p_b], [o_b])
                k.dma(k.sp, S["mif"][tok0:tok0 + 128, :], o_t[:, 0:16], reads=[o_b])
            for c in range(16):
                p_t, p_b = fm_chunk(O_GL + c * 128, wb[3])
                o_t, o_b = of32.next()
                k.op(k.act, lambda: nc.scalar.activation(out=o_t[:], in_=p_t[:], func=AF.Sigmoid,
                                                         bias=gb[:, c:c + 1]),
                     reads=[p_b, gb_b], writes=[o_b])
                k.dma(k.sp, S["glT"][c * 128:(c + 1) * 128, T0:T0 + TT], o_t[:], reads=[o_b])
            if t + 1 < NTT:
                hT_t, hT_b = hT.next()
                for s in range(NS):
                    transp_h(k, ptr, ident, ident_b, hs[s][0], hs[s][1], hT_t, hT_b, s)
        k.barrier()


def phase_attn(k, cbf, cbf_b, cf32, cf32_b, S, lamp, anw_col, lam_init):
    nc = k.nc
    NG = SEQ // 512
    with ExitStack() as st:
        lt, lt_b = load_bcast(k, st, "lamp", lamp, 256)
        lw = sb(st, nc, "lamw", [128, 136], F32)
        lw_b = Buf()
        k.op(k.dve, lambda: nc.vector.tensor_tensor(out=lw[:, 0:128], in0=lt[:, 0:128], in1=lt[:, 128:256],
                                                    op=ALU.mult), reads=[lt_b], writes=[lw_b])
        k.op(k.dve, lambda: nc.vector.tensor_reduce(out=lw[:, 128:130],
                                                    in_=lw[:, 0:128].rearrange("p (a b) -> p a b", a=2),
                                                    axis=AX.X, op=ALU.add), reads=[lw_b], writes=[lw_b])
        k.op(k.act, lambda: nc.scalar.activation(out=lw[:, 130:132], in_=lw[:, 128:130], func=AF.Exp),
             reads=[lw_b], writes=[lw_b])
        k.op(k.dve, lambda: nc.vector.tensor_tensor(out=lw[:, 132:133], in0=lw[:, 130:131], in1=lw[:, 131:132],
                                                    op=ALU.subtract), reads=[lw_b], writes=[lw_b])
        k.op(k.dve, lambda: nc.vector.tensor_scalar(out=lw[:, 133:134], in0=lw[:, 132:133], scalar1=lam_init,
                                                    scalar2=-1.0, op0=ALU.add, op1=ALU.mult),
             reads=[lw_b], writes=[lw_b])
        neg_lam = lw[:, 133:134]
        nw = sb(st, nc, "anw", [128, 1], F32)
        nw_b = Buf()
        k.dma(k.sp, nw[:], anw_col, writes=[nw_b])
        k.op(k.dve, lambda: nc.vector.tensor_scalar(out=nw[:], in0=nw[:], scalar1=1.0 - lam_init, scalar2=None,
                                                    op0=ALU.mult), reads=[nw_b], writes=[nw_b])

        qT = Ring(st, nc, "qT", 2, [128, SEQ], BF16)
        kT = Ring(st, nc, "kT", 2, [128, SEQ], BF16)
        vh = Ring(st, nc, "vh", 2, [128, NT_(), 128], BF16)
        pT = Ring(st, nc, "pT", 6, [128, 512], BF16)
        f32r = Ring(st, nc, "af", 12, [128, 512], F32)
        yo = Ring(st, nc, "yao", 2, [128, 512], BF16)
        psr = Ring(st, nc, "psS", 4, [128, 512], F32, psum=True)
        po = [ps(st, nc, "po", [128, 512], F32) for _ in range(2)]
        pl = [ps(st, nc, "pl", [128, 512], F32) for _ in range(2)]
        po_b = [Buf(), Buf()]
        pl_b = [Buf(), Buf()]
        ones_bf = cbf[:, 1, :]
        tri = cbf[:, 2, :]

        todo = []
        for h in range(A_HEADS):
            q_t, q_b = qT.next()
            k_t, k_b = kT.next()
            v_t, v_b = vh.next()
            k.dma(k.sp, q_t[:], S["qkT"][h * 128:(h + 1) * 128, :], writes=[q_b])
            k.dma(k.sp, k_t[:], S["qkT"][1024 + h * 128:1024 + (h + 1) * 128, :], writes=[k_b])
            vsrc = S["av"][:, h * 128:(h + 1) * 128].rearrange("(t p) d -> p t d", p=128)
            nq = max(1, NT_() // 8)
            for qd in range(nq):
                t0_, t1_ = qd * NT_() // nq, (qd + 1) * NT_() // nq
                k.dma(k.sp, v_t[:, t0_:t1_, :], vsrc[:, t0_:t1_, :], writes=[v_b], join=True)
            for g in range(NG):
                nkt = 4 * g + 4
                pend = {}

                def stage_a(kt):
                    j = kt - 4 * g
                    c0 = 128 * j if j > 0 else 0
                    outs = []
                    sts = [psr.next() for _ in range(2)]
                    k.mm([(lambda m=m: nc.tensor.matmul(
                        sts[m][0][:, c0:512], k_t[m * 64:(m + 1) * 64, kt * 128:(kt + 1) * 128],
                        q_t[m * 64:(m + 1) * 64, g * 512 + c0:(g + 1) * 512], start=True, stop=True))
                        for m in range(2)], reads=[q_b, k_b], writes=[sts[0][1], sts[1][1]])
                    for m in range(2):
                        s_t, s_b = sts[m]
                        p_t, p_b = pT.next()
                        k.op(k.act, lambda: nc.scalar.activation(out=p_t[:, c0:512], in_=s_t[:, c0:512],
                                                                 func=AF.Exp, scale=A_DHEAD ** -0.5),
                             reads=[s_b], writes=[p_b])
                        if j >= 0:
                            k.op(k.pool, lambda: nc.gpsimd.tensor_tensor(out=p_t[:, c0:c0 + 128],
                                                                         in0=p_t[:, c0:c0 + 128], in1=tri,
                                                                         op=ALU.mult),
                                 reads=[p_b, cbf_b], writes=[p_b])
                        outs.append((p_t, p_b, c0))
                    pend[kt] = outs

                def stage_b(kt):
                    for m in range(2):
                        p_t, p_b, c0 = pend[kt][m]
                        k.mm([lambda: nc.tensor.matmul(po[m][:, c0:512], v_t[:, kt, :], p_t[:, c0:512],
                                                       start=(kt == 0), stop=(kt == nkt - 1)),
                              lambda: nc.tensor.matmul(pl[m][:, c0:512], ones_bf, p_t[:, c0:512],
                                                       start=(kt == 0), stop=(kt == nkt - 1))],
                             reads=[p_b, v_b, cbf_b], writes=[po_b[m], pl_b[m]])
                    del pend[kt]

                stage_a(0)
                for kt in range(1, nkt):
                    stage_a(kt)
                    stage_b(kt - 1)
                    if todo and kt >= 3:
                        todo.pop(0)()
                stage_b(nkt - 1)
                while todo:
                    todo.pop(0)()

                oc, lc = [], []
                for m in range(2):
                    o_t, o_b = f32r.next()
                    k.op(k.dve, lambda: nc.vector.tensor_copy(out=o_t[:], in_=po[m][:]), reads=[po_b[m]], writes=[o_b])
                    oc.append((o_t, o_b))
                    l_t, l_b = f32r.next()
                    k.op(k.act, lambda: nc.scalar.copy(out=l_t[:], in_=pl[m][:]), reads=[pl_b[m]], writes=[l_b])
                    lc.append((l_t, l_b))
                for m in range(2):
                    l_t, l_b = lc[m]
                    o_t, o_b = oc[m]
                    k.op(k.dve, lambda: nc.vector.reciprocal(out=l_t[:], in_=l_t[:]), reads=[l_b], writes=[l_b])
                    k.op(k.dve, lambda: nc.vector.tensor_tensor(out=o_t[:], in0=o_t[:], in1=l_t[:], op=ALU.mult),
                         reads=[o_b, l_b], writes=[o_b])
                d_t, d_b = oc[0]
                k.op(k.dve, lambda: nc.vector.scalar_tensor_tensor(out=d_t[:], in0=oc[1][0][:], scalar=neg_lam,
                                                                   in1=d_t[:], op0=ALU.mult, op1=ALU.add),
                     reads=[oc[1][1], d_b, lw_b], writes=[d_b])

                def make_tail(h=h, g=g, d_t=d_t, d_b=d_b):
                    st8 = {}

                    def e3():
                        st8["sq"] = f32r.next()
                        sq_t, sq_b = st8["sq"]
                        k.op(k.act, lambda: nc.scalar.activation(out=sq_t[:], in_=d_t[:], func=AF.Square),
                             reads=[d_b], writes=[sq_b])

                    def e4():
                        sq_t, sq_b = st8["sq"]
                        st8["ss"] = psr.next()
                        s_t, s_b = st8["ss"]
                        k.mm([lambda: nc.tensor.matmul(s_t[:], cf32[:, 0, :], sq_t[:], start=True, stop=True)],
                             reads=[sq_b, cf32_b], writes=[s_b])

                    def e5():
                        s_t, s_b = st8["ss"]
                        st8["rs"] = f32r.next()
                        rs_t, rs_b = st8["rs"]
                        rms_rstd(k, s_t[:], s_b, rs_t[:], rs_b, 128)

                    def e6():
                        rs_t, rs_b = st8["rs"]
                        y_t, y_b = yo.next()
                        k.op(k.dve, lambda: nc.vector.scalar_tensor_tensor(out=y_t[:], in0=d_t[:], scalar=nw[:, 0:1],
                                                                           in1=rs_t[:], op0=ALU.mult, op1=ALU.mult),
                             reads=[d_b, nw_b, rs_b], writes=[y_b])
                        k.dma(k.sp, S["yaT"][h * 128:(h + 1) * 128, g * 512:(g + 1) * 512], y_t[:], reads=[y_b])
                    return [e3, e4, e5, e6]
                todo.extend(make_tail())
        while todo:
            todo.pop(0)()
        k.barrier()


def NT_():
    return SEQ // 128


def make_scratch(nc, kind="Internal"):
    def d(name, shape, dt):
        return nc.dram_tensor(name, shape, dt, kind=kind).ap()
    return {
        "qkT": d("s_qkT", [2048, SEQ], BF16),
        "av": d("s_av", [SEQ, 1024], BF16),
        "mqkT": d("s_mqkT", [1024, SEQ], F32),
        "mv": d("s_mv", [SEQ, 1024], F32),
        "mo": d("s_mo", [SEQ, 1024], F32),
        "mif": d("s_mif", [SEQ, 16], F32),
        "glT": d("s_glT", [2048, SEQ], F32),
        "yaT": d("s_yaT", [1024, SEQ], BF16),
        "ymT": d("s_ymT", [1024, SEQ], BF16),
    }


def host_consts():
    import ml_dtypes
    p = np.arange(128)[:, None]
    f = np.arange(128)[None, :]
    tri = (p <= f).astype(np.float32)
    cbf = np.stack([np.eye(128, dtype=np.float32), np.ones((128, 128), np.float32), tri], axis=1)
    cf32 = np.stack([np.ones((128, 128), np.float32), tri, tri * (M_DQK ** -0.5)], axis=1)
    return {"cbf": np.ascontiguousarray(cbf).astype(ml_dtypes.bfloat16),
            "cf32": np.ascontiguousarray(cf32).astype(np.float32)}


def load_consts(k, st):
    nc = k.nc
    cbf_d = nc.dram_tensor("cbf", [128, 3, 128], BF16, kind="ExternalInput").ap()
    cf32_d = nc.dram_tensor("cf32", [128, 3, 128], F32, kind="ExternalInput").ap()
    cbf = sb(st, nc, "cbf_sb", [128, 3, 128], BF16)
    cf32 = sb(st, nc, "cf32_sb", [128, 3, 128], F32)
    cbf_b, cf32_b = Buf(), Buf()
    k.dma(k.sp, cbf[:], cbf_d, writes=[cbf_b])
    k.dma(k.sp, cf32[:], cf32_d, writes=[cf32_b])
    return cbf, cbf_b, cf32, cf32_b


def phase_mlstm(k, cbf, cbf_b, cf32, cf32_b, S, convw_pc, convb_pc, igb, fgb, mnw):
    nc = k.nc
    NTL = SEQ // 128
    ident = cbf[:, 0, :]
    ones32 = cf32[:, 0, :]
    triu32 = cf32[:, 1, :]
    mask8 = cf32[:, 2, :]
    with ExitStack() as st:
        qkc = sb(st, nc, "qkc", [128, 8, SEQ], BF16)
        qkc_b = [Buf() for _ in range(8)]
        cw = sb(st, nc, "cw", [128, 8, 4], F32)
        cb = sb(st, nc, "cb", [128, 8], F32)
        cw_b, cb_b = Buf(), Buf()
        k.dma(k.sp, cw[:], convw_pc, writes=[cw_b])
        k.dma(k.sp, cb[:], convb_pc, writes=[cb_b])
        with ExitStack() as st1:
            xp = Ring(st1, nc, "xp", 2, [128, SEQ + 3], F32)
            acc = Ring(st1, nc, "cacc", 1, [128, SEQ], F32)
            for i in range(2):
                k.op(k.pool, lambda: nc.gpsimd.memset(xp.t[i][:, 0:3], 0.0), writes=[xp.b[i]])
            for c in range(8):
                x_t, x_b = xp.next()
                k.dma(k.sp, x_t[:, 3:SEQ + 3], S["mqkT"][c * 128:(c + 1) * 128, :], writes=[x_b])
                a_t, a_b = acc.next()
                k.op(k.dve, lambda: nc.vector.tensor_scalar(out=a_t[:], in0=x_t[:, 0:SEQ], scalar1=cw[:, c, 0:1],
                                                            scalar2=None, op0=ALU.mult),
                     reads=[x_b, cw_b], writes=[a_b])
                for j in range(1, 4):
                    k.op(k.dve, lambda: nc.vector.scalar_tensor_tensor(
                        out=a_t[:], in0=x_t[:, j:j + SEQ], scalar=cw[:, c, j:j + 1], in1=a_t[:],
                        op0=ALU.mult, op1=ALU.add), reads=[x_b, cw_b, a_b], writes=[a_b])
                k.op(k.act, lambda: nc.scalar.activation(out=qkc[:, c, :], in_=a_t[:], func=AF.Silu,
                                                         bias=cb[:, c:c + 1]),
                     reads=[a_b, cb_b], writes=[qkc_b[c]])
        k.barrier()

        ktok = sb(st, nc, "ktok", [128, NTL, 512], BF16)
        ktok_b = Buf()
        NC8 = NTL * 8
        gt = sb(st, nc, "gt", [128, NTL, 16], F32)
        gt_b = Buf()
        gsrc = S["mif"].rearrange("(t p) c -> p t c", p=128)
        nq = max(1, NTL // 8)
        for qd in range(nq):
            t0_, t1_ = qd * NTL // nq, (qd + 1) * NTL // nq
            k.dma(k.sp, gt[:, t0_:t1_, :], gsrc[:, t0_:t1_, :], writes=[gt_b], join=True)
        igb_t, igb_b = load_bcast(k, st, "igb", igb, 8)
        fgb_t, fgb_b = load_bcast(k, st, "fgb", fgb, 8)
        mnw_t, mnw_b = load_bcast(k, st, "mnw", mnw, 1024)
        IG = sb(st, nc, "IG", [128, NTL, 8], F32)
        NLF = sb(st, nc, "NLF", [128, NTL, 8], F32)
        NGc = sb(st, nc, "NGc", [128, NTL, 8], F32)
        EQ = sb(st, nc, "EQ", [128, NTL, 8], F32)
        EK = sb(st, nc, "EK", [128, NTL, 8], F32)
        EE = sb(st, nc, "EE", [128, NTL, 8], F32)
        IG_b, NLF_b, NG_b, EQ_b, EK_b, EE_b = [Buf() for _ in range(6)]

        pa = Ring(st, nc, "pa", 2, [128, 4, 128], F32, psum=True)
        pr = Ring(st, nc, "pr", 2, [128, 4, 128], F32, psum=True)
        pu = Ring(st, nc, "pu", 2, [128, 4, 128], F32, psum=True)
        pm = ps(st, nc, "pm", [128, 512], F32)
        prd, prd_b = pm[:, 0:8], Buf()
        pun, pun_b = pm[:, 8:16], Buf()
        ptr = ps(st, nc, "ptrm", [128, 8, 128], BF16)
        ptr_b = Buf()

        for t in range(NTL):
            k.mm([(lambda c=c: nc.tensor.transpose(ptr[:, c, :], qkc[:, 4 + c, t * 128:(t + 1) * 128], ident))
                  for c in range(4)], reads=qkc_b[4:8] + [cbf_b], writes=[ptr_b])
            k.op(k.act, lambda: nc.scalar.copy(out=ktok[:, t, :], in_=ptr[:, 0:4, :].rearrange("p a b -> p (a b)")),
                 reads=[ptr_b], writes=[ktok_b])

        k.op(k.dve, lambda: nc.vector.tensor_tensor(out=IG[:], in0=gt[:, :, 0:8],
                                                    in1=igb_t[:].unsqueeze(1).to_broadcast([128, NTL, 8]),
                                                    op=ALU.add), reads=[gt_b, igb_b], writes=[IG_b])
        k.op(k.dve, lambda: nc.vector.tensor_tensor(out=NLF[:], in0=gt[:, :, 8:16],
                                                    in1=fgb_t[:].unsqueeze(1).to_broadcast([128, NTL, 8]),
                                                    op=ALU.add), reads=[gt_b, fgb_b], writes=[NLF_b])
        k.op(k.act, lambda: nc.scalar.activation(out=NLF[:], in_=NLF[:], func=AF.Exp, scale=-1.0),
             reads=[NLF_b], writes=[NLF_b])
        c_one = k.const(1.0)
        k.op(k.act, lambda: nc.scalar.activation(out=NLF[:], in_=NLF[:], func=AF.Ln, bias=c_one),
             reads=[NLF_b, k.const_b], writes=[NLF_b])
        nlf2 = NLF[:].rearrange("p a b -> p (a b)")
        pg_t, pg_b = pa.next()
        pg2 = pg_t[:].rearrange("p a b -> p (a b)")
        k.mm([lambda: nc.tensor.matmul(pg2[:, 0:NC8], triu32, nlf2, start=True, stop=True)],
             reads=[NLF_b, cf32_b], writes=[pg_b])
        k.op(k.dve, lambda: nc.vector.tensor_copy(out=NGc[:].rearrange("p a b -> p (a b)"), in_=pg2[:, 0:NC8]),
             reads=[pg_b], writes=[NG_b])
        pe_t, pe_b = pa.next()
        pe2 = pe_t[:].rearrange("p a b -> p (a b)")
        k.mm([lambda: nc.tensor.matmul(pe2[:, 0:NC8], ones32, nlf2, start=True, stop=True)],
             reads=[NLF_b, cf32_b], writes=[pe_b])
        k.op(k.act, lambda: nc.scalar.activation(out=EE[:].rearrange("p a b -> p (a b)"), in_=pe2[:, 0:NC8],
                                                 func=AF.Exp, scale=-1.0), reads=[pe_b], writes=[EE_b])
        k.op(k.act, lambda: nc.scalar.activation(out=EQ[:], in_=NGc[:], func=AF.Exp, scale=-1.0),
             reads=[NG_b], writes=[EQ_b])
        k.op(k.dve, lambda: nc.vector.tensor_tensor(out=EK[:], in0=IG[:], in1=NGc[:], op=ALU.add),
             reads=[IG_b, NG_b], writes=[EK_b])
        k.op(k.act, lambda: nc.scalar.activation(out=EK[:], in_=EK[:], func=AF.Exp), reads=[EK_b], writes=[EK_b])

        mvr = Ring(st, nc, "mvr", 2, [128, 8, 128], F32)
        mor = Ring(st, nc, "mor", 2, [128, 1024], F32)
        vtr = Ring(st, nc, "vtr", 2, [128, 8, 128], BF16)
        ekr = Ring(st, nc, "ekr", 2, [128, 8], BF16)
        ptT = Ring(st, nc, "ptT", 2, [128, 8, 128], BF16)
        Tst = sb(st, nc, "Tst", [128, 4, 129], F32)
        T_b = [Buf() for _ in range(8)]
        S8r = Ring(st, nc, "S8r", 2, [128, 4, 129], BF16)
        S8_bufs = {0: [Buf() for _ in range(8)], 1: [Buf() for _ in range(8)]}
        sm = Ring(st, nc, "msm", 2, [128, 48], F32)
        ho = Ring(st, nc, "mho", 1, [128, 8, 128], F32)
        sq = Ring(st, nc, "msq", 1, [128, 8, 128], F32)
        y1 = Ring(st, nc, "my1", 1, [128, 8, 128], F32)
        ymb = Ring(st, nc, "ymb", 2, [128, 1024], BF16)
        ymT = Ring(st, nc, "ymTs", 2, [128, 8, 128], BF16)
        ctx = {}
        s8 = {}

        def stage1(t):
            c = ctx[t] = {}
            tsl = slice(t * 128, (t + 1) * 128)
            mv_t, mv_b = mvr.next()
            k.dma(k.sp, mv_t[:], S["mv"][tsl, :].rearrange("p (a b) -> p a b", a=8), writes=[mv_b])
            c["mo"] = mor.next()
            k.dma(k.sp, c["mo"][0][:], S["mo"][tsl, :], writes=[c["mo"][1]])
            vt_t, vt_b = c["vt"] = vtr.next()
            k.op(k.dve, lambda: nc.vector.tensor_tensor(out=vt_t[:], in0=mv_t[:],
                                                        in1=EK[:, t, :].unsqueeze(2).to_broadcast([128, 8, 128]),
                                                        op=ALU.mult), reads=[mv_b, EK_b], writes=[vt_b])
            ek_t, ek_b = c["ek"] = ekr.next()
            k.op(k.dve, lambda: nc.vector.tensor_copy(out=ek_t[:], in_=EK[:, t, :]), reads=[EK_b], writes=[ek_b])
            pT_t, pT_b = c["pT"] = ptT.next()
            a_te = [pa.next() for _ in range(2)]
            fns = []
            for i in range(4):
                for e in range(2):
                    P0 = e * 64
                    fns.append(lambda i=i, e=e, P0=P0: nc.tensor.matmul(
                        a_te[e][0][:, i, :], qkc[P0:P0 + 64, 4 + i, tsl], qkc[P0:P0 + 64, i, tsl],
                        start=True, stop=True))
            k.mm(fns, reads=qkc_b, writes=[a_te[0][1], a_te[1][1]])
            for e in range(2):
                k.op(k.dve, lambda: nc.vector.tensor_tensor(
                    out=pT_t[:, 4 * e:4 * e + 4, :], in0=a_te[e][0][:],
                    in1=mask8.unsqueeze(1).to_broadcast([128, 4, 128]), op=ALU.mult),
                    reads=[a_te[e][1], cf32_b], writes=[pT_b])

        def stage2(t):
            c = ctx[t]
            tsl = slice(t * 128, (t + 1) * 128)
            pT_t, pT_b = c["pT"]
            vt_t, vt_b = c["vt"]
            ek_t, ek_b = c["ek"]
            s8_prev = s8.get(t - 1)
            r_ts = c["r"] = []
            for bk in range(2):
                r_t, r_b = pr.next()
                fns = []
                for hh in range(4):
                    h = 4 * bk + hh
                    P0 = (h % 2) * 64
                    fns.append(lambda h=h, hh=hh: nc.tensor.matmul(
                        r_t[:, hh, :], pT_t[:, (h % 2) * 4 + h // 2, :], vt_t[:, h, :], start=True, stop=(t == 0)))
                    if t > 0:
                        fns.append(lambda h=h, hh=hh, P0=P0: nc.tensor.matmul(
                            r_t[:, hh, :], qkc[P0:P0 + 64, h // 2, tsl], s8_prev[0][P0:P0 + 64, h // 2, 0:128],
                            start=False, stop=True))
                rd = [pT_b, vt_b] + qkc_b + (s8_prev[1] if t > 0 else [])
                k.mm(fns, reads=rd, writes=[r_b])
                r_ts.append((r_t, r_b))
            fns = []
            for h in range(8):
                P0 = (h % 2) * 64
                fns.append(lambda h=h: nc.tensor.matmul(prd[:, h:h + 1], pT_t[:, (h % 2) * 4 + h // 2, :],
                                                        ek_t[:, h:h + 1], start=True, stop=(t == 0)))
                if t > 0:
                    fns.append(lambda h=h, P0=P0: nc.tensor.matmul(
                        prd[:, h:h + 1], qkc[P0:P0 + 64, h // 2, tsl], s8_prev[0][P0:P0 + 64, h // 2, 128:129],
                        start=False, stop=True))
            k.mm(fns, reads=[pT_b, ek_b] + qkc_b + (s8_prev[1] if t > 0 else []), writes=[prd_b])
            u_ts = []
            for bk in range(2):
                u_t, u_b = pu.next()
                fns = []
                for hh in range(4):
                    h = 4 * bk + hh
                    fns.append(lambda h=h, hh=hh: nc.tensor.matmul(
                        u_t[:, hh, :], ktok[:, t, (h // 2) * 128:(h // 2 + 1) * 128], vt_t[:, h, :],
                        start=True, stop=True))
                k.mm(fns, reads=[ktok_b, vt_b], writes=[u_b])
                u_ts.append((u_t, u_b))
            k.mm([(lambda h=h: nc.tensor.matmul(pun[:, h:h + 1], ktok[:, t, (h // 2) * 128:(h // 2 + 1) * 128],
                                                ek_t[:, h:h + 1], start=True, stop=True)) for h in range(8)],
                 reads=[ktok_b, ek_b], writes=[pun_b])
            s8_t, _ = S8r.next()
            s8_bl = S8_bufs[t % 2]
            for h in range(8):
                P0 = (h % 2) * 64
                hp = h // 2
                u_t, u_b = u_ts[h // 4]
                if t == 0:
                    k.op(k.dve, lambda: nc.vector.tensor_copy(out=Tst[P0:P0 + 64, hp, 0:128],
                                                              in_=u_t[P0:P0 + 64, h % 4, :]),
                         reads=[u_b], writes=[T_b[h]])
                    k.op(k.dve, lambda: nc.vector.tensor_copy(out=Tst[P0:P0 + 64, hp, 128:129],
                                                              in_=pun[P0:P0 + 64, h:h + 1]),
                         reads=[pun_b], writes=[T_b[h]])
                else:
                    k.op(k.dve, lambda: nc.vector.scalar_tensor_tensor(
                        out=Tst[P0:P0 + 64, hp, 0:128], in0=Tst[P0:P0 + 64, hp, 0:128],
                        scalar=EE[P0:P0 + 64, t - 1, h:h + 1], in1=u_t[P0:P0 + 64, h % 4, :],
                        op0=ALU.mult, op1=ALU.add), reads=[T_b[h], EE_b, u_b], writes=[T_b[h]])
                    k.op(k.dve, lambda: nc.vector.scalar_tensor_tensor(
                        out=Tst[P0:P0 + 64, hp, 128:129], in0=Tst[P0:P0 + 64, hp, 128:129],
                        scalar=EE[P0:P0 + 64, t - 1, h:h + 1], in1=pun[P0:P0 + 64, h:h + 1],
                        op0=ALU.mult, op1=ALU.add), reads=[T_b[h], EE_b, pun_b], writes=[T_b[h]])
                if t < NTL - 1:
                    k.op(k.pool, lambda: nc.gpsimd.tensor_scalar(
                        out=s8_t[P0:P0 + 64, hp, :], in0=Tst[P0:P0 + 64, hp, :], scalar1=EE[P0:P0 + 64, t, h:h + 1],
                        scalar2=M_DQK ** -0.5, op0=ALU.mult, op1=ALU.mult),
                        reads=[T_b[h], EE_b], writes=[s8_bl[h]])
            s8[t] = (s8_t, s8_bl)
            s8.pop(t - 2, None)

        def stage3a(t):
            c = ctx[t]
            r_ts = c["r"]
            mo_t, mo_b = c["mo"]
            m_t, m_b = sm.next()
            dn, dneg, rc, cc, ss, rstd = (m_t[:, 0:8], m_t[:, 8:16], m_t[:, 16:24], m_t[:, 24:32],
                                          m_t[:, 32:40], m_t[:, 40:48])
            k.op(k.dve, lambda: nc.vector.tensor_tensor(out=dn, in0=prd, in1=EQ[:, t, :], op=ALU.mult),
                 reads=[prd_b, EQ_b], writes=[m_b])
            k.op(k.dve, lambda: nc.vector.tensor_scalar(out=dneg, in0=dn, scalar1=-1.0, scalar2=None, op0=ALU.mult),
                 reads=[m_b], writes=[m_b])
            k.op(k.dve, lambda: nc.vector.tensor_tensor(out=dn, in0=dn, in1=dneg, op=ALU.max),
                 reads=[m_b], writes=[m_b])
            k.op(k.dve, lambda: nc.vector.tensor_scalar(out=dn, in0=dn, scalar1=1.0, scalar2=None, op0=ALU.max),
                 reads=[m_b], writes=[m_b])
            k.op(k.dve, lambda: nc.vector.reciprocal(out=rc, in_=dn), reads=[m_b], writes=[m_b])
            k.op(k.dve, lambda: nc.vector.tensor_tensor(out=cc, in0=rc, in1=EQ[:, t, :], op=ALU.mult),
                 reads=[m_b, EQ_b], writes=[m_b])
            ho_t, ho_b = ho.next()
            for bk in range(2):
                r_t, r_b = r_ts[bk]
                k.op(k.dve, lambda: nc.vector.tensor_tensor(
                    out=ho_t[:, 4 * bk:4 * bk + 4, :], in0=r_t[:],
                    in1=cc[:, 4 * bk:4 * bk + 4].unsqueeze(2).to_broadcast([128, 4, 128]), op=ALU.mult),
                    reads=[r_b, m_b], writes=[ho_b])
            sq_t, sq_b = sq.next()
            k.op(k.pool, lambda: nc.gpsimd.tensor_tensor(out=sq_t[:], in0=ho_t[:], in1=ho_t[:], op=ALU.mult),
                 reads=[ho_b], writes=[sq_b])
            k.op(k.dve, lambda: nc.vector.tensor_reduce(out=ss, in_=sq_t[:], axis=AX.X, op=ALU.add),
                 reads=[sq_b], writes=[m_b])
            rms_rstd(k, ss, m_b, rstd, m_b, M_DV)
            y1_t, y1_b = y1.next()
            k.op(k.dve, lambda: nc.vector.tensor_tensor(
                out=y1_t[:], in0=ho_t[:], in1=rstd.unsqueeze(2).to_broadcast([128, 8, 128]), op=ALU.mult),
                reads=[ho_b, m_b], writes=[y1_b])
            y1f = y1_t[:].rearrange("p a b -> p (a b)")
            k.op(k.pool, lambda: nc.gpsimd.tensor_tensor(out=y1f, in0=y1f, in1=mnw_t[:], op=ALU.mult),
                 reads=[y1_b, mnw_b], writes=[y1_b])
            yb_t, yb_b = c["yb"] = ymb.next()
            k.op(k.dve, lambda: nc.vector.tensor_tensor(out=yb_t[:], in0=y1f, in1=mo_t[:], op=ALU.mult),
                 reads=[y1_b, mo_b], writes=[yb_b])

        def stage3b(t):
            c = ctx.pop(t)
            tsl = slice(t * 128, (t + 1) * 128)
            yb_t, yb_b = c["yb"]
            k.mm([(lambda cc_=cc_: nc.tensor.transpose(ptr[:, cc_, :], yb_t[:, cc_ * 128:(cc_ + 1) * 128], ident))
                  for cc_ in range(8)], reads=[yb_b, cbf_b], writes=[ptr_b])
            yT_t, yT_b = ymT.next()
            k.op(k.act, lambda: nc.scalar.copy(out=yT_t[:], in_=ptr[:]), reads=[ptr_b], writes=[yT_b])
            k.dma(k.sp, S["ymT"][:, tsl].rearrange("(c p) n -> p c n", p=128), yT_t[:], reads=[yT_b])

        stage1(0)
        stage2(0)
        for t in range(NTL):
            if t + 1 < NTL:
                stage1(t + 1)
            stage3a(t)
            if t + 1 < NTL:
                stage2(t + 1)
            stage3b(t)
        k.barrier()


def phase_out(k, S, x_io, w_pa, w_pm, w_out, n_post):
    nc = k.nc
    TT = 512
    NTT = SEQ // TT
    NS = TT // 128
    with ExitStack() as st:
        ws = []
        for nm, w in (("wpa", w_pa), ("wpm", w_pm), ("wout", w_out)):
            t = sb(st, nc, nm, [128, 8, D_MODEL], BF16)
            b = Buf()
            for hh in range(2):
                k.dma(k.pool, t[:, hh * 4:(hh + 1) * 4, :],
                      w[hh * 512:(hh + 1) * 512, :].rearrange("(c p) n -> p c n", p=128), writes=[b], join=True)
            ws.append((t, b))
        (wpa, wpa_b), (wpm, wpm_b), (wout, wout_b) = ws
        npost, npost_b = load_bcast(k, st, "npost", n_post, D_MODEL)
        yaT = Ring(st, nc, "yaTs", 2, [128, 8, TT], BF16)
        ymT = Ring(st, nc, "ymTs", 2, [128, 8, TT], BF16)
        gT = Ring(st, nc, "gTs", 3, [128, 2, TT], F32)
        mg = Ring(st, nc, "mg", 1, [128, 8, TT], BF16)
        t1 = Ring(st, nc, "t1", 2, [128, TT], F32)
        xr = Ring(st, nc, "xr", 2, [128, D_MODEL], F32)
        yo = Ring(st, nc, "yo", 1, [128, D_MODEL], F32)
        junk = Ring(st, nc, "junk", 1, [128, D_MODEL], BF16)
        stat = Ring(st, nc, "stat", 4, [128, 4], F32)
        ppa = Ring(st, nc, "ppa", 2, [128, TT], F32, psum=True)
        ppm = Ring(st, nc, "ppm", 2, [128, TT], F32, psum=True)
        py = Ring(st, nc, "py", 2, [128, 2, 512], F32, psum=True)
        for t in range(NTT):
            T0 = t * TT
            a_t, a_b = yaT.next()
            m_t, m_b = ymT.next()
            k.dma(k.sp, a_t[:], S["yaT"][:, T0:T0 + TT].rearrange("(c p) n -> p c n", p=128), writes=[a_b])
            k.dma(k.sp, m_t[:], S["ymT"][:, T0:T0 + TT].rearrange("(c p) n -> p c n", p=128), writes=[m_b])
            g_t, g_b = mg.next()
            for c in range(8):
                gg_t, gg_b = gT.next()
                k.dma(k.sp, gg_t[:], S["glT"].rearrange("(a r) n -> r a n", a=2)[c * 128:(c + 1) * 128, :, T0:T0 + TT],
                      writes=[gg_b])
                pa_t, pa_b = ppa.next()
                pm_t, pm_b = ppm.next()
                k.mm([(lambda kc=kc: nc.tensor.matmul(pa_t[:], wpa[:, kc, c * 128:(c + 1) * 128], a_t[:, kc, :],
                                                      start=(kc == 0), stop=(kc == 7))) for kc in range(8)],
                     reads=[wpa_b, a_b], writes=[pa_b])
                k.mm([(lambda kc=kc: nc.tensor.matmul(pm_t[:], wpm[:, kc, c * 128:(c + 1) * 128], m_t[:, kc, :],
                                                      start=(kc == 0), stop=(kc == 7))) for kc in range(8)],
                     reads=[wpm_b, m_b], writes=[pm_b])
                u_t, u_b = t1.next()
                k.op(k.dve, lambda: nc.vector.tensor_tensor(out=u_t[:], in0=pa_t[:], in1=gg_t[:, 0, :], op=ALU.mult),
                     reads=[pa_b, gg_b], writes=[u_b])
                v_t, v_b = t1.next()
                k.op(k.dve, lambda: nc.vector.tensor_tensor(out=v_t[:], in0=pm_t[:], in1=gg_t[:, 1, :], op=ALU.mult),
                     reads=[pm_b, gg_b], writes=[v_b])
                k.op(k.pool, lambda: nc.gpsimd.tensor_tensor(out=g_t[:, c, :], in0=u_t[:], in1=v_t[:], op=ALU.add),
                     reads=[u_b, v_b], writes=[g_b])
            for s in range(NS):
                tok0 = T0 + s * 128
                y_t, y_b = py.next()
                fns = []
                for hf in range(2):
                    for kc in range(8):
                        fns.append(lambda kc=kc, hf=hf: nc.tensor.matmul(
                            y_t[:, hf, :], g_t[:, kc, s * 128:(s + 1) * 128], wout[:, kc, hf * 512:(hf + 1) * 512],
                            start=(kc == 0), stop=(kc == 7)))
                k.mm(fns, reads=[g_b, wout_b], writes=[y_b])
                xr_t, xr_b = xr.next()
                k.dma(k.sp, xr_t[:], x_io[tok0:tok0 + 128, :], writes=[xr_b])
                j_t, j_b = junk.next()
                s_t, s_b = stat.next()
                for hf in range(2):
                    k.op(k.act, lambda: nc.scalar.activation(out=j_t[:, hf * 512:(hf + 1) * 512], in_=y_t[:, hf, :],
                                                             func=AF.Square, accum_out=s_t[:, 2 + hf:3 + hf]),
                         reads=[y_b], writes=[j_b, s_b])
                k.op(k.dve, lambda: nc.vector.tensor_tensor(out=s_t[:, 0:1], in0=s_t[:, 2:3], in1=s_t[:, 3:4],
                                                            op=ALU.add), reads=[s_b], writes=[s_b])
                rms_rstd(k, s_t[:, 0:1], s_b, s_t[:, 1:2], s_b, D_MODEL)
                o_t, o_b = yo.next()
                for hf in range(2):
                    k.op(k.dve, lambda: nc.vector.tensor_tensor(
                        out=o_t[:, hf * 512:(hf + 1) * 512], in0=y_t[:, hf, :], in1=npost[:, hf * 512:(hf + 1) * 512],
                        op=ALU.mult), reads=[y_b, npost_b], writes=[o_b])
                k.op(k.dve, lambda: nc.vector.scalar_tensor_tensor(
                    out=xr_t[:], in0=o_t[:], scalar=s_t[:, 1:2], in1=xr_t[:], op0=ALU.mult, op1=ALU.add),
                    reads=[o_b, xr_b, s_b], writes=[xr_b])
                k.dma(k.sp, x_io[tok0:tok0 + 128, :], xr_t[:], reads=[xr_b])
        k.barrier()


PARAM_NAMES = ["ffn1_norm_pre", "ffn1_w_gu", "ffn1_w_down", "ffn1_norm_post", "mix_norm_pre", "w_in",
               "attn_lam_q1", "attn_lam_k1", "attn_lam_q2", "attn_lam_k2", "attn_norm_w", "conv_w", "conv_b",
               "igate_b", "fgate_b", "mlstm_norm_w", "w_proj_a", "w_proj_m", "gate_b", "w_out", "mix_norm_post",
               "ffn2_norm_pre", "ffn2_w_gu", "ffn2_w_down", "ffn2_norm_post"]


def build_program(depth=DEPTH):
    nc = bass.Bass("TRN2", target_bir_lowering=False)

    def din(name, shape, dt=F32):
        return nc.dram_tensor(name, shape, dt, kind="ExternalInput").ap()

    L = depth
    x = din("x", [SEQ, D_MODEL])
    out = nc.dram_tensor("out", [SEQ, D_MODEL], F32, kind="ExternalOutput").ap()
    P = {
        "ffn1_norm_pre": din("ffn1_norm_pre", [L, D_MODEL]),
        "ffn1_w_gu": din("ffn1_w_gu", [L, D_MODEL, 2 * D_FF]),
        "ffn1_w_down": din("ffn1_w_down", [L, D_FF, D_MODEL]),
        "ffn1_norm_post": din("ffn1_norm_post", [L, D_MODEL]),
        "mix_norm_pre": din("mix_norm_pre", [L, D_MODEL]),
        "w_in": din("w_in", [L, D_MODEL, C_IN]),
        "lamp": din("lamp", [L, 256]),
        "anw": din("anw", [L, 128, 1]),
        "convw_pc": din("convw_pc", [L, 128, 8, 4]),
        "convb_pc": din("convb_pc", [L, 128, 8]),
        "igate_b": din("igate_b", [L, 8]),
        "fgate_b": din("fgate_b", [L, 8]),
        "mlstm_norm_w": din("mlstm_norm_w", [L, 1024]),
        "w_proj_a": din("w_proj_a", [L, 1024, D_MODEL]),
        "w_proj_m": din("w_proj_m", [L, 1024, D_MODEL]),
        "gate_b_pc": din("gate_b_pc", [L, 128, 16]),
        "w_out": din("w_out", [L, D_MODEL, D_MODEL]),
        "mix_norm_post": din("mix_norm_post", [L, D_MODEL]),
        "ffn2_norm_pre": din("ffn2_norm_pre", [L, D_MODEL]),
        "ffn2_w_gu": din("ffn2_w_gu", [L, D_MODEL, 2 * D_FF]),
        "ffn2_w_down": din("ffn2_w_down", [L, D_FF, D_MODEL]),
        "ffn2_norm_post": din("ffn2_norm_post", [L, D_MODEL]),
    }
    S = make_scratch(nc)
    k = K(nc)
    with ExitStack() as st:
        cbf, cbf_b, cf32, cf32_b = load_consts(k, st)
        ident = cbf[:, 0, :]
        for l in range(L):
            lam_init = 0.8 - 0.6 * math.exp(-0.3 * l)
            phase_ffn(k, ident, cbf_b, x if l == 0 else out, out, P["ffn1_w_gu"][l], P["ffn1_w_down"][l],
                      P["ffn1_norm_pre"][l], P["ffn1_norm_post"][l])
            phase_proj(k, ident, cbf_b, out, P["w_in"][l], P["mix_norm_pre"][l], P["gate_b_pc"][l], S)
            phase_attn(k, cbf, cbf_b, cf32, cf32_b, S, P["lamp"][l], P["anw"][l], lam_init)
            phase_mlstm(k, cbf, cbf_b, cf32, cf32_b, S, P["convw_pc"][l], P["convb_pc"][l], P["igate_b"][l],
                        P["fgate_b"][l], P["mlstm_norm_w"][l])
            phase_out(k, S, out, P["w_proj_a"][l], P["w_proj_m"][l], P["w_out"][l], P["mix_norm_post"][l])
            phase_ffn(k, ident, cbf_b, out, out, P["ffn2_w_gu"][l], P["ffn2_w_down"][l],
                      P["ffn2_norm_pre"][l], P["ffn2_norm_post"][l])
        k.finish()
    return nc


def host_params(inp, depth=DEPTH):
    f = lambda a: np.ascontiguousarray(np.asarray(a, dtype=np.float32))
    L = depth
    d = {n: f(inp[n]) for n in ["ffn1_norm_pre", "ffn1_w_gu", "ffn1_w_down", "ffn1_norm_post", "mix_norm_pre",
                                "w_in", "igate_b", "fgate_b", "w_proj_a", "w_proj_m", "w_out", "mix_norm_post",
                                "ffn2_norm_pre", "ffn2_w_gu", "ffn2_w_down", "ffn2_norm_post"]}
    d["lamp"] = f(np.concatenate([inp["attn_lam_q1"], inp["attn_lam_q2"], inp["attn_lam_k1"], inp["attn_lam_k2"]],
                                 axis=1))
    d["anw"] = f(np.asarray(inp["attn_norm_w"]).reshape(L, 128, 1))
    cw = np.asarray(inp["conv_w"])
    d["convw_pc"] = f(cw.transpose(0, 2, 1).reshape(L, 8, 128, 4).transpose(0, 2, 1, 3))
    d["convb_pc"] = f(np.asarray(inp["conv_b"]).reshape(L, 8, 128).transpose(0, 2, 1))
    d["mlstm_norm_w"] = f(np.asarray(inp["mlstm_norm_w"]).reshape(L, 1024))
    d["gate_b_pc"] = f(np.asarray(inp["gate_b"]).reshape(L, 16, 128).transpose(0, 2, 1))
    d.update(host_consts())
    return d


_NC_CACHE = {}


def kernel(**inputs):
    if "nc" not in _NC_CACHE:
        _NC_CACHE["nc"] = build_program()
    nc = _NC_CACHE["nc"]
    params = host_params(inputs)
    x = np.asarray(inputs["x"], dtype=np.float32)
    in_maps = []
    for b in range(BATCH):
        m = dict(params)
        m["x"] = np.ascontiguousarray(x[b])
        in_maps.append(m)
    res = run_bass_kernel_spmd(nc, in_maps, core_ids=list(range(BATCH)))
    return np.stack([np.asarray(r["out"], dtype=np.float32) for r in res.results], axis=0)
```
